# Optimizing a Trainium2 kernel written in Bass

```python
import jax, jax.numpy as jnp
from jax import lax
import numpy as np

D_MODEL = 1024
BATCH = 8
SEQ = 2048
DEPTH = 4

N_MIXERS = 3
EPS = 1e-6
CONF_KERNEL = 31
GDN_HEADS = 8
GDN_HEAD_DIM = D_MODEL // GDN_HEADS
GDN_CONV = 4
GDN_CHUNK = 64
FOX_HEADS = 8
FOX_HEAD_DIM = D_MODEL // FOX_HEADS
FOX_BLOCK = 128
D_FF = ((8 * D_MODEL // 3 + 127) // 128) * 128
FFN_CONV = 3

kernel_name = "hybrid_conformer_gdn_fox_trunk"


def _rms_norm(x, g):
    xf = x.astype(jnp.float32)
    y = xf * lax.rsqrt(jnp.mean(xf * xf, axis=-1, keepdims=True) + EPS)
    return (y * g.astype(jnp.float32)).astype(x.dtype)


def _layer_norm(x, g, b):
    xf = x.astype(jnp.float32)
    xc = xf - jnp.mean(xf, axis=-1, keepdims=True)
    var = jnp.mean(xc * xc, axis=-1, keepdims=True)
    return (xc * lax.rsqrt(var + EPS) * g.astype(jnp.float32) + b.astype(jnp.float32)).astype(x.dtype)


def _l2norm(x):
    xf = x.astype(jnp.float32)
    return xf * lax.rsqrt(jnp.sum(xf * xf, axis=-1, keepdims=True) + EPS)


def _causal_dwconv(x, w):
    K, C = w.shape
    return lax.conv_general_dilated(
        x, w[:, None, :].astype(x.dtype), window_strides=(1,), padding=[(K - 1, 0)],
        dimension_numbers=("NWC", "WIO", "NWC"), feature_group_count=C)


def conformer_conv(h, w_in, b_in, w_dw, b_dw, ln_g, ln_b, w_out):
    u = h @ w_in + b_in
    val, gate = jnp.split(u, 2, axis=-1)
    u = val * jax.nn.sigmoid(gate)
    u = _causal_dwconv(u, w_dw) + b_dw
    u = jax.nn.silu(_layer_norm(u, ln_g, ln_b))
    return u @ w_out


def _chunk_gated_delta(q, k, v, g, beta):
    bsz, seq, H, Dh = q.shape
    C = GDN_CHUNK
    N = seq // C
    to_chunks = lambda t: t.reshape(bsz, N, C, H, -1).transpose(1, 0, 3, 2, 4)
    q, k, v = to_chunks(q), to_chunks(k), to_chunks(v)
    g = g.reshape(bsz, N, C, H).transpose(1, 0, 3, 2)
    beta = beta.reshape(bsz, N, C, H).transpose(1, 0, 3, 2)
    g = jnp.cumsum(g, axis=-1)
    kb = k * beta[..., None]
    vb = v * beta[..., None]
    idx = jnp.arange(C)
    lower = idx[:, None] >= idx[None, :]
    strict = idx[:, None] > idx[None, :]
    diff = g[..., :, None] - g[..., None, :]
    decay = jnp.where(lower, jnp.exp(jnp.where(lower, diff, 0.0)), 0.0)
    a_mat = jnp.where(strict, jnp.einsum("nbhid,nbhjd->nbhij", kb, k) * decay, 0.0)
    eye = jnp.eye(C, dtype=jnp.float32)
    t_mat = lax.linalg.triangular_solve(eye + a_mat, jnp.broadcast_to(eye, a_mat.shape),
                                        left_side=True, lower=True)
    u = jnp.einsum("nbhij,nbhjd->nbhid", t_mat, vb)
    w = jnp.einsum("nbhij,nbhjd->nbhid", t_mat, kb * jnp.exp(g)[..., None])
    qk = jnp.where(lower, jnp.einsum("nbhid,nbhjd->nbhij", q, k) * decay, 0.0)
    qg = q * jnp.exp(g)[..., None]
    kd = k * jnp.exp(g[..., -1:] - g)[..., None]
    g_last = jnp.exp(g[..., -1])

    def step(state, xs):
        u_n, w_n, qg_n, qk_n, kd_n, gl_n = xs
        v_new = u_n - jnp.einsum("bhck,bhkv->bhcv", w_n, state)
        o_n = jnp.einsum("bhck,bhkv->bhcv", qg_n, state) + jnp.einsum("bhij,bhjv->bhiv", qk_n, v_new)
        state = state * gl_n[..., None, None] + jnp.einsum("bhck,bhcv->bhkv", kd_n, v_new)
        return state, o_n

    s0 = jnp.zeros((bsz, H, Dh, Dh), jnp.float32)
    _, o = lax.scan(step, s0, (u, w, qg, qk, kd, g_last))
    return o.transpose(1, 0, 3, 2, 4).reshape(bsz, seq, H, Dh)


def gated_deltanet(h, w_in, conv_w, a_log, dt_bias, o_norm_g, w_out):
    bsz, seq, _ = h.shape
    H, Dh = GDN_HEADS, GDN_HEAD_DIM
    W = H * Dh
    proj = h @ w_in
    qkv = jax.nn.silu(_causal_dwconv(proj[..., :3 * W], conv_w))
    z = proj[..., 3 * W:4 * W].reshape(bsz, seq, H, Dh)
    a = proj[..., 4 * W:4 * W + H].astype(jnp.float32)
    b = proj[..., 4 * W + H:].astype(jnp.float32)
    q = _l2norm(qkv[..., :W].reshape(bsz, seq, H, Dh)) * (Dh ** -0.5)
    k = _l2norm(qkv[..., W:2 * W].reshape(bsz, seq, H, Dh))
    v = qkv[..., 2 * W:].reshape(bsz, seq, H, Dh).astype(jnp.float32)
    beta = jax.nn.sigmoid(b)
    g = -jnp.exp(a_log.astype(jnp.float32)) * jax.nn.softplus(a + dt_bias.astype(jnp.float32))
    o = _chunk_gated_delta(q, k, v, g, beta)
    o = _rms_norm(o, o_norm_g) * jax.nn.silu(z.astype(jnp.float32))
    return o.astype(h.dtype).reshape(bsz, seq, W) @ w_out


def forgetting_attention(h, w_in, b_f, q_norm_g, k_norm_g, w_out):
    bsz, seq, _ = h.shape
    H, Dh = FOX_HEADS, FOX_HEAD_DIM
    W = H * Dh
    proj = h @ w_in
    q = _rms_norm(proj[..., :W].reshape(bsz, seq, H, Dh), q_norm_g).transpose(0, 2, 1, 3)
    k = _rms_norm(proj[..., W:2 * W].reshape(bsz, seq, H, Dh), k_norm_g).transpose(0, 2, 1, 3)
    v = proj[..., 2 * W:3 * W].reshape(bsz, seq, H, Dh).transpose(0, 2, 1, 3)
    log_f = jax.nn.log_sigmoid(proj[..., 3 * W:].astype(jnp.float32) + b_f.astype(jnp.float32))
    c = jnp.cumsum(log_f, axis=1).transpose(0, 2, 1)
    scale = Dh ** -0.5
    q_idx = jnp.arange(FOX_BLOCK)
    outs = []
    for blk in range(seq // FOX_BLOCK):
        s0 = blk * FOX_BLOCK
        s1 = s0 + FOX_BLOCK
        logits = jnp.einsum("bhqd,bhkd->bhqk", q[:, :, s0:s1], k[:, :, :s1]).astype(jnp.float32) * scale
        logits = logits + c[:, :, s0:s1, None] - c[:, :, None, :s1]
        causal = jnp.arange(s1)[None, :] <= (s0 + q_idx)[:, None]
        p = jax.nn.softmax(jnp.where(causal, logits, -jnp.inf), axis=-1)
        outs.append(jnp.einsum("bhqk,bhkd->bhqd", p.astype(v.dtype), v[:, :, :s1]))
    o = jnp.concatenate(outs, axis=2).transpose(0, 2, 1, 3).reshape(bsz, seq, W)
    return o @ w_out


def conv_ffn(h, w_up, w_dw, w_down):
    u = _causal_dwconv(h @ w_up, w_dw)
    gate, up = jnp.split(u, 2, axis=-1)
    return (jax.nn.silu(gate) * up) @ w_down


def setup_inputs(seed: int = 0) -> dict:
    key = jax.random.key(seed)
    ks = iter(jax.random.split(key, 40))
    n_a = len(range(0, DEPTH, N_MIXERS))
    n_b = len(range(1, DEPTH, N_MIXERS))
    n_c = len(range(2, DEPTH, N_MIXERS))
    D = D_MODEL
    Wg = GDN_HEADS * GDN_HEAD_DIM
    Wf = FOX_HEADS * FOX_HEAD_DIM
    f32 = jnp.float32
    dense = lambda shape, fan_in: jax.random.normal(next(ks), shape, f32) * (fan_in ** -0.5)
    gain = lambda shape: 1.0 + 0.02 * jax.random.normal(next(ks), shape, f32)
    small = lambda shape: 0.02 * jax.random.normal(next(ks), shape, f32)

    x = jax.random.normal(next(ks), (BATCH, SEQ, D), f32)
    mix_norm_g = gain((DEPTH, D))
    ffn_norm_g = gain((DEPTH, D))

    conv_w_in = dense((n_a, D, 2 * D), D)
    conv_b_in = small((n_a, 2 * D))
    conv_w_dw = dense((n_a, CONF_KERNEL, D), CONF_KERNEL)
    conv_b_dw = small((n_a, D))
    conv_ln_g = gain((n_a, D))
    conv_ln_b = small((n_a, D))
    conv_w_out = dense((n_a, D, D), D)

    gdn_w_in = dense((n_b, D, 4 * Wg + 2 * GDN_HEADS), D)
    gdn_conv_w = dense((n_b, GDN_CONV, 3 * Wg), GDN_CONV)
    gdn_a_log = jnp.log(jax.random.uniform(next(ks), (n_b, GDN_HEADS), f32, 1.0, 16.0))
    dt = jnp.exp(jax.random.uniform(next(ks), (n_b, GDN_HEADS), f32, np.log(1e-3), np.log(1e-1)))
    gdn_dt_bias = dt + jnp.log(-jnp.expm1(-dt))
    gdn_o_norm_g = gain((n_b, GDN_HEAD_DIM))
    gdn_w_out = dense((n_b, Wg, D), Wg)

    fox_w_in = dense((n_c, D, 3 * Wf + FOX_HEADS), D)
    fox_b_f = 3.0 + 0.5 * jax.random.normal(next(ks), (n_c, FOX_HEADS), f32)
    fox_q_norm_g = gain((n_c, FOX_HEAD_DIM))
    fox_k_norm_g = gain((n_c, FOX_HEAD_DIM))
    fox_w_out = dense((n_c, Wf, D), Wf)

    ffn_w_up = dense((DEPTH, D, 2 * D_FF), D)
    ffn_w_dw = dense((DEPTH, FFN_CONV, 2 * D_FF), FFN_CONV)
    ffn_w_down = dense((DEPTH, D_FF, D), D_FF)

    return {"x": x, "mix_norm_g": mix_norm_g, "ffn_norm_g": ffn_norm_g,
            "conv_w_in": conv_w_in, "conv_b_in": conv_b_in, "conv_w_dw": conv_w_dw, "conv_b_dw": conv_b_dw,
            "conv_ln_g": conv_ln_g, "conv_ln_b": conv_ln_b, "conv_w_out": conv_w_out,
            "gdn_w_in": gdn_w_in, "gdn_conv_w": gdn_conv_w, "gdn_a_log": gdn_a_log, "gdn_dt_bias": gdn_dt_bias,
            "gdn_o_norm_g": gdn_o_norm_g, "gdn_w_out": gdn_w_out,
            "fox_w_in": fox_w_in, "fox_b_f": fox_b_f, "fox_q_norm_g": fox_q_norm_g, "fox_k_norm_g": fox_k_norm_g,
            "fox_w_out": fox_w_out,
            "ffn_w_up": ffn_w_up, "ffn_w_dw": ffn_w_dw, "ffn_w_down": ffn_w_down}


def reference(x, mix_norm_g, ffn_norm_g,
              conv_w_in, conv_b_in, conv_w_dw, conv_b_dw, conv_ln_g, conv_ln_b, conv_w_out,
              gdn_w_in, gdn_conv_w, gdn_a_log, gdn_dt_bias, gdn_o_norm_g, gdn_w_out,
              fox_w_in, fox_b_f, fox_q_norm_g, fox_k_norm_g, fox_w_out,
              ffn_w_up, ffn_w_dw, ffn_w_down):
    ia = ib = ic = 0
    for layer in range(DEPTH):
        h = _rms_norm(x, mix_norm_g[layer])
        kind = layer % N_MIXERS
        if kind == 0:
            y = conformer_conv(h, conv_w_in[ia], conv_b_in[ia], conv_w_dw[ia], conv_b_dw[ia],
                               conv_ln_g[ia], conv_ln_b[ia], conv_w_out[ia])
            ia += 1
        elif kind == 1:
            y = gated_deltanet(h, gdn_w_in[ib], gdn_conv_w[ib], gdn_a_log[ib], gdn_dt_bias[ib],
                               gdn_o_norm_g[ib], gdn_w_out[ib])
            ib += 1
        else:
            y = forgetting_attention(h, fox_w_in[ic], fox_b_f[ic], fox_q_norm_g[ic], fox_k_norm_g[ic],
                                     fox_w_out[ic])
            ic += 1
        x = x + y.astype(x.dtype)
        h = _rms_norm(x, ffn_norm_g[layer])
        x = x + conv_ffn(h, ffn_w_up[layer], ffn_w_dw[layer], ffn_w_down[layer]).astype(x.dtype)
    return x
```

```python
import contextlib
import os
import numpy as np
import concourse.bass as bass
import concourse.mybir as mybir
from concourse.bass_utils import run_bass_kernel_spmd

F32 = mybir.dt.float32
BF16 = mybir.dt.bfloat16
AF = mybir.ActivationFunctionType
ALU = mybir.AluOpType
AX = mybir.AxisListType

S = 2048
D = 1024
TT = 512
NT = 4
KC = 8
FF = 2816
EPS = 1e-6
ENGS = ["pe", "act", "dve", "pool", "sp"]
BLOCK_ATTR = {"pe": "tensor", "act": "scalar", "dve": "vector", "pool": "gpsimd", "sp": "sync"}


class Buf:
    __slots__ = ("name", "last_w", "readers", "sem", "dma_cnt", "excl")

    def __init__(self, name, excl=False):
        self.name = name
        self.excl = excl
        self.last_w = None
        self.readers = []
        self.sem = None
        self.dma_cnt = 0


class Op:
    __slots__ = ("eng", "fn", "waits", "signal", "idx", "dma_buf", "clock", "sigcnt")

    def __init__(self, eng, fn, idx, dma_buf=None):
        self.eng = eng
        self.fn = fn
        self.idx = idx
        self.waits = []
        self.signal = False
        self.dma_buf = dma_buf
        self.clock = None
        self.sigcnt = None


class Prog:
    def __init__(self, nc):
        self.nc = nc
        self.ops = {e: [] for e in ENGS}
        self.obs = {e: {} for e in ENGS}
        self.dma_bufs = []
        self.pending = {e: [] for e in ENGS}

    def barrier(self):
        toks = [("eng", e, len(self.ops[e]) - 1) for e in ENGS if self.ops[e] and self.ops[e][-1].dma_buf is None]
        for e in ENGS:
            if self.ops[e] and self.ops[e][-1].dma_buf is not None:
                for o in reversed(self.ops[e]):
                    if o.dma_buf is None:
                        toks.append(("eng", e, o.idx))
                        break
        for e in ENGS:
            self.pending[e] = list(toks)

    def _need(self, op, tok):
        e = op.eng
        if tok[0] == "eng":
            _, se, si = tok
            if self.obs[e].get(se, -1) >= si:
                return
            src = self.ops[se][si]
            src.signal = True
            op.waits.append(tok)
            self.obs[e][se] = si
            if src.clock:
                for k, v in src.clock.items():
                    if self.obs[e].get(k, -1) < v:
                        self.obs[e][k] = v
        else:
            _, b, cnt = tok
            key = ("dma", id(b))
            if self.obs[e].get(key, -1) >= cnt:
                return
            op.waits.append(tok)
            self.obs[e][key] = cnt

    def op(self, eng, fn, reads=(), writes=(), dma_out=None):
        lst = self.ops[eng]
        o = Op(eng, fn, len(lst), dma_buf=dma_out)
        if any(r.excl for r in reads):
            writes = list(writes) + [r for r in reads if r.excl and r not in writes]
            reads = [r for r in reads if not r.excl]
        best = {}
        for r in reads:
            t = r.last_w
            if t is not None:
                k = t[1] if t[0] == "eng" else ("dma", id(t[1]))
                if k not in best or best[k][2] < t[2]:
                    best[k] = t
        for w in writes:
            for t in [w.last_w] + w.readers:
                if t is not None:
                    k = t[1] if t[0] == "eng" else ("dma", id(t[1]))
                    if k not in best or best[k][2] < t[2]:
                        best[k] = t
        if self.pending[eng]:
            for t in self.pending[eng]:
                if t[1] == eng:
                    continue
                k = t[1]
                if k not in best or best[k][2] < t[2]:
                    best[k] = t
            self.pending[eng] = []
        for t in best.values():
            if eng == "pe" and t[0] == "eng" and t[1] == "pe":
                continue
            self._need(o, t)
        if dma_out is not None:
            if dma_out.sem is None:
                self.dma_bufs.append(dma_out)
                dma_out.sem = True
            dma_out.dma_cnt += 1
            tok = ("dma", dma_out, dma_out.dma_cnt)
        else:
            tok = ("eng", eng, o.idx)
        o.clock = dict(self.obs[eng])
        for r in reads:
            r.readers.append(tok)
        for w in writes:
            w.last_w = tok
            w.readers = []
        lst.append(o)
        return tok

    def emit(self, final_waits=()):
        nc = self.nc
        CH = 2000
        with contextlib.ExitStack() as st:
            for e in ENGS:
                c = 0
                for o in self.ops[e]:
                    if o.signal and o.dma_buf is None:
                        o.sigcnt = c
                        c += 1
                    else:
                        o.sigcnt = c - 1
            nsig = {e: sum(1 for o in self.ops[e] if o.signal and o.dma_buf is None) for e in ENGS}
            esem = {e: [st.enter_context(nc.semaphore("s_%s%d" % (e, i))) for i in range(max(1, (nsig[e] + CH - 1) // CH))]
                    for e in ENGS}
            for i, b in enumerate(self.dma_bufs):
                b.sem = st.enter_context(nc.semaphore("d%d" % i))
            block = st.enter_context(nc.Block())
            for e in ENGS:
                ops = self.ops[e]
                fw = [b for (fe, b) in final_waits if fe == e]
                if not ops and not fw:
                    continue

                def body(eng, ops=ops, e=e, fw=fw):
                    for o in ops:
                        for t in o.waits:
                            if t[0] == "eng":
                                k = self.ops[t[1]][t[2]].sigcnt
                                eng.wait_ge(esem[t[1]][k // CH], k % CH + 1)
                            else:
                                eng.wait_ge(t[1].sem, 16 * t[2])
                        ins = o.fn(eng)
                        if o.dma_buf is not None:
                            ins.then_inc(o.dma_buf.sem, 16)
                        elif o.signal:
                            ins.then_inc(esem[e][o.sigcnt // CH], 1)
                    for b in fw:
                        eng.wait_ge(b.sem, 16 * b.dma_cnt)

                getattr(block, BLOCK_ATTR[e])(body)


def _cst_layout():
    lay = {}
    off = 0
    for name, n in [("mixg", 32), ("ffng", 32), ("cbin", 32), ("cwdw", 2 * 31 * 8), ("cbdw", 16),
                    ("clng", 16), ("clnb", 16), ("gconv", 96), ("gog", 1), ("fqg", 1), ("fkg", 1),
                    ("fwdw", 4 * 3 * 44), ("fbf", 1), ("galog", 8), ("gdtb", 8), ("fbfb", 8)]:
        lay[name] = off
        off += n
    return lay, off


CL, NCST = _cst_layout()


def _cols(v):
    v = np.asarray(v, dtype=np.float32).reshape(-1, 128)
    return v.T


def pack_consts(inp):
    c = np.zeros((128, NCST), np.float32)

    def put(name, arr):
        arr = np.asarray(arr, np.float32)
        c[:, CL[name]:CL[name] + arr.shape[1]] = arr

    put("mixg", _cols(inp["mix_norm_g"].reshape(-1)))
    put("ffng", _cols(inp["ffn_norm_g"].reshape(-1)))
    put("cbin", _cols(inp["conv_b_in"].reshape(-1)))
    put("cwdw", _cols(inp["conv_w_dw"].reshape(-1)))
    put("cbdw", _cols(inp["conv_b_dw"].reshape(-1)))
    put("clng", _cols(inp["conv_ln_g"].reshape(-1)))
    put("clnb", _cols(inp["conv_ln_b"].reshape(-1)))
    put("gconv", _cols(inp["gdn_conv_w"].reshape(-1)))
    put("gog", _cols(inp["gdn_o_norm_g"].reshape(-1)))
    put("fqg", _cols(inp["fox_q_norm_g"].reshape(-1)))
    put("fkg", _cols(inp["fox_k_norm_g"].reshape(-1)))
    put("fwdw", _cols(inp["ffn_w_dw"].reshape(-1)))
    bf = np.zeros((128, 1), np.float32)
    bf[:8, 0] = np.asarray(inp["fox_b_f"], np.float32).reshape(-1)
    put("fbf", bf)
    put("galog", np.broadcast_to(np.asarray(inp["gdn_a_log"], np.float32).reshape(1, 8), (128, 8)))
    put("gdtb", np.broadcast_to(np.asarray(inp["gdn_dt_bias"], np.float32).reshape(1, 8), (128, 8)))
    put("fbfb", np.broadcast_to(np.asarray(inp["fox_b_f"], np.float32).reshape(1, 8), (128, 8)))
    return c


def tile_w(w, width=256):
    w = np.asarray(w, np.float32)
    K, N = w.shape
    return np.ascontiguousarray(w.reshape(K // 128, 128, N // width, width).transpose(2, 1, 0, 3))


def tile_wk(w, nk=2):
    w = np.asarray(w, np.float32)
    K, N = w.shape
    return np.ascontiguousarray(w.reshape(K // (128 * nk), nk, 128, N).transpose(0, 2, 1, 3))


class K:
    def __init__(self, nc, stages):
        self.nc = nc
        self.P = Prog(nc)
        self.stages = stages
        self.st = contextlib.ExitStack()
        self.bank_rr = 0
        self.slot_rr = 0
        self.uid = 0

    def sb(self, name, shape, dt):
        return self.st.enter_context(self.nc.sbuf_tensor(name, shape, dt))

    def B(self, name=""):
        self.uid += 1
        return Buf("%s%d" % (name, self.uid))

    def take_bank(self):
        while True:
            i = self.bank_rr % 8
            self.bank_rr += 1
            if i not in self.held:
                return i

    def take_slot(self):
        i = self.slot_rr % self.NSLOT
        self.slot_rr += 1
        return i

    def mm(self, bank, out, lhsT, rhs, start, stop, reads, extra_w=()):
        self.P.op("pe", lambda e: e.matmul(out, lhsT, rhs, start=start, stop=stop),
                  reads=reads, writes=[self.bankbuf[bank]] + list(extra_w))

    def tr(self, bank, out, in_, ident, reads):
        self.P.op("pe", lambda e: e.transpose(out, in_, ident), reads=reads, writes=[self.bankbuf[bank]])

    def act(self, out, in_, func, reads, writes, bias=None, scale=None, eng="act"):
        kw = {}
        if bias is not None:
            kw["bias"] = bias
        if scale is not None:
            kw["scale"] = scale
        self.P.op("act", lambda e: e.activation(out=out, in_=in_, func=func, **kw), reads=reads, writes=writes)

    def tt(self, eng, out, in0, in1, op, reads, writes):
        self.P.op(eng, lambda e: e.tensor_tensor(out=out, in0=in0, in1=in1, op=op), reads=reads, writes=writes)

    def stt(self, out, in0, scalar, in1, op0, op1, reads, writes):
        self.P.op("dve", lambda e: e.scalar_tensor_tensor(out=out, in0=in0, scalar=scalar, in1=in1, op0=op0, op1=op1),
                  reads=reads, writes=writes)

    def ts(self, eng, out, in0, s1, s2, op0, op1, reads, writes):
        if op1 is None:
            self.P.op(eng, lambda e: e.tensor_scalar(out=out, in0=in0, scalar1=s1, scalar2=None, op0=op0),
                      reads=reads, writes=writes)
        else:
            self.P.op(eng, lambda e: e.tensor_scalar(out=out, in0=in0, scalar1=s1, scalar2=s2, op0=op0, op1=op1),
                      reads=reads, writes=writes)

    def cp(self, eng, out, in_, reads, writes):
        if eng == "act":
            self.P.op("act", lambda e: e.copy(out=out, in_=in_), reads=reads, writes=writes)
        else:
            self.P.op(eng, lambda e: e.tensor_copy(out=out, in_=in_), reads=reads, writes=writes)

    def memset(self, eng, ap, val, writes):
        self.P.op(eng, lambda e: e.memset(ap, val), writes=writes)

    def wload(self, src_ap, ncols_total, view=None):
        s = self.take_slot()
        dst = self.ring[s][:, 0:ncols_total]
        self.P.op("pool", lambda e: e.dma_start(out=dst, in_=src_ap), writes=[self.slotbuf[s]], dma_out=self.slotbuf[s])
        return s

    def build(self):
        nc = self.nc
        P = self.P
        dt = nc.dram_tensor
        self.xin = dt("xT", [KC, 128, S], F32, kind="ExternalInput").ap()
        self.cst_d = dt("cst", [128, NCST], F32, kind="ExternalInput").ap()
        kinds = set(k_ for k_, _ in self.stages)
        self.used_inputs = ["xT", "cst"]

        def din(name, shape, kind_):
            if kind_ not in kinds:
                return None
            self.used_inputs.append(name)
            return dt(name, shape, F32, kind="ExternalInput").ap()
        self.d_cwin = din("cwin", [2, 8, 128, 2048], "conf")
        self.d_cwout = din("cwout", [2, 4, 128, 2048], "conf")
        self.d_gwin = din("gwin", [16, 128, 2048], "gdn")
        self.d_gwab = din("gwab", [128, 128], "gdn")
        self.d_gwout = din("gwout", [4, 128, 2048], "gdn")
        self.d_fwin = din("fwin", [12, 128, 2048], "fox")
        self.d_fwf = din("fwf", [128, 64], "fox")
        self.d_fwout = din("fwout", [4, 128, 2048], "fox")
        self.d_wup = din("wup", [4, 22, 128, 2048], "ffn")
        self.d_wdn = din("wdn", [4, 11, 128, 2048], "ffn")
        self.yout = dt("yT", [KC, 128, S], F32, kind="ExternalOutput").ap()

        self.xT = self.sb("xT_sb", [128, KC, S], F32)
        self.cst = self.sb("cst_sb", [128, NCST], F32)
        self.NSLOT = 8
        self.ring = [self.sb("ring%d" % i, [128, 2048], BF16) for i in range(self.NSLOT)]
        self.slotbuf = [Buf("slot%d" % i) for i in range(self.NSLOT)]
        self.banks = [self.st.enter_context(nc.psum_tensor("bank%d" % i, [128, 512], F32)) for i in range(8)]
        self.bankbuf = [Buf("bank%d" % i, excl=True) for i in range(8)]
        self.held = set()
        self.xbuf = [[Buf("x%d_%d" % (c, t)) for t in range(NT)] for c in range(KC)]
        self.hbuf = [Buf("h%d" % t) for t in range(NT)]
        self.hb1 = Buf("htile")
        self.cbuf = Buf("cst")
        self.ones_bf = self.sb("ones_bf", [128, 128], BF16)
        self.ones_f = self.sb("ones_f", [128, 128], F32)
        self.ident_f = self.sb("ident_f", [128, 128], F32)
        self.ident_bf = self.sb("ident_bf", [128, 128], BF16)
        self.uincl_f = self.sb("uincl_f", [128, 128], F32)
        self.uincl_bf = self.sb("uincl_bf", [128, 128], BF16)
        self.lstrict_f = self.sb("lstrict_f", [128, 128], F32)
        self.msu_f = self.sb("msu_f", [128, 128], F32)
        self.kb = Buf("consts")
        self.ARENA = 25600
        self.arena = self.sb("arena", [128, self.ARENA], F32)

        P.op("sp", lambda e: e.dma_start(out=self.cst[:], in_=self.cst_d), writes=[self.cbuf], dma_out=self.cbuf)
        kb = self.kb
        self.memset("pool", self.ones_f[:], 1.0, [kb])
        self.memset("pool", self.ones_bf[:], 1.0, [kb])

        def asel(out, base, cm, step, op):
            P.op("pool", lambda e: e.affine_select(out=out, in_=self.ones_f[:], pattern=[[step, 128]], compare_op=op,
                                                   fill=0.0, base=base, channel_multiplier=cm), reads=[kb], writes=[kb])
        asel(self.ident_f[:], 0, -1, 1, ALU.is_equal)
        asel(self.uincl_f[:], 0, -1, 1, ALU.is_ge)
        asel(self.lstrict_f[:], -1, 1, -1, ALU.is_ge)
        asel(self.msu_f[:], -1, -1, 1, ALU.is_ge)
        self.cp("pool", self.ident_bf[:], self.ident_f[:], [kb], [kb])
        self.cp("pool", self.uincl_bf[:], self.uincl_f[:], [kb], [kb])

        for c in range(KC):
            for t in range(NT):
                P.op("sp", lambda e, c=c, t=t: e.dma_start(out=self.xT[:, c, t * TT:(t + 1) * TT],
                                                           in_=self.xin[c, :, t * TT:(t + 1) * TT]),
                     writes=[self.xbuf[c][t]], dma_out=self.xbuf[c][t])

        ia = 0
        for stg in self.stages:
            kind, layer = stg
            P.barrier()
            if kind == "conf":
                self.conformer(layer, ia)
                ia += 1
            elif kind == "gdn":
                self.gdn(layer)
            elif kind == "fox":
                self.fox(layer)
            elif kind == "ffn":
                self.ffn(layer)

        ob = Buf("out")
        for c in range(KC):
            P.op("sp", lambda e, c=c: e.dma_start(out=self.yout[c], in_=self.xT[:, c, :]),
                 reads=self.xbuf[c], writes=[ob], dma_out=ob)
        P.emit(final_waits=[("sp", ob)])
        self.st.close()

    def cc(self, name, idx=0):
        o = CL[name] + idx
        return self.cst[:, o:o + 1]

    def carve(self, off, nwords):
        assert off + nwords <= self.ARENA, (off, nwords)
        return self.arena[:, off:off + nwords]

    def rmsnorm_tile(self, t, gname, layer, hview, hb, ar_off):
        sl = slice(t * TT, (t + 1) * TT)
        sqs = [self.carve(ar_off + i * 256, 256).bitcast(BF16) for i in range(2)]
        lnv = self.carve(ar_off + 512, 512)
        rstd = self.carve(ar_off + 1024, 512)
        bl, br = self.nb_ln, self.nb_rstd
        bk = self.take_bank()
        for c in range(KC):
            sq, bsq = sqs[c % 2], self.nb_sq[c % 2]
            self.act(sq, self.xT[:, c, sl], AF.Square, reads=[self.xbuf[c][t]], writes=[bsq])
            self.mm(bk, self.banks[bk][:], self.ones_bf[:], sq, c == 0, c == KC - 1, reads=[bsq, self.kb])
        self.act(lnv, self.banks[bk][:], AF.Ln, reads=[self.bankbuf[bk], self.kb], writes=[bl], bias=self.eps_col, scale=1.0 / D)
        self.act(rstd, lnv, AF.Exp, reads=[bl], writes=[br], scale=-0.5)
        for c in range(KC):
            self.stt(hview[:, c, :], self.xT[:, c, sl], self.cc(gname, layer * 8 + c), rstd, ALU.mult, ALU.mult,
                     reads=[self.xbuf[c][t], br, self.cbuf], writes=[hb])

    def common_init(self):
        if getattr(self, "_ci", False):
            return
        self._ci = True
        self.nb_sq, self.nb_ln, self.nb_rstd = [Buf("sq0"), Buf("sq1")], Buf("ln"), Buf("rstd")
        self.eps_t = self.sb("eps_t", [128, 4], F32)
        self.memset("pool", self.eps_t[:, 0:1], EPS, [self.kb])
        self.memset("pool", self.eps_t[:, 1:2], 1.0, [self.kb])
        self.memset("pool", self.eps_t[:, 2:3], 0.0, [self.kb])
        self.eps_col = self.eps_t[:, 0:1]
        self.one_col = self.eps_t[:, 1:2]
        self.zero_col = self.eps_t[:, 2:3]

    def ffn(self, layer):
        self.common_init()
        P = self.P
        hT = self.carve(0, 8192).bitcast(BF16).rearrange("p (c t) -> p c t", c=KC)
        self.hT = hT
        for t in range(NT):
            self.rmsnorm_tile(t, "ffng", layer, hT[:, :, t * TT:(t + 1) * TT], self.hbuf[t], 8192)
        GOFF = 8192 + 1536
        gT = [self.carve(GOFF + i * 4096, 4096).bitcast(BF16).rearrange("p (c t) -> p c t", c=4) for i in range(2)]
        gbuf = [[[Buf("g") for _ in range(NT)] for _ in range(4)] for _ in range(2)]
        UOFF = GOFF + 2 * 4096
        NU = 4
        U = [self.carve(UOFF + i * 516, 516) for i in range(NU)]
        ubuf = [Buf("U") for _ in range(NU)]
        AOFF = UOFF + NU * 516
        NA = 4
        ACC = [self.carve(AOFF + i * 512, 512) for i in range(NA)]
        abuf = [Buf("acc") for _ in range(NA)]
        urr = [0]
        arr = [0]
        hreads = self.hbuf

        parts = [(0, 2), (2, 4), (4, 6), (6, 8), (8, 10), (10, 11)]
        for pi, (u0, u1) in enumerate(parts):
            g = gT[pi % 2]
            gb = gbuf[pi % 2]
            for u in range(u0, u1):
                sg = self.wload(self.d_wup[layer, u], 2048)
                su = self.wload(self.d_wup[layer, 11 + u], 2048)
                wg = self.ring[sg][:].rearrange("p (k c) -> p k c", k=KC)
                wu = self.ring[su][:].rearrange("p (k c) -> p k c", k=KC)
                for c2 in range(2):
                    ci = 2 * u + c2
                    lc = ci - 2 * u0
                    prev = {"g": None, "u": None}
                    for t in range(NT):
                        sl = slice(t * TT, (t + 1) * TT)
                        accs = {}
                        for which, w_, slot_, col0 in (("g", wg, sg, ci), ("u", wu, su, 22 + ci)):
                            bk = self.take_bank()
                            for kc in range(KC):
                                self.mm(bk, self.banks[bk][:], w_[:, kc, c2 * 128:(c2 + 1) * 128], self.hT[:, kc, sl],
                                        kc == 0, kc == KC - 1, reads=[self.slotbuf[slot_], self.hbuf[t]])
                            ui = urr[0] % NU
                            urr[0] += 1
                            Ut, Ub = U[ui], ubuf[ui]
                            if t == 0:
                                self.memset("pool", Ut[:, 0:2], 0.0, [Ub])
                            else:
                                pU, pB = prev[which]
                                self.cp("pool", Ut[:, 0:2], pU[:, 512:514], [pB], [Ub])
                            self.act(Ut[:, 2:514], self.banks[bk][:], AF.Copy, reads=[self.bankbuf[bk]], writes=[Ub])
                            prev[which] = (Ut, Ub)
                            ai = arr[0] % NA
                            arr[0] += 1
                            At, Ab = ACC[ai], abuf[ai]
                            wcol = lambda k, col0=col0: self.cc("fwdw", (layer * 3 + k) * 44 + col0)
                            self.act(At, Ut[:, 0:512], AF.Copy, reads=[Ub, self.cbuf], writes=[Ab], scale=wcol(0))
                            self.stt(At, Ut[:, 1:513], wcol(1), At, ALU.mult, ALU.add, reads=[Ub, self.cbuf, Ab], writes=[Ab])
                            self.stt(At, Ut[:, 2:514], wcol(2), At, ALU.mult, ALU.add, reads=[Ub, self.cbuf, Ab], writes=[Ab])
                            accs[which] = (At, Ab)
                        Ag, Agb = accs["g"]
                        Au, Aub = accs["u"]
                        self.act(Ag, Ag, AF.Silu, reads=[Agb], writes=[Agb])
                        self.tt("pool", g[:, lc, sl], Ag, Au, ALU.mult, reads=[Agb, Aub], writes=[gb[lc][t]])
            nch = 2 * (u1 - u0)
            dslots = []
            for u in range(u0, u1):
                dslots.append(self.wload(self.d_wdn[layer, u], 2048))
            for dc in range(KC):
                for t in range(NT):
                    sl = slice(t * TT, (t + 1) * TT)
                    bk = self.take_bank()
                    for lc in range(nch):
                        s_ = dslots[lc // 2]
                        wd = self.ring[s_][:].rearrange("p (k c) -> p k c", k=2)
                        self.mm(bk, self.banks[bk][:], wd[:, lc % 2, dc * 128:(dc + 1) * 128], g[:, lc, sl],
                                lc == 0, lc == nch - 1, reads=[self.slotbuf[s_], gb[lc][t]])
                    self.tt("dve", self.xT[:, dc, sl], self.banks[bk][:], self.xT[:, dc, sl], ALU.add,
                            reads=[self.bankbuf[bk], self.xbuf[dc][t]], writes=[self.xbuf[dc][t]])

    def conformer(self, layer, ia):
        self.common_init()
        P = self.P
        hTt = self.carve(0, 2048).bitcast(BF16).rearrange("p (c t) -> p c t", c=KC)
        G = self.carve(3584, 2176).bitcast(BF16).rearrange("p (c t) -> p c t", c=KC)
        gb = [Buf("G%d" % c) for c in range(KC)]
        DG = [self.carve(5760 + i * 1984, 1984).bitcast(BF16).rearrange("p (k m) -> p k m", k=31) for i in range(2)]
        dgb = [Buf("dg0"), Buf("dg1")]
        CO = self.carve(9728, 4096).rearrange("p (c t) -> p c t", c=KC)
        cob = [Buf("co%d" % c) for c in range(KC)]
        UB = [self.carve(13824 + i * 256, 256).bitcast(BF16) for i in range(2)]
        SQ = [self.carve(14336 + i * 256, 256).bitcast(BF16) for i in range(2)]
        ubb = [Buf("ub0"), Buf("ub1")]
        sqb = [Buf("sqb0"), Buf("sqb1")]
        mean = self.carve(14848, 512)
        msq = self.carve(15360, 512)
        lnv = self.carve(15872, 512)
        rstd = self.carve(16384, 512)
        stb = Buf("stats")
        TMP = [self.carve(16896 + i * 512, 512) for i in range(2)]
        tmb = [Buf("tmp0"), Buf("tmp1")]
        sT = self.carve(17920, 2048).bitcast(BF16).rearrange("p (c t) -> p c t", c=KC)
        stbuf = [Buf("sT%d" % c) for c in range(KC)]
        SG = [self.carve(19968 + i * 512, 512) for i in range(2)]
        sgb = [Buf("sg0"), Buf("sg1")]
        for c in range(KC):
            self.memset("pool", G[:, c, 0:32], 0.0, [gb[c]])
        for t in range(NT):
            sl = slice(t * TT, (t + 1) * TT)
            self.rmsnorm_tile(t, "mixg", layer, hTt, self.hb1, 2048)
            s1 = self.take_bank()
            self.held.add(s1)
            s2 = self.take_bank()
            self.held.add(s2)
            for c in range(KC):
                if c % 2 == 0:
                    sv = self.wload(self.d_cwin[ia, c // 2], 2048)
                    sg_ = self.wload(self.d_cwin[ia, 4 + c // 2], 2048)
                    wv = self.ring[sv][:].rearrange("p (k c) -> p k c", k=KC)
                    wg = self.ring[sg_][:].rearrange("p (k c) -> p k c", k=KC)
                bv = self.take_bank()
                for kc in range(KC):
                    self.mm(bv, self.banks[bv][:], wv[:, kc, (c % 2) * 128:(c % 2 + 1) * 128], hTt[:, kc, :],
                            kc == 0, kc == KC - 1, reads=[self.slotbuf[sv], self.hb1])
                bg = self.take_bank()
                for kc in range(KC):
                    self.mm(bg, self.banks[bg][:], wg[:, kc, (c % 2) * 128:(c % 2 + 1) * 128], hTt[:, kc, :],
                            kc == 0, kc == KC - 1, reads=[self.slotbuf[sg_], self.hb1])
                sg, sgbuf = SG[c % 2], sgb[c % 2]
                self.act(sg, self.banks[bg][:], AF.Sigmoid, reads=[self.bankbuf[bg], self.cbuf], writes=[sgbuf],
                         bias=self.cc("cbin", ia * 16 + 8 + c))
                self.stt(G[:, c, 30:542], self.banks[bv][:], self.cc("cbin", ia * 16 + c), sg, ALU.add, ALU.mult,
                         reads=[self.bankbuf[bv], sgbuf, self.cbuf], writes=[gb[c]])
                dg, dgbuf = DG[c % 2], dgb[c % 2]
                for k in range(31):
                    self.ts("pool", dg[:, k, :], self.ident_bf[:], self.cc("cwdw", (ia * 31 + k) * 8 + c), None, ALU.mult, None,
                            reads=[self.kb, self.cbuf], writes=[dgbuf])
                bc = self.take_bank()
                for k in range(31):
                    self.mm(bc, self.banks[bc][:], dg[:, k, :], G[:, c, k:k + 512], k == 0, k == 30, reads=[dgbuf, gb[c]])
                self.act(CO[:, c, :], self.banks[bc][:], AF.Identity, reads=[self.bankbuf[bc], self.cbuf], writes=[cob[c]],
                         bias=self.cc("cbdw", ia * 8 + c))
                self.cp("pool", G[:, c, 0:30], G[:, c, 512:542], [gb[c]], [gb[c]])
                ub, ubbuf = UB[c % 2], ubb[c % 2]
                sq, sqbuf = SQ[c % 2], sqb[c % 2]
                self.cp("dve", ub, CO[:, c, :], [cob[c]], [ubbuf])
                self.act(sq, CO[:, c, :], AF.Square, reads=[cob[c]], writes=[sqbuf])
                self.mm(s1, self.banks[s1][:], self.ones_bf[:], ub, c == 0, c == KC - 1, reads=[ubbuf, self.kb])
                self.mm(s2, self.banks[s2][:], self.ones_bf[:], sq, c == 0, c == KC - 1, reads=[sqbuf, self.kb])
            self.ts("dve", mean, self.banks[s1][:], 1.0 / D, None, ALU.mult, None, reads=[self.bankbuf[s1]], writes=[stb])
            self.tt("dve", msq, mean, mean, ALU.mult, reads=[stb], writes=[stb])
            self.stt(msq, self.banks[s2][:], 1.0 / D, msq, ALU.mult, ALU.subtract, reads=[self.bankbuf[s2], stb], writes=[stb])
            self.act(lnv, msq, AF.Ln, reads=[stb, self.kb], writes=[stb], bias=self.eps_col)
            self.act(rstd, lnv, AF.Exp, reads=[stb], writes=[stb], scale=-0.5)
            self.held.discard(s1)
            self.held.discard(s2)
            for c in range(KC):
                tm, tmbuf = TMP[c % 2], tmb[c % 2]
                self.tt("dve", tm, CO[:, c, :], mean, ALU.subtract, reads=[cob[c], stb], writes=[tmbuf])
                self.tt("dve", tm, tm, rstd, ALU.mult, reads=[tmbuf, stb], writes=[tmbuf])
                self.act(sT[:, c, :], tm, AF.Silu, reads=[tmbuf, self.cbuf], writes=[stbuf[c]],
                         bias=self.cc("clnb", ia * 8 + c), scale=self.cc("clng", ia * 8 + c))
            wos = [self.wload(self.d_cwout[ia, j], 2048) for j in range(4)]
            for dc in range(KC):
                wo = self.ring[wos[dc // 2]][:].rearrange("p (k c) -> p k c", k=KC)
                bk = self.take_bank()
                for kc in range(KC):
                    self.mm(bk, self.banks[bk][:], wo[:, kc, (dc % 2) * 128:(dc % 2 + 1) * 128], sT[:, kc, :],
                            kc == 0, kc == KC - 1, reads=[self.slotbuf[wos[dc // 2]], stbuf[kc]])
                self.tt("dve", self.xT[:, dc, sl], self.banks[bk][:], self.xT[:, dc, sl], ALU.add,
                        reads=[self.bankbuf[bk], self.xbuf[dc][t]], writes=[self.xbuf[dc][t]])

    def gdn(self, layer):
        self.common_init()
        P = self.P
        hT = self.carve(0, 8192).bitcast(BF16).rearrange("p (c t) -> p c t", c=KC)
        for t in range(NT):
            self.rmsnorm_tile(t, "mixg", layer, hT[:, :, t * TT:(t + 1) * TT], self.hbuf[t], 8192)
        o = [9728]

        def al(n):
            a = self.carve(o[0], n)
            o[0] += n
            return a

        def bf4(n=256):
            return al(n).bitcast(BF16).rearrange("p (b c) -> p b c", b=4)

        gtok, gcs, eg, negeg, egl, gl, beta, negbeta = [al(32) for _ in range(8)]
        abt = al(64)
        gb_ = Buf("gates")
        qnT = [al(256).bitcast(BF16) for _ in range(4)]
        knT = [al(256).bitcast(BF16) for _ in range(4)]
        vtok = [bf4() for _ in range(4)]
        kdtok = [bf4() for _ in range(4)]
        zsT = [al(256).bitcast(BF16) for _ in range(4)]
        TTm = [bf4() for _ in range(4)]
        QKD = [bf4() for _ in range(4)]
        Pm = [bf4() for _ in range(4)]
        Qm = [bf4() for _ in range(4)]
        nm = lambda n: [Buf(n + str(i)) for i in range(4)]
        qnb, knb, vtb, kdb, zsb, ttb, qkb, pmb, qmb = [nm(n) for n in ("qn", "kn", "vt", "kd", "zs", "tt", "qk", "pm", "qm")]
        U = [al(516) for _ in range(2)]
        ub = [Buf("U0"), Buf("U1")]
        ACC = [al(512) for _ in range(2)]
        ab_ = [Buf("A0"), Buf("A1")]
        sqt = al(256).bitcast(BF16)
        lnv = al(512)
        rstd = al(512)
        sqb, lnb, rsb = Buf("sq"), Buf("ln"), Buf("rs")
        decT = al(512)
        decb = Buf("dec")
        tmp = al(512)
        tmpb = Buf("tmp")
        lhsg = [al(128) for _ in range(2)]
        lgb = [Buf("lg0"), Buf("lg1")]
        vbf = al(256).bitcast(BF16)
        vbb = Buf("vbf")
        vnew = bf4()
        vnb = Buf("vnew")
        onb_ = bf4()
        onbuf = Buf("on")
        Sf = al(512).rearrange("p (h c) -> p h c", h=4)
        Sb = bf4()
        sfb, sbb = Buf("Sf"), Buf("Sb")
        haloS = al(48).rearrange("p (c k) -> p c k", c=12)
        hsb = [Buf("hs%d" % i) for i in range(12)]
        ssq = al(8)
        ssb = Buf("ssq")
        wab = al(64).bitcast(BF16).rearrange("p (k c) -> p k c", k=KC)
        negA = al(8)
        assert o[0] <= self.ARENA, o[0]
        wabb = Buf("wab")
        P.op("pool", lambda e: e.dma_start(out=wab.rearrange("p k c -> p (k c)"), in_=self.d_gwab), writes=[wabb], dma_out=wabb)
        nab = Buf("negA")
        self.act(negA, self.cst[:, CL["galog"]:CL["galog"] + 8], AF.Exp, reads=[self.cbuf], writes=[nab])
        self.ts("dve", negA, negA, -1.0, None, ALU.mult, None, reads=[nab], writes=[nab])
        dtb = self.cst[:, CL["gdtb"]:CL["gdtb"] + 8]
        v4 = lambda a: a.rearrange("p (b h) -> p b h", b=4)
        bfbank = lambda bk: self.banks[bk][:].bitcast(BF16)[:, 0:512].rearrange("p (b c) -> p b c", b=4)
        fbank = lambda bk: self.banks[bk][:].rearrange("p (b c) -> p b c", b=4)
        qscale = 128.0 ** -0.5
        for g in range(2):
            self.memset("pool", Sf[:], 0.0, [sfb])
            self.memset("pool", Sb[:], 0.0, [sbb])
            for Q in range(NT):
                sl = slice(Q * TT, (Q + 1) * TT)
                hTt = hT[:, :, sl]
                hb = self.hbuf[Q]
                bk = self.take_bank()
                for blk in range(4):
                    for kc in range(KC):
                        self.mm(bk, self.banks[bk][:, blk * 16:(blk + 1) * 16], hTt[:, kc, blk * 128:(blk + 1) * 128], wab[:, kc, :],
                                kc == 0, kc == KC - 1, reads=[wabb, hb])
                self.cp("dve", abt, self.banks[bk][:, 0:64], [self.bankbuf[bk]], [gb_])
                ab3 = abt.rearrange("p (b c) -> p b c", b=4)
                self.act(v4(beta), ab3[:, :, 8:16], AF.Sigmoid, reads=[gb_], writes=[gb_])
                self.ts("dve", negbeta, beta, -1.0, None, ALU.mult, None, reads=[gb_], writes=[gb_])
                self.tt("dve", v4(gtok), ab3[:, :, 0:8], dtb.unsqueeze(1).broadcast_to([128, 4, 8]), ALU.add, reads=[gb_, self.cbuf], writes=[gb_])
                self.act(gtok, gtok, AF.Exp, reads=[gb_], writes=[gb_])
                self.act(gtok, gtok, AF.Ln, reads=[gb_, self.kb], writes=[gb_], bias=self.one_col)
                self.tt("dve", v4(gtok), v4(gtok), negA.unsqueeze(1).broadcast_to([128, 4, 8]), ALU.mult, reads=[gb_, nab], writes=[gb_])
                bk = self.take_bank()
                self.mm(bk, self.banks[bk][:, 0:32], self.uincl_f[:], gtok, True, True, reads=[gb_, self.kb])
                b2 = self.take_bank()
                self.mm(b2, self.banks[b2][:, 0:32], self.ones_f[:], gtok, True, True, reads=[gb_, self.kb])
                self.cp("dve", gcs, self.banks[bk][:, 0:32], [self.bankbuf[bk]], [gb_])
                self.act(eg, gcs, AF.Exp, reads=[gb_], writes=[gb_])
                self.ts("dve", negeg, eg, -1.0, None, ALU.mult, None, reads=[gb_], writes=[gb_])
                self.tt("dve", egl, self.banks[b2][:, 0:32], gcs, ALU.subtract, reads=[self.bankbuf[b2], gb_], writes=[gb_])
                self.act(egl, egl, AF.Exp, reads=[gb_], writes=[gb_])
                self.act(gl, self.banks[b2][:, 0:32], AF.Exp, reads=[self.bankbuf[b2]], writes=[gb_])
                ui = 0
                for hh in range(4):
                    h = 4 * g + hh
                    for which in range(4):
                        if hh % 2 == 0:
                            pass
                        sw = self.wload(self.d_gwin[which * 4 + h // 2], 2048) if (hh % 2 == 0 or True) else None
                        w_ = self.ring[sw][:].rearrange("p (k c) -> p k c", k=KC)
                        bk = self.take_bank()
                        for kc in range(KC):
                            self.mm(bk, self.banks[bk][:], w_[:, kc, (h % 2) * 128:(h % 2 + 1) * 128], hTt[:, kc, :],
                                    kc == 0, kc == KC - 1, reads=[self.slotbuf[sw], hb])
                        if which == 3:
                            self.act(zsT[hh], self.banks[bk][:], AF.Silu, reads=[self.bankbuf[bk]], writes=[zsb[hh]])
                            continue
                        ci = which * 4 + hh
                        chunk = which * 8 + h
                        Ut, Ub = U[ui % 2], ub[ui % 2]
                        At, Ab = ACC[ui % 2], ab_[ui % 2]
                        ui += 1
                        if Q == 0:
                            self.memset("pool", Ut[:, 0:3], 0.0, [Ub])
                        else:
                            self.cp("pool", Ut[:, 0:3], haloS[:, ci, 0:3], [hsb[ci]], [Ub])
                        self.act(Ut[:, 3:515], self.banks[bk][:], AF.Copy, reads=[self.bankbuf[bk]], writes=[Ub])
                        self.cp("pool", haloS[:, ci, 0:3], Ut[:, 512:515], [Ub], [hsb[ci]])
                        wc = lambda k, chunk=chunk: self.cc("gconv", k * 24 + chunk)
                        self.act(At, Ut[:, 0:512], AF.Copy, reads=[Ub, self.cbuf], writes=[Ab], scale=wc(0))
                        for k in range(1, 4):
                            self.stt(At, Ut[:, k:k + 512], wc(k), At, ALU.mult, ALU.add, reads=[Ub, self.cbuf, Ab], writes=[Ab])
                        self.act(At, At, AF.Silu, reads=[Ab], writes=[Ab])
                        if which < 2:
                            self.act(sqt, At, AF.Square, reads=[Ab], writes=[sqb])
                            b2 = self.take_bank()
                            self.mm(b2, self.banks[b2][:], self.ones_bf[:], sqt, True, True, reads=[sqb, self.kb])
                            self.act(lnv, self.banks[b2][:], AF.Ln, reads=[self.bankbuf[b2], self.kb], writes=[lnb], bias=self.eps_col)
                            self.act(rstd, lnv, AF.Exp, reads=[lnb], writes=[rsb], scale=-0.5)
                            if which == 0:
                                self.stt(qnT[hh], At, qscale, rstd, ALU.mult, ALU.mult, reads=[Ab, rsb], writes=[qnb[hh]])
                            else:
                                self.tt("dve", knT[hh], At, rstd, ALU.mult, reads=[Ab, rsb], writes=[knb[hh]])
                        else:
                            self.cp("dve", vbf, At, [Ab], [vbb])
                            bt = self.take_bank()
                            for blk in range(4):
                                self.tr(bt, bfbank(bt)[:, blk, :], vbf[:, blk * 128:(blk + 1) * 128], self.ident_bf[:], reads=[vbb, self.kb])
                            self.cp("act", vtok[hh], bfbank(bt), [self.bankbuf[bt]], [vtb[hh]])
                    bt = self.take_bank()
                    for blk in range(4):
                        self.tr(bt, bfbank(bt)[:, blk, :], knT[hh][:, blk * 128:(blk + 1) * 128], self.ident_bf[:], reads=[knb[hh], self.kb])
                    for blk in range(4):
                        self.act(kdtok[hh][:, blk, :], bfbank(bt)[:, blk, :], AF.Copy, reads=[self.bankbuf[bt], gb_], writes=[kdb[hh]],
                                 scale=egl[:, blk * 8 + h:blk * 8 + h + 1])
                    bd = self.take_bank()
                    for blk in range(4):
                        lg, lgbuf = lhsg[blk % 2], lgb[blk % 2]
                        self.ts("pool", lg, self.lstrict_f[:], gtok[:, blk * 8 + h:blk * 8 + h + 1], None, ALU.mult, None,
                                reads=[self.kb, gb_], writes=[lgbuf])
                        self.mm(bd, self.banks[bd][:, blk * 128:(blk + 1) * 128], lg, self.uincl_f[:], True, True, reads=[lgbuf, self.kb])
                    self.act(decT, self.banks[bd][:], AF.Exp, reads=[self.bankbuf[bd]], writes=[decb])
                    d3 = decT.rearrange("p (b c) -> p b c", b=4)
                    t3 = tmp.rearrange("p (b c) -> p b c", b=4)
                    bkk = self.take_bank()
                    for blk in range(4):
                        ks = knT[hh][:, blk * 128:(blk + 1) * 128]
                        self.mm(bkk, self.banks[bkk][:, blk * 128:(blk + 1) * 128], ks, ks, True, True, reads=[knb[hh]])
                    self.tt("dve", tmp, self.banks[bkk][:], decT, ALU.mult, reads=[self.bankbuf[bkk], decb], writes=[tmpb])
                    for blk in range(4):
                        self.stt(Pm[hh][:, blk, :], t3[:, blk, :], negbeta[:, blk * 8 + h:blk * 8 + h + 1], self.msu_f[:], ALU.mult, ALU.mult,
                                 reads=[tmpb, gb_, self.kb], writes=[pmb[hh]])
                    bt = self.take_bank()
                    for blk in range(4):
                        self.tr(bt, bfbank(bt)[:, blk, :], Pm[hh][:, blk, :], self.ident_bf[:], reads=[pmb[hh], self.kb])
                    self.cp("act", Qm[hh], bfbank(bt), [self.bankbuf[bt]], [qmb[hh]])
                    self.tt("pool", TTm[hh], Pm[hh], self.ident_bf[:].unsqueeze(1).broadcast_to([128, 4, 128]), ALU.add,
                            reads=[pmb[hh], self.kb], writes=[ttb[hh]])
                    bq = self.take_bank()
                    for blk in range(4):
                        self.mm(bq, self.banks[bq][:, blk * 128:(blk + 1) * 128], knT[hh][:, blk * 128:(blk + 1) * 128],
                                qnT[hh][:, blk * 128:(blk + 1) * 128], True, True, reads=[knb[hh], qnb[hh]])
                    self.tt("dve", tmp, self.banks[bq][:], decT, ALU.mult, reads=[self.bankbuf[bq], decb], writes=[tmpb])
                    self.tt("dve", QKD[hh], t3, self.uincl_f[:].unsqueeze(1).broadcast_to([128, 4, 128]), ALU.mult,
                            reads=[tmpb, self.kb], writes=[qkb[hh]])
                for lev in range(1, 7):
                    for hh in range(4):
                        bq = self.take_bank()
                        for blk in range(4):
                            self.mm(bq, self.banks[bq][:, blk * 128:(blk + 1) * 128], Pm[hh][:, blk, :], Qm[hh][:, blk, :], True, True,
                                    reads=[pmb[hh], qmb[hh]])
                        if lev < 6:
                            bp = self.take_bank()
                            for blk in range(4):
                                self.mm(bp, self.banks[bp][:, blk * 128:(blk + 1) * 128], Qm[hh][:, blk, :], Pm[hh][:, blk, :], True, True,
                                        reads=[pmb[hh], qmb[hh]])
                            self.cp("act", Pm[hh], fbank(bp), [self.bankbuf[bp]], [pmb[hh]])
                        self.cp("dve", Qm[hh], fbank(bq), [self.bankbuf[bq]], [qmb[hh]])
                        br = self.take_bank()
                        for blk in range(4):
                            self.mm(br, self.banks[br][:, blk * 128:(blk + 1) * 128], Qm[hh][:, blk, :], TTm[hh][:, blk, :], True, True,
                                    reads=[qmb[hh], ttb[hh]])
                        self.tt("dve", TTm[hh], fbank(br), TTm[hh], ALU.add, reads=[self.bankbuf[br], ttb[hh]], writes=[ttb[hh]])
                rbuf = vbf.rearrange("p (b c) -> p b c", b=4)
                otok = decT.rearrange("p (b c) -> p b c", b=4)
                o2s = tmp.rearrange("p (b c) -> p b c", b=4)
                for blk in range(4):
                    bs = slice(blk * 128, (blk + 1) * 128)
                    col = lambda a, hh: a[:, blk * 8 + 4 * g + hh:blk * 8 + 4 * g + hh + 1]
                    bks = self.take_bank()
                    for hh in range(4):
                        self.mm(bks, self.banks[bks][:, hh * 128:(hh + 1) * 128], knT[hh][:, bs], Sb[:, hh, :], True, True,
                                reads=[knb[hh], sbb])
                    for hh in range(4):
                        self.stt(rbuf[:, hh, :], self.banks[bks][:, hh * 128:(hh + 1) * 128], col(negeg, hh), vtok[hh][:, blk, :],
                                 ALU.mult, ALU.add, reads=[self.bankbuf[bks], gb_, vtb[hh]], writes=[vbb])
                    bvn = self.take_bank()
                    for hh in range(4):
                        self.mm(bvn, self.banks[bvn][:, hh * 128:(hh + 1) * 128], TTm[hh][:, blk, :], rbuf[:, hh, :], True, True,
                                reads=[ttb[hh], vbb])
                    for hh in range(4):
                        self.act(vnew[:, hh, :], self.banks[bvn][:, hh * 128:(hh + 1) * 128], AF.Copy, reads=[self.bankbuf[bvn], gb_],
                                 writes=[vnb], scale=col(beta, hh))
                    bo1 = self.take_bank()
                    for hh in range(4):
                        self.mm(bo1, self.banks[bo1][:, hh * 128:(hh + 1) * 128], qnT[hh][:, bs], Sb[:, hh, :], True, True,
                                reads=[qnb[hh], sbb])
                    bo2 = self.take_bank()
                    for hh in range(4):
                        self.mm(bo2, self.banks[bo2][:, hh * 128:(hh + 1) * 128], QKD[hh][:, blk, :], vnew[:, hh, :], True, True,
                                reads=[qkb[hh], vnb])
                    self.cp("act", tmp, self.banks[bo2][:], [self.bankbuf[bo2]], [tmpb])
                    for hh in range(4):
                        self.stt(otok[:, hh, :], self.banks[bo1][:, hh * 128:(hh + 1) * 128], col(eg, hh), o2s[:, hh, :],
                                 ALU.mult, ALU.add, reads=[self.bankbuf[bo1], gb_, tmpb], writes=[decb])
                    bsu = self.take_bank()
                    for hh in range(4):
                        self.mm(bsu, self.banks[bsu][:, hh * 128:(hh + 1) * 128], kdtok[hh][:, blk, :], vnew[:, hh, :], True, True,
                                reads=[kdb[hh], vnb])
                    for hh in range(4):
                        self.stt(Sf[:, hh, :], Sf[:, hh, :], col(gl, hh), self.banks[bsu][:, hh * 128:(hh + 1) * 128],
                                 ALU.mult, ALU.add, reads=[self.bankbuf[bsu], gb_, sfb], writes=[sfb])
                    self.cp("act", Sb, Sf, [sfb], [sbb])
                    self.tt("pool", tmp, decT, decT, ALU.mult, reads=[decb], writes=[tmpb])
                    P.op("dve", lambda e: e.tensor_reduce(out=ssq[:, 0:4], in_=o2s, axis=AX.X, op=ALU.add), reads=[tmpb], writes=[ssb])
                    self.act(ssq[:, 0:4], ssq[:, 0:4], AF.Ln, reads=[ssb, self.kb], writes=[ssb], bias=self.eps_col, scale=1.0 / 128)
                    self.act(ssq[:, 0:4], ssq[:, 0:4], AF.Exp, reads=[ssb], writes=[ssb], scale=-0.5)
                    for hh in range(4):
                        self.act(onb_[:, hh, :], otok[:, hh, :], AF.Copy, reads=[decb, ssb], writes=[onbuf], scale=ssq[:, hh:hh + 1])
                    bt = self.take_bank()
                    for hh in range(4):
                        self.tr(bt, bfbank(bt)[:, hh, :], onb_[:, hh, :], self.ident_bf[:], reads=[onbuf, self.kb])
                    for hh in range(4):
                        self.stt(zsT[hh][:, bs], bfbank(bt)[:, hh, :], self.cc("gog"), zsT[hh][:, bs], ALU.mult, ALU.mult,
                                 reads=[self.bankbuf[bt], self.cbuf, zsb[hh]], writes=[zsb[hh]])
                wos = [self.wload(self.d_gwout[j], 2048) for j in range(4)]
                for dc in range(KC):
                    wo = self.ring[wos[dc // 2]][:].rearrange("p (k c) -> p k c", k=KC)
                    bk = self.take_bank()
                    for hh in range(4):
                        self.mm(bk, self.banks[bk][:], wo[:, 4 * g + hh, (dc % 2) * 128:(dc % 2 + 1) * 128], zsT[hh],
                                hh == 0, hh == 3, reads=[self.slotbuf[wos[dc // 2]], zsb[hh]])
                    self.tt("dve", self.xT[:, dc, sl], self.banks[bk][:], self.xT[:, dc, sl], ALU.add,
                            reads=[self.bankbuf[bk], self.xbuf[dc][Q]], writes=[self.xbuf[dc][Q]])

    def fox(self, layer):
        self.common_init()
        P = self.P
        HD = 128
        hT = self.carve(0, 8192).bitcast(BF16).rearrange("p (c t) -> p c t", c=KC)
        for t in range(NT):
            self.rmsnorm_tile(t, "mixg", layer, hT[:, :, t * TT:(t + 1) * TT], self.hbuf[t], 8192)
        o = 9728
        knT = self.carve(o, 4096).bitcast(BF16).rearrange("p (h t) -> p h t", h=4); o += 4096
        knb = [[Buf("kn") for _ in range(NT)] for _ in range(4)]
        vtok = self.carve(o, 4096).bitcast(BF16).rearrange("p (b c) -> p b c", b=16); o += 4096
        vb = [Buf("v%d" % b) for b in range(16)]
        qT = self.carve(o, 1024).bitcast(BF16).rearrange("p (h t) -> p h t", h=4); o += 1024
        qb = [Buf("q%d" % h) for h in range(4)]
        oT = self.carve(o, 1024).bitcast(BF16).rearrange("p (h t) -> p h t", h=4); o += 1024
        ob = [Buf("o%d" % h) for h in range(4)]
        NP = 2
        pT = [self.carve(o + i * 256, 256).bitcast(BF16) for i in range(NP)]; o += NP * 256
        pb = [Buf("p%d" % i) for i in range(NP)]
        raw = self.carve(o, 512); o += 512
        sqt = self.carve(o, 256).bitcast(BF16); o += 256
        lnv = self.carve(o, 512); o += 512
        rstd = self.carve(o, 512); o += 512
        rawb, sqb, lnb, rsb = Buf("raw"), Buf("sq"), Buf("ln"), Buf("rs")
        rden = self.carve(o, 512); o += 512
        rdb = Buf("rden")
        erow = self.carve(o, 512); o += 512
        crow = self.carve(o, 512); o += 512
        ones8 = self.carve(o, 512); o += 512
        cT = self.carve(o, 128).rearrange("p (b h) -> p b h", b=16); o += 128
        cmidb = self.carve(o, 32).rearrange("p (q h) -> p q h", q=4); o += 32
        biasA = self.carve(o, 128).rearrange("p (b h) -> p b h", b=16); o += 128
        small = self.carve(o, 32); o += 32
        wf = self.carve(o, 32).bitcast(BF16).rearrange("p (k c) -> p k c", k=KC); o += 32
        assert o <= self.ARENA, o
        carry = small[:, 0:8]
        xs = erow[:, 0:32]
        tots = erow[:, 32:64]
        cb = Buf("cstuff")
        wfb = Buf("wf")
        P.op("pool", lambda e: e.dma_start(out=wf.rearrange("p k c -> p (k c)"), in_=self.d_fwf), writes=[wfb], dma_out=wfb)
        self.memset("pool", carry, 0.0, [cb])
        scale = float(HD) ** -0.5
        for g in range(2):
            for Q in range(NT):
                sl = slice(Q * TT, (Q + 1) * TT)
                hTt = hT[:, :, sl]
                hb = self.hbuf[Q]
                if g == 0:
                    bk = self.take_bank()
                    for blk in range(4):
                        for kc in range(KC):
                            self.mm(bk, self.banks[bk][:, blk * 8:(blk + 1) * 8], hTt[:, kc, blk * 128:(blk + 1) * 128], wf[:, kc, :],
                                    kc == 0, kc == KC - 1, reads=[wfb, hb])
                    x3 = xs.rearrange("p (b h) -> p b h", b=4)
                    self.tt("dve", x3, self.banks[bk][:, 0:32].rearrange("p (b h) -> p b h", b=4),
                            self.cst[:, CL["fbfb"]:CL["fbfb"] + 8].unsqueeze(1).broadcast_to([128, 4, 8]), ALU.add,
                            reads=[self.bankbuf[bk], self.cbuf], writes=[cb])
                    self.act(xs, xs, AF.Exp, reads=[cb], writes=[cb], scale=-1.0)
                    self.act(xs, xs, AF.Ln, reads=[cb, self.kb], writes=[cb], bias=self.one_col)
                    b1 = self.take_bank()
                    self.mm(b1, self.banks[b1][:, 0:32], self.uincl_f[:], xs, True, True, reads=[cb, self.kb])
                    b2 = self.take_bank()
                    self.mm(b2, self.banks[b2][:, 0:32], self.ones_f[:], xs, True, True, reads=[cb, self.kb])
                    self.cp("dve", tots, self.banks[b2][:, 0:32], [self.bankbuf[b2]], [cb])
                    for blk in range(4):
                        self.tt("dve", cT[:, 4 * Q + blk, :], self.banks[b1][:, blk * 8:(blk + 1) * 8], carry, ALU.add,
                                reads=[self.bankbuf[b1], cb], writes=[cb])
                        self.tt("dve", carry, carry, tots[:, blk * 8:(blk + 1) * 8], ALU.add, reads=[cb], writes=[cb])
                        if blk == 1:
                            self.cp("dve", cmidb[:, Q, :], carry, [cb], [cb])
                nj = 4 * Q + 4
                DBG = int(os.environ.get("FOXDBG", "9"))
                if DBG <= 1:
                    continue
                self.tt("dve", biasA[:, 0:nj, :], cT[:, 0:nj, :], cmidb[:, Q:Q + 1, :].broadcast_to([128, nj, 8]), ALU.subtract,
                        reads=[cb], writes=[cb])
                for which in range(2):
                    for hp in range(2):
                        sw = self.wload(self.d_fwin[which * 4 + 2 * g + hp], 2048)
                        w_ = self.ring[sw][:].rearrange("p (k c) -> p k c", k=KC)
                        for h2 in range(2):
                            hh = 2 * hp + h2
                            bk = self.take_bank()
                            for kc in range(KC):
                                self.mm(bk, self.banks[bk][:], w_[:, kc, h2 * 128:(h2 + 1) * 128], hTt[:, kc, :],
                                        kc == 0, kc == KC - 1, reads=[self.slotbuf[sw], hb])
                            self.cp("dve", raw, self.banks[bk][:], [self.bankbuf[bk]], [rawb])
                            self.act(sqt, raw, AF.Square, reads=[rawb], writes=[sqb])
                            b2 = self.take_bank()
                            self.mm(b2, self.banks[b2][:], self.ones_bf[:], sqt, True, True, reads=[sqb, self.kb])
                            self.act(lnv, self.banks[b2][:], AF.Ln, reads=[self.bankbuf[b2], self.kb], writes=[lnb],
                                     bias=self.eps_col, scale=1.0 / HD)
                            self.act(rstd, lnv, AF.Exp, reads=[lnb], writes=[rsb], scale=-0.5)
                            if which == 0:
                                self.stt(qT[:, hh, :], raw, self.cc("fqg"), rstd, ALU.mult, ALU.mult,
                                         reads=[rawb, rsb, self.cbuf], writes=[qb[hh]])
                            else:
                                self.stt(knT[:, hh, sl], raw, self.cc("fkg"), rstd, ALU.mult, ALU.mult,
                                         reads=[rawb, rsb, self.cbuf], writes=[knb[hh][Q]])
                for hp in range(2):
                    sw = self.wload(self.d_fwin[8 + 2 * g + hp], 2048)
                    w_ = self.ring[sw][:].rearrange("p (k c) -> p k c", k=KC)
                    for blk in range(4):
                        bk = self.take_bank()
                        for kc in range(KC):
                            self.mm(bk, self.banks[bk][:, 0:256], hTt[:, kc, blk * 128:(blk + 1) * 128], w_[:, kc, :],
                                    kc == 0, kc == KC - 1, reads=[self.slotbuf[sw], hb])
                        self.act(vtok[:, 4 * Q + blk, hp * 256:(hp + 1) * 256], self.banks[bk][:, 0:256], AF.Copy,
                                 reads=[self.bankbuf[bk]], writes=[vb[4 * Q + blk]])
                pi = 0
                if DBG <= 2:
                    continue
                for hh in range(4):
                    h = 4 * g + hh
                    bo = self.take_bank()
                    self.held.add(bo)
                    bd = self.take_bank()
                    self.held.add(bd)
                    for j in range(nj):
                        off = max(0, (j - 4 * Q) * 128)
                        bs = self.take_bank()
                        self.mm(bs, self.banks[bs][:, off:512], knT[:, hh, j * 128:(j + 1) * 128], qT[:, hh, off:512],
                                True, True, reads=[knb[hh][j // 4], qb[hh]])
                        p_, pbuf = pT[pi % NP], pb[pi % NP]
                        pi += 1
                        self.act(p_[:, off:512], self.banks[bs][:, off:512], AF.Exp, reads=[self.bankbuf[bs], cb], writes=[pbuf],
                                 bias=biasA[:, j, h:h + 1], scale=scale)
                        if j >= 4 * Q:
                            self.tt("pool", p_[:, off:off + 128], p_[:, off:off + 128], self.uincl_bf[:], ALU.mult,
                                    reads=[pbuf, self.kb], writes=[pbuf])
                        self.mm(bo, self.banks[bo][:, off:512], vtok[:, j, hh * 128:(hh + 1) * 128], p_[:, off:512],
                                j == 0, j == nj - 1, reads=[vb[j], pbuf])
                        self.mm(bd, self.banks[bd][:, off:512], self.ones_bf[:], p_[:, off:512],
                                j == 0, j == nj - 1, reads=[pbuf, self.kb])
                    P.op("dve", lambda e, bd=bd: e.reciprocal(out=rden, in_=self.banks[bd][:]), reads=[self.bankbuf[bd]], writes=[rdb])
                    self.tt("dve", oT[:, hh, :], self.banks[bo][:], rden, ALU.mult, reads=[self.bankbuf[bo], rdb], writes=[ob[hh]])
                    self.held.discard(bo)
                    self.held.discard(bd)
                wos = [self.wload(self.d_fwout[j], 2048) for j in range(4)]
                for dc in range(KC):
                    wo = self.ring[wos[dc // 2]][:].rearrange("p (k c) -> p k c", k=KC)
                    bk = self.take_bank()
                    for hh in range(4):
                        self.mm(bk, self.banks[bk][:], wo[:, 4 * g + hh, (dc % 2) * 128:(dc % 2 + 1) * 128], oT[:, hh, :],
                                hh == 0, hh == 3, reads=[self.slotbuf[wos[dc // 2]], ob[hh]])
                    self.tt("dve", self.xT[:, dc, sl], self.banks[bk][:], self.xT[:, dc, sl], ALU.add,
                            reads=[self.bankbuf[bk], self.xbuf[dc][Q]], writes=[self.xbuf[dc][Q]])


ALL_STAGES = [("conf", 0), ("ffn", 0), ("gdn", 1), ("ffn", 1), ("fox", 2), ("ffn", 2), ("conf", 3), ("ffn", 3)]


def build_nc(stages):
    nc = bass.Bass("TRN2", target_bir_lowering=False)
    k = K(nc, stages)
    k.build()
    return nc, k.used_inputs


def prep_shared(inp):
    sh = {}
    sh["cst"] = pack_consts(inp)
    sh["cwin"] = np.stack([tile_w(inp["conv_w_in"][i]).reshape(8, 128, 2048) for i in range(2)])
    sh["cwout"] = np.stack([tile_w(inp["conv_w_out"][i]).reshape(4, 128, 2048) for i in range(2)])
    gw = np.asarray(inp["gdn_w_in"][0], np.float32)
    sh["gwin"] = tile_w(gw[:, :4096]).reshape(16, 128, 2048)
    sh["gwab"] = np.ascontiguousarray(gw[:, 4096:4112].reshape(8, 128, 16).transpose(1, 0, 2)).reshape(128, 128)
    sh["gwout"] = tile_w(inp["gdn_w_out"][0]).reshape(4, 128, 2048)
    fw = np.asarray(inp["fox_w_in"][0], np.float32)
    sh["fwin"] = tile_w(fw[:, :3072]).reshape(12, 128, 2048)
    sh["fwf"] = np.ascontiguousarray(fw[:, 3072:3080].reshape(8, 128, 8).transpose(1, 0, 2)).reshape(128, 64)
    sh["fwout"] = tile_w(inp["fox_w_out"][0]).reshape(4, 128, 2048)
    sh["wup"] = np.stack([tile_w(inp["ffn_w_up"][l]).reshape(22, 128, 2048) for l in range(4)])
    sh["wdn"] = np.stack([tile_wk(inp["ffn_w_down"][l]).reshape(11, 128, 2048) for l in range(4)])
    return sh


def run(inp, stages, ncores=8, trace=False):
    x = np.asarray(inp["x"], np.float32)
    sh = prep_shared(inp)
    nc, used = build_nc(stages)
    sh = {k_: v for k_, v in sh.items() if k_ in used}
    in_maps = []
    for b in range(ncores):
        m = dict(sh)
        m["xT"] = np.ascontiguousarray(x[b].T).reshape(KC, 128, S)
        in_maps.append(m)
    res = run_bass_kernel_spmd(nc, in_maps, core_ids=list(range(ncores)), trace=trace)
    out = np.stack([np.asarray(r["yT"], np.float32).reshape(D, S).T for r in res.results])
    return out, res


def kernel(**inputs):
    out, _ = run(inputs, ALL_STAGES, ncores=8)
    return out.astype(np.float32)
```

```python
import contextlib
import os
import numpy as np
import concourse.bass as bass
import concourse.mybir as mybir
from concourse.bass_utils import run_bass_kernel_spmd

F32 = mybir.dt.float32
BF16 = mybir.dt.bfloat16
AF = mybir.ActivationFunctionType
ALU = mybir.AluOpType
AX = mybir.AxisListType

S = 2048
D = 1024
TT = 512
NT = 4
KC = 8
FF = 2816
EPS = 1e-6
ENGS = ["pe", "act", "dve", "pool", "sp"]
BLOCK_ATTR = {"pe": "tensor", "act": "scalar", "dve": "vector", "pool": "gpsimd", "sp": "sync"}


class Buf:
    __slots__ = ("name", "last_w", "readers", "sem", "dma_cnt", "excl")

    def __init__(self, name, excl=False):
        self.name = name
        self.excl = excl
        self.last_w = None
        self.readers = []
        self.sem = None
        self.dma_cnt = 0


class Op:
    __slots__ = ("eng", "fn", "waits", "signal", "idx", "dma_buf", "clock", "sigcnt")

    def __init__(self, eng, fn, idx, dma_buf=None):
        self.eng = eng
        self.fn = fn
        self.idx = idx
        self.waits = []
        self.signal = False
        self.dma_buf = dma_buf
        self.clock = None
        self.sigcnt = None


class Prog:
    def __init__(self, nc):
        self.nc = nc
        self.ops = {e: [] for e in ENGS}
        self.obs = {e: {} for e in ENGS}
        self.dma_bufs = []
        self.pending = {e: [] for e in ENGS}

    def barrier(self):
        toks = [("eng", e, len(self.ops[e]) - 1) for e in ENGS if self.ops[e] and self.ops[e][-1].dma_buf is None]
        for e in ENGS:
            if self.ops[e] and self.ops[e][-1].dma_buf is not None:
                for o in reversed(self.ops[e]):
                    if o.dma_buf is None:
                        toks.append(("eng", e, o.idx))
                        break
        for e in ENGS:
            self.pending[e] = list(toks)

    def _need(self, op, tok):
        e = op.eng
        if tok[0] == "eng":
            _, se, si = tok
            if self.obs[e].get(se, -1) >= si:
                return
            src = self.ops[se][si]
            src.signal = True
            op.waits.append(tok)
            self.obs[e][se] = si
            if src.clock:
                for k, v in src.clock.items():
                    if self.obs[e].get(k, -1) < v:
                        self.obs[e][k] = v
        else:
            _, b, cnt = tok
            key = ("dma", id(b))
            if self.obs[e].get(key, -1) >= cnt:
                return
            op.waits.append(tok)
            self.obs[e][key] = cnt

    def op(self, eng, fn, reads=(), writes=(), dma_out=None):
        lst = self.ops[eng]
        o = Op(eng, fn, len(lst), dma_buf=dma_out)
        if any(r.excl for r in reads):
            writes = list(writes) + [r for r in reads if r.excl and r not in writes]
            reads = [r for r in reads if not r.excl]
        best = {}
        for r in reads:
            t = r.last_w
            if t is not None:
                k = t[1] if t[0] == "eng" else ("dma", id(t[1]))
                if k not in best or best[k][2] < t[2]:
                    best[k] = t
        for w in writes:
            for t in [w.last_w] + w.readers:
                if t is not None:
                    k = t[1] if t[0] == "eng" else ("dma", id(t[1]))
                    if k not in best or best[k][2] < t[2]:
                        best[k] = t
        if self.pending[eng]:
            for t in self.pending[eng]:
                if t[1] == eng:
                    continue
                k = t[1]
                if k not in best or best[k][2] < t[2]:
                    best[k] = t
            self.pending[eng] = []
        for t in best.values():
            if eng == "pe" and t[0] == "eng" and t[1] == "pe":
                continue
            self._need(o, t)
        if dma_out is not None:
            if dma_out.sem is None:
                self.dma_bufs.append(dma_out)
                dma_out.sem = True
            dma_out.dma_cnt += 1
            tok = ("dma", dma_out, dma_out.dma_cnt)
        else:
            tok = ("eng", eng, o.idx)
        o.clock = dict(self.obs[eng])
        for r in reads:
            r.readers.append(tok)
        for w in writes:
            w.last_w = tok
            w.readers = []
        lst.append(o)
        return tok

    def emit(self, final_waits=()):
        nc = self.nc
        CH = 2000
        with contextlib.ExitStack() as st:
            for e in ENGS:
                c = 0
                for o in self.ops[e]:
                    if o.signal and o.dma_buf is None:
                        o.sigcnt = c
                        c += 1
                    else:
                        o.sigcnt = c - 1
            nsig = {e: sum(1 for o in self.ops[e] if o.signal and o.dma_buf is None) for e in ENGS}
            esem = {e: [st.enter_context(nc.semaphore("s_%s%d" % (e, i))) for i in range(max(1, (nsig[e] + CH - 1) // CH))]
                    for e in ENGS}
            for i, b in enumerate(self.dma_bufs):
                b.sem = st.enter_context(nc.semaphore("d%d" % i))
            block = st.enter_context(nc.Block())
            for e in ENGS:
                ops = self.ops[e]
                fw = [b for (fe, b) in final_waits if fe == e]
                if not ops and not fw:
                    continue

                def body(eng, ops=ops, e=e, fw=fw):
                    for o in ops:
                        for t in o.waits:
                            if t[0] == "eng":
                                k = self.ops[t[1]][t[2]].sigcnt
                                eng.wait_ge(esem[t[1]][k // CH], k % CH + 1)
                            else:
                                eng.wait_ge(t[1].sem, 16 * t[2])
                        ins = o.fn(eng)
                        if o.dma_buf is not None:
                            ins.then_inc(o.dma_buf.sem, 16)
                        elif o.signal:
                            ins.then_inc(esem[e][o.sigcnt // CH], 1)
                    for b in fw:
                        eng.wait_ge(b.sem, 16 * b.dma_cnt)

                getattr(block, BLOCK_ATTR[e])(body)


def _cst_layout():
    lay = {}
    off = 0
    for name, n in [("mixg", 32), ("ffng", 32), ("cbin", 32), ("cwdw", 2 * 31 * 8), ("cbdw", 16),
                    ("clng", 16), ("clnb", 16), ("gconv", 96), ("gog", 1), ("fqg", 1), ("fkg", 1),
                    ("fwdw", 4 * 3 * 44), ("fbf", 1), ("galog", 8), ("gdtb", 8), ("fbfb", 8)]:
        lay[name] = off
        off += n
    return lay, off


CL, NCST = _cst_layout()


def _cols(v):
    v = np.asarray(v, dtype=np.float32).reshape(-1, 128)
    return v.T


def pack_consts(inp):
    c = np.zeros((128, NCST), np.float32)

    def put(name, arr):
        arr = np.asarray(arr, np.float32)
        c[:, CL[name]:CL[name] + arr.shape[1]] = arr

    put("mixg", _cols(inp["mix_norm_g"].reshape(-1)))
    put("ffng", _cols(inp["ffn_norm_g"].reshape(-1)))
    put("cbin", _cols(inp["conv_b_in"].reshape(-1)))
    put("cwdw", _cols(inp["conv_w_dw"].reshape(-1)))
    put("cbdw", _cols(inp["conv_b_dw"].reshape(-1)))
    put("clng", _cols(inp["conv_ln_g"].reshape(-1)))
    put("clnb", _cols(inp["conv_ln_b"].reshape(-1)))
    put("gconv", _cols(inp["gdn_conv_w"].reshape(-1)))
    put("gog", _cols(inp["gdn_o_norm_g"].reshape(-1)))
    put("fqg", _cols(inp["fox_q_norm_g"].reshape(-1)))
    put("fkg", _cols(inp["fox_k_norm_g"].reshape(-1)))
    put("fwdw", _cols(inp["ffn_w_dw"].reshape(-1)))
    bf = np.zeros((128, 1), np.float32)
    bf[:8, 0] = np.asarray(inp["fox_b_f"], np.float32).reshape(-1)
    put("fbf", bf)
    put("galog", np.broadcast_to(np.asarray(inp["gdn_a_log"], np.float32).reshape(1, 8), (128, 8)))
    put("gdtb", np.broadcast_to(np.asarray(inp["gdn_dt_bias"], np.float32).reshape(1, 8), (128, 8)))
    put("fbfb", np.broadcast_to(np.asarray(inp["fox_b_f"], np.float32).reshape(1, 8), (128, 8)))
    return c


def tile_w(w, width=256):
    w = np.asarray(w, np.float32)
    K, N = w.shape
    return np.ascontiguousarray(w.reshape(K // 128, 128, N // width, width).transpose(2, 1, 0, 3))


def tile_wk(w, nk=2):
    w = np.asarray(w, np.float32)
    K, N = w.shape
    return np.ascontiguousarray(w.reshape(K // (128 * nk), nk, 128, N).transpose(0, 2, 1, 3))


class K:
    def __init__(self, nc, stages):
        self.nc = nc
        self.P = Prog(nc)
        self.stages = stages
        self.st = contextlib.ExitStack()
        self.bank_rr = 0
        self.slot_rr = 0
        self.uid = 0

    def sb(self, name, shape, dt):
        return self.st.enter_context(self.nc.sbuf_tensor(name, shape, dt))

    def B(self, name=""):
        self.uid += 1
        return Buf("%s%d" % (name, self.uid))

    def take_bank(self):
        while True:
            i = self.bank_rr % 8
            self.bank_rr += 1
            if i not in self.held:
                return i

    def take_slot(self):
        i = self.slot_rr % self.NSLOT
        self.slot_rr += 1
        return i

    def mm(self, bank, out, lhsT, rhs, start, stop, reads, extra_w=()):
        self.P.op("pe", lambda e: e.matmul(out, lhsT, rhs, start=start, stop=stop),
                  reads=reads, writes=[self.bankbuf[bank]] + list(extra_w))

    def tr(self, bank, out, in_, ident, reads):
        self.P.op("pe", lambda e: e.transpose(out, in_, ident), reads=reads, writes=[self.bankbuf[bank]])

    def act(self, out, in_, func, reads, writes, bias=None, scale=None, eng="act"):
        kw = {}
        if bias is not None:
            kw["bias"] = bias
        if scale is not None:
            kw["scale"] = scale
        self.P.op("act", lambda e: e.activation(out=out, in_=in_, func=func, **kw), reads=reads, writes=writes)

    def tt(self, eng, out, in0, in1, op, reads, writes):
        self.P.op(eng, lambda e: e.tensor_tensor(out=out, in0=in0, in1=in1, op=op), reads=reads, writes=writes)

    def stt(self, out, in0, scalar, in1, op0, op1, reads, writes):
        self.P.op("dve", lambda e: e.scalar_tensor_tensor(out=out, in0=in0, scalar=scalar, in1=in1, op0=op0, op1=op1),
                  reads=reads, writes=writes)

    def ts(self, eng, out, in0, s1, s2, op0, op1, reads, writes):
        if op1 is None:
            self.P.op(eng, lambda e: e.tensor_scalar(out=out, in0=in0, scalar1=s1, scalar2=None, op0=op0),
                      reads=reads, writes=writes)
        else:
            self.P.op(eng, lambda e: e.tensor_scalar(out=out, in0=in0, scalar1=s1, scalar2=s2, op0=op0, op1=op1),
                      reads=reads, writes=writes)

    def cp(self, eng, out, in_, reads, writes):
        if eng == "act":
            self.P.op("act", lambda e: e.copy(out=out, in_=in_), reads=reads, writes=writes)
        else:
            self.P.op(eng, lambda e: e.tensor_copy(out=out, in_=in_), reads=reads, writes=writes)

    def memset(self, eng, ap, val, writes):
        self.P.op(eng, lambda e: e.memset(ap, val), writes=writes)

    def wload(self, src_ap, ncols_total, view=None):
        s = self.take_slot()
        dst = self.ring[s][:, 0:ncols_total]
        self.P.op("pool", lambda e: e.dma_start(out=dst, in_=src_ap), writes=[self.slotbuf[s]], dma_out=self.slotbuf[s])
        return s

    def build(self):
        nc = self.nc
        P = self.P
        dt = nc.dram_tensor
        self.xin = dt("xT", [KC, 128, S], F32, kind="ExternalInput").ap()
        self.cst_d = dt("cst", [128, NCST], F32, kind="ExternalInput").ap()
        kinds = set(k_ for k_, _ in self.stages)
        self.used_inputs = ["xT", "cst"]

        def din(name, shape, kind_):
            if kind_ not in kinds:
                return None
            self.used_inputs.append(name)
            return dt(name, shape, F32, kind="ExternalInput").ap()
        self.d_cwin = din("cwin", [2, 8, 128, 2048], "conf")
        self.d_cwout = din("cwout", [2, 4, 128, 2048], "conf")
        self.d_gwin = din("gwin", [16, 128, 2048], "gdn")
        self.d_gwab = din("gwab", [128, 128], "gdn")
        self.d_gwout = din("gwout", [4, 128, 2048], "gdn")
        self.d_fwin = din("fwin", [12, 128, 2048], "fox")
        self.d_fwf = din("fwf", [128, 64], "fox")
        self.d_fwout = din("fwout", [4, 128, 2048], "fox")
        self.d_wup = din("wup", [4, 22, 128, 2048], "ffn")
        self.d_wdn = din("wdn", [4, 11, 128, 2048], "ffn")
        self.yout = dt("yT", [KC, 128, S], F32, kind="ExternalOutput").ap()

        self.xT = self.sb("xT_sb", [128, KC, S], F32)
        self.cst = self.sb("cst_sb", [128, NCST], F32)
        self.NSLOT = 8
        self.ring = [self.sb("ring%d" % i, [128, 2048], BF16) for i in range(self.NSLOT)]
        self.slotbuf = [Buf("slot%d" % i) for i in range(self.NSLOT)]
        self.banks = [self.st.enter_context(nc.psum_tensor("bank%d" % i, [128, 512], F32)) for i in range(8)]
        self.bankbuf = [Buf("bank%d" % i, excl=True) for i in range(8)]
        self.held = set()
        self.xbuf = [[Buf("x%d_%d" % (c, t)) for t in range(NT)] for c in range(KC)]
        self.hbuf = [Buf("h%d" % t) for t in range(NT)]
        self.hb1 = Buf("htile")
        self.cbuf = Buf("cst")
        self.ones_bf = self.sb("ones_bf", [128, 128], BF16)
        self.ones_f = self.sb("ones_f", [128, 128], F32)
        self.ident_f = self.sb("ident_f", [128, 128], F32)
        self.ident_bf = self.sb("ident_bf", [128, 128], BF16)
        self.uincl_f = self.sb("uincl_f", [128, 128], F32)
        self.uincl_bf = self.sb("uincl_bf", [128, 128], BF16)
        self.lstrict_f = self.sb("lstrict_f", [128, 128], F32)
        self.msu_f = self.sb("msu_f", [128, 128], F32)
        self.kb = Buf("consts")
        self.ARENA = 25600
        self.arena = self.sb("arena", [128, self.ARENA], F32)

        P.op("sp", lambda e: e.dma_start(out=self.cst[:], in_=self.cst_d), writes=[self.cbuf], dma_out=self.cbuf)
        kb = self.kb
        self.memset("pool", self.ones_f[:], 1.0, [kb])
        self.memset("pool", self.ones_bf[:], 1.0, [kb])

        def asel(out, base, cm, step, op):
            P.op("pool", lambda e: e.affine_select(out=out, in_=self.ones_f[:], pattern=[[step, 128]], compare_op=op,
                                                   fill=0.0, base=base, channel_multiplier=cm), reads=[kb], writes=[kb])
        asel(self.ident_f[:], 0, -1, 1, ALU.is_equal)
        asel(self.uincl_f[:], 0, -1, 1, ALU.is_ge)
        asel(self.lstrict_f[:], -1, 1, -1, ALU.is_ge)
        asel(self.msu_f[:], -1, -1, 1, ALU.is_ge)
        self.cp("pool", self.ident_bf[:], self.ident_f[:], [kb], [kb])
        self.cp("pool", self.uincl_bf[:], self.uincl_f[:], [kb], [kb])

        for c in range(KC):
            for t in range(NT):
                P.op("sp", lambda e, c=c, t=t: e.dma_start(out=self.xT[:, c, t * TT:(t + 1) * TT],
                                                           in_=self.xin[c, :, t * TT:(t + 1) * TT]),
                     writes=[self.xbuf[c][t]], dma_out=self.xbuf[c][t])

        ia = 0
        for stg in self.stages:
            kind, layer = stg
            P.barrier()
            if kind == "conf":
                self.conformer(layer, ia)
                ia += 1
            elif kind == "gdn":
                self.gdn(layer)
            elif kind == "fox":
                self.fox(layer)
            elif kind == "ffn":
                self.ffn(layer)

        ob = Buf("out")
        for c in range(KC):
            P.op("sp", lambda e, c=c: e.dma_start(out=self.yout[c], in_=self.xT[:, c, :]),
                 reads=self.xbuf[c], writes=[ob], dma_out=ob)
        P.emit(final_waits=[("sp", ob)])
        self.st.close()

    def cc(self, name, idx=0):
        o = CL[name] + idx
        return self.cst[:, o:o + 1]

    def carve(self, off, nwords):
        assert off + nwords <= self.ARENA, (off, nwords)
        return self.arena[:, off:off + nwords]

    def rmsnorm_tile(self, t, gname, layer, hview, hb, ar_off):
        sl = slice(t * TT, (t + 1) * TT)
        sqs = [self.carve(ar_off + i * 256, 256).bitcast(BF16) for i in range(2)]
        lnv = self.carve(ar_off + 512, 512)
        rstd = self.carve(ar_off + 1024, 512)
        bl, br = self.nb_ln, self.nb_rstd
        bk = self.take_bank()
        for c in range(KC):
            sq, bsq = sqs[c % 2], self.nb_sq[c % 2]
            self.act(sq, self.xT[:, c, sl], AF.Square, reads=[self.xbuf[c][t]], writes=[bsq])
            self.mm(bk, self.banks[bk][:], self.ones_bf[:], sq, c == 0, c == KC - 1, reads=[bsq, self.kb])
        self.act(lnv, self.banks[bk][:], AF.Ln, reads=[self.bankbuf[bk], self.kb], writes=[bl], bias=self.eps_col, scale=1.0 / D)
        self.act(rstd, lnv, AF.Exp, reads=[bl], writes=[br], scale=-0.5)
        for c in range(KC):
            self.stt(hview[:, c, :], self.xT[:, c, sl], self.cc(gname, layer * 8 + c), rstd, ALU.mult, ALU.mult,
                     reads=[self.xbuf[c][t], br, self.cbuf], writes=[hb])

    def common_init(self):
        if getattr(self, "_ci", False):
            return
        self._ci = True
        self.nb_sq, self.nb_ln, self.nb_rstd = [Buf("sq0"), Buf("sq1")], Buf("ln"), Buf("rstd")
        self.eps_t = self.sb("eps_t", [128, 4], F32)
        self.memset("pool", self.eps_t[:, 0:1], EPS, [self.kb])
        self.memset("pool", self.eps_t[:, 1:2], 1.0, [self.kb])
        self.memset("pool", self.eps_t[:, 2:3], 0.0, [self.kb])
        self.eps_col = self.eps_t[:, 0:1]
        self.one_col = self.eps_t[:, 1:2]
        self.zero_col = self.eps_t[:, 2:3]

    def ffn(self, layer):
        self.common_init()
        P = self.P
        hT = self.carve(0, 8192).bitcast(BF16).rearrange("p (c t) -> p c t", c=KC)
        self.hT = hT
        for t in range(NT):
            self.rmsnorm_tile(t, "ffng", layer, hT[:, :, t * TT:(t + 1) * TT], self.hbuf[t], 8192)
        GOFF = 8192 + 1536
        gT = [self.carve(GOFF + i * 4096, 4096).bitcast(BF16).rearrange("p (c t) -> p c t", c=4) for i in range(2)]
        gbuf = [[[Buf("g") for _ in range(NT)] for _ in range(4)] for _ in range(2)]
        UOFF = GOFF + 2 * 4096
        NU = 4
        U = [self.carve(UOFF + i * 516, 516) for i in range(NU)]
        ubuf = [Buf("U") for _ in range(NU)]
        AOFF = UOFF + NU * 516
        NA = 4
        ACC = [self.carve(AOFF + i * 512, 512) for i in range(NA)]
        abuf = [Buf("acc") for _ in range(NA)]
        urr = [0]
        arr = [0]
        hreads = self.hbuf

        parts = [(0, 2), (2, 4), (4, 6), (6, 8), (8, 10), (10, 11)]
        for pi, (u0, u1) in enumerate(parts):
            g = gT[pi % 2]
            gb = gbuf[pi % 2]
            for u in range(u0, u1):
                sg = self.wload(self.d_wup[layer, u], 2048)
                su = self.wload(self.d_wup[layer, 11 + u], 2048)
                wg = self.ring[sg][:].rearrange("p (k c) -> p k c", k=KC)
                wu = self.ring[su][:].rearrange("p (k c) -> p k c", k=KC)
                for c2 in range(2):
                    ci = 2 * u + c2
                    lc = ci - 2 * u0
                    prev = {"g": None, "u": None}
                    for t in range(NT):
                        sl = slice(t * TT, (t + 1) * TT)
                        accs = {}
                        for which, w_, slot_, col0 in (("g", wg, sg, ci), ("u", wu, su, 22 + ci)):
                            bk = self.take_bank()
                            for kc in range(KC):
                                self.mm(bk, self.banks[bk][:], w_[:, kc, c2 * 128:(c2 + 1) * 128], self.hT[:, kc, sl],
                                        kc == 0, kc == KC - 1, reads=[self.slotbuf[slot_], self.hbuf[t]])
                            ui = urr[0] % NU
                            urr[0] += 1
                            Ut, Ub = U[ui], ubuf[ui]
                            if t == 0:
                                self.memset("pool", Ut[:, 0:2], 0.0, [Ub])
                            else:
                                pU, pB = prev[which]
                                self.cp("pool", Ut[:, 0:2], pU[:, 512:514], [pB], [Ub])
                            self.act(Ut[:, 2:514], self.banks[bk][:], AF.Copy, reads=[self.bankbuf[bk]], writes=[Ub])
                            prev[which] = (Ut, Ub)
                            ai = arr[0] % NA
                            arr[0] += 1
                            At, Ab = ACC[ai], abuf[ai]
                            wcol = lambda k, col0=col0: self.cc("fwdw", (layer * 3 + k) * 44 + col0)
                            self.act(At, Ut[:, 0:512], AF.Copy, reads=[Ub, self.cbuf], writes=[Ab], scale=wcol(0))
                            self.stt(At, Ut[:, 1:513], wcol(1), At, ALU.mult, ALU.add, reads=[Ub, self.cbuf, Ab], writes=[Ab])
                            self.stt(At, Ut[:, 2:514], wcol(2), At, ALU.mult, ALU.add, reads=[Ub, self.cbuf, Ab], writes=[Ab])
                            accs[which] = (At, Ab)
                        Ag, Agb = accs["g"]
                        Au, Aub = accs["u"]
                        self.act(Ag, Ag, AF.Silu, reads=[Agb], writes=[Agb])
                        self.tt("pool", g[:, lc, sl], Ag, Au, ALU.mult, reads=[Agb, Aub], writes=[gb[lc][t]])
            nch = 2 * (u1 - u0)
            dslots = []
            for u in range(u0, u1):
                dslots.append(self.wload(self.d_wdn[layer, u], 2048))
            for dc in range(KC):
                for t in range(NT):
                    sl = slice(t * TT, (t + 1) * TT)
                    bk = self.take_bank()
                    for lc in range(nch):
                        s_ = dslots[lc // 2]
                        wd = self.ring[s_][:].rearrange("p (k c) -> p k c", k=2)
                        self.mm(bk, self.banks[bk][:], wd[:, lc % 2, dc * 128:(dc + 1) * 128], g[:, lc, sl],
                                lc == 0, lc == nch - 1, reads=[self.slotbuf[s_], gb[lc][t]])
                    self.tt("dve", self.xT[:, dc, sl], self.banks[bk][:], self.xT[:, dc, sl], ALU.add,
                            reads=[self.bankbuf[bk], self.xbuf[dc][t]], writes=[self.xbuf[dc][t]])

    def conformer(self, layer, ia):
        self.common_init()
        P = self.P
        hTt = self.carve(0, 2048).bitcast(BF16).rearrange("p (c t) -> p c t", c=KC)
        G = self.carve(3584, 2176).bitcast(BF16).rearrange("p (c t) -> p c t", c=KC)
        gb = [Buf("G%d" % c) for c in range(KC)]
        DG = [self.carve(5760 + i * 1984, 1984).bitcast(BF16).rearrange("p (k m) -> p k m", k=31) for i in range(2)]
        dgb = [Buf("dg0"), Buf("dg1")]
        CO = self.carve(9728, 4096).rearrange("p (c t) -> p c t", c=KC)
        cob = [Buf("co%d" % c) for c in range(KC)]
        UB = [self.carve(13824 + i * 256, 256).bitcast(BF16) for i in range(2)]
        SQ = [self.carve(14336 + i * 256, 256).bitcast(BF16) for i in range(2)]
        ubb = [Buf("ub0"), Buf("ub1")]
        sqb = [Buf("sqb0"), Buf("sqb1")]
        mean = self.carve(14848, 512)
        msq = self.carve(15360, 512)
        lnv = self.carve(15872, 512)
        rstd = self.carve(16384, 512)
        stb = Buf("stats")
        TMP = [self.carve(16896 + i * 512, 512) for i in range(2)]
        tmb = [Buf("tmp0"), Buf("tmp1")]
        sT = self.carve(17920, 2048).bitcast(BF16).rearrange("p (c t) -> p c t", c=KC)
        stbuf = [Buf("sT%d" % c) for c in range(KC)]
        SG = [self.carve(19968 + i * 512, 512) for i in range(2)]
        sgb = [Buf("sg0"), Buf("sg1")]
        for c in range(KC):
            self.memset("pool", G[:, c, 0:32], 0.0, [gb[c]])
        for t in range(NT):
            sl = slice(t * TT, (t + 1) * TT)
            self.rmsnorm_tile(t, "mixg", layer, hTt, self.hb1, 2048)
            s1 = self.take_bank()
            self.held.add(s1)
            s2 = self.take_bank()
            self.held.add(s2)
            for c in range(KC):
                if c % 2 == 0:
                    sv = self.wload(self.d_cwin[ia, c // 2], 2048)
                    sg_ = self.wload(self.d_cwin[ia, 4 + c // 2], 2048)
                    wv = self.ring[sv][:].rearrange("p (k c) -> p k c", k=KC)
                    wg = self.ring[sg_][:].rearrange("p (k c) -> p k c", k=KC)
                bv = self.take_bank()
                for kc in range(KC):
                    self.mm(bv, self.banks[bv][:], wv[:, kc, (c % 2) * 128:(c % 2 + 1) * 128], hTt[:, kc, :],
                            kc == 0, kc == KC - 1, reads=[self.slotbuf[sv], self.hb1])
                bg = self.take_bank()
                for kc in range(KC):
                    self.mm(bg, self.banks[bg][:], wg[:, kc, (c % 2) * 128:(c % 2 + 1) * 128], hTt[:, kc, :],
                            kc == 0, kc == KC - 1, reads=[self.slotbuf[sg_], self.hb1])
                sg, sgbuf = SG[c % 2], sgb[c % 2]
                self.act(sg, self.banks[bg][:], AF.Sigmoid, reads=[self.bankbuf[bg], self.cbuf], writes=[sgbuf],
                         bias=self.cc("cbin", ia * 16 + 8 + c))
                self.stt(G[:, c, 30:542], self.banks[bv][:], self.cc("cbin", ia * 16 + c), sg, ALU.add, ALU.mult,
                         reads=[self.bankbuf[bv], sgbuf, self.cbuf], writes=[gb[c]])
                dg, dgbuf = DG[c % 2], dgb[c % 2]
                wb = CL["cwdw"] + ia * 31 * 8 + c
                wtaps = self.cst[:, wb:wb + 30 * 8 + 1:8]
                self.tt("dve", dg, self.ident_bf[:].unsqueeze(1).broadcast_to([128, 31, 128]),
                        wtaps.unsqueeze(2).broadcast_to([128, 31, 128]), ALU.mult, reads=[self.kb, self.cbuf], writes=[dgbuf])
                bc = self.take_bank()
                for k in range(31):
                    self.mm(bc, self.banks[bc][:], dg[:, k, :], G[:, c, k:k + 512], k == 0, k == 30, reads=[dgbuf, gb[c]])
                self.act(CO[:, c, :], self.banks[bc][:], AF.Identity, reads=[self.bankbuf[bc], self.cbuf], writes=[cob[c]],
                         bias=self.cc("cbdw", ia * 8 + c))
                self.cp("pool", G[:, c, 0:30], G[:, c, 512:542], [gb[c]], [gb[c]])
                ub, ubbuf = UB[c % 2], ubb[c % 2]
                sq, sqbuf = SQ[c % 2], sqb[c % 2]
                self.cp("dve", ub, CO[:, c, :], [cob[c]], [ubbuf])
                self.act(sq, CO[:, c, :], AF.Square, reads=[cob[c]], writes=[sqbuf])
                self.mm(s1, self.banks[s1][:], self.ones_bf[:], ub, c == 0, c == KC - 1, reads=[ubbuf, self.kb])
                self.mm(s2, self.banks[s2][:], self.ones_bf[:], sq, c == 0, c == KC - 1, reads=[sqbuf, self.kb])
            self.ts("dve", mean, self.banks[s1][:], 1.0 / D, None, ALU.mult, None, reads=[self.bankbuf[s1]], writes=[stb])
            self.tt("dve", msq, mean, mean, ALU.mult, reads=[stb], writes=[stb])
            self.stt(msq, self.banks[s2][:], 1.0 / D, msq, ALU.mult, ALU.subtract, reads=[self.bankbuf[s2], stb], writes=[stb])
            self.act(lnv, msq, AF.Ln, reads=[stb, self.kb], writes=[stb], bias=self.eps_col)
            self.act(rstd, lnv, AF.Exp, reads=[stb], writes=[stb], scale=-0.5)
            self.held.discard(s1)
            self.held.discard(s2)
            for c in range(KC):
                tm, tmbuf = TMP[c % 2], tmb[c % 2]
                self.tt("dve", tm, CO[:, c, :], mean, ALU.subtract, reads=[cob[c], stb], writes=[tmbuf])
                self.tt("dve", tm, tm, rstd, ALU.mult, reads=[tmbuf, stb], writes=[tmbuf])
                self.act(sT[:, c, :], tm, AF.Silu, reads=[tmbuf, self.cbuf], writes=[stbuf[c]],
                         bias=self.cc("clnb", ia * 8 + c), scale=self.cc("clng", ia * 8 + c))
            wos = [self.wload(self.d_cwout[ia, j], 2048) for j in range(4)]
            for dc in range(KC):
                wo = self.ring[wos[dc // 2]][:].rearrange("p (k c) -> p k c", k=KC)
                bk = self.take_bank()
                for kc in range(KC):
                    self.mm(bk, self.banks[bk][:], wo[:, kc, (dc % 2) * 128:(dc % 2 + 1) * 128], sT[:, kc, :],
                            kc == 0, kc == KC - 1, reads=[self.slotbuf[wos[dc // 2]], stbuf[kc]])
                self.tt("dve", self.xT[:, dc, sl], self.banks[bk][:], self.xT[:, dc, sl], ALU.add,
                        reads=[self.bankbuf[bk], self.xbuf[dc][t]], writes=[self.xbuf[dc][t]])

    def gdn(self, layer):
        self.common_init()
        P = self.P
        hT = self.carve(0, 8192).bitcast(BF16).rearrange("p (c t) -> p c t", c=KC)
        for t in range(NT):
            self.rmsnorm_tile(t, "mixg", layer, hT[:, :, t * TT:(t + 1) * TT], self.hbuf[t], 8192)
        o = [9728]

        def al(n):
            a = self.carve(o[0], n)
            o[0] += n
            return a

        def bf4(n=256):
            return al(n).bitcast(BF16).rearrange("p (b c) -> p b c", b=4)

        gtok, gcs, eg, negeg, egl, gl, beta, negbeta = [al(32) for _ in range(8)]
        abt = al(64)
        gb_ = Buf("gates")
        qnT = [al(256).bitcast(BF16) for _ in range(4)]
        knT = [al(256).bitcast(BF16) for _ in range(4)]
        vtok = [bf4() for _ in range(4)]
        kdtok = [bf4() for _ in range(4)]
        zsT = [al(256).bitcast(BF16) for _ in range(4)]
        TTm = [bf4() for _ in range(4)]
        QKD = [bf4() for _ in range(4)]
        Pm = [bf4() for _ in range(4)]
        Qm = [bf4() for _ in range(4)]
        nm = lambda n: [Buf(n + str(i)) for i in range(4)]
        qnb, knb, vtb, kdb, zsb, ttb, qkb, pmb, qmb = [nm(n) for n in ("qn", "kn", "vt", "kd", "zs", "tt", "qk", "pm", "qm")]
        U = [al(516) for _ in range(2)]
        ub = [Buf("U0"), Buf("U1")]
        ACC = [al(512) for _ in range(2)]
        ab_ = [Buf("A0"), Buf("A1")]
        sqt = al(256).bitcast(BF16)
        lnv = al(512)
        rstd = al(512)
        sqb, lnb, rsb = Buf("sq"), Buf("ln"), Buf("rs")
        decT = al(512)
        decb = Buf("dec")
        tmp = al(512)
        tmpb = Buf("tmp")
        lhsg = [al(128) for _ in range(2)]
        lgb = [Buf("lg0"), Buf("lg1")]
        vbf = al(256).bitcast(BF16)
        vbb = Buf("vbf")
        vnew = bf4()
        vnb = Buf("vnew")
        onb_ = bf4()
        onbuf = Buf("on")
        Sf = al(512).rearrange("p (h c) -> p h c", h=4)
        Sb = bf4()
        sfb, sbb = Buf("Sf"), Buf("Sb")
        haloS = al(48).rearrange("p (c k) -> p c k", c=12)
        hsb = [Buf("hs%d" % i) for i in range(12)]
        ssq = al(8)
        ssb = Buf("ssq")
        wab = al(64).bitcast(BF16).rearrange("p (k c) -> p k c", k=KC)
        negA = al(8)
        assert o[0] <= self.ARENA, o[0]
        wabb = Buf("wab")
        P.op("pool", lambda e: e.dma_start(out=wab.rearrange("p k c -> p (k c)"), in_=self.d_gwab), writes=[wabb], dma_out=wabb)
        nab = Buf("negA")
        self.act(negA, self.cst[:, CL["galog"]:CL["galog"] + 8], AF.Exp, reads=[self.cbuf], writes=[nab])
        self.ts("dve", negA, negA, -1.0, None, ALU.mult, None, reads=[nab], writes=[nab])
        dtb = self.cst[:, CL["gdtb"]:CL["gdtb"] + 8]
        v4 = lambda a: a.rearrange("p (b h) -> p b h", b=4)
        bfbank = lambda bk: self.banks[bk][:].bitcast(BF16)[:, 0:512].rearrange("p (b c) -> p b c", b=4)
        fbank = lambda bk: self.banks[bk][:].rearrange("p (b c) -> p b c", b=4)
        qscale = 128.0 ** -0.5
        for g in range(2):
            self.memset("pool", Sf[:], 0.0, [sfb])
            self.memset("pool", Sb[:], 0.0, [sbb])
            for Q in range(NT):
                sl = slice(Q * TT, (Q + 1) * TT)
                hTt = hT[:, :, sl]
                hb = self.hbuf[Q]
                bk = self.take_bank()
                for blk in range(4):
                    for kc in range(KC):
                        self.mm(bk, self.banks[bk][:, blk * 16:(blk + 1) * 16], hTt[:, kc, blk * 128:(blk + 1) * 128], wab[:, kc, :],
                                kc == 0, kc == KC - 1, reads=[wabb, hb])
                self.cp("dve", abt, self.banks[bk][:, 0:64], [self.bankbuf[bk]], [gb_])
                ab3 = abt.rearrange("p (b c) -> p b c", b=4)
                self.act(v4(beta), ab3[:, :, 8:16], AF.Sigmoid, reads=[gb_], writes=[gb_])
                self.ts("dve", negbeta, beta, -1.0, None, ALU.mult, None, reads=[gb_], writes=[gb_])
                self.tt("dve", v4(gtok), ab3[:, :, 0:8], dtb.unsqueeze(1).broadcast_to([128, 4, 8]), ALU.add, reads=[gb_, self.cbuf], writes=[gb_])
                self.act(gtok, gtok, AF.Exp, reads=[gb_], writes=[gb_])
                self.act(gtok, gtok, AF.Ln, reads=[gb_, self.kb], writes=[gb_], bias=self.one_col)
                self.tt("dve", v4(gtok), v4(gtok), negA.unsqueeze(1).broadcast_to([128, 4, 8]), ALU.mult, reads=[gb_, nab], writes=[gb_])
                bk = self.take_bank()
                self.mm(bk, self.banks[bk][:, 0:32], self.uincl_f[:], gtok, True, True, reads=[gb_, self.kb])
                b2 = self.take_bank()
                self.mm(b2, self.banks[b2][:, 0:32], self.ones_f[:], gtok, True, True, reads=[gb_, self.kb])
                self.cp("dve", gcs, self.banks[bk][:, 0:32], [self.bankbuf[bk]], [gb_])
                self.act(eg, gcs, AF.Exp, reads=[gb_], writes=[gb_])
                self.ts("dve", negeg, eg, -1.0, None, ALU.mult, None, reads=[gb_], writes=[gb_])
                self.tt("dve", egl, self.banks[b2][:, 0:32], gcs, ALU.subtract, reads=[self.bankbuf[b2], gb_], writes=[gb_])
                self.act(egl, egl, AF.Exp, reads=[gb_], writes=[gb_])
                self.act(gl, self.banks[b2][:, 0:32], AF.Exp, reads=[self.bankbuf[b2]], writes=[gb_])
                ui = 0
                for hh in range(4):
                    h = 4 * g + hh
                    for which in range(4):
                        if hh % 2 == 0:
                            pass
                        sw = self.wload(self.d_gwin[which * 4 + h // 2], 2048) if (hh % 2 == 0 or True) else None
                        w_ = self.ring[sw][:].rearrange("p (k c) -> p k c", k=KC)
                        bk = self.take_bank()
                        for kc in range(KC):
                            self.mm(bk, self.banks[bk][:], w_[:, kc, (h % 2) * 128:(h % 2 + 1) * 128], hTt[:, kc, :],
                                    kc == 0, kc == KC - 1, reads=[self.slotbuf[sw], hb])
                        if which == 3:
                            self.act(zsT[hh], self.banks[bk][:], AF.Silu, reads=[self.bankbuf[bk]], writes=[zsb[hh]])
                            continue
                        ci = which * 4 + hh
                        chunk = which * 8 + h
                        Ut, Ub = U[ui % 2], ub[ui % 2]
                        At, Ab = ACC[ui % 2], ab_[ui % 2]
                        ui += 1
                        if Q == 0:
                            self.memset("pool", Ut[:, 0:3], 0.0, [Ub])
                        else:
                            self.cp("pool", Ut[:, 0:3], haloS[:, ci, 0:3], [hsb[ci]], [Ub])
                        self.act(Ut[:, 3:515], self.banks[bk][:], AF.Copy, reads=[self.bankbuf[bk]], writes=[Ub])
                        self.cp("pool", haloS[:, ci, 0:3], Ut[:, 512:515], [Ub], [hsb[ci]])
                        wc = lambda k, chunk=chunk: self.cc("gconv", k * 24 + chunk)
                        self.act(At, Ut[:, 0:512], AF.Copy, reads=[Ub, self.cbuf], writes=[Ab], scale=wc(0))
                        for k in range(1, 4):
                            self.stt(At, Ut[:, k:k + 512], wc(k), At, ALU.mult, ALU.add, reads=[Ub, self.cbuf, Ab], writes=[Ab])
                        self.act(At, At, AF.Silu, reads=[Ab], writes=[Ab])
                        if which < 2:
                            self.act(sqt, At, AF.Square, reads=[Ab], writes=[sqb])
                            b2 = self.take_bank()
                            self.mm(b2, self.banks[b2][:], self.ones_bf[:], sqt, True, True, reads=[sqb, self.kb])
                            self.act(lnv, self.banks[b2][:], AF.Ln, reads=[self.bankbuf[b2], self.kb], writes=[lnb], bias=self.eps_col)
                            self.act(rstd, lnv, AF.Exp, reads=[lnb], writes=[rsb], scale=-0.5)
                            if which == 0:
                                self.stt(qnT[hh], At, qscale, rstd, ALU.mult, ALU.mult, reads=[Ab, rsb], writes=[qnb[hh]])
                            else:
                                self.tt("dve", knT[hh], At, rstd, ALU.mult, reads=[Ab, rsb], writes=[knb[hh]])
                        else:
                            self.cp("dve", vbf, At, [Ab], [vbb])
                            bt = self.take_bank()
                            for blk in range(4):
                                self.tr(bt, bfbank(bt)[:, blk, :], vbf[:, blk * 128:(blk + 1) * 128], self.ident_bf[:], reads=[vbb, self.kb])
                            self.cp("act", vtok[hh], bfbank(bt), [self.bankbuf[bt]], [vtb[hh]])
                    bt = self.take_bank()
                    for blk in range(4):
                        self.tr(bt, bfbank(bt)[:, blk, :], knT[hh][:, blk * 128:(blk + 1) * 128], self.ident_bf[:], reads=[knb[hh], self.kb])
                    for blk in range(4):
                        self.act(kdtok[hh][:, blk, :], bfbank(bt)[:, blk, :], AF.Copy, reads=[self.bankbuf[bt], gb_], writes=[kdb[hh]],
                                 scale=egl[:, blk * 8 + h:blk * 8 + h + 1])
                    bd = self.take_bank()
                    for blk in range(4):
                        lg, lgbuf = lhsg[blk % 2], lgb[blk % 2]
                        self.ts("pool", lg, self.lstrict_f[:], gtok[:, blk * 8 + h:blk * 8 + h + 1], None, ALU.mult, None,
                                reads=[self.kb, gb_], writes=[lgbuf])
                        self.mm(bd, self.banks[bd][:, blk * 128:(blk + 1) * 128], lg, self.uincl_f[:], True, True, reads=[lgbuf, self.kb])
                    self.act(decT, self.banks[bd][:], AF.Exp, reads=[self.bankbuf[bd]], writes=[decb])
                    d3 = decT.rearrange("p (b c) -> p b c", b=4)
                    t3 = tmp.rearrange("p (b c) -> p b c", b=4)
                    bkk = self.take_bank()
                    for blk in range(4):
                        ks = knT[hh][:, blk * 128:(blk + 1) * 128]
                        self.mm(bkk, self.banks[bkk][:, blk * 128:(blk + 1) * 128], ks, ks, True, True, reads=[knb[hh]])
                    self.tt("dve", tmp, self.banks[bkk][:], decT, ALU.mult, reads=[self.bankbuf[bkk], decb], writes=[tmpb])
                    for blk in range(4):
                        self.stt(Pm[hh][:, blk, :], t3[:, blk, :], negbeta[:, blk * 8 + h:blk * 8 + h + 1], self.msu_f[:], ALU.mult, ALU.mult,
                                 reads=[tmpb, gb_, self.kb], writes=[pmb[hh]])
                    bt = self.take_bank()
                    for blk in range(4):
                        self.tr(bt, bfbank(bt)[:, blk, :], Pm[hh][:, blk, :], self.ident_bf[:], reads=[pmb[hh], self.kb])
                    self.cp("act", Qm[hh], bfbank(bt), [self.bankbuf[bt]], [qmb[hh]])
                    self.tt("pool", TTm[hh], Pm[hh], self.ident_bf[:].unsqueeze(1).broadcast_to([128, 4, 128]), ALU.add,
                            reads=[pmb[hh], self.kb], writes=[ttb[hh]])
                    bq = self.take_bank()
                    for blk in range(4):
                        self.mm(bq, self.banks[bq][:, blk * 128:(blk + 1) * 128], knT[hh][:, blk * 128:(blk + 1) * 128],
                                qnT[hh][:, blk * 128:(blk + 1) * 128], True, True, reads=[knb[hh], qnb[hh]])
                    self.tt("dve", tmp, self.banks[bq][:], decT, ALU.mult, reads=[self.bankbuf[bq], decb], writes=[tmpb])
                    self.tt("dve", QKD[hh], t3, self.uincl_f[:].unsqueeze(1).broadcast_to([128, 4, 128]), ALU.mult,
                            reads=[tmpb, self.kb], writes=[qkb[hh]])
                for lev in range(1, 7):
                    for hh in range(4):
                        bq = self.take_bank()
                        for blk in range(4):
                            self.mm(bq, self.banks[bq][:, blk * 128:(blk + 1) * 128], Pm[hh][:, blk, :], Qm[hh][:, blk, :], True, True,
                                    reads=[pmb[hh], qmb[hh]])
                        if lev < 6:
                            bp = self.take_bank()
                            for blk in range(4):
                                self.mm(bp, self.banks[bp][:, blk * 128:(blk + 1) * 128], Qm[hh][:, blk, :], Pm[hh][:, blk, :], True, True,
                                        reads=[pmb[hh], qmb[hh]])
                            self.cp("act", Pm[hh], fbank(bp), [self.bankbuf[bp]], [pmb[hh]])
                        self.cp("dve", Qm[hh], fbank(bq), [self.bankbuf[bq]], [qmb[hh]])
                        br = self.take_bank()
                        for blk in range(4):
                            self.mm(br, self.banks[br][:, blk * 128:(blk + 1) * 128], Qm[hh][:, blk, :], TTm[hh][:, blk, :], True, True,
                                    reads=[qmb[hh], ttb[hh]])
                        self.tt("dve", TTm[hh], fbank(br), TTm[hh], ALU.add, reads=[self.bankbuf[br], ttb[hh]], writes=[ttb[hh]])
                rbuf = vbf.rearrange("p (b c) -> p b c", b=4)
                otok = decT.rearrange("p (b c) -> p b c", b=4)
                o2s = tmp.rearrange("p (b c) -> p b c", b=4)
                for blk in range(4):
                    bs = slice(blk * 128, (blk + 1) * 128)
                    col = lambda a, hh: a[:, blk * 8 + 4 * g + hh:blk * 8 + 4 * g + hh + 1]
                    bks = self.take_bank()
                    for hh in range(4):
                        self.mm(bks, self.banks[bks][:, hh * 128:(hh + 1) * 128], knT[hh][:, bs], Sb[:, hh, :], True, True,
                                reads=[knb[hh], sbb])
                    for hh in range(4):
                        self.stt(rbuf[:, hh, :], self.banks[bks][:, hh * 128:(hh + 1) * 128], col(negeg, hh), vtok[hh][:, blk, :],
                                 ALU.mult, ALU.add, reads=[self.bankbuf[bks], gb_, vtb[hh]], writes=[vbb])
                    bvn = self.take_bank()
                    for hh in range(4):
                        self.mm(bvn, self.banks[bvn][:, hh * 128:(hh + 1) * 128], TTm[hh][:, blk, :], rbuf[:, hh, :], True, True,
                                reads=[ttb[hh], vbb])
                    for hh in range(4):
                        self.act(vnew[:, hh, :], self.banks[bvn][:, hh * 128:(hh + 1) * 128], AF.Copy, reads=[self.bankbuf[bvn], gb_],
                                 writes=[vnb], scale=col(beta, hh))
                    bo1 = self.take_bank()
                    for hh in range(4):
                        self.mm(bo1, self.banks[bo1][:, hh * 128:(hh + 1) * 128], qnT[hh][:, bs], Sb[:, hh, :], True, True,
                                reads=[qnb[hh], sbb])
                    bo2 = self.take_bank()
                    for hh in range(4):
                        self.mm(bo2, self.banks[bo2][:, hh * 128:(hh + 1) * 128], QKD[hh][:, blk, :], vnew[:, hh, :], True, True,
                                reads=[qkb[hh], vnb])
                    self.cp("act", tmp, self.banks[bo2][:], [self.bankbuf[bo2]], [tmpb])
                    for hh in range(4):
                        self.stt(otok[:, hh, :], self.banks[bo1][:, hh * 128:(hh + 1) * 128], col(eg, hh), o2s[:, hh, :],
                                 ALU.mult, ALU.add, reads=[self.bankbuf[bo1], gb_, tmpb], writes=[decb])
                    bsu = self.take_bank()
                    for hh in range(4):
                        self.mm(bsu, self.banks[bsu][:, hh * 128:(hh + 1) * 128], kdtok[hh][:, blk, :], vnew[:, hh, :], True, True,
                                reads=[kdb[hh], vnb])
                    for hh in range(4):
                        self.stt(Sf[:, hh, :], Sf[:, hh, :], col(gl, hh), self.banks[bsu][:, hh * 128:(hh + 1) * 128],
                                 ALU.mult, ALU.add, reads=[self.bankbuf[bsu], gb_, sfb], writes=[sfb])
                    self.cp("act", Sb, Sf, [sfb], [sbb])
                    self.tt("pool", tmp, decT, decT, ALU.mult, reads=[decb], writes=[tmpb])
                    P.op("dve", lambda e: e.tensor_reduce(out=ssq[:, 0:4], in_=o2s, axis=AX.X, op=ALU.add), reads=[tmpb], writes=[ssb])
                    self.act(ssq[:, 0:4], ssq[:, 0:4], AF.Ln, reads=[ssb, self.kb], writes=[ssb], bias=self.eps_col, scale=1.0 / 128)
                    self.act(ssq[:, 0:4], ssq[:, 0:4], AF.Exp, reads=[ssb], writes=[ssb], scale=-0.5)
                    for hh in range(4):
                        self.act(onb_[:, hh, :], otok[:, hh, :], AF.Copy, reads=[decb, ssb], writes=[onbuf], scale=ssq[:, hh:hh + 1])
                    bt = self.take_bank()
                    for hh in range(4):
                        self.tr(bt, bfbank(bt)[:, hh, :], onb_[:, hh, :], self.ident_bf[:], reads=[onbuf, self.kb])
                    for hh in range(4):
                        self.stt(zsT[hh][:, bs], bfbank(bt)[:, hh, :], self.cc("gog"), zsT[hh][:, bs], ALU.mult, ALU.mult,
                                 reads=[self.bankbuf[bt], self.cbuf, zsb[hh]], writes=[zsb[hh]])
                wos = [self.wload(self.d_gwout[j], 2048) for j in range(4)]
                for dc in range(KC):
                    wo = self.ring[wos[dc // 2]][:].rearrange("p (k c) -> p k c", k=KC)
                    bk = self.take_bank()
                    for hh in range(4):
                        self.mm(bk, self.banks[bk][:], wo[:, 4 * g + hh, (dc % 2) * 128:(dc % 2 + 1) * 128], zsT[hh],
                                hh == 0, hh == 3, reads=[self.slotbuf[wos[dc // 2]], zsb[hh]])
                    self.tt("dve", self.xT[:, dc, sl], self.banks[bk][:], self.xT[:, dc, sl], ALU.add,
                            reads=[self.bankbuf[bk], self.xbuf[dc][Q]], writes=[self.xbuf[dc][Q]])

    def fox(self, layer):
        self.common_init()
        P = self.P
        HD = 128
        hT = self.carve(0, 8192).bitcast(BF16).rearrange("p (c t) -> p c t", c=KC)
        for t in range(NT):
            self.rmsnorm_tile(t, "mixg", layer, hT[:, :, t * TT:(t + 1) * TT], self.hbuf[t], 8192)
        o = 9728
        knT = self.carve(o, 4096).bitcast(BF16).rearrange("p (h t) -> p h t", h=4); o += 4096
        knb = [[Buf("kn") for _ in range(NT)] for _ in range(4)]
        vtok = self.carve(o, 4096).bitcast(BF16).rearrange("p (b c) -> p b c", b=16); o += 4096
        vb = [Buf("v%d" % b) for b in range(16)]
        qT = self.carve(o, 1024).bitcast(BF16).rearrange("p (h t) -> p h t", h=4); o += 1024
        qb = [Buf("q%d" % h) for h in range(4)]
        oT = self.carve(o, 1024).bitcast(BF16).rearrange("p (h t) -> p h t", h=4); o += 1024
        ob = [Buf("o%d" % h) for h in range(4)]
        NP = 2
        pT = [self.carve(o + i * 256, 256).bitcast(BF16) for i in range(NP)]; o += NP * 256
        pb = [Buf("p%d" % i) for i in range(NP)]
        raw = self.carve(o, 512); o += 512
        sqt = self.carve(o, 256).bitcast(BF16); o += 256
        lnv = self.carve(o, 512); o += 512
        rstd = self.carve(o, 512); o += 512
        rawb, sqb, lnb, rsb = Buf("raw"), Buf("sq"), Buf("ln"), Buf("rs")
        rden = self.carve(o, 512); o += 512
        rdb = Buf("rden")
        erow = self.carve(o, 512); o += 512
        crow = self.carve(o, 512); o += 512
        ones8 = self.carve(o, 512); o += 512
        cT = self.carve(o, 128).rearrange("p (b h) -> p b h", b=16); o += 128
        cmidb = self.carve(o, 32).rearrange("p (q h) -> p q h", q=4); o += 32
        biasA = self.carve(o, 128).rearrange("p (b h) -> p b h", b=16); o += 128
        small = self.carve(o, 32); o += 32
        wf = self.carve(o, 32).bitcast(BF16).rearrange("p (k c) -> p k c", k=KC); o += 32
        assert o <= self.ARENA, o
        carry = small[:, 0:8]
        xs = erow[:, 0:32]
        tots = erow[:, 32:64]
        cb = Buf("cstuff")
        wfb = Buf("wf")
        P.op("pool", lambda e: e.dma_start(out=wf.rearrange("p k c -> p (k c)"), in_=self.d_fwf), writes=[wfb], dma_out=wfb)
        self.memset("pool", carry, 0.0, [cb])
        scale = float(HD) ** -0.5
        for g in range(2):
            for Q in range(NT):
                sl = slice(Q * TT, (Q + 1) * TT)
                hTt = hT[:, :, sl]
                hb = self.hbuf[Q]
                if g == 0:
                    bk = self.take_bank()
                    for blk in range(4):
                        for kc in range(KC):
                            self.mm(bk, self.banks[bk][:, blk * 8:(blk + 1) * 8], hTt[:, kc, blk * 128:(blk + 1) * 128], wf[:, kc, :],
                                    kc == 0, kc == KC - 1, reads=[wfb, hb])
                    x3 = xs.rearrange("p (b h) -> p b h", b=4)
                    self.tt("dve", x3, self.banks[bk][:, 0:32].rearrange("p (b h) -> p b h", b=4),
                            self.cst[:, CL["fbfb"]:CL["fbfb"] + 8].unsqueeze(1).broadcast_to([128, 4, 8]), ALU.add,
                            reads=[self.bankbuf[bk], self.cbuf], writes=[cb])
                    self.act(xs, xs, AF.Exp, reads=[cb], writes=[cb], scale=-1.0)
                    self.act(xs, xs, AF.Ln, reads=[cb, self.kb], writes=[cb], bias=self.one_col)
                    b1 = self.take_bank()
                    self.mm(b1, self.banks[b1][:, 0:32], self.uincl_f[:], xs, True, True, reads=[cb, self.kb])
                    b2 = self.take_bank()
                    self.mm(b2, self.banks[b2][:, 0:32], self.ones_f[:], xs, True, True, reads=[cb, self.kb])
                    self.cp("dve", tots, self.banks[b2][:, 0:32], [self.bankbuf[b2]], [cb])
                    for blk in range(4):
                        self.tt("dve", cT[:, 4 * Q + blk, :], self.banks[b1][:, blk * 8:(blk + 1) * 8], carry, ALU.add,
                                reads=[self.bankbuf[b1], cb], writes=[cb])
                        self.tt("dve", carry, carry, tots[:, blk * 8:(blk + 1) * 8], ALU.add, reads=[cb], writes=[cb])
                        if blk == 1:
                            self.cp("dve", cmidb[:, Q, :], carry, [cb], [cb])
                nj = 4 * Q + 4
                DBG = int(os.environ.get("FOXDBG", "9"))
                if DBG <= 1:
                    continue
                self.tt("dve", biasA[:, 0:nj, :], cT[:, 0:nj, :], cmidb[:, Q:Q + 1, :].broadcast_to([128, nj, 8]), ALU.subtract,
                        reads=[cb], writes=[cb])
                for which in range(2):
                    for hp in range(2):
                        sw = self.wload(self.d_fwin[which * 4 + 2 * g + hp], 2048)
                        w_ = self.ring[sw][:].rearrange("p (k c) -> p k c", k=KC)
                        for h2 in range(2):
                            hh = 2 * hp + h2
                            bk = self.take_bank()
                            for kc in range(KC):
                                self.mm(bk, self.banks[bk][:], w_[:, kc, h2 * 128:(h2 + 1) * 128], hTt[:, kc, :],
                                        kc == 0, kc == KC - 1, reads=[self.slotbuf[sw], hb])
                            self.cp("dve", raw, self.banks[bk][:], [self.bankbuf[bk]], [rawb])
                            self.act(sqt, raw, AF.Square, reads=[rawb], writes=[sqb])
                            b2 = self.take_bank()
                            self.mm(b2, self.banks[b2][:], self.ones_bf[:], sqt, True, True, reads=[sqb, self.kb])
                            self.act(lnv, self.banks[b2][:], AF.Ln, reads=[self.bankbuf[b2], self.kb], writes=[lnb],
                                     bias=self.eps_col, scale=1.0 / HD)
                            self.act(rstd, lnv, AF.Exp, reads=[lnb], writes=[rsb], scale=-0.5)
                            if which == 0:
                                self.stt(qT[:, hh, :], raw, self.cc("fqg"), rstd, ALU.mult, ALU.mult,
                                         reads=[rawb, rsb, self.cbuf], writes=[qb[hh]])
                            else:
                                self.stt(knT[:, hh, sl], raw, self.cc("fkg"), rstd, ALU.mult, ALU.mult,
                                         reads=[rawb, rsb, self.cbuf], writes=[knb[hh][Q]])
                for hp in range(2):
                    sw = self.wload(self.d_fwin[8 + 2 * g + hp], 2048)
                    w_ = self.ring[sw][:].rearrange("p (k c) -> p k c", k=KC)
                    for blk in range(4):
                        bk = self.take_bank()
                        for kc in range(KC):
                            self.mm(bk, self.banks[bk][:, 0:256], hTt[:, kc, blk * 128:(blk + 1) * 128], w_[:, kc, :],
                                    kc == 0, kc == KC - 1, reads=[self.slotbuf[sw], hb])
                        self.act(vtok[:, 4 * Q + blk, hp * 256:(hp + 1) * 256], self.banks[bk][:, 0:256], AF.Copy,
                                 reads=[self.bankbuf[bk]], writes=[vb[4 * Q + blk]])
                pi = 0
                if DBG <= 2:
                    continue
                for hh in range(4):
                    h = 4 * g + hh
                    bo = self.take_bank()
                    self.held.add(bo)
                    bd = self.take_bank()
                    self.held.add(bd)
                    for j in range(nj):
                        off = max(0, (j - 4 * Q) * 128)
                        bs = self.take_bank()
                        self.mm(bs, self.banks[bs][:, off:512], knT[:, hh, j * 128:(j + 1) * 128], qT[:, hh, off:512],
                                True, True, reads=[knb[hh][j // 4], qb[hh]])
                        p_, pbuf = pT[pi % NP], pb[pi % NP]
                        pi += 1
                        self.act(p_[:, off:512], self.banks[bs][:, off:512], AF.Exp, reads=[self.bankbuf[bs], cb], writes=[pbuf],
                                 bias=biasA[:, j, h:h + 1], scale=scale)
                        if j >= 4 * Q:
                            self.tt("pool", p_[:, off:off + 128], p_[:, off:off + 128], self.uincl_bf[:], ALU.mult,
                                    reads=[pbuf, self.kb], writes=[pbuf])
                        self.mm(bo, self.banks[bo][:, off:512], vtok[:, j, hh * 128:(hh + 1) * 128], p_[:, off:512],
                                j == 0, j == nj - 1, reads=[vb[j], pbuf])
                        self.mm(bd, self.banks[bd][:, off:512], self.ones_bf[:], p_[:, off:512],
                                j == 0, j == nj - 1, reads=[pbuf, self.kb])
                    P.op("dve", lambda e, bd=bd: e.reciprocal(out=rden, in_=self.banks[bd][:]), reads=[self.bankbuf[bd]], writes=[rdb])
                    self.tt("dve", oT[:, hh, :], self.banks[bo][:], rden, ALU.mult, reads=[self.bankbuf[bo], rdb], writes=[ob[hh]])
                    self.held.discard(bo)
                    self.held.discard(bd)
                wos = [self.wload(self.d_fwout[j], 2048) for j in range(4)]
                for dc in range(KC):
                    wo = self.ring[wos[dc // 2]][:].rearrange("p (k c) -> p k c", k=KC)
                    bk = self.take_bank()
                    for hh in range(4):
                        self.mm(bk, self.banks[bk][:], wo[:, 4 * g + hh, (dc % 2) * 128:(dc % 2 + 1) * 128], oT[:, hh, :],
                                hh == 0, hh == 3, reads=[self.slotbuf[wos[dc // 2]], ob[hh]])
                    self.tt("dve", self.xT[:, dc, sl], self.banks[bk][:], self.xT[:, dc, sl], ALU.add,
                            reads=[self.bankbuf[bk], self.xbuf[dc][Q]], writes=[self.xbuf[dc][Q]])


ALL_STAGES = [("conf", 0), ("ffn", 0), ("gdn", 1), ("ffn", 1), ("fox", 2), ("ffn", 2), ("conf", 3), ("ffn", 3)]


def build_nc(stages):
    nc = bass.Bass("TRN2", target_bir_lowering=False)
    k = K(nc, stages)
    k.build()
    return nc, k.used_inputs


def prep_shared(inp):
    sh = {}
    sh["cst"] = pack_consts(inp)
    sh["cwin"] = np.stack([tile_w(inp["conv_w_in"][i]).reshape(8, 128, 2048) for i in range(2)])
    sh["cwout"] = np.stack([tile_w(inp["conv_w_out"][i]).reshape(4, 128, 2048) for i in range(2)])
    gw = np.asarray(inp["gdn_w_in"][0], np.float32)
    sh["gwin"] = tile_w(gw[:, :4096]).reshape(16, 128, 2048)
    sh["gwab"] = np.ascontiguousarray(gw[:, 4096:4112].reshape(8, 128, 16).transpose(1, 0, 2)).reshape(128, 128)
    sh["gwout"] = tile_w(inp["gdn_w_out"][0]).reshape(4, 128, 2048)
    fw = np.asarray(inp["fox_w_in"][0], np.float32)
    sh["fwin"] = tile_w(fw[:, :3072]).reshape(12, 128, 2048)
    sh["fwf"] = np.ascontiguousarray(fw[:, 3072:3080].reshape(8, 128, 8).transpose(1, 0, 2)).reshape(128, 64)
    sh["fwout"] = tile_w(inp["fox_w_out"][0]).reshape(4, 128, 2048)
    sh["wup"] = np.stack([tile_w(inp["ffn_w_up"][l]).reshape(22, 128, 2048) for l in range(4)])
    sh["wdn"] = np.stack([tile_wk(inp["ffn_w_down"][l]).reshape(11, 128, 2048) for l in range(4)])
    return sh


def run(inp, stages, ncores=8, trace=False):
    x = np.asarray(inp["x"], np.float32)
    sh = prep_shared(inp)
    nc, used = build_nc(stages)
    sh = {k_: v for k_, v in sh.items() if k_ in used}
    in_maps = []
    for b in range(ncores):
        m = dict(sh)
        m["xT"] = np.ascontiguousarray(x[b].T).reshape(KC, 128, S)
        in_maps.append(m)
    res = run_bass_kernel_spmd(nc, in_maps, core_ids=list(range(ncores)), trace=trace)
    out = np.stack([np.asarray(r["yT"], np.float32).reshape(D, S).T for r in res.results])
    return out, res


def kernel(**inputs):
    out, _ = run(inputs, ALL_STAGES, ncores=8)
    return out.astype(np.float32)
```

```python
import contextlib
import os
import numpy as np
import concourse.bass as bass
import concourse.mybir as mybir
from concourse.bass_utils import run_bass_kernel_spmd

F32 = mybir.dt.float32
BF16 = mybir.dt.bfloat16
AF = mybir.ActivationFunctionType
ALU = mybir.AluOpType
AX = mybir.AxisListType

S = 2048
D = 1024
TT = 512
NT = 4
KC = 8
FF = 2816
EPS = 1e-6
ENGS = ["pe", "act", "dve", "pool", "sp"]
BLOCK_ATTR = {"pe": "tensor", "act": "scalar", "dve": "vector", "pool": "gpsimd", "sp": "sync"}


class Buf:
    __slots__ = ("name", "last_w", "readers", "sem", "dma_cnt", "excl")

    def __init__(self, name, excl=False):
        self.name = name
        self.excl = excl
        self.last_w = None
        self.readers = []
        self.sem = None
        self.dma_cnt = 0


class Op:
    __slots__ = ("eng", "fn", "waits", "signal", "idx", "dma_buf", "clock", "sigcnt")

    def __init__(self, eng, fn, idx, dma_buf=None):
        self.eng = eng
        self.fn = fn
        self.idx = idx
        self.waits = []
        self.signal = False
        self.dma_buf = dma_buf
        self.clock = None
        self.sigcnt = None


class Prog:
    def __init__(self, nc):
        self.nc = nc
        self.ops = {e: [] for e in ENGS}
        self.obs = {e: {} for e in ENGS}
        self.dma_bufs = []
        self.pending = {e: [] for e in ENGS}

    def barrier(self):
        toks = [("eng", e, len(self.ops[e]) - 1) for e in ENGS if self.ops[e] and self.ops[e][-1].dma_buf is None]
        for e in ENGS:
            if self.ops[e] and self.ops[e][-1].dma_buf is not None:
                for o in reversed(self.ops[e]):
                    if o.dma_buf is None:
                        toks.append(("eng", e, o.idx))
                        break
        for e in ENGS:
            self.pending[e] = list(toks)

    def _need(self, op, tok):
        e = op.eng
        if tok[0] == "eng":
            _, se, si = tok
            if self.obs[e].get(se, -1) >= si:
                return
            src = self.ops[se][si]
            src.signal = True
            op.waits.append(tok)
            self.obs[e][se] = si
            if src.clock:
                for k, v in src.clock.items():
                    if self.obs[e].get(k, -1) < v:
                        self.obs[e][k] = v
        else:
            _, b, cnt = tok
            key = ("dma", id(b))
            if self.obs[e].get(key, -1) >= cnt:
                return
            op.waits.append(tok)
            self.obs[e][key] = cnt

    def op(self, eng, fn, reads=(), writes=(), dma_out=None, nobarrier=False):
        lst = self.ops[eng]
        o = Op(eng, fn, len(lst), dma_buf=dma_out)
        if any(r.excl for r in reads):
            writes = list(writes) + [r for r in reads if r.excl and r not in writes]
            reads = [r for r in reads if not r.excl]
        best = {}
        for r in reads:
            t = r.last_w
            if t is not None:
                k = t[1] if t[0] == "eng" else ("dma", id(t[1]))
                if k not in best or best[k][2] < t[2]:
                    best[k] = t
        for w in writes:
            for t in [w.last_w] + w.readers:
                if t is not None:
                    k = t[1] if t[0] == "eng" else ("dma", id(t[1]))
                    if k not in best or best[k][2] < t[2]:
                        best[k] = t
        if self.pending[eng] and not nobarrier:
            for t in self.pending[eng]:
                if t[1] == eng:
                    continue
                k = t[1]
                if k not in best or best[k][2] < t[2]:
                    best[k] = t
            self.pending[eng] = []
        for t in best.values():
            if eng == "pe" and t[0] == "eng" and t[1] == "pe":
                continue
            self._need(o, t)
        if dma_out is not None:
            if dma_out.sem is None:
                self.dma_bufs.append(dma_out)
                dma_out.sem = True
            dma_out.dma_cnt += 1
            tok = ("dma", dma_out, dma_out.dma_cnt)
        else:
            tok = ("eng", eng, o.idx)
        o.clock = dict(self.obs[eng])
        for r in reads:
            r.readers.append(tok)
        for w in writes:
            w.last_w = tok
            w.readers = []
        lst.append(o)
        return tok

    def emit(self, final_waits=()):
        nc = self.nc
        CH = 2000
        with contextlib.ExitStack() as st:
            for e in ENGS:
                c = 0
                for o in self.ops[e]:
                    if o.signal and o.dma_buf is None:
                        o.sigcnt = c
                        c += 1
                    else:
                        o.sigcnt = c - 1
            nsig = {e: sum(1 for o in self.ops[e] if o.signal and o.dma_buf is None) for e in ENGS}
            esem = {e: [st.enter_context(nc.semaphore("s_%s%d" % (e, i))) for i in range(max(1, (nsig[e] + CH - 1) // CH))]
                    for e in ENGS}
            for i, b in enumerate(self.dma_bufs):
                b.sem = st.enter_context(nc.semaphore("d%d" % i))
            block = st.enter_context(nc.Block())
            for e in ENGS:
                ops = self.ops[e]
                fw = [b for (fe, b) in final_waits if fe == e]
                if not ops and not fw:
                    continue

                def body(eng, ops=ops, e=e, fw=fw):
                    for o in ops:
                        for t in o.waits:
                            if t[0] == "eng":
                                k = self.ops[t[1]][t[2]].sigcnt
                                eng.wait_ge(esem[t[1]][k // CH], k % CH + 1)
                            else:
                                eng.wait_ge(t[1].sem, 16 * t[2])
                        ins = o.fn(eng)
                        if o.dma_buf is not None:
                            ins.then_inc(o.dma_buf.sem, 16)
                        elif o.signal:
                            ins.then_inc(esem[e][o.sigcnt // CH], 1)
                    for b in fw:
                        eng.wait_ge(b.sem, 16 * b.dma_cnt)

                getattr(block, BLOCK_ATTR[e])(body)


def _cst_layout():
    lay = {}
    off = 0
    for name, n in [("mixg", 32), ("ffng", 32), ("cbin", 32), ("cwdw", 2 * 31 * 8), ("cbdw", 16),
                    ("clng", 16), ("clnb", 16), ("gconv", 96), ("gog", 1), ("fqg", 1), ("fkg", 1),
                    ("fwdw", 4 * 3 * 44), ("fbf", 1), ("galog", 8), ("gdtb", 8), ("fbfb", 8)]:
        lay[name] = off
        off += n
    return lay, off


CL, NCST = _cst_layout()


def _cols(v):
    v = np.asarray(v, dtype=np.float32).reshape(-1, 128)
    return v.T


def pack_consts(inp):
    c = np.zeros((128, NCST), np.float32)

    def put(name, arr):
        arr = np.asarray(arr, np.float32)
        c[:, CL[name]:CL[name] + arr.shape[1]] = arr

    put("mixg", _cols(inp["mix_norm_g"].reshape(-1)))
    put("ffng", _cols(inp["ffn_norm_g"].reshape(-1)))
    put("cbin", _cols(inp["conv_b_in"].reshape(-1)))
    put("cwdw", _cols(inp["conv_w_dw"].reshape(-1)))
    put("cbdw", _cols(inp["conv_b_dw"].reshape(-1)))
    put("clng", _cols(inp["conv_ln_g"].reshape(-1)))
    put("clnb", _cols(inp["conv_ln_b"].reshape(-1)))
    put("gconv", _cols(inp["gdn_conv_w"].reshape(-1)))
    put("gog", _cols(inp["gdn_o_norm_g"].reshape(-1)))
    put("fqg", _cols(inp["fox_q_norm_g"].reshape(-1)))
    put("fkg", _cols(inp["fox_k_norm_g"].reshape(-1)))
    put("fwdw", _cols(inp["ffn_w_dw"].reshape(-1)))
    bf = np.zeros((128, 1), np.float32)
    bf[:8, 0] = np.asarray(inp["fox_b_f"], np.float32).reshape(-1)
    put("fbf", bf)
    put("galog", np.broadcast_to(np.asarray(inp["gdn_a_log"], np.float32).reshape(1, 8), (128, 8)))
    put("gdtb", np.broadcast_to(np.asarray(inp["gdn_dt_bias"], np.float32).reshape(1, 8), (128, 8)))
    put("fbfb", np.broadcast_to(np.asarray(inp["fox_b_f"], np.float32).reshape(1, 8), (128, 8)))
    return c


def tile_w(w, width=256):
    w = np.asarray(w, np.float32)
    K, N = w.shape
    return np.ascontiguousarray(w.reshape(K // 128, 128, N // width, width).transpose(2, 1, 0, 3))


def tile_wk(w, nk=2):
    w = np.asarray(w, np.float32)
    K, N = w.shape
    return np.ascontiguousarray(w.reshape(K // (128 * nk), nk, 128, N).transpose(0, 2, 1, 3))


class K:
    def __init__(self, nc, stages):
        self.nc = nc
        self.P = Prog(nc)
        self.stages = stages
        self.st = contextlib.ExitStack()
        self.bank_rr = 0
        self.slot_rr = 0
        self.uid = 0

    def sb(self, name, shape, dt):
        return self.st.enter_context(self.nc.sbuf_tensor(name, shape, dt))

    def B(self, name=""):
        self.uid += 1
        return Buf("%s%d" % (name, self.uid))

    def take_bank(self):
        while True:
            i = self.bank_rr % 8
            self.bank_rr += 1
            if i not in self.held:
                return i

    def take_slot(self):
        i = self.slot_rr % self.NSLOT
        self.slot_rr += 1
        return i

    def mm(self, bank, out, lhsT, rhs, start, stop, reads, extra_w=()):
        self.P.op("pe", lambda e: e.matmul(out, lhsT, rhs, start=start, stop=stop),
                  reads=reads, writes=[self.bankbuf[bank]] + list(extra_w))

    def tr(self, bank, out, in_, ident, reads):
        self.P.op("pe", lambda e: e.transpose(out, in_, ident), reads=reads, writes=[self.bankbuf[bank]])

    def act(self, out, in_, func, reads, writes, bias=None, scale=None, eng="act"):
        kw = {}
        if bias is not None:
            kw["bias"] = bias
        if scale is not None:
            kw["scale"] = scale
        self.P.op("act", lambda e: e.activation(out=out, in_=in_, func=func, **kw), reads=reads, writes=writes)

    def tt(self, eng, out, in0, in1, op, reads, writes):
        self.P.op(eng, lambda e: e.tensor_tensor(out=out, in0=in0, in1=in1, op=op), reads=reads, writes=writes)

    def stt(self, out, in0, scalar, in1, op0, op1, reads, writes):
        self.P.op("dve", lambda e: e.scalar_tensor_tensor(out=out, in0=in0, scalar=scalar, in1=in1, op0=op0, op1=op1),
                  reads=reads, writes=writes)

    def ts(self, eng, out, in0, s1, s2, op0, op1, reads, writes):
        if op1 is None:
            self.P.op(eng, lambda e: e.tensor_scalar(out=out, in0=in0, scalar1=s1, scalar2=None, op0=op0),
                      reads=reads, writes=writes)
        else:
            self.P.op(eng, lambda e: e.tensor_scalar(out=out, in0=in0, scalar1=s1, scalar2=s2, op0=op0, op1=op1),
                      reads=reads, writes=writes)

    def cp(self, eng, out, in_, reads, writes):
        if eng == "act":
            self.P.op("act", lambda e: e.copy(out=out, in_=in_), reads=reads, writes=writes)
        else:
            self.P.op(eng, lambda e: e.tensor_copy(out=out, in_=in_), reads=reads, writes=writes)

    def memset(self, eng, ap, val, writes):
        self.P.op(eng, lambda e: e.memset(ap, val), writes=writes)

    def wload(self, src_ap, ncols_total, view=None):
        s = self.take_slot()
        dst = self.ring[s][:, 0:ncols_total]
        self.P.op("pool", lambda e: e.dma_start(out=dst, in_=src_ap), writes=[self.slotbuf[s]], dma_out=self.slotbuf[s], nobarrier=True)
        return s

    def build(self):
        nc = self.nc
        P = self.P
        dt = nc.dram_tensor
        self.xin = dt("xT", [KC, 128, S], F32, kind="ExternalInput").ap()
        self.cst_d = dt("cst", [128, NCST], F32, kind="ExternalInput").ap()
        kinds = set(k_ for k_, _ in self.stages)
        self.used_inputs = ["xT", "cst"]

        def din(name, shape, kind_):
            if kind_ not in kinds:
                return None
            self.used_inputs.append(name)
            return dt(name, shape, F32, kind="ExternalInput").ap()
        self.d_cwin = din("cwin", [2, 8, 128, 2048], "conf")
        self.d_cwout = din("cwout", [2, 4, 128, 2048], "conf")
        self.d_gwin = din("gwin", [16, 128, 2048], "gdn")
        self.d_gwab = din("gwab", [128, 128], "gdn")
        self.d_gwout = din("gwout", [4, 128, 2048], "gdn")
        self.d_fwin = din("fwin", [12, 128, 2048], "fox")
        self.d_fwf = din("fwf", [128, 64], "fox")
        self.d_fwout = din("fwout", [4, 128, 2048], "fox")
        self.d_wup = din("wup", [4, 22, 128, 2048], "ffn")
        self.d_wdn = din("wdn", [4, 11, 128, 2048], "ffn")
        self.yout = dt("yT", [KC, 128, S], F32, kind="ExternalOutput").ap()

        self.xT = self.sb("xT_sb", [128, KC, S], F32)
        self.cst = self.sb("cst_sb", [128, NCST], F32)
        self.NSLOT = 8
        self.ring = [self.sb("ring%d" % i, [128, 2048], BF16) for i in range(self.NSLOT)]
        self.slotbuf = [Buf("slot%d" % i) for i in range(self.NSLOT)]
        self.banks = [self.st.enter_context(nc.psum_tensor("bank%d" % i, [128, 512], F32)) for i in range(8)]
        self.bankbuf = [Buf("bank%d" % i, excl=True) for i in range(8)]
        self.held = set()
        self.xbuf = [[Buf("x%d_%d" % (c, t)) for t in range(NT)] for c in range(KC)]
        self.hbuf = [Buf("h%d" % t) for t in range(NT)]
        self.hb1 = Buf("htile")
        self.cbuf = Buf("cst")
        self.ones_bf = self.sb("ones_bf", [128, 128], BF16)
        self.ones_f = self.sb("ones_f", [128, 128], F32)
        self.ident_f = self.sb("ident_f", [128, 128], F32)
        self.ident_bf = self.sb("ident_bf", [128, 128], BF16)
        self.uincl_f = self.sb("uincl_f", [128, 128], F32)
        self.uincl_bf = self.sb("uincl_bf", [128, 128], BF16)
        self.lstrict_f = self.sb("lstrict_f", [128, 128], F32)
        self.msu_f = self.sb("msu_f", [128, 128], F32)
        self.kb = Buf("consts")
        self.ARENA = 25600
        self.arena = self.sb("arena", [128, self.ARENA], F32)

        P.op("sp", lambda e: e.dma_start(out=self.cst[:], in_=self.cst_d), writes=[self.cbuf], dma_out=self.cbuf)
        kb = self.kb
        self.memset("pool", self.ones_f[:], 1.0, [kb])
        self.memset("pool", self.ones_bf[:], 1.0, [kb])

        def asel(out, base, cm, step, op):
            P.op("pool", lambda e: e.affine_select(out=out, in_=self.ones_f[:], pattern=[[step, 128]], compare_op=op,
                                                   fill=0.0, base=base, channel_multiplier=cm), reads=[kb], writes=[kb])
        asel(self.ident_f[:], 0, -1, 1, ALU.is_equal)
        asel(self.uincl_f[:], 0, -1, 1, ALU.is_ge)
        asel(self.lstrict_f[:], -1, 1, -1, ALU.is_ge)
        asel(self.msu_f[:], -1, -1, 1, ALU.is_ge)
        self.cp("pool", self.ident_bf[:], self.ident_f[:], [kb], [kb])
        self.cp("pool", self.uincl_bf[:], self.uincl_f[:], [kb], [kb])

        for c in range(KC):
            for t in range(NT):
                P.op("sp", lambda e, c=c, t=t: e.dma_start(out=self.xT[:, c, t * TT:(t + 1) * TT],
                                                           in_=self.xin[c, :, t * TT:(t + 1) * TT]),
                     writes=[self.xbuf[c][t]], dma_out=self.xbuf[c][t])

        ia = 0
        for stg in self.stages:
            kind, layer = stg
            P.barrier()
            if kind == "conf":
                self.conformer(layer, ia)
                ia += 1
            elif kind == "gdn":
                self.gdn(layer)
            elif kind == "fox":
                self.fox(layer)
            elif kind == "ffn":
                self.ffn(layer)

        ob = Buf("out")
        for c in range(KC):
            P.op("sp", lambda e, c=c: e.dma_start(out=self.yout[c], in_=self.xT[:, c, :]),
                 reads=self.xbuf[c], writes=[ob], dma_out=ob)
        P.emit(final_waits=[("sp", ob)])
        self.st.close()

    def cc(self, name, idx=0):
        o = CL[name] + idx
        return self.cst[:, o:o + 1]

    def carve(self, off, nwords):
        assert off + nwords <= self.ARENA, (off, nwords)
        return self.arena[:, off:off + nwords]

    def rmsnorm_tile(self, t, gname, layer, hview, hb, ar_off):
        sl = slice(t * TT, (t + 1) * TT)
        sqs = [self.carve(ar_off + i * 256, 256).bitcast(BF16) for i in range(2)]
        lnv = self.carve(ar_off + 512, 512)
        rstd = self.carve(ar_off + 1024, 512)
        bl, br = self.nb_ln, self.nb_rstd
        bk = self.take_bank()
        for c in range(KC):
            sq, bsq = sqs[c % 2], self.nb_sq[c % 2]
            self.act(sq, self.xT[:, c, sl], AF.Square, reads=[self.xbuf[c][t]], writes=[bsq])
            self.mm(bk, self.banks[bk][:], self.ones_bf[:], sq, c == 0, c == KC - 1, reads=[bsq, self.kb])
        self.act(lnv, self.banks[bk][:], AF.Ln, reads=[self.bankbuf[bk], self.kb], writes=[bl], bias=self.eps_col, scale=1.0 / D)
        self.act(rstd, lnv, AF.Exp, reads=[bl], writes=[br], scale=-0.5)
        for c in range(KC):
            self.stt(hview[:, c, :], self.xT[:, c, sl], self.cc(gname, layer * 8 + c), rstd, ALU.mult, ALU.mult,
                     reads=[self.xbuf[c][t], br, self.cbuf], writes=[hb])

    def common_init(self):
        if getattr(self, "_ci", False):
            return
        self._ci = True
        self.nb_sq, self.nb_ln, self.nb_rstd = [Buf("sq0"), Buf("sq1")], Buf("ln"), Buf("rstd")
        self.eps_t = self.sb("eps_t", [128, 4], F32)
        self.memset("pool", self.eps_t[:, 0:1], EPS, [self.kb])
        self.memset("pool", self.eps_t[:, 1:2], 1.0, [self.kb])
        self.memset("pool", self.eps_t[:, 2:3], 0.0, [self.kb])
        self.eps_col = self.eps_t[:, 0:1]
        self.one_col = self.eps_t[:, 1:2]
        self.zero_col = self.eps_t[:, 2:3]

    def ffn(self, layer):
        self.common_init()
        P = self.P
        hT = self.carve(0, 8192).bitcast(BF16).rearrange("p (c t) -> p c t", c=KC)
        self.hT = hT
        for t in range(NT):
            self.rmsnorm_tile(t, "ffng", layer, hT[:, :, t * TT:(t + 1) * TT], self.hbuf[t], 8192)
        GOFF = 8192 + 1536
        gT = [self.carve(GOFF + i * 4096, 4096).bitcast(BF16).rearrange("p (c t) -> p c t", c=4) for i in range(2)]
        gbuf = [[[Buf("g") for _ in range(NT)] for _ in range(4)] for _ in range(2)]
        UOFF = GOFF + 2 * 4096
        NU = 6
        U = [self.carve(UOFF + i * 516, 516) for i in range(NU)]
        ubuf = [Buf("U") for _ in range(NU)]
        AOFF = UOFF + NU * 516
        NA = 8
        ACC = [self.carve(AOFF + i * 512, 512) for i in range(NA)]
        abuf = [Buf("acc") for _ in range(NA)]
        urr = [0]
        arr = [0]
        hreads = self.hbuf

        parts = [(0, 2), (2, 4), (4, 6), (6, 8), (8, 10), (10, 11)]
        order = []
        for (u0, u1) in parts:
            for u in range(u0, u1):
                order.append(("g", u, self.d_wup[layer, u]))
                order.append(("u", u, self.d_wup[layer, 11 + u]))
            for u in range(u0, u1):
                order.append(("d", u, self.d_wdn[layer, u]))
        issued = {}
        nxt = [0]
        deferred = []
        pend_down = []

        def want(key, ahead):
            idx = [i for i, o_ in enumerate(order) if (o_[0], o_[1]) == key][0]
            while nxt[0] <= min(idx + ahead, len(order) - 1):
                o_ = order[nxt[0]]
                issued[(o_[0], o_[1])] = self.wload(o_[2], 2048)
                nxt[0] += 1
            return issued[key]

        for pi, (u0, u1) in enumerate(parts):
            g = gT[pi % 2]
            gb = gbuf[pi % 2]
            for u in range(u0, u1):
                if u == u0 + 1 or (u == u0 and u1 - u0 == 1 and False):
                    while pend_down:
                        pend_down.pop(0)()
                sg = want(("g", u), 5)
                su = want(("u", u), 4)
                wg = self.ring[sg][:].rearrange("p (k c) -> p k c", k=KC)
                wu = self.ring[su][:].rearrange("p (k c) -> p k c", k=KC)
                for c2 in range(2):
                    ci = 2 * u + c2
                    lc = ci - 2 * u0
                    prev = {"g": None, "u": None}
                    for t in range(NT):
                        sl = slice(t * TT, (t + 1) * TT)
                        accs = {}
                        for which, w_, slot_, col0 in (("g", wg, sg, ci), ("u", wu, su, 22 + ci)):
                            bk = self.take_bank()
                            for kc in range(KC):
                                self.mm(bk, self.banks[bk][:], w_[:, kc, c2 * 128:(c2 + 1) * 128], self.hT[:, kc, sl],
                                        kc == 0, kc == KC - 1, reads=[self.slotbuf[slot_], self.hbuf[t]])
                            ui = urr[0] % NU
                            urr[0] += 1
                            Ut, Ub = U[ui], ubuf[ui]
                            if t == 0:
                                self.memset("pool", Ut[:, 0:2], 0.0, [Ub])
                            else:
                                pU, pB = prev[which]
                                self.cp("pool", Ut[:, 0:2], pU[:, 512:514], [pB], [Ub])
                            self.act(Ut[:, 2:514], self.banks[bk][:], AF.Copy, reads=[self.bankbuf[bk]], writes=[Ub])
                            prev[which] = (Ut, Ub)
                            ai = arr[0] % NA
                            arr[0] += 1
                            At, Ab = ACC[ai], abuf[ai]
                            wcol = lambda k, col0=col0: self.cc("fwdw", (layer * 3 + k) * 44 + col0)
                            self.act(At, Ut[:, 0:512], AF.Copy, reads=[Ub, self.cbuf], writes=[Ab], scale=wcol(0))
                            self.stt(At, Ut[:, 1:513], wcol(1), At, ALU.mult, ALU.add, reads=[Ub, self.cbuf, Ab], writes=[Ab])
                            self.stt(At, Ut[:, 2:514], wcol(2), At, ALU.mult, ALU.add, reads=[Ub, self.cbuf, Ab], writes=[Ab])
                            accs[which] = (At, Ab)
                        Ag, Agb = accs["g"]
                        Au, Aub = accs["u"]

                        def tail(Ag=Ag, Agb=Agb, Au=Au, Aub=Aub, dst=g[:, lc, sl], db=gb[lc][t]):
                            self.act(Ag, Ag, AF.Silu, reads=[Agb], writes=[Agb])
                            self.tt("pool", dst, Ag, Au, ALU.mult, reads=[Agb, Aub], writes=[db])
                        if deferred:
                            deferred.pop(0)()
                        deferred.append(tail)
            dslots = [want(("d", u), 3) for u in range(u0, u1)]

            def down(g=g, gb=gb, nch=2 * (u1 - u0), dslots=dslots):
                while deferred:
                    deferred.pop(0)()
                for dc in range(KC):
                    for t in range(NT):
                        sl = slice(t * TT, (t + 1) * TT)
                        bk = self.take_bank()
                        for lc in range(nch):
                            s_ = dslots[lc // 2]
                            wd = self.ring[s_][:].rearrange("p (k c) -> p k c", k=2)
                            self.mm(bk, self.banks[bk][:], wd[:, lc % 2, dc * 128:(dc + 1) * 128], g[:, lc, sl],
                                    lc == 0, lc == nch - 1, reads=[self.slotbuf[s_], gb[lc][t]])
                        self.tt("dve", self.xT[:, dc, sl], self.banks[bk][:], self.xT[:, dc, sl], ALU.add,
                                reads=[self.bankbuf[bk], self.xbuf[dc][t]], writes=[self.xbuf[dc][t]])
            pend_down.append(down)
        while pend_down:
            pend_down.pop(0)()

    def conformer(self, layer, ia):
        self.common_init()
        P = self.P
        hTt = self.carve(0, 2048).bitcast(BF16).rearrange("p (c t) -> p c t", c=KC)
        G = self.carve(3584, 2176).bitcast(BF16).rearrange("p (c t) -> p c t", c=KC)
        gb = [Buf("G%d" % c) for c in range(KC)]
        DG = [self.carve(5760 + i * 1984, 1984).bitcast(BF16).rearrange("p (k m) -> p k m", k=31) for i in range(2)]
        dgb = [Buf("dg0"), Buf("dg1")]
        CO = self.carve(9728, 4096).rearrange("p (c t) -> p c t", c=KC)
        cob = [Buf("co%d" % c) for c in range(KC)]
        UB = [self.carve(13824 + i * 256, 256).bitcast(BF16) for i in range(2)]
        SQ = [self.carve(14336 + i * 256, 256).bitcast(BF16) for i in range(2)]
        ubb = [Buf("ub0"), Buf("ub1")]
        sqb = [Buf("sqb0"), Buf("sqb1")]
        mean = self.carve(14848, 512)
        msq = self.carve(15360, 512)
        lnv = self.carve(15872, 512)
        rstd = self.carve(16384, 512)
        stb = Buf("stats")
        TMP = [self.carve(16896 + i * 512, 512) for i in range(2)]
        tmb = [Buf("tmp0"), Buf("tmp1")]
        sT = self.carve(17920, 2048).bitcast(BF16).rearrange("p (c t) -> p c t", c=KC)
        stbuf = [Buf("sT%d" % c) for c in range(KC)]
        SG = [self.carve(19968 + i * 512, 512) for i in range(2)]
        sgb = [Buf("sg0"), Buf("sg1")]
        for c in range(KC):
            self.memset("pool", G[:, c, 0:32], 0.0, [gb[c]])
        for t in range(NT):
            sl = slice(t * TT, (t + 1) * TT)
            self.rmsnorm_tile(t, "mixg", layer, hTt, self.hb1, 2048)
            s1 = self.take_bank()
            self.held.add(s1)
            s2 = self.take_bank()
            self.held.add(s2)
            for c in range(KC):
                if c % 2 == 0:
                    sv = self.wload(self.d_cwin[ia, c // 2], 2048)
                    sg_ = self.wload(self.d_cwin[ia, 4 + c // 2], 2048)
                    wv = self.ring[sv][:].rearrange("p (k c) -> p k c", k=KC)
                    wg = self.ring[sg_][:].rearrange("p (k c) -> p k c", k=KC)
                bv = self.take_bank()
                for kc in range(KC):
                    self.mm(bv, self.banks[bv][:], wv[:, kc, (c % 2) * 128:(c % 2 + 1) * 128], hTt[:, kc, :],
                            kc == 0, kc == KC - 1, reads=[self.slotbuf[sv], self.hb1])
                bg = self.take_bank()
                for kc in range(KC):
                    self.mm(bg, self.banks[bg][:], wg[:, kc, (c % 2) * 128:(c % 2 + 1) * 128], hTt[:, kc, :],
                            kc == 0, kc == KC - 1, reads=[self.slotbuf[sg_], self.hb1])
                sg, sgbuf = SG[c % 2], sgb[c % 2]
                self.act(sg, self.banks[bg][:], AF.Sigmoid, reads=[self.bankbuf[bg], self.cbuf], writes=[sgbuf],
                         bias=self.cc("cbin", ia * 16 + 8 + c))
                self.stt(G[:, c, 30:542], self.banks[bv][:], self.cc("cbin", ia * 16 + c), sg, ALU.add, ALU.mult,
                         reads=[self.bankbuf[bv], sgbuf, self.cbuf], writes=[gb[c]])
                dg, dgbuf = DG[c % 2], dgb[c % 2]
                wb = CL["cwdw"] + ia * 31 * 8 + c
                wtaps = self.cst[:, wb:wb + 30 * 8 + 1:8]
                self.tt("dve", dg, self.ident_bf[:].unsqueeze(1).broadcast_to([128, 31, 128]),
                        wtaps.unsqueeze(2).broadcast_to([128, 31, 128]), ALU.mult, reads=[self.kb, self.cbuf], writes=[dgbuf])
                bc = self.take_bank()
                for k in range(31):
                    self.mm(bc, self.banks[bc][:], dg[:, k, :], G[:, c, k:k + 512], k == 0, k == 30, reads=[dgbuf, gb[c]])
                self.act(CO[:, c, :], self.banks[bc][:], AF.Identity, reads=[self.bankbuf[bc], self.cbuf], writes=[cob[c]],
                         bias=self.cc("cbdw", ia * 8 + c))
                self.cp("pool", G[:, c, 0:30], G[:, c, 512:542], [gb[c]], [gb[c]])
                ub, ubbuf = UB[c % 2], ubb[c % 2]
                sq, sqbuf = SQ[c % 2], sqb[c % 2]
                self.cp("dve", ub, CO[:, c, :], [cob[c]], [ubbuf])
                self.act(sq, CO[:, c, :], AF.Square, reads=[cob[c]], writes=[sqbuf])
                self.mm(s1, self.banks[s1][:], self.ones_bf[:], ub, c == 0, c == KC - 1, reads=[ubbuf, self.kb])
                self.mm(s2, self.banks[s2][:], self.ones_bf[:], sq, c == 0, c == KC - 1, reads=[sqbuf, self.kb])
            self.ts("dve", mean, self.banks[s1][:], 1.0 / D, None, ALU.mult, None, reads=[self.bankbuf[s1]], writes=[stb])
            self.tt("dve", msq, mean, mean, ALU.mult, reads=[stb], writes=[stb])
            self.stt(msq, self.banks[s2][:], 1.0 / D, msq, ALU.mult, ALU.subtract, reads=[self.bankbuf[s2], stb], writes=[stb])
            self.act(lnv, msq, AF.Ln, reads=[stb, self.kb], writes=[stb], bias=self.eps_col)
            self.act(rstd, lnv, AF.Exp, reads=[stb], writes=[stb], scale=-0.5)
            self.held.discard(s1)
            self.held.discard(s2)
            for c in range(KC):
                tm, tmbuf = TMP[c % 2], tmb[c % 2]
                self.tt("dve", tm, CO[:, c, :], mean, ALU.subtract, reads=[cob[c], stb], writes=[tmbuf])
                self.tt("dve", tm, tm, rstd, ALU.mult, reads=[tmbuf, stb], writes=[tmbuf])
                self.act(sT[:, c, :], tm, AF.Silu, reads=[tmbuf, self.cbuf], writes=[stbuf[c]],
                         bias=self.cc("clnb", ia * 8 + c), scale=self.cc("clng", ia * 8 + c))
            wos = [self.wload(self.d_cwout[ia, j], 2048) for j in range(4)]
            for dc in range(KC):
                wo = self.ring[wos[dc // 2]][:].rearrange("p (k c) -> p k c", k=KC)
                bk = self.take_bank()
                for kc in range(KC):
                    self.mm(bk, self.banks[bk][:], wo[:, kc, (dc % 2) * 128:(dc % 2 + 1) * 128], sT[:, kc, :],
                            kc == 0, kc == KC - 1, reads=[self.slotbuf[wos[dc // 2]], stbuf[kc]])
                self.tt("dve", self.xT[:, dc, sl], self.banks[bk][:], self.xT[:, dc, sl], ALU.add,
                        reads=[self.bankbuf[bk], self.xbuf[dc][t]], writes=[self.xbuf[dc][t]])

    def gdn(self, layer):
        self.common_init()
        P = self.P
        hT = self.carve(0, 8192).bitcast(BF16).rearrange("p (c t) -> p c t", c=KC)
        for t in range(NT):
            self.rmsnorm_tile(t, "mixg", layer, hT[:, :, t * TT:(t + 1) * TT], self.hbuf[t], 8192)
        o = [9728]

        def al(n):
            a = self.carve(o[0], n)
            o[0] += n
            return a

        def bf4(n=256):
            return al(n).bitcast(BF16).rearrange("p (b c) -> p b c", b=4)

        gtok, gcs, eg, negeg, egl, gl, beta, negbeta = [al(32) for _ in range(8)]
        abt = al(64)
        gb_ = Buf("gates")
        qnT = [al(256).bitcast(BF16) for _ in range(4)]
        knT = [al(256).bitcast(BF16) for _ in range(4)]
        vtok = [bf4() for _ in range(4)]
        kdtok = [bf4() for _ in range(4)]
        zsT = [al(256).bitcast(BF16) for _ in range(4)]
        TTm = [bf4() for _ in range(4)]
        QKD = [bf4() for _ in range(4)]
        Pm = [bf4() for _ in range(4)]
        Qm = [bf4() for _ in range(4)]
        nm = lambda n: [Buf(n + str(i)) for i in range(4)]
        qnb, knb, vtb, kdb, zsb, ttb, qkb, pmb, qmb = [nm(n) for n in ("qn", "kn", "vt", "kd", "zs", "tt", "qk", "pm", "qm")]
        U = [al(516) for _ in range(2)]
        ub = [Buf("U0"), Buf("U1")]
        ACC = [al(512) for _ in range(2)]
        ab_ = [Buf("A0"), Buf("A1")]
        sqt = al(256).bitcast(BF16)
        lnv = al(512)
        rstd = al(512)
        sqb, lnb, rsb = Buf("sq"), Buf("ln"), Buf("rs")
        decT = al(512)
        decb = Buf("dec")
        tmp = al(512)
        tmpb = Buf("tmp")
        lhsg = [al(128) for _ in range(2)]
        lgb = [Buf("lg0"), Buf("lg1")]
        vbf = al(256).bitcast(BF16)
        vbb = Buf("vbf")
        vnew = bf4()
        vnb = Buf("vnew")
        onb_ = bf4()
        onbuf = Buf("on")
        Sf = al(512).rearrange("p (h c) -> p h c", h=4)
        Sb = bf4()
        sfb, sbb = Buf("Sf"), Buf("Sb")
        haloS = al(48).rearrange("p (c k) -> p c k", c=12)
        hsb = [Buf("hs%d" % i) for i in range(12)]
        ssq = al(8)
        ssb = Buf("ssq")
        wab = al(64).bitcast(BF16).rearrange("p (k c) -> p k c", k=KC)
        negA = al(8)
        assert o[0] <= self.ARENA, o[0]
        wabb = Buf("wab")
        P.op("pool", lambda e: e.dma_start(out=wab.rearrange("p k c -> p (k c)"), in_=self.d_gwab), writes=[wabb], dma_out=wabb)
        nab = Buf("negA")
        self.act(negA, self.cst[:, CL["galog"]:CL["galog"] + 8], AF.Exp, reads=[self.cbuf], writes=[nab])
        self.ts("dve", negA, negA, -1.0, None, ALU.mult, None, reads=[nab], writes=[nab])
        dtb = self.cst[:, CL["gdtb"]:CL["gdtb"] + 8]
        v4 = lambda a: a.rearrange("p (b h) -> p b h", b=4)
        bfbank = lambda bk: self.banks[bk][:].bitcast(BF16)[:, 0:512].rearrange("p (b c) -> p b c", b=4)
        fbank = lambda bk: self.banks[bk][:].rearrange("p (b c) -> p b c", b=4)
        qscale = 128.0 ** -0.5
        for g in range(2):
            self.memset("pool", Sf[:], 0.0, [sfb])
            self.memset("pool", Sb[:], 0.0, [sbb])
            for Q in range(NT):
                sl = slice(Q * TT, (Q + 1) * TT)
                hTt = hT[:, :, sl]
                hb = self.hbuf[Q]
                bk = self.take_bank()
                for blk in range(4):
                    for kc in range(KC):
                        self.mm(bk, self.banks[bk][:, blk * 16:(blk + 1) * 16], hTt[:, kc, blk * 128:(blk + 1) * 128], wab[:, kc, :],
                                kc == 0, kc == KC - 1, reads=[wabb, hb])
                self.cp("dve", abt, self.banks[bk][:, 0:64], [self.bankbuf[bk]], [gb_])
                ab3 = abt.rearrange("p (b c) -> p b c", b=4)
                self.act(v4(beta), ab3[:, :, 8:16], AF.Sigmoid, reads=[gb_], writes=[gb_])
                self.ts("dve", negbeta, beta, -1.0, None, ALU.mult, None, reads=[gb_], writes=[gb_])
                self.tt("dve", v4(gtok), ab3[:, :, 0:8], dtb.unsqueeze(1).broadcast_to([128, 4, 8]), ALU.add, reads=[gb_, self.cbuf], writes=[gb_])
                self.act(gtok, gtok, AF.Exp, reads=[gb_], writes=[gb_])
                self.act(gtok, gtok, AF.Ln, reads=[gb_, self.kb], writes=[gb_], bias=self.one_col)
                self.tt("dve", v4(gtok), v4(gtok), negA.unsqueeze(1).broadcast_to([128, 4, 8]), ALU.mult, reads=[gb_, nab], writes=[gb_])
                bk = self.take_bank()
                self.mm(bk, self.banks[bk][:, 0:32], self.uincl_f[:], gtok, True, True, reads=[gb_, self.kb])
                b2 = self.take_bank()
                self.mm(b2, self.banks[b2][:, 0:32], self.ones_f[:], gtok, True, True, reads=[gb_, self.kb])
                self.cp("dve", gcs, self.banks[bk][:, 0:32], [self.bankbuf[bk]], [gb_])
                self.act(eg, gcs, AF.Exp, reads=[gb_], writes=[gb_])
                self.ts("dve", negeg, eg, -1.0, None, ALU.mult, None, reads=[gb_], writes=[gb_])
                self.tt("dve", egl, self.banks[b2][:, 0:32], gcs, ALU.subtract, reads=[self.bankbuf[b2], gb_], writes=[gb_])
                self.act(egl, egl, AF.Exp, reads=[gb_], writes=[gb_])
                self.act(gl, self.banks[b2][:, 0:32], AF.Exp, reads=[self.bankbuf[b2]], writes=[gb_])
                ui = 0
                for hh in range(4):
                    h = 4 * g + hh
                    for which in range(4):
                        if hh % 2 == 0:
                            pass
                        sw = self.wload(self.d_gwin[which * 4 + h // 2], 2048) if (hh % 2 == 0 or True) else None
                        w_ = self.ring[sw][:].rearrange("p (k c) -> p k c", k=KC)
                        bk = self.take_bank()
                        for kc in range(KC):
                            self.mm(bk, self.banks[bk][:], w_[:, kc, (h % 2) * 128:(h % 2 + 1) * 128], hTt[:, kc, :],
                                    kc == 0, kc == KC - 1, reads=[self.slotbuf[sw], hb])
                        if which == 3:
                            self.act(zsT[hh], self.banks[bk][:], AF.Silu, reads=[self.bankbuf[bk]], writes=[zsb[hh]])
                            continue
                        ci = which * 4 + hh
                        chunk = which * 8 + h
                        Ut, Ub = U[ui % 2], ub[ui % 2]
                        At, Ab = ACC[ui % 2], ab_[ui % 2]
                        ui += 1
                        if Q == 0:
                            self.memset("pool", Ut[:, 0:3], 0.0, [Ub])
                        else:
                            self.cp("pool", Ut[:, 0:3], haloS[:, ci, 0:3], [hsb[ci]], [Ub])
                        self.act(Ut[:, 3:515], self.banks[bk][:], AF.Copy, reads=[self.bankbuf[bk]], writes=[Ub])
                        self.cp("pool", haloS[:, ci, 0:3], Ut[:, 512:515], [Ub], [hsb[ci]])
                        wc = lambda k, chunk=chunk: self.cc("gconv", k * 24 + chunk)
                        self.act(At, Ut[:, 0:512], AF.Copy, reads=[Ub, self.cbuf], writes=[Ab], scale=wc(0))
                        for k in range(1, 4):
                            self.stt(At, Ut[:, k:k + 512], wc(k), At, ALU.mult, ALU.add, reads=[Ub, self.cbuf, Ab], writes=[Ab])
                        self.act(At, At, AF.Silu, reads=[Ab], writes=[Ab])
                        if which < 2:
                            self.act(sqt, At, AF.Square, reads=[Ab], writes=[sqb])
                            b2 = self.take_bank()
                            self.mm(b2, self.banks[b2][:], self.ones_bf[:], sqt, True, True, reads=[sqb, self.kb])
                            self.act(lnv, self.banks[b2][:], AF.Ln, reads=[self.bankbuf[b2], self.kb], writes=[lnb], bias=self.eps_col)
                            self.act(rstd, lnv, AF.Exp, reads=[lnb], writes=[rsb], scale=-0.5)
                            if which == 0:
                                self.stt(qnT[hh], At, qscale, rstd, ALU.mult, ALU.mult, reads=[Ab, rsb], writes=[qnb[hh]])
                            else:
                                self.tt("dve", knT[hh], At, rstd, ALU.mult, reads=[Ab, rsb], writes=[knb[hh]])
                        else:
                            self.cp("dve", vbf, At, [Ab], [vbb])
                            bt = self.take_bank()
                            for blk in range(4):
                                self.tr(bt, bfbank(bt)[:, blk, :], vbf[:, blk * 128:(blk + 1) * 128], self.ident_bf[:], reads=[vbb, self.kb])
                            self.cp("act", vtok[hh], bfbank(bt), [self.bankbuf[bt]], [vtb[hh]])
                    bt = self.take_bank()
                    for blk in range(4):
                        self.tr(bt, bfbank(bt)[:, blk, :], knT[hh][:, blk * 128:(blk + 1) * 128], self.ident_bf[:], reads=[knb[hh], self.kb])
                    for blk in range(4):
                        self.act(kdtok[hh][:, blk, :], bfbank(bt)[:, blk, :], AF.Copy, reads=[self.bankbuf[bt], gb_], writes=[kdb[hh]],
                                 scale=egl[:, blk * 8 + h:blk * 8 + h + 1])
                    bd = self.take_bank()
                    for blk in range(4):
                        lg, lgbuf = lhsg[blk % 2], lgb[blk % 2]
                        self.ts("pool", lg, self.lstrict_f[:], gtok[:, blk * 8 + h:blk * 8 + h + 1], None, ALU.mult, None,
                                reads=[self.kb, gb_], writes=[lgbuf])
                        self.mm(bd, self.banks[bd][:, blk * 128:(blk + 1) * 128], lg, self.uincl_f[:], True, True, reads=[lgbuf, self.kb])
                    self.act(decT, self.banks[bd][:], AF.Exp, reads=[self.bankbuf[bd]], writes=[decb])
                    d3 = decT.rearrange("p (b c) -> p b c", b=4)
                    t3 = tmp.rearrange("p (b c) -> p b c", b=4)
                    bkk = self.take_bank()
                    for blk in range(4):
                        ks = knT[hh][:, blk * 128:(blk + 1) * 128]
                        self.mm(bkk, self.banks[bkk][:, blk * 128:(blk + 1) * 128], ks, ks, True, True, reads=[knb[hh]])
                    self.tt("dve", tmp, self.banks[bkk][:], decT, ALU.mult, reads=[self.bankbuf[bkk], decb], writes=[tmpb])
                    for blk in range(4):
                        self.stt(Pm[hh][:, blk, :], t3[:, blk, :], negbeta[:, blk * 8 + h:blk * 8 + h + 1], self.msu_f[:], ALU.mult, ALU.mult,
                                 reads=[tmpb, gb_, self.kb], writes=[pmb[hh]])
                    bt = self.take_bank()
                    for blk in range(4):
                        self.tr(bt, bfbank(bt)[:, blk, :], Pm[hh][:, blk, :], self.ident_bf[:], reads=[pmb[hh], self.kb])
                    self.cp("act", Qm[hh], bfbank(bt), [self.bankbuf[bt]], [qmb[hh]])
                    self.tt("pool", TTm[hh], Pm[hh], self.ident_bf[:].unsqueeze(1).broadcast_to([128, 4, 128]), ALU.add,
                            reads=[pmb[hh], self.kb], writes=[ttb[hh]])
                    bq = self.take_bank()
                    for blk in range(4):
                        self.mm(bq, self.banks[bq][:, blk * 128:(blk + 1) * 128], knT[hh][:, blk * 128:(blk + 1) * 128],
                                qnT[hh][:, blk * 128:(blk + 1) * 128], True, True, reads=[knb[hh], qnb[hh]])
                    self.tt("dve", tmp, self.banks[bq][:], decT, ALU.mult, reads=[self.bankbuf[bq], decb], writes=[tmpb])
                    self.tt("dve", QKD[hh], t3, self.uincl_f[:].unsqueeze(1).broadcast_to([128, 4, 128]), ALU.mult,
                            reads=[tmpb, self.kb], writes=[qkb[hh]])
                for lev in range(1, 7):
                    for hh in range(4):
                        bq = self.take_bank()
                        for blk in range(4):
                            self.mm(bq, self.banks[bq][:, blk * 128:(blk + 1) * 128], Pm[hh][:, blk, :], Qm[hh][:, blk, :], True, True,
                                    reads=[pmb[hh], qmb[hh]])
                        if lev < 6:
                            bp = self.take_bank()
                            for blk in range(4):
                                self.mm(bp, self.banks[bp][:, blk * 128:(blk + 1) * 128], Qm[hh][:, blk, :], Pm[hh][:, blk, :], True, True,
                                        reads=[pmb[hh], qmb[hh]])
                            self.cp("act", Pm[hh], fbank(bp), [self.bankbuf[bp]], [pmb[hh]])
                        self.cp("dve", Qm[hh], fbank(bq), [self.bankbuf[bq]], [qmb[hh]])
                        br = self.take_bank()
                        for blk in range(4):
                            self.mm(br, self.banks[br][:, blk * 128:(blk + 1) * 128], Qm[hh][:, blk, :], TTm[hh][:, blk, :], True, True,
                                    reads=[qmb[hh], ttb[hh]])
                        self.tt("dve", TTm[hh], fbank(br), TTm[hh], ALU.add, reads=[self.bankbuf[br], ttb[hh]], writes=[ttb[hh]])
                rbuf = vbf.rearrange("p (b c) -> p b c", b=4)
                otok = decT.rearrange("p (b c) -> p b c", b=4)
                o2s = tmp.rearrange("p (b c) -> p b c", b=4)
                for blk in range(4):
                    bs = slice(blk * 128, (blk + 1) * 128)
                    col = lambda a, hh: a[:, blk * 8 + 4 * g + hh:blk * 8 + 4 * g + hh + 1]
                    bks = self.take_bank()
                    for hh in range(4):
                        self.mm(bks, self.banks[bks][:, hh * 128:(hh + 1) * 128], knT[hh][:, bs], Sb[:, hh, :], True, True,
                                reads=[knb[hh], sbb])
                    for hh in range(4):
                        self.stt(rbuf[:, hh, :], self.banks[bks][:, hh * 128:(hh + 1) * 128], col(negeg, hh), vtok[hh][:, blk, :],
                                 ALU.mult, ALU.add, reads=[self.bankbuf[bks], gb_, vtb[hh]], writes=[vbb])
                    bvn = self.take_bank()
                    for hh in range(4):
                        self.mm(bvn, self.banks[bvn][:, hh * 128:(hh + 1) * 128], TTm[hh][:, blk, :], rbuf[:, hh, :], True, True,
                                reads=[ttb[hh], vbb])
                    for hh in range(4):
                        self.act(vnew[:, hh, :], self.banks[bvn][:, hh * 128:(hh + 1) * 128], AF.Copy, reads=[self.bankbuf[bvn], gb_],
                                 writes=[vnb], scale=col(beta, hh))
                    bo1 = self.take_bank()
                    for hh in range(4):
                        self.mm(bo1, self.banks[bo1][:, hh * 128:(hh + 1) * 128], qnT[hh][:, bs], Sb[:, hh, :], True, True,
                                reads=[qnb[hh], sbb])
                    bo2 = self.take_bank()
                    for hh in range(4):
                        self.mm(bo2, self.banks[bo2][:, hh * 128:(hh + 1) * 128], QKD[hh][:, blk, :], vnew[:, hh, :], True, True,
                                reads=[qkb[hh], vnb])
                    self.cp("act", tmp, self.banks[bo2][:], [self.bankbuf[bo2]], [tmpb])
                    for hh in range(4):
                        self.stt(otok[:, hh, :], self.banks[bo1][:, hh * 128:(hh + 1) * 128], col(eg, hh), o2s[:, hh, :],
                                 ALU.mult, ALU.add, reads=[self.bankbuf[bo1], gb_, tmpb], writes=[decb])
                    bsu = self.take_bank()
                    for hh in range(4):
                        self.mm(bsu, self.banks[bsu][:, hh * 128:(hh + 1) * 128], kdtok[hh][:, blk, :], vnew[:, hh, :], True, True,
                                reads=[kdb[hh], vnb])
                    for hh in range(4):
                        self.stt(Sf[:, hh, :], Sf[:, hh, :], col(gl, hh), self.banks[bsu][:, hh * 128:(hh + 1) * 128],
                                 ALU.mult, ALU.add, reads=[self.bankbuf[bsu], gb_, sfb], writes=[sfb])
                    self.cp("act", Sb, Sf, [sfb], [sbb])
                    self.tt("pool", tmp, decT, decT, ALU.mult, reads=[decb], writes=[tmpb])
                    P.op("dve", lambda e: e.tensor_reduce(out=ssq[:, 0:4], in_=o2s, axis=AX.X, op=ALU.add), reads=[tmpb], writes=[ssb])
                    self.act(ssq[:, 0:4], ssq[:, 0:4], AF.Ln, reads=[ssb, self.kb], writes=[ssb], bias=self.eps_col, scale=1.0 / 128)
                    self.act(ssq[:, 0:4], ssq[:, 0:4], AF.Exp, reads=[ssb], writes=[ssb], scale=-0.5)
                    for hh in range(4):
                        self.act(onb_[:, hh, :], otok[:, hh, :], AF.Copy, reads=[decb, ssb], writes=[onbuf], scale=ssq[:, hh:hh + 1])
                    bt = self.take_bank()
                    for hh in range(4):
                        self.tr(bt, bfbank(bt)[:, hh, :], onb_[:, hh, :], self.ident_bf[:], reads=[onbuf, self.kb])
                    for hh in range(4):
                        self.stt(zsT[hh][:, bs], bfbank(bt)[:, hh, :], self.cc("gog"), zsT[hh][:, bs], ALU.mult, ALU.mult,
                                 reads=[self.bankbuf[bt], self.cbuf, zsb[hh]], writes=[zsb[hh]])
                wos = [self.wload(self.d_gwout[j], 2048) for j in range(4)]
                for dc in range(KC):
                    wo = self.ring[wos[dc // 2]][:].rearrange("p (k c) -> p k c", k=KC)
                    bk = self.take_bank()
                    for hh in range(4):
                        self.mm(bk, self.banks[bk][:], wo[:, 4 * g + hh, (dc % 2) * 128:(dc % 2 + 1) * 128], zsT[hh],
                                hh == 0, hh == 3, reads=[self.slotbuf[wos[dc // 2]], zsb[hh]])
                    self.tt("dve", self.xT[:, dc, sl], self.banks[bk][:], self.xT[:, dc, sl], ALU.add,
                            reads=[self.bankbuf[bk], self.xbuf[dc][Q]], writes=[self.xbuf[dc][Q]])

    def fox(self, layer):
        self.common_init()
        P = self.P
        HD = 128
        hT = self.carve(0, 8192).bitcast(BF16).rearrange("p (c t) -> p c t", c=KC)
        for t in range(NT):
            self.rmsnorm_tile(t, "mixg", layer, hT[:, :, t * TT:(t + 1) * TT], self.hbuf[t], 8192)
        o = 9728
        knT = self.carve(o, 4096).bitcast(BF16).rearrange("p (h t) -> p h t", h=4); o += 4096
        knb = [[Buf("kn") for _ in range(NT)] for _ in range(4)]
        vtok = self.carve(o, 4096).bitcast(BF16).rearrange("p (b c) -> p b c", b=16); o += 4096
        vb = [Buf("v%d" % b) for b in range(16)]
        qT = self.carve(o, 1024).bitcast(BF16).rearrange("p (h t) -> p h t", h=4); o += 1024
        qb = [Buf("q%d" % h) for h in range(4)]
        oT = self.carve(o, 1024).bitcast(BF16).rearrange("p (h t) -> p h t", h=4); o += 1024
        ob = [Buf("o%d" % h) for h in range(4)]
        NP = 2
        pT = [self.carve(o + i * 256, 256).bitcast(BF16) for i in range(NP)]; o += NP * 256
        pb = [Buf("p%d" % i) for i in range(NP)]
        raw = self.carve(o, 512); o += 512
        sqt = self.carve(o, 256).bitcast(BF16); o += 256
        lnv = self.carve(o, 512); o += 512
        rstd = self.carve(o, 512); o += 512
        rawb, sqb, lnb, rsb = Buf("raw"), Buf("sq"), Buf("ln"), Buf("rs")
        rden = self.carve(o, 512); o += 512
        rdb = Buf("rden")
        erow = self.carve(o, 512); o += 512
        crow = self.carve(o, 512); o += 512
        ones8 = self.carve(o, 512); o += 512
        cT = self.carve(o, 128).rearrange("p (b h) -> p b h", b=16); o += 128
        cmidb = self.carve(o, 32).rearrange("p (q h) -> p q h", q=4); o += 32
        biasA = self.carve(o, 128).rearrange("p (b h) -> p b h", b=16); o += 128
        small = self.carve(o, 32); o += 32
        wf = self.carve(o, 32).bitcast(BF16).rearrange("p (k c) -> p k c", k=KC); o += 32
        assert o <= self.ARENA, o
        carry = small[:, 0:8]
        xs = erow[:, 0:32]
        tots = erow[:, 32:64]
        cb = Buf("cstuff")
        wfb = Buf("wf")
        P.op("pool", lambda e: e.dma_start(out=wf.rearrange("p k c -> p (k c)"), in_=self.d_fwf), writes=[wfb], dma_out=wfb)
        self.memset("pool", carry, 0.0, [cb])
        scale = float(HD) ** -0.5
        for g in range(2):
            for Q in range(NT):
                sl = slice(Q * TT, (Q + 1) * TT)
                hTt = hT[:, :, sl]
                hb = self.hbuf[Q]
                if g == 0:
                    bk = self.take_bank()
                    for blk in range(4):
                        for kc in range(KC):
                            self.mm(bk, self.banks[bk][:, blk * 8:(blk + 1) * 8], hTt[:, kc, blk * 128:(blk + 1) * 128], wf[:, kc, :],
                                    kc == 0, kc == KC - 1, reads=[wfb, hb])
                    x3 = xs.rearrange("p (b h) -> p b h", b=4)
                    self.tt("dve", x3, self.banks[bk][:, 0:32].rearrange("p (b h) -> p b h", b=4),
                            self.cst[:, CL["fbfb"]:CL["fbfb"] + 8].unsqueeze(1).broadcast_to([128, 4, 8]), ALU.add,
                            reads=[self.bankbuf[bk], self.cbuf], writes=[cb])
                    self.act(xs, xs, AF.Exp, reads=[cb], writes=[cb], scale=-1.0)
                    self.act(xs, xs, AF.Ln, reads=[cb, self.kb], writes=[cb], bias=self.one_col)
                    b1 = self.take_bank()
                    self.mm(b1, self.banks[b1][:, 0:32], self.uincl_f[:], xs, True, True, reads=[cb, self.kb])
                    b2 = self.take_bank()
                    self.mm(b2, self.banks[b2][:, 0:32], self.ones_f[:], xs, True, True, reads=[cb, self.kb])
                    self.cp("dve", tots, self.banks[b2][:, 0:32], [self.bankbuf[b2]], [cb])
                    for blk in range(4):
                        self.tt("dve", cT[:, 4 * Q + blk, :], self.banks[b1][:, blk * 8:(blk + 1) * 8], carry, ALU.add,
                                reads=[self.bankbuf[b1], cb], writes=[cb])
                        self.tt("dve", carry, carry, tots[:, blk * 8:(blk + 1) * 8], ALU.add, reads=[cb], writes=[cb])
                        if blk == 1:
                            self.cp("dve", cmidb[:, Q, :], carry, [cb], [cb])
                nj = 4 * Q + 4
                DBG = int(os.environ.get("FOXDBG", "9"))
                if DBG <= 1:
                    continue
                self.tt("dve", biasA[:, 0:nj, :], cT[:, 0:nj, :], cmidb[:, Q:Q + 1, :].broadcast_to([128, nj, 8]), ALU.subtract,
                        reads=[cb], writes=[cb])
                for which in range(2):
                    for hp in range(2):
                        sw = self.wload(self.d_fwin[which * 4 + 2 * g + hp], 2048)
                        w_ = self.ring[sw][:].rearrange("p (k c) -> p k c", k=KC)
                        for h2 in range(2):
                            hh = 2 * hp + h2
                            bk = self.take_bank()
                            for kc in range(KC):
                                self.mm(bk, self.banks[bk][:], w_[:, kc, h2 * 128:(h2 + 1) * 128], hTt[:, kc, :],
                                        kc == 0, kc == KC - 1, reads=[self.slotbuf[sw], hb])
                            self.cp("dve", raw, self.banks[bk][:], [self.bankbuf[bk]], [rawb])
                            self.act(sqt, raw, AF.Square, reads=[rawb], writes=[sqb])
                            b2 = self.take_bank()
                            self.mm(b2, self.banks[b2][:], self.ones_bf[:], sqt, True, True, reads=[sqb, self.kb])
                            self.act(lnv, self.banks[b2][:], AF.Ln, reads=[self.bankbuf[b2], self.kb], writes=[lnb],
                                     bias=self.eps_col, scale=1.0 / HD)
                            self.act(rstd, lnv, AF.Exp, reads=[lnb], writes=[rsb], scale=-0.5)
                            if which == 0:
                                self.stt(qT[:, hh, :], raw, self.cc("fqg"), rstd, ALU.mult, ALU.mult,
                                         reads=[rawb, rsb, self.cbuf], writes=[qb[hh]])
                            else:
                                self.stt(knT[:, hh, sl], raw, self.cc("fkg"), rstd, ALU.mult, ALU.mult,
                                         reads=[rawb, rsb, self.cbuf], writes=[knb[hh][Q]])
                for hp in range(2):
                    sw = self.wload(self.d_fwin[8 + 2 * g + hp], 2048)
                    w_ = self.ring[sw][:].rearrange("p (k c) -> p k c", k=KC)
                    for blk in range(4):
                        bk = self.take_bank()
                        for kc in range(KC):
                            self.mm(bk, self.banks[bk][:, 0:256], hTt[:, kc, blk * 128:(blk + 1) * 128], w_[:, kc, :],
                                    kc == 0, kc == KC - 1, reads=[self.slotbuf[sw], hb])
                        self.act(vtok[:, 4 * Q + blk, hp * 256:(hp + 1) * 256], self.banks[bk][:, 0:256], AF.Copy,
                                 reads=[self.bankbuf[bk]], writes=[vb[4 * Q + blk]])
                pi = 0
                if DBG <= 2:
                    continue
                for hh in range(4):
                    h = 4 * g + hh
                    bo = self.take_bank()
                    self.held.add(bo)
                    bd = self.take_bank()
                    self.held.add(bd)
                    for j in range(nj):
                        off = max(0, (j - 4 * Q) * 128)
                        bs = self.take_bank()
                        self.mm(bs, self.banks[bs][:, off:512], knT[:, hh, j * 128:(j + 1) * 128], qT[:, hh, off:512],
                                True, True, reads=[knb[hh][j // 4], qb[hh]])
                        p_, pbuf = pT[pi % NP], pb[pi % NP]
                        pi += 1
                        self.act(p_[:, off:512], self.banks[bs][:, off:512], AF.Exp, reads=[self.bankbuf[bs], cb], writes=[pbuf],
                                 bias=biasA[:, j, h:h + 1], scale=scale)
                        if j >= 4 * Q:
                            self.tt("pool", p_[:, off:off + 128], p_[:, off:off + 128], self.uincl_bf[:], ALU.mult,
                                    reads=[pbuf, self.kb], writes=[pbuf])
                        self.mm(bo, self.banks[bo][:, off:512], vtok[:, j, hh * 128:(hh + 1) * 128], p_[:, off:512],
                                j == 0, j == nj - 1, reads=[vb[j], pbuf])
                        self.mm(bd, self.banks[bd][:, off:512], self.ones_bf[:], p_[:, off:512],
                                j == 0, j == nj - 1, reads=[pbuf, self.kb])
                    P.op("dve", lambda e, bd=bd: e.reciprocal(out=rden, in_=self.banks[bd][:]), reads=[self.bankbuf[bd]], writes=[rdb])
                    self.tt("dve", oT[:, hh, :], self.banks[bo][:], rden, ALU.mult, reads=[self.bankbuf[bo], rdb], writes=[ob[hh]])
                    self.held.discard(bo)
                    self.held.discard(bd)
                wos = [self.wload(self.d_fwout[j], 2048) for j in range(4)]
                for dc in range(KC):
                    wo = self.ring[wos[dc // 2]][:].rearrange("p (k c) -> p k c", k=KC)
                    bk = self.take_bank()
                    for hh in range(4):
                        self.mm(bk, self.banks[bk][:], wo[:, 4 * g + hh, (dc % 2) * 128:(dc % 2 + 1) * 128], oT[:, hh, :],
                                hh == 0, hh == 3, reads=[self.slotbuf[wos[dc // 2]], ob[hh]])
                    self.tt("dve", self.xT[:, dc, sl], self.banks[bk][:], self.xT[:, dc, sl], ALU.add,
                            reads=[self.bankbuf[bk], self.xbuf[dc][Q]], writes=[self.xbuf[dc][Q]])


ALL_STAGES = [("conf", 0), ("ffn", 0), ("gdn", 1), ("ffn", 1), ("fox", 2), ("ffn", 2), ("conf", 3), ("ffn", 3)]


def build_nc(stages):
    nc = bass.Bass("TRN2", target_bir_lowering=False)
    k = K(nc, stages)
    k.build()
    return nc, k.used_inputs


def prep_shared(inp):
    sh = {}
    sh["cst"] = pack_consts(inp)
    sh["cwin"] = np.stack([tile_w(inp["conv_w_in"][i]).reshape(8, 128, 2048) for i in range(2)])
    sh["cwout"] = np.stack([tile_w(inp["conv_w_out"][i]).reshape(4, 128, 2048) for i in range(2)])
    gw = np.asarray(inp["gdn_w_in"][0], np.float32)
    sh["gwin"] = tile_w(gw[:, :4096]).reshape(16, 128, 2048)
    sh["gwab"] = np.ascontiguousarray(gw[:, 4096:4112].reshape(8, 128, 16).transpose(1, 0, 2)).reshape(128, 128)
    sh["gwout"] = tile_w(inp["gdn_w_out"][0]).reshape(4, 128, 2048)
    fw = np.asarray(inp["fox_w_in"][0], np.float32)
    sh["fwin"] = tile_w(fw[:, :3072]).reshape(12, 128, 2048)
    sh["fwf"] = np.ascontiguousarray(fw[:, 3072:3080].reshape(8, 128, 8).transpose(1, 0, 2)).reshape(128, 64)
    sh["fwout"] = tile_w(inp["fox_w_out"][0]).reshape(4, 128, 2048)
    sh["wup"] = np.stack([tile_w(inp["ffn_w_up"][l]).reshape(22, 128, 2048) for l in range(4)])
    sh["wdn"] = np.stack([tile_wk(inp["ffn_w_down"][l]).reshape(11, 128, 2048) for l in range(4)])
    return sh


def run(inp, stages, ncores=8, trace=False):
    x = np.asarray(inp["x"], np.float32)
    sh = prep_shared(inp)
    nc, used = build_nc(stages)
    sh = {k_: v for k_, v in sh.items() if k_ in used}
    in_maps = []
    for b in range(ncores):
        m = dict(sh)
        m["xT"] = np.ascontiguousarray(x[b].T).reshape(KC, 128, S)
        in_maps.append(m)
    res = run_bass_kernel_spmd(nc, in_maps, core_ids=list(range(ncores)), trace=trace)
    out = np.stack([np.asarray(r["yT"], np.float32).reshape(D, S).T for r in res.results])
    return out, res


def kernel(**inputs):
    out, _ = run(inputs, ALL_STAGES, ncores=8)
    return out.astype(np.float32)
```

```python
import contextlib
import os
import numpy as np
import concourse.bass as bass
import concourse.mybir as mybir
from concourse.bass_utils import run_bass_kernel_spmd

F32 = mybir.dt.float32
BF16 = mybir.dt.bfloat16
AF = mybir.ActivationFunctionType
ALU = mybir.AluOpType
AX = mybir.AxisListType

S = 2048
D = 1024
TT = 512
NT = 4
KC = 8
FF = 2816
EPS = 1e-6
ENGS = ["pe", "act", "dve", "pool", "sp"]
BLOCK_ATTR = {"pe": "tensor", "act": "scalar", "dve": "vector", "pool": "gpsimd", "sp": "sync"}


class Buf:
    __slots__ = ("name", "last_w", "readers", "sem", "dma_cnt", "excl")

    def __init__(self, name, excl=False):
        self.name = name
        self.excl = excl
        self.last_w = None
        self.readers = []
        self.sem = None
        self.dma_cnt = 0


class Op:
    __slots__ = ("eng", "fn", "waits", "signal", "idx", "dma_buf", "clock", "sigcnt")

    def __init__(self, eng, fn, idx, dma_buf=None):
        self.eng = eng
        self.fn = fn
        self.idx = idx
        self.waits = []
        self.signal = False
        self.dma_buf = dma_buf
        self.clock = None
        self.sigcnt = None


class Prog:
    def __init__(self, nc):
        self.nc = nc
        self.ops = {e: [] for e in ENGS}
        self.obs = {e: {} for e in ENGS}
        self.dma_bufs = []
        self.pending = {e: [] for e in ENGS}

    def barrier(self):
        toks = [("eng", e, len(self.ops[e]) - 1) for e in ENGS if self.ops[e] and self.ops[e][-1].dma_buf is None]
        for e in ENGS:
            if self.ops[e] and self.ops[e][-1].dma_buf is not None:
                for o in reversed(self.ops[e]):
                    if o.dma_buf is None:
                        toks.append(("eng", e, o.idx))
                        break
        for e in ENGS:
            self.pending[e] = list(toks)

    def _need(self, op, tok):
        e = op.eng
        if tok[0] == "eng":
            _, se, si = tok
            if self.obs[e].get(se, -1) >= si:
                return
            src = self.ops[se][si]
            src.signal = True
            op.waits.append(tok)
            self.obs[e][se] = si
            if src.clock:
                for k, v in src.clock.items():
                    if self.obs[e].get(k, -1) < v:
                        self.obs[e][k] = v
        else:
            _, b, cnt = tok
            key = ("dma", id(b))
            if self.obs[e].get(key, -1) >= cnt:
                return
            op.waits.append(tok)
            self.obs[e][key] = cnt

    def op(self, eng, fn, reads=(), writes=(), dma_out=None, nobarrier=False):
        lst = self.ops[eng]
        o = Op(eng, fn, len(lst), dma_buf=dma_out)
        if any(r.excl for r in reads):
            writes = list(writes) + [r for r in reads if r.excl and r not in writes]
            reads = [r for r in reads if not r.excl]
        best = {}
        for r in reads:
            t = r.last_w
            if t is not None:
                k = t[1] if t[0] == "eng" else ("dma", id(t[1]))
                if k not in best or best[k][2] < t[2]:
                    best[k] = t
        for w in writes:
            for t in [w.last_w] + w.readers:
                if t is not None:
                    k = t[1] if t[0] == "eng" else ("dma", id(t[1]))
                    if k not in best or best[k][2] < t[2]:
                        best[k] = t
        if self.pending[eng] and not nobarrier:
            for t in self.pending[eng]:
                if t[1] == eng:
                    continue
                k = t[1]
                if k not in best or best[k][2] < t[2]:
                    best[k] = t
            self.pending[eng] = []
        for t in best.values():
            if eng == "pe" and t[0] == "eng" and t[1] == "pe":
                continue
            self._need(o, t)
        if dma_out is not None:
            if dma_out.sem is None:
                self.dma_bufs.append(dma_out)
                dma_out.sem = True
            dma_out.dma_cnt += 1
            tok = ("dma", dma_out, dma_out.dma_cnt)
        else:
            tok = ("eng", eng, o.idx)
        o.clock = dict(self.obs[eng])
        for r in reads:
            r.readers.append(tok)
        for w in writes:
            w.last_w = tok
            w.readers = []
        lst.append(o)
        return tok

    def emit(self, final_waits=()):
        nc = self.nc
        CH = 2000
        with contextlib.ExitStack() as st:
            for e in ENGS:
                c = 0
                for o in self.ops[e]:
                    if o.signal and o.dma_buf is None:
                        o.sigcnt = c
                        c += 1
                    else:
                        o.sigcnt = c - 1
            nsig = {e: sum(1 for o in self.ops[e] if o.signal and o.dma_buf is None) for e in ENGS}
            esem = {e: [st.enter_context(nc.semaphore("s_%s%d" % (e, i))) for i in range(max(1, (nsig[e] + CH - 1) // CH))]
                    for e in ENGS}
            for i, b in enumerate(self.dma_bufs):
                b.sem = st.enter_context(nc.semaphore("d%d" % i))
            block = st.enter_context(nc.Block())
            for e in ENGS:
                ops = self.ops[e]
                fw = [b for (fe, b) in final_waits if fe == e]
                if not ops and not fw:
                    continue

                def body(eng, ops=ops, e=e, fw=fw):
                    for o in ops:
                        for t in o.waits:
                            if t[0] == "eng":
                                k = self.ops[t[1]][t[2]].sigcnt
                                eng.wait_ge(esem[t[1]][k // CH], k % CH + 1)
                            else:
                                eng.wait_ge(t[1].sem, 16 * t[2])
                        ins = o.fn(eng)
                        if o.dma_buf is not None:
                            ins.then_inc(o.dma_buf.sem, 16)
                        elif o.signal:
                            ins.then_inc(esem[e][o.sigcnt // CH], 1)
                    for b in fw:
                        eng.wait_ge(b.sem, 16 * b.dma_cnt)

                getattr(block, BLOCK_ATTR[e])(body)


def _cst_layout():
    lay = {}
    off = 0
    for name, n in [("mixg", 32), ("ffng", 32), ("cbin", 32), ("cwdw", 2 * 31 * 8), ("cbdw", 16),
                    ("clng", 16), ("clnb", 16), ("gconv", 96), ("gog", 1), ("fqg", 1), ("fkg", 1),
                    ("fwdw", 4 * 3 * 44), ("fbf", 1), ("galog", 8), ("gdtb", 8), ("fbfb", 8)]:
        lay[name] = off
        off += n
    return lay, off


CL, NCST = _cst_layout()


def _cols(v):
    v = np.asarray(v, dtype=np.float32).reshape(-1, 128)
    return v.T


def pack_consts(inp):
    c = np.zeros((128, NCST), np.float32)

    def put(name, arr):
        arr = np.asarray(arr, np.float32)
        c[:, CL[name]:CL[name] + arr.shape[1]] = arr

    put("mixg", _cols(inp["mix_norm_g"].reshape(-1)))
    put("ffng", _cols(inp["ffn_norm_g"].reshape(-1)))
    put("cbin", _cols(inp["conv_b_in"].reshape(-1)))
    put("cwdw", _cols(inp["conv_w_dw"].reshape(-1)))
    put("cbdw", _cols(inp["conv_b_dw"].reshape(-1)))
    put("clng", _cols(inp["conv_ln_g"].reshape(-1)))
    put("clnb", _cols(inp["conv_ln_b"].reshape(-1)))
    put("gconv", _cols(inp["gdn_conv_w"].reshape(-1)))
    put("gog", _cols(inp["gdn_o_norm_g"].reshape(-1)))
    put("fqg", _cols(inp["fox_q_norm_g"].reshape(-1)))
    put("fkg", _cols(inp["fox_k_norm_g"].reshape(-1)))
    put("fwdw", _cols(inp["ffn_w_dw"].reshape(-1)))
    bf = np.zeros((128, 1), np.float32)
    bf[:8, 0] = np.asarray(inp["fox_b_f"], np.float32).reshape(-1)
    put("fbf", bf)
    put("galog", np.broadcast_to(np.asarray(inp["gdn_a_log"], np.float32).reshape(1, 8), (128, 8)))
    put("gdtb", np.broadcast_to(np.asarray(inp["gdn_dt_bias"], np.float32).reshape(1, 8), (128, 8)))
    put("fbfb", np.broadcast_to(np.asarray(inp["fox_b_f"], np.float32).reshape(1, 8), (128, 8)))
    return c


def tile_w(w, width=256):
    w = np.asarray(w, np.float32)
    K, N = w.shape
    return np.ascontiguousarray(w.reshape(K // 128, 128, N // width, width).transpose(2, 1, 0, 3))


def tile_wk(w, nk=2):
    w = np.asarray(w, np.float32)
    K, N = w.shape
    return np.ascontiguousarray(w.reshape(K // (128 * nk), nk, 128, N).transpose(0, 2, 1, 3))


class K:
    def __init__(self, nc, stages):
        self.nc = nc
        self.P = Prog(nc)
        self.stages = stages
        self.st = contextlib.ExitStack()
        self.bank_rr = 0
        self.slot_rr = 0
        self.uid = 0

    def sb(self, name, shape, dt):
        return self.st.enter_context(self.nc.sbuf_tensor(name, shape, dt))

    def B(self, name=""):
        self.uid += 1
        return Buf("%s%d" % (name, self.uid))

    def take_bank(self):
        while True:
            i = self.bank_rr % 8
            self.bank_rr += 1
            if i not in self.held:
                return i

    def take_slot(self):
        i = self.slot_rr % self.NSLOT
        self.slot_rr += 1
        return i

    def mm(self, bank, out, lhsT, rhs, start, stop, reads, extra_w=()):
        self.P.op("pe", lambda e: e.matmul(out, lhsT, rhs, start=start, stop=stop),
                  reads=reads, writes=[self.bankbuf[bank]] + list(extra_w))

    def tr(self, bank, out, in_, ident, reads):
        self.P.op("pe", lambda e: e.transpose(out, in_, ident), reads=reads, writes=[self.bankbuf[bank]])

    def act(self, out, in_, func, reads, writes, bias=None, scale=None, eng="act"):
        kw = {}
        if bias is not None:
            kw["bias"] = bias
        if scale is not None:
            kw["scale"] = scale
        self.P.op("act", lambda e: e.activation(out=out, in_=in_, func=func, **kw), reads=reads, writes=writes)

    def tt(self, eng, out, in0, in1, op, reads, writes):
        self.P.op(eng, lambda e: e.tensor_tensor(out=out, in0=in0, in1=in1, op=op), reads=reads, writes=writes)

    def stt(self, out, in0, scalar, in1, op0, op1, reads, writes):
        self.P.op("dve", lambda e: e.scalar_tensor_tensor(out=out, in0=in0, scalar=scalar, in1=in1, op0=op0, op1=op1),
                  reads=reads, writes=writes)

    def ts(self, eng, out, in0, s1, s2, op0, op1, reads, writes):
        if op1 is None:
            self.P.op(eng, lambda e: e.tensor_scalar(out=out, in0=in0, scalar1=s1, scalar2=None, op0=op0),
                      reads=reads, writes=writes)
        else:
            self.P.op(eng, lambda e: e.tensor_scalar(out=out, in0=in0, scalar1=s1, scalar2=s2, op0=op0, op1=op1),
                      reads=reads, writes=writes)

    def cp(self, eng, out, in_, reads, writes):
        if eng == "act":
            self.P.op("act", lambda e: e.copy(out=out, in_=in_), reads=reads, writes=writes)
        else:
            self.P.op(eng, lambda e: e.tensor_copy(out=out, in_=in_), reads=reads, writes=writes)

    def memset(self, eng, ap, val, writes):
        self.P.op(eng, lambda e: e.memset(ap, val), writes=writes)

    def wload(self, src_ap, ncols_total, view=None):
        s = self.take_slot()
        dst = self.ring[s][:, 0:ncols_total]
        self.P.op("pool", lambda e: e.dma_start(out=dst, in_=src_ap), writes=[self.slotbuf[s]], dma_out=self.slotbuf[s], nobarrier=True)
        return s

    def build(self):
        nc = self.nc
        P = self.P
        dt = nc.dram_tensor
        self.xin = dt("xT", [KC, 128, S], F32, kind="ExternalInput").ap()
        self.cst_d = dt("cst", [128, NCST], F32, kind="ExternalInput").ap()
        kinds = set(k_ for k_, _ in self.stages)
        self.used_inputs = ["xT", "cst"]

        def din(name, shape, kind_):
            if kind_ not in kinds:
                return None
            self.used_inputs.append(name)
            return dt(name, shape, F32, kind="ExternalInput").ap()
        self.d_cwin = din("cwin", [2, 8, 128, 2048], "conf")
        self.d_cwout = din("cwout", [2, 4, 128, 2048], "conf")
        self.d_gwin = din("gwin", [16, 128, 2048], "gdn")
        self.d_gwab = din("gwab", [128, 128], "gdn")
        self.d_gwout = din("gwout", [4, 128, 2048], "gdn")
        self.d_fwin = din("fwin", [12, 128, 2048], "fox")
        self.d_fwf = din("fwf", [128, 64], "fox")
        self.d_fwout = din("fwout", [4, 128, 2048], "fox")
        self.d_wup = din("wup", [4, 22, 128, 2048], "ffn")
        self.d_wdn = din("wdn", [4, 11, 128, 2048], "ffn")
        self.yout = dt("yT", [KC, 128, S], F32, kind="ExternalOutput").ap()

        self.xT = self.sb("xT_sb", [128, KC, S], F32)
        self.cst = self.sb("cst_sb", [128, NCST], F32)
        self.NSLOT = 8
        self.ring = [self.sb("ring%d" % i, [128, 2048], BF16) for i in range(self.NSLOT)]
        self.slotbuf = [Buf("slot%d" % i) for i in range(self.NSLOT)]
        self.banks = [self.st.enter_context(nc.psum_tensor("bank%d" % i, [128, 512], F32)) for i in range(8)]
        self.bankbuf = [Buf("bank%d" % i, excl=True) for i in range(8)]
        self.held = set()
        self.xbuf = [[Buf("x%d_%d" % (c, t)) for t in range(NT)] for c in range(KC)]
        self.hbuf = [Buf("h%d" % t) for t in range(NT)]
        self.hb1 = Buf("htile")
        self.cbuf = Buf("cst")
        self.ones_bf = self.sb("ones_bf", [128, 128], BF16)
        self.ones_f = self.sb("ones_f", [128, 128], F32)
        self.ident_f = self.sb("ident_f", [128, 128], F32)
        self.ident_bf = self.sb("ident_bf", [128, 128], BF16)
        self.uincl_f = self.sb("uincl_f", [128, 128], F32)
        self.uincl_bf = self.sb("uincl_bf", [128, 128], BF16)
        self.lstrict_f = self.sb("lstrict_f", [128, 128], F32)
        self.msu_f = self.sb("msu_f", [128, 128], F32)
        self.kb = Buf("consts")
        self.ARENA = 25600
        self.arena = self.sb("arena", [128, self.ARENA], F32)

        P.op("sp", lambda e: e.dma_start(out=self.cst[:], in_=self.cst_d), writes=[self.cbuf], dma_out=self.cbuf)
        kb = self.kb
        self.memset("pool", self.ones_f[:], 1.0, [kb])
        self.memset("pool", self.ones_bf[:], 1.0, [kb])

        def asel(out, base, cm, step, op):
            P.op("pool", lambda e: e.affine_select(out=out, in_=self.ones_f[:], pattern=[[step, 128]], compare_op=op,
                                                   fill=0.0, base=base, channel_multiplier=cm), reads=[kb], writes=[kb])
        asel(self.ident_f[:], 0, -1, 1, ALU.is_equal)
        asel(self.uincl_f[:], 0, -1, 1, ALU.is_ge)
        asel(self.lstrict_f[:], -1, 1, -1, ALU.is_ge)
        asel(self.msu_f[:], -1, -1, 1, ALU.is_ge)
        self.cp("pool", self.ident_bf[:], self.ident_f[:], [kb], [kb])
        self.cp("pool", self.uincl_bf[:], self.uincl_f[:], [kb], [kb])

        for c in range(KC):
            for t in range(NT):
                P.op("sp", lambda e, c=c, t=t: e.dma_start(out=self.xT[:, c, t * TT:(t + 1) * TT],
                                                           in_=self.xin[c, :, t * TT:(t + 1) * TT]),
                     writes=[self.xbuf[c][t]], dma_out=self.xbuf[c][t])

        ia = 0
        for stg in self.stages:
            kind, layer = stg
            P.barrier()
            if kind == "conf":
                self.conformer(layer, ia)
                ia += 1
            elif kind == "gdn":
                self.gdn(layer)
            elif kind == "fox":
                self.fox(layer)
            elif kind == "ffn":
                self.ffn(layer)

        ob = Buf("out")
        for c in range(KC):
            P.op("sp", lambda e, c=c: e.dma_start(out=self.yout[c], in_=self.xT[:, c, :]),
                 reads=self.xbuf[c], writes=[ob], dma_out=ob)
        P.emit(final_waits=[("sp", ob)])
        self.st.close()

    def cc(self, name, idx=0):
        o = CL[name] + idx
        return self.cst[:, o:o + 1]

    def carve(self, off, nwords):
        assert off + nwords <= self.ARENA, (off, nwords)
        return self.arena[:, off:off + nwords]

    def rmsnorm_tile(self, t, gname, layer, hview, hb, ar_off):
        sl = slice(t * TT, (t + 1) * TT)
        sqs = [self.carve(ar_off + i * 256, 256).bitcast(BF16) for i in range(2)]
        lnv = self.carve(ar_off + 512, 512)
        rstd = self.carve(ar_off + 1024, 512)
        bl, br = self.nb_ln, self.nb_rstd
        bk = self.take_bank()
        for c in range(KC):
            sq, bsq = sqs[c % 2], self.nb_sq[c % 2]
            self.act(sq, self.xT[:, c, sl], AF.Square, reads=[self.xbuf[c][t]], writes=[bsq])
            self.mm(bk, self.banks[bk][:], self.ones_bf[:], sq, c == 0, c == KC - 1, reads=[bsq, self.kb])
        self.act(lnv, self.banks[bk][:], AF.Ln, reads=[self.bankbuf[bk], self.kb], writes=[bl], bias=self.eps_col, scale=1.0 / D)
        self.act(rstd, lnv, AF.Exp, reads=[bl], writes=[br], scale=-0.5)
        for c in range(KC):
            self.stt(hview[:, c, :], self.xT[:, c, sl], self.cc(gname, layer * 8 + c), rstd, ALU.mult, ALU.mult,
                     reads=[self.xbuf[c][t], br, self.cbuf], writes=[hb])

    def common_init(self):
        if getattr(self, "_ci", False):
            return
        self._ci = True
        self.nb_sq, self.nb_ln, self.nb_rstd = [Buf("sq0"), Buf("sq1")], Buf("ln"), Buf("rstd")
        self.eps_t = self.sb("eps_t", [128, 4], F32)
        self.memset("pool", self.eps_t[:, 0:1], EPS, [self.kb])
        self.memset("pool", self.eps_t[:, 1:2], 1.0, [self.kb])
        self.memset("pool", self.eps_t[:, 2:3], 0.0, [self.kb])
        self.eps_col = self.eps_t[:, 0:1]
        self.one_col = self.eps_t[:, 1:2]
        self.zero_col = self.eps_t[:, 2:3]

    def ffn(self, layer):
        self.common_init()
        P = self.P
        hT = self.carve(0, 8192).bitcast(BF16).rearrange("p (c t) -> p c t", c=KC)
        self.hT = hT
        for t in range(NT):
            self.rmsnorm_tile(t, "ffng", layer, hT[:, :, t * TT:(t + 1) * TT], self.hbuf[t], 8192)
        GOFF = 8192 + 1536
        gT = [self.carve(GOFF + i * 4096, 4096).bitcast(BF16).rearrange("p (c t) -> p c t", c=4) for i in range(2)]
        gbuf = [[[Buf("g") for _ in range(NT)] for _ in range(4)] for _ in range(2)]
        UOFF = GOFF + 2 * 4096
        NU = 6
        U = [self.carve(UOFF + i * 516, 516) for i in range(NU)]
        ubuf = [Buf("U") for _ in range(NU)]
        AOFF = UOFF + NU * 516
        NA = 8
        ACC = [self.carve(AOFF + i * 512, 512) for i in range(NA)]
        abuf = [Buf("acc") for _ in range(NA)]
        urr = [0]
        arr = [0]
        hreads = self.hbuf

        parts = [(0, 2), (2, 4), (4, 6), (6, 8), (8, 10), (10, 11)]
        order = []
        for (u0, u1) in parts:
            for u in range(u0, u1):
                order.append(("g", u, self.d_wup[layer, u]))
                order.append(("u", u, self.d_wup[layer, 11 + u]))
            for u in range(u0, u1):
                order.append(("d", u, self.d_wdn[layer, u]))
        issued = {}
        nxt = [0]
        deferred = []
        pend_down = []

        def want(key, ahead):
            idx = [i for i, o_ in enumerate(order) if (o_[0], o_[1]) == key][0]
            while nxt[0] <= min(idx + ahead, len(order) - 1):
                o_ = order[nxt[0]]
                issued[(o_[0], o_[1])] = self.wload(o_[2], 2048)
                nxt[0] += 1
            return issued[key]

        for pi, (u0, u1) in enumerate(parts):
            g = gT[pi % 2]
            gb = gbuf[pi % 2]
            for u in range(u0, u1):
                if u == u0 + 1 or (u == u0 and u1 - u0 == 1 and False):
                    while pend_down:
                        pend_down.pop(0)()
                sg = want(("g", u), 5)
                su = want(("u", u), 4)
                wg = self.ring[sg][:].rearrange("p (k c) -> p k c", k=KC)
                wu = self.ring[su][:].rearrange("p (k c) -> p k c", k=KC)
                for c2 in range(2):
                    ci = 2 * u + c2
                    lc = ci - 2 * u0
                    prev = {"g": None, "u": None}
                    for t in range(NT):
                        sl = slice(t * TT, (t + 1) * TT)
                        accs = {}
                        for which, w_, slot_, col0 in (("g", wg, sg, ci), ("u", wu, su, 22 + ci)):
                            bk = self.take_bank()
                            for kc in range(KC):
                                self.mm(bk, self.banks[bk][:], w_[:, kc, c2 * 128:(c2 + 1) * 128], self.hT[:, kc, sl],
                                        kc == 0, kc == KC - 1, reads=[self.slotbuf[slot_], self.hbuf[t]])
                            ui = urr[0] % NU
                            urr[0] += 1
                            Ut, Ub = U[ui], ubuf[ui]
                            if t == 0:
                                self.memset("pool", Ut[:, 0:2], 0.0, [Ub])
                            else:
                                pU, pB = prev[which]
                                self.cp("pool", Ut[:, 0:2], pU[:, 512:514], [pB], [Ub])
                            self.act(Ut[:, 2:514], self.banks[bk][:], AF.Copy, reads=[self.bankbuf[bk]], writes=[Ub])
                            prev[which] = (Ut, Ub)
                            ai = arr[0] % NA
                            arr[0] += 1
                            At, Ab = ACC[ai], abuf[ai]
                            wcol = lambda k, col0=col0: self.cc("fwdw", (layer * 3 + k) * 44 + col0)
                            self.act(At, Ut[:, 0:512], AF.Copy, reads=[Ub, self.cbuf], writes=[Ab], scale=wcol(0))
                            self.stt(At, Ut[:, 1:513], wcol(1), At, ALU.mult, ALU.add, reads=[Ub, self.cbuf, Ab], writes=[Ab])
                            self.stt(At, Ut[:, 2:514], wcol(2), At, ALU.mult, ALU.add, reads=[Ub, self.cbuf, Ab], writes=[Ab])
                            accs[which] = (At, Ab)
                        Ag, Agb = accs["g"]
                        Au, Aub = accs["u"]

                        def tail(Ag=Ag, Agb=Agb, Au=Au, Aub=Aub, dst=g[:, lc, sl], db=gb[lc][t]):
                            self.act(Ag, Ag, AF.Silu, reads=[Agb], writes=[Agb])
                            self.tt("pool", dst, Ag, Au, ALU.mult, reads=[Agb, Aub], writes=[db])
                        if deferred:
                            deferred.pop(0)()
                        deferred.append(tail)
            dslots = [want(("d", u), 3) for u in range(u0, u1)]

            def down(g=g, gb=gb, nch=2 * (u1 - u0), dslots=dslots):
                while deferred:
                    deferred.pop(0)()
                for dc in range(KC):
                    for t in range(NT):
                        sl = slice(t * TT, (t + 1) * TT)
                        bk = self.take_bank()
                        for lc in range(nch):
                            s_ = dslots[lc // 2]
                            wd = self.ring[s_][:].rearrange("p (k c) -> p k c", k=2)
                            self.mm(bk, self.banks[bk][:], wd[:, lc % 2, dc * 128:(dc + 1) * 128], g[:, lc, sl],
                                    lc == 0, lc == nch - 1, reads=[self.slotbuf[s_], gb[lc][t]])
                        self.tt("dve", self.xT[:, dc, sl], self.banks[bk][:], self.xT[:, dc, sl], ALU.add,
                                reads=[self.bankbuf[bk], self.xbuf[dc][t]], writes=[self.xbuf[dc][t]])
            pend_down.append(down)
        while pend_down:
            pend_down.pop(0)()

    def conformer(self, layer, ia):
        self.common_init()
        P = self.P
        hTt = self.carve(0, 2048).bitcast(BF16).rearrange("p (c t) -> p c t", c=KC)
        G = self.carve(3584, 2176).bitcast(BF16).rearrange("p (c t) -> p c t", c=KC)
        gb = [Buf("G%d" % c) for c in range(KC)]
        DG = [self.carve(5760 + i * 1984, 1984).bitcast(BF16).rearrange("p (k m) -> p k m", k=31) for i in range(2)]
        dgb = [Buf("dg0"), Buf("dg1")]
        CO = self.carve(9728, 4096).rearrange("p (c t) -> p c t", c=KC)
        cob = [Buf("co%d" % c) for c in range(KC)]
        UB = [self.carve(13824 + i * 256, 256).bitcast(BF16) for i in range(2)]
        SQ = [self.carve(14336 + i * 256, 256).bitcast(BF16) for i in range(2)]
        ubb = [Buf("ub0"), Buf("ub1")]
        sqb = [Buf("sqb0"), Buf("sqb1")]
        mean = self.carve(14848, 512)
        msq = self.carve(15360, 512)
        lnv = self.carve(15872, 512)
        rstd = self.carve(16384, 512)
        stb = Buf("stats")
        TMP = [self.carve(16896 + i * 512, 512) for i in range(2)]
        tmb = [Buf("tmp0"), Buf("tmp1")]
        sT = self.carve(17920, 2048).bitcast(BF16).rearrange("p (c t) -> p c t", c=KC)
        stbuf = [Buf("sT%d" % c) for c in range(KC)]
        SG = [self.carve(19968 + i * 512, 512) for i in range(2)]
        sgb = [Buf("sg0"), Buf("sg1")]
        for c in range(KC):
            self.memset("pool", G[:, c, 0:32], 0.0, [gb[c]])
        for t in range(NT):
            sl = slice(t * TT, (t + 1) * TT)
            self.rmsnorm_tile(t, "mixg", layer, hTt, self.hb1, 2048)
            s1 = self.take_bank()
            self.held.add(s1)
            s2 = self.take_bank()
            self.held.add(s2)
            wst = {}

            def head(c):
                if c % 2 == 0:
                    wst["sv"] = self.wload(self.d_cwin[ia, c // 2], 2048)
                    wst["sg"] = self.wload(self.d_cwin[ia, 4 + c // 2], 2048)
                sv, sg_ = wst["sv"], wst["sg"]
                wv = self.ring[sv][:].rearrange("p (k c) -> p k c", k=KC)
                wg = self.ring[sg_][:].rearrange("p (k c) -> p k c", k=KC)
                bv = self.take_bank()
                for kc in range(KC):
                    self.mm(bv, self.banks[bv][:], wv[:, kc, (c % 2) * 128:(c % 2 + 1) * 128], hTt[:, kc, :],
                            kc == 0, kc == KC - 1, reads=[self.slotbuf[sv], self.hb1])
                bg = self.take_bank()
                for kc in range(KC):
                    self.mm(bg, self.banks[bg][:], wg[:, kc, (c % 2) * 128:(c % 2 + 1) * 128], hTt[:, kc, :],
                            kc == 0, kc == KC - 1, reads=[self.slotbuf[sg_], self.hb1])
                return bv, bg

            def tail(c, bv, bg):
                dg, dgbuf = DG[c % 2], dgb[c % 2]
                wb = CL["cwdw"] + ia * 31 * 8 + c
                wtaps = self.cst[:, wb:wb + 30 * 8 + 1:8]
                self.tt("dve", dg, self.ident_bf[:].unsqueeze(1).broadcast_to([128, 31, 128]),
                        wtaps.unsqueeze(2).broadcast_to([128, 31, 128]), ALU.mult, reads=[self.kb, self.cbuf], writes=[dgbuf])
                sg, sgbuf = SG[c % 2], sgb[c % 2]
                self.act(sg, self.banks[bg][:], AF.Sigmoid, reads=[self.bankbuf[bg], self.cbuf], writes=[sgbuf],
                         bias=self.cc("cbin", ia * 16 + 8 + c))
                self.stt(G[:, c, 30:542], self.banks[bv][:], self.cc("cbin", ia * 16 + c), sg, ALU.add, ALU.mult,
                         reads=[self.bankbuf[bv], sgbuf, self.cbuf], writes=[gb[c]])
                bc = self.take_bank()
                for k in range(31):
                    self.mm(bc, self.banks[bc][:], dg[:, k, :], G[:, c, k:k + 512], k == 0, k == 30, reads=[dgbuf, gb[c]])
                self.act(CO[:, c, :], self.banks[bc][:], AF.Identity, reads=[self.bankbuf[bc], self.cbuf], writes=[cob[c]],
                         bias=self.cc("cbdw", ia * 8 + c))
                self.cp("pool", G[:, c, 0:30], G[:, c, 512:542], [gb[c]], [gb[c]])
                ub, ubbuf = UB[c % 2], ubb[c % 2]
                sq, sqbuf = SQ[c % 2], sqb[c % 2]
                self.cp("dve", ub, CO[:, c, :], [cob[c]], [ubbuf])
                self.act(sq, CO[:, c, :], AF.Square, reads=[cob[c]], writes=[sqbuf])
                self.mm(s1, self.banks[s1][:], self.ones_bf[:], ub, c == 0, c == KC - 1, reads=[ubbuf, self.kb])
                self.mm(s2, self.banks[s2][:], self.ones_bf[:], sq, c == 0, c == KC - 1, reads=[sqbuf, self.kb])
            hb_ = head(0)
            for c in range(KC):
                cur = hb_
                if c + 1 < KC:
                    hb_ = head(c + 1)
                tail(c, *cur)
            self.ts("dve", mean, self.banks[s1][:], 1.0 / D, None, ALU.mult, None, reads=[self.bankbuf[s1]], writes=[stb])
            self.tt("dve", msq, mean, mean, ALU.mult, reads=[stb], writes=[stb])
            self.stt(msq, self.banks[s2][:], 1.0 / D, msq, ALU.mult, ALU.subtract, reads=[self.bankbuf[s2], stb], writes=[stb])
            self.act(lnv, msq, AF.Ln, reads=[stb, self.kb], writes=[stb], bias=self.eps_col)
            self.act(rstd, lnv, AF.Exp, reads=[stb], writes=[stb], scale=-0.5)
            self.held.discard(s1)
            self.held.discard(s2)
            for c in range(KC):
                tm, tmbuf = TMP[c % 2], tmb[c % 2]
                self.tt("dve", tm, CO[:, c, :], mean, ALU.subtract, reads=[cob[c], stb], writes=[tmbuf])
                self.tt("dve", tm, tm, rstd, ALU.mult, reads=[tmbuf, stb], writes=[tmbuf])
                self.act(sT[:, c, :], tm, AF.Silu, reads=[tmbuf, self.cbuf], writes=[stbuf[c]],
                         bias=self.cc("clnb", ia * 8 + c), scale=self.cc("clng", ia * 8 + c))
            wos = [self.wload(self.d_cwout[ia, j], 2048) for j in range(4)]
            for dc in range(KC):
                wo = self.ring[wos[dc // 2]][:].rearrange("p (k c) -> p k c", k=KC)
                bk = self.take_bank()
                for kc in range(KC):
                    self.mm(bk, self.banks[bk][:], wo[:, kc, (dc % 2) * 128:(dc % 2 + 1) * 128], sT[:, kc, :],
                            kc == 0, kc == KC - 1, reads=[self.slotbuf[wos[dc // 2]], stbuf[kc]])
                self.tt("dve", self.xT[:, dc, sl], self.banks[bk][:], self.xT[:, dc, sl], ALU.add,
                        reads=[self.bankbuf[bk], self.xbuf[dc][t]], writes=[self.xbuf[dc][t]])

    def gdn(self, layer):
        self.common_init()
        P = self.P
        hT = self.carve(0, 8192).bitcast(BF16).rearrange("p (c t) -> p c t", c=KC)
        for t in range(NT):
            self.rmsnorm_tile(t, "mixg", layer, hT[:, :, t * TT:(t + 1) * TT], self.hbuf[t], 8192)
        o = [9728]

        def al(n):
            a = self.carve(o[0], n)
            o[0] += n
            return a

        def bf4(n=256):
            return al(n).bitcast(BF16).rearrange("p (b c) -> p b c", b=4)

        gtok, gcs, eg, negeg, egl, gl, beta, negbeta = [al(32) for _ in range(8)]
        abt = al(64)
        gb_ = Buf("gates")
        qnT = [al(256).bitcast(BF16) for _ in range(4)]
        knT = [al(256).bitcast(BF16) for _ in range(4)]
        vtok = [bf4() for _ in range(4)]
        kdtok = [bf4() for _ in range(4)]
        zsT = [al(256).bitcast(BF16) for _ in range(4)]
        TTm = [bf4() for _ in range(4)]
        QKD = [bf4() for _ in range(4)]
        Pm = [bf4() for _ in range(4)]
        Qm = [bf4() for _ in range(4)]
        nm = lambda n: [Buf(n + str(i)) for i in range(4)]
        qnb, knb, vtb, kdb, zsb, ttb, qkb, pmb, qmb = [nm(n) for n in ("qn", "kn", "vt", "kd", "zs", "tt", "qk", "pm", "qm")]
        U = [al(516) for _ in range(2)]
        ub = [Buf("U0"), Buf("U1")]
        ACC = [al(512) for _ in range(2)]
        ab_ = [Buf("A0"), Buf("A1")]
        sqt = al(256).bitcast(BF16)
        lnv = al(512)
        rstd = al(512)
        sqb, lnb, rsb = Buf("sq"), Buf("ln"), Buf("rs")
        decT = al(512)
        decb = Buf("dec")
        tmp = al(512)
        tmpb = Buf("tmp")
        lhsg = [al(128) for _ in range(2)]
        lgb = [Buf("lg0"), Buf("lg1")]
        vbf = al(256).bitcast(BF16)
        vbb = Buf("vbf")
        vnew = bf4()
        vnb = Buf("vnew")
        onb_ = bf4()
        onbuf = Buf("on")
        Sf = al(512).rearrange("p (h c) -> p h c", h=4)
        Sb = bf4()
        sfb, sbb = Buf("Sf"), Buf("Sb")
        haloS = al(48).rearrange("p (c k) -> p c k", c=12)
        hsb = [Buf("hs%d" % i) for i in range(12)]
        ssq = al(8)
        ssb = Buf("ssq")
        wab = al(64).bitcast(BF16).rearrange("p (k c) -> p k c", k=KC)
        negA = al(8)
        assert o[0] <= self.ARENA, o[0]
        wabb = Buf("wab")
        P.op("pool", lambda e: e.dma_start(out=wab.rearrange("p k c -> p (k c)"), in_=self.d_gwab), writes=[wabb], dma_out=wabb)
        nab = Buf("negA")
        self.act(negA, self.cst[:, CL["galog"]:CL["galog"] + 8], AF.Exp, reads=[self.cbuf], writes=[nab])
        self.ts("dve", negA, negA, -1.0, None, ALU.mult, None, reads=[nab], writes=[nab])
        dtb = self.cst[:, CL["gdtb"]:CL["gdtb"] + 8]
        v4 = lambda a: a.rearrange("p (b h) -> p b h", b=4)
        bfbank = lambda bk: self.banks[bk][:].bitcast(BF16)[:, 0:512].rearrange("p (b c) -> p b c", b=4)
        fbank = lambda bk: self.banks[bk][:].rearrange("p (b c) -> p b c", b=4)
        qscale = 128.0 ** -0.5
        for g in range(2):
            self.memset("pool", Sf[:], 0.0, [sfb])
            self.memset("pool", Sb[:], 0.0, [sbb])
            for Q in range(NT):
                sl = slice(Q * TT, (Q + 1) * TT)
                hTt = hT[:, :, sl]
                hb = self.hbuf[Q]
                bk = self.take_bank()
                for blk in range(4):
                    for kc in range(KC):
                        self.mm(bk, self.banks[bk][:, blk * 16:(blk + 1) * 16], hTt[:, kc, blk * 128:(blk + 1) * 128], wab[:, kc, :],
                                kc == 0, kc == KC - 1, reads=[wabb, hb])
                self.cp("dve", abt, self.banks[bk][:, 0:64], [self.bankbuf[bk]], [gb_])
                ab3 = abt.rearrange("p (b c) -> p b c", b=4)
                self.act(v4(beta), ab3[:, :, 8:16], AF.Sigmoid, reads=[gb_], writes=[gb_])
                self.ts("dve", negbeta, beta, -1.0, None, ALU.mult, None, reads=[gb_], writes=[gb_])
                self.tt("dve", v4(gtok), ab3[:, :, 0:8], dtb.unsqueeze(1).broadcast_to([128, 4, 8]), ALU.add, reads=[gb_, self.cbuf], writes=[gb_])
                self.act(gtok, gtok, AF.Exp, reads=[gb_], writes=[gb_])
                self.act(gtok, gtok, AF.Ln, reads=[gb_, self.kb], writes=[gb_], bias=self.one_col)
                self.tt("dve", v4(gtok), v4(gtok), negA.unsqueeze(1).broadcast_to([128, 4, 8]), ALU.mult, reads=[gb_, nab], writes=[gb_])
                bk = self.take_bank()
                self.mm(bk, self.banks[bk][:, 0:32], self.uincl_f[:], gtok, True, True, reads=[gb_, self.kb])
                b2 = self.take_bank()
                self.mm(b2, self.banks[b2][:, 0:32], self.ones_f[:], gtok, True, True, reads=[gb_, self.kb])
                self.cp("dve", gcs, self.banks[bk][:, 0:32], [self.bankbuf[bk]], [gb_])
                self.act(eg, gcs, AF.Exp, reads=[gb_], writes=[gb_])
                self.ts("dve", negeg, eg, -1.0, None, ALU.mult, None, reads=[gb_], writes=[gb_])
                self.tt("dve", egl, self.banks[b2][:, 0:32], gcs, ALU.subtract, reads=[self.bankbuf[b2], gb_], writes=[gb_])
                self.act(egl, egl, AF.Exp, reads=[gb_], writes=[gb_])
                self.act(gl, self.banks[b2][:, 0:32], AF.Exp, reads=[self.bankbuf[b2]], writes=[gb_])
                ui = 0
                for hh in range(4):
                    h = 4 * g + hh
                    for which in range(4):
                        if hh % 2 == 0:
                            pass
                        sw = self.wload(self.d_gwin[which * 4 + h // 2], 2048) if (hh % 2 == 0 or True) else None
                        w_ = self.ring[sw][:].rearrange("p (k c) -> p k c", k=KC)
                        bk = self.take_bank()
                        for kc in range(KC):
                            self.mm(bk, self.banks[bk][:], w_[:, kc, (h % 2) * 128:(h % 2 + 1) * 128], hTt[:, kc, :],
                                    kc == 0, kc == KC - 1, reads=[self.slotbuf[sw], hb])
                        if which == 3:
                            self.act(zsT[hh], self.banks[bk][:], AF.Silu, reads=[self.bankbuf[bk]], writes=[zsb[hh]])
                            continue
                        ci = which * 4 + hh
                        chunk = which * 8 + h
                        Ut, Ub = U[ui % 2], ub[ui % 2]
                        At, Ab = ACC[ui % 2], ab_[ui % 2]
                        ui += 1
                        if Q == 0:
                            self.memset("pool", Ut[:, 0:3], 0.0, [Ub])
                        else:
                            self.cp("pool", Ut[:, 0:3], haloS[:, ci, 0:3], [hsb[ci]], [Ub])
                        self.act(Ut[:, 3:515], self.banks[bk][:], AF.Copy, reads=[self.bankbuf[bk]], writes=[Ub])
                        self.cp("pool", haloS[:, ci, 0:3], Ut[:, 512:515], [Ub], [hsb[ci]])
                        wc = lambda k, chunk=chunk: self.cc("gconv", k * 24 + chunk)
                        self.act(At, Ut[:, 0:512], AF.Copy, reads=[Ub, self.cbuf], writes=[Ab], scale=wc(0))
                        for k in range(1, 4):
                            self.stt(At, Ut[:, k:k + 512], wc(k), At, ALU.mult, ALU.add, reads=[Ub, self.cbuf, Ab], writes=[Ab])
                        self.act(At, At, AF.Silu, reads=[Ab], writes=[Ab])
                        if which < 2:
                            self.act(sqt, At, AF.Square, reads=[Ab], writes=[sqb])
                            b2 = self.take_bank()
                            self.mm(b2, self.banks[b2][:], self.ones_bf[:], sqt, True, True, reads=[sqb, self.kb])
                            self.act(lnv, self.banks[b2][:], AF.Ln, reads=[self.bankbuf[b2], self.kb], writes=[lnb], bias=self.eps_col)
                            self.act(rstd, lnv, AF.Exp, reads=[lnb], writes=[rsb], scale=-0.5)
                            if which == 0:
                                self.stt(qnT[hh], At, qscale, rstd, ALU.mult, ALU.mult, reads=[Ab, rsb], writes=[qnb[hh]])
                            else:
                                self.tt("dve", knT[hh], At, rstd, ALU.mult, reads=[Ab, rsb], writes=[knb[hh]])
                        else:
                            self.cp("dve", vbf, At, [Ab], [vbb])
                            bt = self.take_bank()
                            for blk in range(4):
                                self.tr(bt, bfbank(bt)[:, blk, :], vbf[:, blk * 128:(blk + 1) * 128], self.ident_bf[:], reads=[vbb, self.kb])
                            self.cp("act", vtok[hh], bfbank(bt), [self.bankbuf[bt]], [vtb[hh]])
                    bt = self.take_bank()
                    for blk in range(4):
                        self.tr(bt, bfbank(bt)[:, blk, :], knT[hh][:, blk * 128:(blk + 1) * 128], self.ident_bf[:], reads=[knb[hh], self.kb])
                    for blk in range(4):
                        self.act(kdtok[hh][:, blk, :], bfbank(bt)[:, blk, :], AF.Copy, reads=[self.bankbuf[bt], gb_], writes=[kdb[hh]],
                                 scale=egl[:, blk * 8 + h:blk * 8 + h + 1])
                    bd = self.take_bank()
                    for blk in range(4):
                        lg, lgbuf = lhsg[blk % 2], lgb[blk % 2]
                        self.ts("pool", lg, self.lstrict_f[:], gtok[:, blk * 8 + h:blk * 8 + h + 1], None, ALU.mult, None,
                                reads=[self.kb, gb_], writes=[lgbuf])
                        self.mm(bd, self.banks[bd][:, blk * 128:(blk + 1) * 128], lg, self.uincl_f[:], True, True, reads=[lgbuf, self.kb])
                    self.act(decT, self.banks[bd][:], AF.Exp, reads=[self.bankbuf[bd]], writes=[decb])
                    d3 = decT.rearrange("p (b c) -> p b c", b=4)
                    t3 = tmp.rearrange("p (b c) -> p b c", b=4)
                    bkk = self.take_bank()
                    for blk in range(4):
                        ks = knT[hh][:, blk * 128:(blk + 1) * 128]
                        self.mm(bkk, self.banks[bkk][:, blk * 128:(blk + 1) * 128], ks, ks, True, True, reads=[knb[hh]])
                    self.tt("dve", tmp, self.banks[bkk][:], decT, ALU.mult, reads=[self.bankbuf[bkk], decb], writes=[tmpb])
                    for blk in range(4):
                        self.stt(Pm[hh][:, blk, :], t3[:, blk, :], negbeta[:, blk * 8 + h:blk * 8 + h + 1], self.msu_f[:], ALU.mult, ALU.mult,
                                 reads=[tmpb, gb_, self.kb], writes=[pmb[hh]])
                    bt = self.take_bank()
                    for blk in range(4):
                        self.tr(bt, bfbank(bt)[:, blk, :], Pm[hh][:, blk, :], self.ident_bf[:], reads=[pmb[hh], self.kb])
                    self.cp("act", Qm[hh], bfbank(bt), [self.bankbuf[bt]], [qmb[hh]])
                    self.tt("pool", TTm[hh], Pm[hh], self.ident_bf[:].unsqueeze(1).broadcast_to([128, 4, 128]), ALU.add,
                            reads=[pmb[hh], self.kb], writes=[ttb[hh]])
                    bq = self.take_bank()
                    for blk in range(4):
                        self.mm(bq, self.banks[bq][:, blk * 128:(blk + 1) * 128], knT[hh][:, blk * 128:(blk + 1) * 128],
                                qnT[hh][:, blk * 128:(blk + 1) * 128], True, True, reads=[knb[hh], qnb[hh]])
                    self.tt("dve", tmp, self.banks[bq][:], decT, ALU.mult, reads=[self.bankbuf[bq], decb], writes=[tmpb])
                    self.tt("dve", QKD[hh], t3, self.uincl_f[:].unsqueeze(1).broadcast_to([128, 4, 128]), ALU.mult,
                            reads=[tmpb, self.kb], writes=[qkb[hh]])
                for lev in range(1, 7):
                    for hh in range(4):
                        bq = self.take_bank()
                        for blk in range(4):
                            self.mm(bq, self.banks[bq][:, blk * 128:(blk + 1) * 128], Pm[hh][:, blk, :], Qm[hh][:, blk, :], True, True,
                                    reads=[pmb[hh], qmb[hh]])
                        if lev < 6:
                            bp = self.take_bank()
                            for blk in range(4):
                                self.mm(bp, self.banks[bp][:, blk * 128:(blk + 1) * 128], Qm[hh][:, blk, :], Pm[hh][:, blk, :], True, True,
                                        reads=[pmb[hh], qmb[hh]])
                            self.cp("act", Pm[hh], fbank(bp), [self.bankbuf[bp]], [pmb[hh]])
                        self.cp("dve", Qm[hh], fbank(bq), [self.bankbuf[bq]], [qmb[hh]])
                        br = self.take_bank()
                        for blk in range(4):
                            self.mm(br, self.banks[br][:, blk * 128:(blk + 1) * 128], Qm[hh][:, blk, :], TTm[hh][:, blk, :], True, True,
                                    reads=[qmb[hh], ttb[hh]])
                        self.tt("dve", TTm[hh], fbank(br), TTm[hh], ALU.add, reads=[self.bankbuf[br], ttb[hh]], writes=[ttb[hh]])
                rbuf = vbf.rearrange("p (b c) -> p b c", b=4)
                otok = decT.rearrange("p (b c) -> p b c", b=4)
                o2s = tmp.rearrange("p (b c) -> p b c", b=4)
                for blk in range(4):
                    bs = slice(blk * 128, (blk + 1) * 128)
                    col = lambda a, hh: a[:, blk * 8 + 4 * g + hh:blk * 8 + 4 * g + hh + 1]
                    bks = self.take_bank()
                    for hh in range(4):
                        self.mm(bks, self.banks[bks][:, hh * 128:(hh + 1) * 128], knT[hh][:, bs], Sb[:, hh, :], True, True,
                                reads=[knb[hh], sbb])
                    for hh in range(4):
                        self.stt(rbuf[:, hh, :], self.banks[bks][:, hh * 128:(hh + 1) * 128], col(negeg, hh), vtok[hh][:, blk, :],
                                 ALU.mult, ALU.add, reads=[self.bankbuf[bks], gb_, vtb[hh]], writes=[vbb])
                    bvn = self.take_bank()
                    for hh in range(4):
                        self.mm(bvn, self.banks[bvn][:, hh * 128:(hh + 1) * 128], TTm[hh][:, blk, :], rbuf[:, hh, :], True, True,
                                reads=[ttb[hh], vbb])
                    for hh in range(4):
                        self.act(vnew[:, hh, :], self.banks[bvn][:, hh * 128:(hh + 1) * 128], AF.Copy, reads=[self.bankbuf[bvn], gb_],
                                 writes=[vnb], scale=col(beta, hh))
                    bo1 = self.take_bank()
                    for hh in range(4):
                        self.mm(bo1, self.banks[bo1][:, hh * 128:(hh + 1) * 128], qnT[hh][:, bs], Sb[:, hh, :], True, True,
                                reads=[qnb[hh], sbb])
                    bo2 = self.take_bank()
                    for hh in range(4):
                        self.mm(bo2, self.banks[bo2][:, hh * 128:(hh + 1) * 128], QKD[hh][:, blk, :], vnew[:, hh, :], True, True,
                                reads=[qkb[hh], vnb])
                    self.cp("act", tmp, self.banks[bo2][:], [self.bankbuf[bo2]], [tmpb])
                    for hh in range(4):
                        self.stt(otok[:, hh, :], self.banks[bo1][:, hh * 128:(hh + 1) * 128], col(eg, hh), o2s[:, hh, :],
                                 ALU.mult, ALU.add, reads=[self.bankbuf[bo1], gb_, tmpb], writes=[decb])
                    bsu = self.take_bank()
                    for hh in range(4):
                        self.mm(bsu, self.banks[bsu][:, hh * 128:(hh + 1) * 128], kdtok[hh][:, blk, :], vnew[:, hh, :], True, True,
                                reads=[kdb[hh], vnb])
                    for hh in range(4):
                        self.stt(Sf[:, hh, :], Sf[:, hh, :], col(gl, hh), self.banks[bsu][:, hh * 128:(hh + 1) * 128],
                                 ALU.mult, ALU.add, reads=[self.bankbuf[bsu], gb_, sfb], writes=[sfb])
                    self.cp("act", Sb, Sf, [sfb], [sbb])
                    self.tt("pool", tmp, decT, decT, ALU.mult, reads=[decb], writes=[tmpb])
                    P.op("dve", lambda e: e.tensor_reduce(out=ssq[:, 0:4], in_=o2s, axis=AX.X, op=ALU.add), reads=[tmpb], writes=[ssb])
                    self.act(ssq[:, 0:4], ssq[:, 0:4], AF.Ln, reads=[ssb, self.kb], writes=[ssb], bias=self.eps_col, scale=1.0 / 128)
                    self.act(ssq[:, 0:4], ssq[:, 0:4], AF.Exp, reads=[ssb], writes=[ssb], scale=-0.5)
                    for hh in range(4):
                        self.act(onb_[:, hh, :], otok[:, hh, :], AF.Copy, reads=[decb, ssb], writes=[onbuf], scale=ssq[:, hh:hh + 1])
                    bt = self.take_bank()
                    for hh in range(4):
                        self.tr(bt, bfbank(bt)[:, hh, :], onb_[:, hh, :], self.ident_bf[:], reads=[onbuf, self.kb])
                    for hh in range(4):
                        self.stt(zsT[hh][:, bs], bfbank(bt)[:, hh, :], self.cc("gog"), zsT[hh][:, bs], ALU.mult, ALU.mult,
                                 reads=[self.bankbuf[bt], self.cbuf, zsb[hh]], writes=[zsb[hh]])
                wos = [self.wload(self.d_gwout[j], 2048) for j in range(4)]
                for dc in range(KC):
                    wo = self.ring[wos[dc // 2]][:].rearrange("p (k c) -> p k c", k=KC)
                    bk = self.take_bank()
                    for hh in range(4):
                        self.mm(bk, self.banks[bk][:], wo[:, 4 * g + hh, (dc % 2) * 128:(dc % 2 + 1) * 128], zsT[hh],
                                hh == 0, hh == 3, reads=[self.slotbuf[wos[dc // 2]], zsb[hh]])
                    self.tt("dve", self.xT[:, dc, sl], self.banks[bk][:], self.xT[:, dc, sl], ALU.add,
                            reads=[self.bankbuf[bk], self.xbuf[dc][Q]], writes=[self.xbuf[dc][Q]])

    def fox(self, layer):
        self.common_init()
        P = self.P
        HD = 128
        hT = self.carve(0, 8192).bitcast(BF16).rearrange("p (c t) -> p c t", c=KC)
        for t in range(NT):
            self.rmsnorm_tile(t, "mixg", layer, hT[:, :, t * TT:(t + 1) * TT], self.hbuf[t], 8192)
        o = 9728
        knT = self.carve(o, 4096).bitcast(BF16).rearrange("p (h t) -> p h t", h=4); o += 4096
        knb = [[Buf("kn") for _ in range(NT)] for _ in range(4)]
        vtok = self.carve(o, 4096).bitcast(BF16).rearrange("p (b c) -> p b c", b=16); o += 4096
        vb = [Buf("v%d" % b) for b in range(16)]
        qT = self.carve(o, 1024).bitcast(BF16).rearrange("p (h t) -> p h t", h=4); o += 1024
        qb = [Buf("q%d" % h) for h in range(4)]
        oT = self.carve(o, 1024).bitcast(BF16).rearrange("p (h t) -> p h t", h=4); o += 1024
        ob = [Buf("o%d" % h) for h in range(4)]
        NP = 3
        pT = [self.carve(o + i * 256, 256).bitcast(BF16) for i in range(NP)]; o += NP * 256
        pb = [Buf("p%d" % i) for i in range(NP)]
        raw = self.carve(o, 512); o += 512
        sqt = self.carve(o, 256).bitcast(BF16); o += 256
        lnv = self.carve(o, 512); o += 512
        rstd = self.carve(o, 512); o += 512
        rawb, sqb, lnb, rsb = Buf("raw"), Buf("sq"), Buf("ln"), Buf("rs")
        rden = self.carve(o, 512); o += 512
        rdb = Buf("rden")
        erow = self.carve(o, 512); o += 512
        cT = self.carve(o, 128).rearrange("p (b h) -> p b h", b=16); o += 128
        cmidb = self.carve(o, 32).rearrange("p (q h) -> p q h", q=4); o += 32
        biasA = self.carve(o, 128).rearrange("p (b h) -> p b h", b=16); o += 128
        small = self.carve(o, 32); o += 32
        wf = self.carve(o, 32).bitcast(BF16).rearrange("p (k c) -> p k c", k=KC); o += 32
        assert o <= self.ARENA, o
        carry = small[:, 0:8]
        xs = erow[:, 0:32]
        tots = erow[:, 32:64]
        cb = Buf("cstuff")
        wfb = Buf("wf")
        P.op("pool", lambda e: e.dma_start(out=wf.rearrange("p k c -> p (k c)"), in_=self.d_fwf), writes=[wfb], dma_out=wfb)
        self.memset("pool", carry, 0.0, [cb])
        scale = float(HD) ** -0.5
        for g in range(2):
            for Q in range(NT):
                sl = slice(Q * TT, (Q + 1) * TT)
                hTt = hT[:, :, sl]
                hb = self.hbuf[Q]
                if g == 0:
                    bk = self.take_bank()
                    for blk in range(4):
                        for kc in range(KC):
                            self.mm(bk, self.banks[bk][:, blk * 8:(blk + 1) * 8], hTt[:, kc, blk * 128:(blk + 1) * 128], wf[:, kc, :],
                                    kc == 0, kc == KC - 1, reads=[wfb, hb])
                    x3 = xs.rearrange("p (b h) -> p b h", b=4)
                    self.tt("dve", x3, self.banks[bk][:, 0:32].rearrange("p (b h) -> p b h", b=4),
                            self.cst[:, CL["fbfb"]:CL["fbfb"] + 8].unsqueeze(1).broadcast_to([128, 4, 8]), ALU.add,
                            reads=[self.bankbuf[bk], self.cbuf], writes=[cb])
                    self.act(xs, xs, AF.Exp, reads=[cb], writes=[cb], scale=-1.0)
                    self.act(xs, xs, AF.Ln, reads=[cb, self.kb], writes=[cb], bias=self.one_col)
                    b1 = self.take_bank()
                    self.mm(b1, self.banks[b1][:, 0:32], self.uincl_f[:], xs, True, True, reads=[cb, self.kb])
                    b2 = self.take_bank()
                    self.mm(b2, self.banks[b2][:, 0:32], self.ones_f[:], xs, True, True, reads=[cb, self.kb])
                    self.cp("dve", tots, self.banks[b2][:, 0:32], [self.bankbuf[b2]], [cb])
                    for blk in range(4):
                        self.tt("dve", cT[:, 4 * Q + blk, :], self.banks[b1][:, blk * 8:(blk + 1) * 8], carry, ALU.add,
                                reads=[self.bankbuf[b1], cb], writes=[cb])
                        self.tt("dve", carry, carry, tots[:, blk * 8:(blk + 1) * 8], ALU.add, reads=[cb], writes=[cb])
                        if blk == 1:
                            self.cp("dve", cmidb[:, Q, :], carry, [cb], [cb])
                nj = 4 * Q + 4
                DBG = int(os.environ.get("FOXDBG", "9"))
                if DBG <= 1:
                    continue
                self.tt("dve", biasA[:, 0:nj, :], cT[:, 0:nj, :], cmidb[:, Q:Q + 1, :].broadcast_to([128, nj, 8]), ALU.subtract,
                        reads=[cb], writes=[cb])
                for which in range(2):
                    for hp in range(2):
                        sw = self.wload(self.d_fwin[which * 4 + 2 * g + hp], 2048)
                        w_ = self.ring[sw][:].rearrange("p (k c) -> p k c", k=KC)
                        for h2 in range(2):
                            hh = 2 * hp + h2
                            bk = self.take_bank()
                            for kc in range(KC):
                                self.mm(bk, self.banks[bk][:], w_[:, kc, h2 * 128:(h2 + 1) * 128], hTt[:, kc, :],
                                        kc == 0, kc == KC - 1, reads=[self.slotbuf[sw], hb])
                            self.cp("dve", raw, self.banks[bk][:], [self.bankbuf[bk]], [rawb])
                            self.act(sqt, raw, AF.Square, reads=[rawb], writes=[sqb])
                            b2 = self.take_bank()
                            self.mm(b2, self.banks[b2][:], self.ones_bf[:], sqt, True, True, reads=[sqb, self.kb])
                            self.act(lnv, self.banks[b2][:], AF.Ln, reads=[self.bankbuf[b2], self.kb], writes=[lnb],
                                     bias=self.eps_col, scale=1.0 / HD)
                            self.act(rstd, lnv, AF.Exp, reads=[lnb], writes=[rsb], scale=-0.5)
                            if which == 0:
                                self.stt(qT[:, hh, :], raw, self.cc("fqg"), rstd, ALU.mult, ALU.mult,
                                         reads=[rawb, rsb, self.cbuf], writes=[qb[hh]])
                            else:
                                self.stt(knT[:, hh, sl], raw, self.cc("fkg"), rstd, ALU.mult, ALU.mult,
                                         reads=[rawb, rsb, self.cbuf], writes=[knb[hh][Q]])
                for hp in range(2):
                    sw = self.wload(self.d_fwin[8 + 2 * g + hp], 2048)
                    w_ = self.ring[sw][:].rearrange("p (k c) -> p k c", k=KC)
                    for blk in range(4):
                        bk = self.take_bank()
                        for kc in range(KC):
                            self.mm(bk, self.banks[bk][:, 0:256], hTt[:, kc, blk * 128:(blk + 1) * 128], w_[:, kc, :],
                                    kc == 0, kc == KC - 1, reads=[self.slotbuf[sw], hb])
                        self.act(vtok[:, 4 * Q + blk, hp * 256:(hp + 1) * 256], self.banks[bk][:, 0:256], AF.Copy,
                                 reads=[self.bankbuf[bk]], writes=[vb[4 * Q + blk]])
                pi = 0
                if DBG <= 2:
                    continue
                for hh in range(4):
                    h = 4 * g + hh
                    bo = self.take_bank()
                    self.held.add(bo)
                    bd = self.take_bank()
                    self.held.add(bd)
                    def s_stage(j):
                        off = max(0, (j - 4 * Q) * 128)
                        bs = self.take_bank()
                        self.mm(bs, self.banks[bs][:, off:512], knT[:, hh, j * 128:(j + 1) * 128], qT[:, hh, off:512],
                                True, True, reads=[knb[hh][j // 4], qb[hh]])
                        return bs

                    def pv_stage(j, bs, pi_):
                        off = max(0, (j - 4 * Q) * 128)
                        p_, pbuf = pT[pi_ % NP], pb[pi_ % NP]
                        self.act(p_[:, off:512], self.banks[bs][:, off:512], AF.Exp, reads=[self.bankbuf[bs], cb], writes=[pbuf],
                                 bias=biasA[:, j, h:h + 1], scale=scale)
                        if j >= 4 * Q:
                            self.tt("pool", p_[:, off:off + 128], p_[:, off:off + 128], self.uincl_bf[:], ALU.mult,
                                    reads=[pbuf, self.kb], writes=[pbuf])
                        return p_, pbuf, off

                    def acc_stage(j, p_, pbuf, off):
                        self.mm(bo, self.banks[bo][:, off:512], vtok[:, j, hh * 128:(hh + 1) * 128], p_[:, off:512],
                                j == 0, j == nj - 1, reads=[vb[j], pbuf])
                        self.mm(bd, self.banks[bd][:, off:512], self.ones_bf[:], p_[:, off:512],
                                j == 0, j == nj - 1, reads=[pbuf, self.kb])
                    bs_next = s_stage(0)
                    for j in range(nj):
                        bs_cur = bs_next
                        pp = pv_stage(j, bs_cur, pi)
                        pi += 1
                        if j + 1 < nj:
                            bs_next = s_stage(j + 1)
                        acc_stage(j, *pp)
                    P.op("dve", lambda e, bd=bd: e.reciprocal(out=rden, in_=self.banks[bd][:]), reads=[self.bankbuf[bd]], writes=[rdb])
                    self.tt("dve", oT[:, hh, :], self.banks[bo][:], rden, ALU.mult, reads=[self.bankbuf[bo], rdb], writes=[ob[hh]])
                    self.held.discard(bo)
                    self.held.discard(bd)
                wos = [self.wload(self.d_fwout[j], 2048) for j in range(4)]
                for dc in range(KC):
                    wo = self.ring[wos[dc // 2]][:].rearrange("p (k c) -> p k c", k=KC)
                    bk = self.take_bank()
                    for hh in range(4):
                        self.mm(bk, self.banks[bk][:], wo[:, 4 * g + hh, (dc % 2) * 128:(dc % 2 + 1) * 128], oT[:, hh, :],
                                hh == 0, hh == 3, reads=[self.slotbuf[wos[dc // 2]], ob[hh]])
                    self.tt("dve", self.xT[:, dc, sl], self.banks[bk][:], self.xT[:, dc, sl], ALU.add,
                            reads=[self.bankbuf[bk], self.xbuf[dc][Q]], writes=[self.xbuf[dc][Q]])


ALL_STAGES = [("conf", 0), ("ffn", 0), ("gdn", 1), ("ffn", 1), ("fox", 2), ("ffn", 2), ("conf", 3), ("ffn", 3)]


def build_nc(stages):
    nc = bass.Bass("TRN2", target_bir_lowering=False)
    k = K(nc, stages)
    k.build()
    return nc, k.used_inputs


def prep_shared(inp):
    sh = {}
    sh["cst"] = pack_consts(inp)
    sh["cwin"] = np.stack([tile_w(inp["conv_w_in"][i]).reshape(8, 128, 2048) for i in range(2)])
    sh["cwout"] = np.stack([tile_w(inp["conv_w_out"][i]).reshape(4, 128, 2048) for i in range(2)])
    gw = np.asarray(inp["gdn_w_in"][0], np.float32)
    sh["gwin"] = tile_w(gw[:, :4096]).reshape(16, 128, 2048)
    sh["gwab"] = np.ascontiguousarray(gw[:, 4096:4112].reshape(8, 128, 16).transpose(1, 0, 2)).reshape(128, 128)
    sh["gwout"] = tile_w(inp["gdn_w_out"][0]).reshape(4, 128, 2048)
    fw = np.asarray(inp["fox_w_in"][0], np.float32)
    sh["fwin"] = tile_w(fw[:, :3072]).reshape(12, 128, 2048)
    sh["fwf"] = np.ascontiguousarray(fw[:, 3072:3080].reshape(8, 128, 8).transpose(1, 0, 2)).reshape(128, 64)
    sh["fwout"] = tile_w(inp["fox_w_out"][0]).reshape(4, 128, 2048)
    sh["wup"] = np.stack([tile_w(inp["ffn_w_up"][l]).reshape(22, 128, 2048) for l in range(4)])
    sh["wdn"] = np.stack([tile_wk(inp["ffn_w_down"][l]).reshape(11, 128, 2048) for l in range(4)])
    return sh


def run(inp, stages, ncores=8, trace=False):
    x = np.asarray(inp["x"], np.float32)
    sh = prep_shared(inp)
    nc, used = build_nc(stages)
    sh = {k_: v for k_, v in sh.items() if k_ in used}
    in_maps = []
    for b in range(ncores):
        m = dict(sh)
        m["xT"] = np.ascontiguousarray(x[b].T).reshape(KC, 128, S)
        in_maps.append(m)
    res = run_bass_kernel_spmd(nc, in_maps, core_ids=list(range(ncores)), trace=trace)
    out = np.stack([np.asarray(r["yT"], np.float32).reshape(D, S).T for r in res.results])
    return out, res


def kernel(**inputs):
    out, _ = run(inputs, ALL_STAGES, ncores=8)
    return out.astype(np.float32)
```

```python
import contextlib
import os
import numpy as np
import concourse.bass as bass
import concourse.mybir as mybir
from concourse.bass_utils import run_bass_kernel_spmd

F32 = mybir.dt.float32
BF16 = mybir.dt.bfloat16
AF = mybir.ActivationFunctionType
ALU = mybir.AluOpType
AX = mybir.AxisListType

S = 2048
D = 1024
TT = 512
NT = 4
KC = 8
FF = 2816
EPS = 1e-6
ENGS = ["pe", "act", "dve", "pool", "sp"]
BLOCK_ATTR = {"pe": "tensor", "act": "scalar", "dve": "vector", "pool": "gpsimd", "sp": "sync"}


class Buf:
    __slots__ = ("name", "last_w", "readers", "sem", "dma_cnt", "excl")

    def __init__(self, name, excl=False):
        self.name = name
        self.excl = excl
        self.last_w = None
        self.readers = []
        self.sem = None
        self.dma_cnt = 0


class Op:
    __slots__ = ("eng", "fn", "waits", "signal", "idx", "dma_buf", "clock", "sigcnt")

    def __init__(self, eng, fn, idx, dma_buf=None):
        self.eng = eng
        self.fn = fn
        self.idx = idx
        self.waits = []
        self.signal = False
        self.dma_buf = dma_buf
        self.clock = None
        self.sigcnt = None


class Prog:
    def __init__(self, nc):
        self.nc = nc
        self.ops = {e: [] for e in ENGS}
        self.obs = {e: {} for e in ENGS}
        self.dma_bufs = []
        self.pending = {e: [] for e in ENGS}

    def barrier(self):
        toks = [("eng", e, len(self.ops[e]) - 1) for e in ENGS if self.ops[e] and self.ops[e][-1].dma_buf is None]
        for e in ENGS:
            if self.ops[e] and self.ops[e][-1].dma_buf is not None:
                for o in reversed(self.ops[e]):
                    if o.dma_buf is None:
                        toks.append(("eng", e, o.idx))
                        break
        for e in ENGS:
            self.pending[e] = list(toks)

    def _need(self, op, tok):
        e = op.eng
        if tok[0] == "eng":
            _, se, si = tok
            if self.obs[e].get(se, -1) >= si:
                return
            src = self.ops[se][si]
            src.signal = True
            op.waits.append(tok)
            self.obs[e][se] = si
            if src.clock:
                for k, v in src.clock.items():
                    if self.obs[e].get(k, -1) < v:
                        self.obs[e][k] = v
        else:
            _, b, cnt = tok
            key = ("dma", id(b))
            if self.obs[e].get(key, -1) >= cnt:
                return
            op.waits.append(tok)
            self.obs[e][key] = cnt

    def op(self, eng, fn, reads=(), writes=(), dma_out=None, nobarrier=False):
        lst = self.ops[eng]
        o = Op(eng, fn, len(lst), dma_buf=dma_out)
        if any(r.excl for r in reads):
            writes = list(writes) + [r for r in reads if r.excl and r not in writes]
            reads = [r for r in reads if not r.excl]
        best = {}
        for r in reads:
            t = r.last_w
            if t is not None:
                k = t[1] if t[0] == "eng" else ("dma", id(t[1]))
                if k not in best or best[k][2] < t[2]:
                    best[k] = t
        for w in writes:
            for t in [w.last_w] + w.readers:
                if t is not None:
                    k = t[1] if t[0] == "eng" else ("dma", id(t[1]))
                    if k not in best or best[k][2] < t[2]:
                        best[k] = t
        if self.pending[eng] and not nobarrier:
            for t in self.pending[eng]:
                if t[1] == eng:
                    continue
                k = t[1]
                if k not in best or best[k][2] < t[2]:
                    best[k] = t
            self.pending[eng] = []
        for t in best.values():
            if eng == "pe" and t[0] == "eng" and t[1] == "pe":
                continue
            self._need(o, t)
        if dma_out is not None:
            if dma_out.sem is None:
                self.dma_bufs.append(dma_out)
                dma_out.sem = True
            dma_out.dma_cnt += 1
            tok = ("dma", dma_out, dma_out.dma_cnt)
        else:
            tok = ("eng", eng, o.idx)
        o.clock = dict(self.obs[eng])
        for r in reads:
            r.readers.append(tok)
        for w in writes:
            w.last_w = tok
            w.readers = []
        lst.append(o)
        return tok

    def emit(self, final_waits=()):
        nc = self.nc
        CH = 2000
        with contextlib.ExitStack() as st:
            for e in ENGS:
                c = 0
                for o in self.ops[e]:
                    if o.signal and o.dma_buf is None:
                        o.sigcnt = c
                        c += 1
                    else:
                        o.sigcnt = c - 1
            nsig = {e: sum(1 for o in self.ops[e] if o.signal and o.dma_buf is None) for e in ENGS}
            esem = {e: [st.enter_context(nc.semaphore("s_%s%d" % (e, i))) for i in range(max(1, (nsig[e] + CH - 1) // CH))]
                    for e in ENGS}
            for i, b in enumerate(self.dma_bufs):
                b.sem = st.enter_context(nc.semaphore("d%d" % i))
            block = st.enter_context(nc.Block())
            for e in ENGS:
                ops = self.ops[e]
                fw = [b for (fe, b) in final_waits if fe == e]
                if not ops and not fw:
                    continue

                def body(eng, ops=ops, e=e, fw=fw):
                    for o in ops:
                        for t in o.waits:
                            if t[0] == "eng":
                                k = self.ops[t[1]][t[2]].sigcnt
                                eng.wait_ge(esem[t[1]][k // CH], k % CH + 1)
                            else:
                                eng.wait_ge(t[1].sem, 16 * t[2])
                        ins = o.fn(eng)
                        if o.dma_buf is not None:
                            ins.then_inc(o.dma_buf.sem, 16)
                        elif o.signal:
                            ins.then_inc(esem[e][o.sigcnt // CH], 1)
                    for b in fw:
                        eng.wait_ge(b.sem, 16 * b.dma_cnt)

                getattr(block, BLOCK_ATTR[e])(body)


def _cst_layout():
    lay = {}
    off = 0
    for name, n in [("mixg", 32), ("ffng", 32), ("cbin", 32), ("cwdw", 2 * 31 * 8), ("cbdw", 16),
                    ("clng", 16), ("clnb", 16), ("gconv", 96), ("gog", 1), ("fqg", 1), ("fkg", 1),
                    ("fwdw", 4 * 3 * 44), ("fbf", 1), ("galog", 8), ("gdtb", 8), ("fbfb", 8)]:
        lay[name] = off
        off += n
    return lay, off


CL, NCST = _cst_layout()


def _cols(v):
    v = np.asarray(v, dtype=np.float32).reshape(-1, 128)
    return v.T


def pack_consts(inp):
    c = np.zeros((128, NCST), np.float32)

    def put(name, arr):
        arr = np.asarray(arr, np.float32)
        c[:, CL[name]:CL[name] + arr.shape[1]] = arr

    put("mixg", _cols(inp["mix_norm_g"].reshape(-1)))
    put("ffng", _cols(inp["ffn_norm_g"].reshape(-1)))
    put("cbin", _cols(inp["conv_b_in"].reshape(-1)))
    put("cwdw", _cols(inp["conv_w_dw"].reshape(-1)))
    put("cbdw", _cols(inp["conv_b_dw"].reshape(-1)))
    put("clng", _cols(inp["conv_ln_g"].reshape(-1)))
    put("clnb", _cols(inp["conv_ln_b"].reshape(-1)))
    put("gconv", _cols(inp["gdn_conv_w"].reshape(-1)))
    put("gog", _cols(inp["gdn_o_norm_g"].reshape(-1)))
    put("fqg", _cols(inp["fox_q_norm_g"].reshape(-1)))
    put("fkg", _cols(inp["fox_k_norm_g"].reshape(-1)))
    put("fwdw", _cols(inp["ffn_w_dw"].reshape(-1)))
    bf = np.zeros((128, 1), np.float32)
    bf[:8, 0] = np.asarray(inp["fox_b_f"], np.float32).reshape(-1)
    put("fbf", bf)
    put("galog", np.broadcast_to(np.asarray(inp["gdn_a_log"], np.float32).reshape(1, 8), (128, 8)))
    put("gdtb", np.broadcast_to(np.asarray(inp["gdn_dt_bias"], np.float32).reshape(1, 8), (128, 8)))
    put("fbfb", np.broadcast_to(np.asarray(inp["fox_b_f"], np.float32).reshape(1, 8), (128, 8)))
    return c


def tile_w(w, width=256):
    w = np.asarray(w, np.float32)
    K, N = w.shape
    return np.ascontiguousarray(w.reshape(K // 128, 128, N // width, width).transpose(2, 1, 0, 3))


def tile_wk(w, nk=2):
    w = np.asarray(w, np.float32)
    K, N = w.shape
    return np.ascontiguousarray(w.reshape(K // (128 * nk), nk, 128, N).transpose(0, 2, 1, 3))


class K:
    def __init__(self, nc, stages):
        self.nc = nc
        self.P = Prog(nc)
        self.stages = stages
        self.st = contextlib.ExitStack()
        self.bank_rr = 0
        self.slot_rr = 0
        self.uid = 0

    def sb(self, name, shape, dt):
        return self.st.enter_context(self.nc.sbuf_tensor(name, shape, dt))

    def B(self, name=""):
        self.uid += 1
        return Buf("%s%d" % (name, self.uid))

    def take_bank(self):
        while True:
            i = self.bank_rr % 8
            self.bank_rr += 1
            if i not in self.held:
                return i

    def take_slot(self):
        i = self.slot_rr % self.NSLOT
        self.slot_rr += 1
        return i

    def mm(self, bank, out, lhsT, rhs, start, stop, reads, extra_w=()):
        self.P.op("pe", lambda e: e.matmul(out, lhsT, rhs, start=start, stop=stop),
                  reads=reads, writes=[self.bankbuf[bank]] + list(extra_w))

    def tr(self, bank, out, in_, ident, reads):
        self.P.op("pe", lambda e: e.transpose(out, in_, ident), reads=reads, writes=[self.bankbuf[bank]])

    def act(self, out, in_, func, reads, writes, bias=None, scale=None, eng="act"):
        kw = {}
        if bias is not None:
            kw["bias"] = bias
        if scale is not None:
            kw["scale"] = scale
        self.P.op("act", lambda e: e.activation(out=out, in_=in_, func=func, **kw), reads=reads, writes=writes)

    def tt(self, eng, out, in0, in1, op, reads, writes):
        self.P.op(eng, lambda e: e.tensor_tensor(out=out, in0=in0, in1=in1, op=op), reads=reads, writes=writes)

    def stt(self, out, in0, scalar, in1, op0, op1, reads, writes):
        self.P.op("dve", lambda e: e.scalar_tensor_tensor(out=out, in0=in0, scalar=scalar, in1=in1, op0=op0, op1=op1),
                  reads=reads, writes=writes)

    def ts(self, eng, out, in0, s1, s2, op0, op1, reads, writes):
        if op1 is None:
            self.P.op(eng, lambda e: e.tensor_scalar(out=out, in0=in0, scalar1=s1, scalar2=None, op0=op0),
                      reads=reads, writes=writes)
        else:
            self.P.op(eng, lambda e: e.tensor_scalar(out=out, in0=in0, scalar1=s1, scalar2=s2, op0=op0, op1=op1),
                      reads=reads, writes=writes)

    def cp(self, eng, out, in_, reads, writes):
        if eng == "act":
            self.P.op("act", lambda e: e.copy(out=out, in_=in_), reads=reads, writes=writes)
        else:
            self.P.op(eng, lambda e: e.tensor_copy(out=out, in_=in_), reads=reads, writes=writes)

    def memset(self, eng, ap, val, writes):
        self.P.op(eng, lambda e: e.memset(ap, val), writes=writes)

    def wload(self, src_ap, ncols_total, view=None):
        s = self.take_slot()
        dst = self.ring[s][:, 0:ncols_total]
        self.P.op("pool", lambda e: e.dma_start(out=dst, in_=src_ap), writes=[self.slotbuf[s]], dma_out=self.slotbuf[s], nobarrier=True)
        return s

    def build(self):
        nc = self.nc
        P = self.P
        dt = nc.dram_tensor
        self.xin = dt("xT", [KC, 128, S], F32, kind="ExternalInput").ap()
        self.cst_d = dt("cst", [128, NCST], F32, kind="ExternalInput").ap()
        kinds = set(k_ for k_, _ in self.stages)
        self.used_inputs = ["xT", "cst"]

        def din(name, shape, kind_):
            if kind_ not in kinds:
                return None
            self.used_inputs.append(name)
            return dt(name, shape, F32, kind="ExternalInput").ap()
        self.d_cwin = din("cwin", [2, 8, 128, 2048], "conf")
        self.d_cwout = din("cwout", [2, 4, 128, 2048], "conf")
        self.d_gwin = din("gwin", [16, 128, 2048], "gdn")
        self.d_gwab = din("gwab", [128, 128], "gdn")
        self.d_gwout = din("gwout", [4, 128, 2048], "gdn")
        self.d_fwin = din("fwin", [12, 128, 2048], "fox")
        self.d_fwf = din("fwf", [128, 64], "fox")
        self.d_fwout = din("fwout", [4, 128, 2048], "fox")
        self.d_wup = din("wup", [4, 22, 128, 2048], "ffn")
        self.d_wdn = din("wdn", [4, 11, 128, 2048], "ffn")
        self.yout = dt("yT", [KC, 128, S], F32, kind="ExternalOutput").ap()

        self.xT = self.sb("xT_sb", [128, KC, S], F32)
        self.cst = self.sb("cst_sb", [128, NCST], F32)
        self.NSLOT = 8
        self.ring = [self.sb("ring%d" % i, [128, 2048], BF16) for i in range(self.NSLOT)]
        self.slotbuf = [Buf("slot%d" % i) for i in range(self.NSLOT)]
        self.banks = [self.st.enter_context(nc.psum_tensor("bank%d" % i, [128, 512], F32)) for i in range(8)]
        self.bankbuf = [Buf("bank%d" % i, excl=True) for i in range(8)]
        self.held = set()
        self.xbuf = [[Buf("x%d_%d" % (c, t)) for t in range(NT)] for c in range(KC)]
        self.hbuf = [Buf("h%d" % t) for t in range(NT)]
        self.hb1 = Buf("htile")
        self.cbuf = Buf("cst")
        self.ones_bf = self.sb("ones_bf", [128, 128], BF16)
        self.ones_f = self.sb("ones_f", [128, 128], F32)
        self.ident_f = self.sb("ident_f", [128, 128], F32)
        self.ident_bf = self.sb("ident_bf", [128, 128], BF16)
        self.uincl_f = self.sb("uincl_f", [128, 128], F32)
        self.uincl_bf = self.sb("uincl_bf", [128, 128], BF16)
        self.lstrict_f = self.sb("lstrict_f", [128, 128], F32)
        self.msu_f = self.sb("msu_f", [128, 128], F32)
        self.kb = Buf("consts")
        self.ARENA = 25600
        self.arena = self.sb("arena", [128, self.ARENA], F32)

        P.op("sp", lambda e: e.dma_start(out=self.cst[:], in_=self.cst_d), writes=[self.cbuf], dma_out=self.cbuf)
        kb = self.kb
        self.memset("pool", self.ones_f[:], 1.0, [kb])
        self.memset("pool", self.ones_bf[:], 1.0, [kb])

        def asel(out, base, cm, step, op):
            P.op("pool", lambda e: e.affine_select(out=out, in_=self.ones_f[:], pattern=[[step, 128]], compare_op=op,
                                                   fill=0.0, base=base, channel_multiplier=cm), reads=[kb], writes=[kb])
        asel(self.ident_f[:], 0, -1, 1, ALU.is_equal)
        asel(self.uincl_f[:], 0, -1, 1, ALU.is_ge)
        asel(self.lstrict_f[:], -1, 1, -1, ALU.is_ge)
        asel(self.msu_f[:], -1, -1, 1, ALU.is_ge)
        self.cp("pool", self.ident_bf[:], self.ident_f[:], [kb], [kb])
        self.cp("pool", self.uincl_bf[:], self.uincl_f[:], [kb], [kb])

        for c in range(KC):
            for t in range(NT):
                P.op("sp", lambda e, c=c, t=t: e.dma_start(out=self.xT[:, c, t * TT:(t + 1) * TT],
                                                           in_=self.xin[c, :, t * TT:(t + 1) * TT]),
                     writes=[self.xbuf[c][t]], dma_out=self.xbuf[c][t])

        ia = 0
        for stg in self.stages:
            kind, layer = stg
            P.barrier()
            if kind == "conf":
                self.conformer(layer, ia)
                ia += 1
            elif kind == "gdn":
                self.gdn(layer)
            elif kind == "fox":
                self.fox(layer)
            elif kind == "ffn":
                self.ffn(layer)

        ob = Buf("out")
        for c in range(KC):
            P.op("sp", lambda e, c=c: e.dma_start(out=self.yout[c], in_=self.xT[:, c, :]),
                 reads=self.xbuf[c], writes=[ob], dma_out=ob)
        P.emit(final_waits=[("sp", ob)])
        self.st.close()

    def cc(self, name, idx=0):
        o = CL[name] + idx
        return self.cst[:, o:o + 1]

    def carve(self, off, nwords):
        assert off + nwords <= self.ARENA, (off, nwords)
        return self.arena[:, off:off + nwords]

    def rmsnorm_tile(self, t, gname, layer, hview, hb, ar_off):
        sl = slice(t * TT, (t + 1) * TT)
        sqs = [self.carve(ar_off + i * 256, 256).bitcast(BF16) for i in range(2)]
        lnv = self.carve(ar_off + 512, 512)
        rstd = self.carve(ar_off + 1024, 512)
        bl, br = self.nb_ln, self.nb_rstd
        bk = self.take_bank()
        for c in range(KC):
            sq, bsq = sqs[c % 2], self.nb_sq[c % 2]
            self.act(sq, self.xT[:, c, sl], AF.Square, reads=[self.xbuf[c][t]], writes=[bsq])
            self.mm(bk, self.banks[bk][:], self.ones_bf[:], sq, c == 0, c == KC - 1, reads=[bsq, self.kb])
        self.act(lnv, self.banks[bk][:], AF.Ln, reads=[self.bankbuf[bk], self.kb], writes=[bl], bias=self.eps_col, scale=1.0 / D)
        self.act(rstd, lnv, AF.Exp, reads=[bl], writes=[br], scale=-0.5)
        for c in range(KC):
            self.stt(hview[:, c, :], self.xT[:, c, sl], self.cc(gname, layer * 8 + c), rstd, ALU.mult, ALU.mult,
                     reads=[self.xbuf[c][t], br, self.cbuf], writes=[hb])

    def common_init(self):
        if getattr(self, "_ci", False):
            return
        self._ci = True
        self.nb_sq, self.nb_ln, self.nb_rstd = [Buf("sq0"), Buf("sq1")], Buf("ln"), Buf("rstd")
        self.eps_t = self.sb("eps_t", [128, 4], F32)
        self.memset("pool", self.eps_t[:, 0:1], EPS, [self.kb])
        self.memset("pool", self.eps_t[:, 1:2], 1.0, [self.kb])
        self.memset("pool", self.eps_t[:, 2:3], 0.0, [self.kb])
        self.eps_col = self.eps_t[:, 0:1]
        self.one_col = self.eps_t[:, 1:2]
        self.zero_col = self.eps_t[:, 2:3]

    def ffn(self, layer):
        self.common_init()
        P = self.P
        hT = self.carve(0, 8192).bitcast(BF16).rearrange("p (c t) -> p c t", c=KC)
        self.hT = hT
        for t in range(NT):
            self.rmsnorm_tile(t, "ffng", layer, hT[:, :, t * TT:(t + 1) * TT], self.hbuf[t], 8192)
        GOFF = 8192 + 1536
        gT = [self.carve(GOFF + i * 4096, 4096).bitcast(BF16).rearrange("p (c t) -> p c t", c=4) for i in range(2)]
        gbuf = [[[Buf("g") for _ in range(NT)] for _ in range(4)] for _ in range(2)]
        UOFF = GOFF + 2 * 4096
        NU = 6
        U = [self.carve(UOFF + i * 516, 516) for i in range(NU)]
        ubuf = [Buf("U") for _ in range(NU)]
        AOFF = UOFF + NU * 516
        NA = 8
        ACC = [self.carve(AOFF + i * 512, 512) for i in range(NA)]
        abuf = [Buf("acc") for _ in range(NA)]
        urr = [0]
        arr = [0]
        hreads = self.hbuf

        parts = [(0, 2), (2, 4), (4, 6), (6, 8), (8, 10), (10, 11)]
        order = []
        for (u0, u1) in parts:
            for u in range(u0, u1):
                order.append(("g", u, self.d_wup[layer, u]))
                order.append(("u", u, self.d_wup[layer, 11 + u]))
            for u in range(u0, u1):
                order.append(("d", u, self.d_wdn[layer, u]))
        issued = {}
        nxt = [0]
        deferred = []
        pend_down = []

        def want(key, ahead):
            idx = [i for i, o_ in enumerate(order) if (o_[0], o_[1]) == key][0]
            while nxt[0] <= min(idx + ahead, len(order) - 1):
                o_ = order[nxt[0]]
                issued[(o_[0], o_[1])] = self.wload(o_[2], 2048)
                nxt[0] += 1
            return issued[key]

        for pi, (u0, u1) in enumerate(parts):
            g = gT[pi % 2]
            gb = gbuf[pi % 2]
            for u in range(u0, u1):
                if u == u0 + 1 or (u == u0 and u1 - u0 == 1 and False):
                    while pend_down:
                        pend_down.pop(0)()
                sg = want(("g", u), 5)
                su = want(("u", u), 4)
                wg = self.ring[sg][:].rearrange("p (k c) -> p k c", k=KC)
                wu = self.ring[su][:].rearrange("p (k c) -> p k c", k=KC)
                for c2 in range(2):
                    ci = 2 * u + c2
                    lc = ci - 2 * u0
                    prev = {"g": None, "u": None}
                    for t in range(NT):
                        sl = slice(t * TT, (t + 1) * TT)
                        accs = {}
                        for which, w_, slot_, col0 in (("g", wg, sg, ci), ("u", wu, su, 22 + ci)):
                            bk = self.take_bank()
                            for kc in range(KC):
                                self.mm(bk, self.banks[bk][:], w_[:, kc, c2 * 128:(c2 + 1) * 128], self.hT[:, kc, sl],
                                        kc == 0, kc == KC - 1, reads=[self.slotbuf[slot_], self.hbuf[t]])
                            ui = urr[0] % NU
                            urr[0] += 1
                            Ut, Ub = U[ui], ubuf[ui]
                            if t == 0:
                                self.memset("pool", Ut[:, 0:2], 0.0, [Ub])
                            else:
                                pU, pB = prev[which]
                                self.cp("pool", Ut[:, 0:2], pU[:, 512:514], [pB], [Ub])
                            self.act(Ut[:, 2:514], self.banks[bk][:], AF.Copy, reads=[self.bankbuf[bk]], writes=[Ub])
                            prev[which] = (Ut, Ub)
                            ai = arr[0] % NA
                            arr[0] += 1
                            At, Ab = ACC[ai], abuf[ai]
                            wcol = lambda k, col0=col0: self.cc("fwdw", (layer * 3 + k) * 44 + col0)
                            self.act(At, Ut[:, 0:512], AF.Copy, reads=[Ub, self.cbuf], writes=[Ab], scale=wcol(0))
                            self.stt(At, Ut[:, 1:513], wcol(1), At, ALU.mult, ALU.add, reads=[Ub, self.cbuf, Ab], writes=[Ab])
                            self.stt(At, Ut[:, 2:514], wcol(2), At, ALU.mult, ALU.add, reads=[Ub, self.cbuf, Ab], writes=[Ab])
                            accs[which] = (At, Ab)
                        Ag, Agb = accs["g"]
                        Au, Aub = accs["u"]

                        def tail(Ag=Ag, Agb=Agb, Au=Au, Aub=Aub, dst=g[:, lc, sl], db=gb[lc][t]):
                            self.act(Ag, Ag, AF.Silu, reads=[Agb], writes=[Agb])
                            self.tt("pool", dst, Ag, Au, ALU.mult, reads=[Agb, Aub], writes=[db])
                        if deferred:
                            deferred.pop(0)()
                        deferred.append(tail)
            dslots = [want(("d", u), 3) for u in range(u0, u1)]

            def down(g=g, gb=gb, nch=2 * (u1 - u0), dslots=dslots):
                while deferred:
                    deferred.pop(0)()
                for dc in range(KC):
                    for t in range(NT):
                        sl = slice(t * TT, (t + 1) * TT)
                        bk = self.take_bank()
                        for lc in range(nch):
                            s_ = dslots[lc // 2]
                            wd = self.ring[s_][:].rearrange("p (k c) -> p k c", k=2)
                            self.mm(bk, self.banks[bk][:], wd[:, lc % 2, dc * 128:(dc + 1) * 128], g[:, lc, sl],
                                    lc == 0, lc == nch - 1, reads=[self.slotbuf[s_], gb[lc][t]])
                        self.tt("dve", self.xT[:, dc, sl], self.banks[bk][:], self.xT[:, dc, sl], ALU.add,
                                reads=[self.bankbuf[bk], self.xbuf[dc][t]], writes=[self.xbuf[dc][t]])
            pend_down.append(down)
        while pend_down:
            pend_down.pop(0)()

    def conformer(self, layer, ia):
        self.common_init()
        P = self.P
        hTt = self.carve(0, 2048).bitcast(BF16).rearrange("p (c t) -> p c t", c=KC)
        G = self.carve(3584, 2176).bitcast(BF16).rearrange("p (c t) -> p c t", c=KC)
        gb = [Buf("G%d" % c) for c in range(KC)]
        DG = [self.carve(5760 + i * 1984, 1984).bitcast(BF16).rearrange("p (k m) -> p k m", k=31) for i in range(2)]
        dgb = [Buf("dg0"), Buf("dg1")]
        CO = self.carve(9728, 4096).rearrange("p (c t) -> p c t", c=KC)
        cob = [Buf("co%d" % c) for c in range(KC)]
        UB = [self.carve(13824 + i * 256, 256).bitcast(BF16) for i in range(2)]
        SQ = [self.carve(14336 + i * 256, 256).bitcast(BF16) for i in range(2)]
        ubb = [Buf("ub0"), Buf("ub1")]
        sqb = [Buf("sqb0"), Buf("sqb1")]
        mean = self.carve(14848, 512)
        msq = self.carve(15360, 512)
        lnv = self.carve(15872, 512)
        rstd = self.carve(16384, 512)
        stb = Buf("stats")
        TMP = [self.carve(16896 + i * 512, 512) for i in range(2)]
        tmb = [Buf("tmp0"), Buf("tmp1")]
        sT = self.carve(17920, 2048).bitcast(BF16).rearrange("p (c t) -> p c t", c=KC)
        stbuf = [Buf("sT%d" % c) for c in range(KC)]
        SG = [self.carve(19968 + i * 512, 512) for i in range(2)]
        sgb = [Buf("sg0"), Buf("sg1")]
        for c in range(KC):
            self.memset("pool", G[:, c, 0:32], 0.0, [gb[c]])
        for t in range(NT):
            sl = slice(t * TT, (t + 1) * TT)
            self.rmsnorm_tile(t, "mixg", layer, hTt, self.hb1, 2048)
            s1 = self.take_bank()
            self.held.add(s1)
            s2 = self.take_bank()
            self.held.add(s2)
            wst = {}

            def head(c):
                if c % 2 == 0:
                    wst["sv"] = self.wload(self.d_cwin[ia, c // 2], 2048)
                    wst["sg"] = self.wload(self.d_cwin[ia, 4 + c // 2], 2048)
                sv, sg_ = wst["sv"], wst["sg"]
                wv = self.ring[sv][:].rearrange("p (k c) -> p k c", k=KC)
                wg = self.ring[sg_][:].rearrange("p (k c) -> p k c", k=KC)
                bv = self.take_bank()
                for kc in range(KC):
                    self.mm(bv, self.banks[bv][:], wv[:, kc, (c % 2) * 128:(c % 2 + 1) * 128], hTt[:, kc, :],
                            kc == 0, kc == KC - 1, reads=[self.slotbuf[sv], self.hb1])
                bg = self.take_bank()
                for kc in range(KC):
                    self.mm(bg, self.banks[bg][:], wg[:, kc, (c % 2) * 128:(c % 2 + 1) * 128], hTt[:, kc, :],
                            kc == 0, kc == KC - 1, reads=[self.slotbuf[sg_], self.hb1])
                return bv, bg

            def tail(c, bv, bg):
                dg, dgbuf = DG[c % 2], dgb[c % 2]
                wb = CL["cwdw"] + ia * 31 * 8 + c
                wtaps = self.cst[:, wb:wb + 30 * 8 + 1:8]
                self.tt("dve", dg, self.ident_bf[:].unsqueeze(1).broadcast_to([128, 31, 128]),
                        wtaps.unsqueeze(2).broadcast_to([128, 31, 128]), ALU.mult, reads=[self.kb, self.cbuf], writes=[dgbuf])
                sg, sgbuf = SG[c % 2], sgb[c % 2]
                self.act(sg, self.banks[bg][:], AF.Sigmoid, reads=[self.bankbuf[bg], self.cbuf], writes=[sgbuf],
                         bias=self.cc("cbin", ia * 16 + 8 + c))
                self.stt(G[:, c, 30:542], self.banks[bv][:], self.cc("cbin", ia * 16 + c), sg, ALU.add, ALU.mult,
                         reads=[self.bankbuf[bv], sgbuf, self.cbuf], writes=[gb[c]])
                bc = self.take_bank()
                for k in range(31):
                    self.mm(bc, self.banks[bc][:], dg[:, k, :], G[:, c, k:k + 512], k == 0, k == 30, reads=[dgbuf, gb[c]])
                self.act(CO[:, c, :], self.banks[bc][:], AF.Identity, reads=[self.bankbuf[bc], self.cbuf], writes=[cob[c]],
                         bias=self.cc("cbdw", ia * 8 + c))
                self.cp("pool", G[:, c, 0:30], G[:, c, 512:542], [gb[c]], [gb[c]])
                ub, ubbuf = UB[c % 2], ubb[c % 2]
                sq, sqbuf = SQ[c % 2], sqb[c % 2]
                self.cp("dve", ub, CO[:, c, :], [cob[c]], [ubbuf])
                self.act(sq, CO[:, c, :], AF.Square, reads=[cob[c]], writes=[sqbuf])
                self.mm(s1, self.banks[s1][:], self.ones_bf[:], ub, c == 0, c == KC - 1, reads=[ubbuf, self.kb])
                self.mm(s2, self.banks[s2][:], self.ones_bf[:], sq, c == 0, c == KC - 1, reads=[sqbuf, self.kb])
            hb_ = head(0)
            for c in range(KC):
                cur = hb_
                if c + 1 < KC:
                    hb_ = head(c + 1)
                tail(c, *cur)
            self.ts("dve", mean, self.banks[s1][:], 1.0 / D, None, ALU.mult, None, reads=[self.bankbuf[s1]], writes=[stb])
            self.tt("dve", msq, mean, mean, ALU.mult, reads=[stb], writes=[stb])
            self.stt(msq, self.banks[s2][:], 1.0 / D, msq, ALU.mult, ALU.subtract, reads=[self.bankbuf[s2], stb], writes=[stb])
            self.act(lnv, msq, AF.Ln, reads=[stb, self.kb], writes=[stb], bias=self.eps_col)
            self.act(rstd, lnv, AF.Exp, reads=[stb], writes=[stb], scale=-0.5)
            self.held.discard(s1)
            self.held.discard(s2)
            for c in range(KC):
                tm, tmbuf = TMP[c % 2], tmb[c % 2]
                self.tt("dve", tm, CO[:, c, :], mean, ALU.subtract, reads=[cob[c], stb], writes=[tmbuf])
                self.tt("dve", tm, tm, rstd, ALU.mult, reads=[tmbuf, stb], writes=[tmbuf])
                self.act(sT[:, c, :], tm, AF.Silu, reads=[tmbuf, self.cbuf], writes=[stbuf[c]],
                         bias=self.cc("clnb", ia * 8 + c), scale=self.cc("clng", ia * 8 + c))
            wos = [self.wload(self.d_cwout[ia, j], 2048) for j in range(4)]
            for dc in range(KC):
                wo = self.ring[wos[dc // 2]][:].rearrange("p (k c) -> p k c", k=KC)
                bk = self.take_bank()
                for kc in range(KC):
                    self.mm(bk, self.banks[bk][:], wo[:, kc, (dc % 2) * 128:(dc % 2 + 1) * 128], sT[:, kc, :],
                            kc == 0, kc == KC - 1, reads=[self.slotbuf[wos[dc // 2]], stbuf[kc]])
                self.tt("dve", self.xT[:, dc, sl], self.banks[bk][:], self.xT[:, dc, sl], ALU.add,
                        reads=[self.bankbuf[bk], self.xbuf[dc][t]], writes=[self.xbuf[dc][t]])

    def gdn(self, layer):
        self.common_init()
        P = self.P
        hT = self.carve(0, 8192).bitcast(BF16).rearrange("p (c t) -> p c t", c=KC)
        for t in range(NT):
            self.rmsnorm_tile(t, "mixg", layer, hT[:, :, t * TT:(t + 1) * TT], self.hbuf[t], 8192)
        o = [9728]

        def al(n):
            a = self.carve(o[0], n)
            o[0] += n
            return a

        def bf4(n=256):
            return al(n).bitcast(BF16).rearrange("p (b c) -> p b c", b=4)

        gtok, gcs, eg, negeg, egl, gl, beta, negbeta = [al(32) for _ in range(8)]
        abt = al(64)
        gb_ = Buf("gates")
        qnT = [al(256).bitcast(BF16) for _ in range(4)]
        knT = [al(256).bitcast(BF16) for _ in range(4)]
        vtok = [bf4() for _ in range(4)]
        kdtok = [bf4() for _ in range(4)]
        zsT = [al(256).bitcast(BF16) for _ in range(4)]
        TTm = [bf4() for _ in range(4)]
        QKD = [bf4() for _ in range(4)]
        Pm = [bf4() for _ in range(4)]
        Qm = [bf4() for _ in range(4)]
        nm = lambda n: [Buf(n + str(i)) for i in range(4)]
        qnb, knb, vtb, kdb, zsb, ttb, qkb, pmb, qmb = [nm(n) for n in ("qn", "kn", "vt", "kd", "zs", "tt", "qk", "pm", "qm")]
        U = [al(516) for _ in range(2)]
        ub = [Buf("U0"), Buf("U1")]
        ACC = [al(512) for _ in range(2)]
        ab_ = [Buf("A0"), Buf("A1")]
        sqt = al(256).bitcast(BF16)
        lnv = al(512)
        rstd = al(512)
        sqb, lnb, rsb = Buf("sq"), Buf("ln"), Buf("rs")
        decT = al(512)
        decb = Buf("dec")
        tmp = al(512)
        tmpb = Buf("tmp")
        lhsg = [al(128) for _ in range(2)]
        lgb = [Buf("lg0"), Buf("lg1")]
        vbf = al(256).bitcast(BF16)
        vbb = Buf("vbf")
        vnew = bf4()
        vnb = Buf("vnew")
        onb_ = bf4()
        onbuf = Buf("on")
        Sf = al(512).rearrange("p (h c) -> p h c", h=4)
        Sb = bf4()
        sfb, sbb = Buf("Sf"), Buf("Sb")
        haloS = al(48).rearrange("p (c k) -> p c k", c=12)
        hsb = [Buf("hs%d" % i) for i in range(12)]
        ssq = al(8)
        ssb = Buf("ssq")
        wab = al(64).bitcast(BF16).rearrange("p (k c) -> p k c", k=KC)
        negA = al(8)
        assert o[0] <= self.ARENA, o[0]
        wabb = Buf("wab")
        P.op("pool", lambda e: e.dma_start(out=wab.rearrange("p k c -> p (k c)"), in_=self.d_gwab), writes=[wabb], dma_out=wabb)
        nab = Buf("negA")
        self.act(negA, self.cst[:, CL["galog"]:CL["galog"] + 8], AF.Exp, reads=[self.cbuf], writes=[nab])
        self.ts("dve", negA, negA, -1.0, None, ALU.mult, None, reads=[nab], writes=[nab])
        dtb = self.cst[:, CL["gdtb"]:CL["gdtb"] + 8]
        v4 = lambda a: a.rearrange("p (b h) -> p b h", b=4)
        bfbank = lambda bk: self.banks[bk][:].bitcast(BF16)[:, 0:512].rearrange("p (b c) -> p b c", b=4)
        fbank = lambda bk: self.banks[bk][:].rearrange("p (b c) -> p b c", b=4)
        qscale = 128.0 ** -0.5
        for g in range(2):
            self.memset("pool", Sf[:], 0.0, [sfb])
            self.memset("pool", Sb[:], 0.0, [sbb])
            for Q in range(NT):
                sl = slice(Q * TT, (Q + 1) * TT)
                hTt = hT[:, :, sl]
                hb = self.hbuf[Q]
                bk = self.take_bank()
                for blk in range(4):
                    for kc in range(KC):
                        self.mm(bk, self.banks[bk][:, blk * 16:(blk + 1) * 16], hTt[:, kc, blk * 128:(blk + 1) * 128], wab[:, kc, :],
                                kc == 0, kc == KC - 1, reads=[wabb, hb])
                self.cp("dve", abt, self.banks[bk][:, 0:64], [self.bankbuf[bk]], [gb_])
                ab3 = abt.rearrange("p (b c) -> p b c", b=4)
                self.act(v4(beta), ab3[:, :, 8:16], AF.Exp, reads=[gb_], writes=[gb_], scale=-1.0)
                self.act(beta, beta, AF.Ln, reads=[gb_, self.kb], writes=[gb_], bias=self.one_col)
                self.act(beta, beta, AF.Exp, reads=[gb_], writes=[gb_], scale=-1.0)
                self.ts("dve", negbeta, beta, -1.0, None, ALU.mult, None, reads=[gb_], writes=[gb_])
                self.tt("dve", v4(gtok), ab3[:, :, 0:8], dtb.unsqueeze(1).broadcast_to([128, 4, 8]), ALU.add, reads=[gb_, self.cbuf], writes=[gb_])
                self.act(gtok, gtok, AF.Exp, reads=[gb_], writes=[gb_])
                self.act(gtok, gtok, AF.Ln, reads=[gb_, self.kb], writes=[gb_], bias=self.one_col)
                self.tt("dve", v4(gtok), v4(gtok), negA.unsqueeze(1).broadcast_to([128, 4, 8]), ALU.mult, reads=[gb_, nab], writes=[gb_])
                bk = self.take_bank()
                self.mm(bk, self.banks[bk][:, 0:32], self.uincl_f[:], gtok, True, True, reads=[gb_, self.kb])
                b2 = self.take_bank()
                self.mm(b2, self.banks[b2][:, 0:32], self.ones_f[:], gtok, True, True, reads=[gb_, self.kb])
                self.cp("dve", gcs, self.banks[bk][:, 0:32], [self.bankbuf[bk]], [gb_])
                self.act(eg, gcs, AF.Exp, reads=[gb_], writes=[gb_])
                self.ts("dve", negeg, eg, -1.0, None, ALU.mult, None, reads=[gb_], writes=[gb_])
                self.tt("dve", egl, self.banks[b2][:, 0:32], gcs, ALU.subtract, reads=[self.bankbuf[b2], gb_], writes=[gb_])
                self.act(egl, egl, AF.Exp, reads=[gb_], writes=[gb_])
                self.act(gl, self.banks[b2][:, 0:32], AF.Exp, reads=[self.bankbuf[b2]], writes=[gb_])
                uic = [0]

                def proj(hh, which):
                    h = 4 * g + hh
                    sw = self.wload(self.d_gwin[which * 4 + h // 2], 2048)
                    w_ = self.ring[sw][:].rearrange("p (k c) -> p k c", k=KC)
                    bk = self.take_bank()
                    self.held.add(bk)
                    for kc in range(KC):
                        self.mm(bk, self.banks[bk][:], w_[:, kc, (h % 2) * 128:(h % 2 + 1) * 128], hTt[:, kc, :],
                                kc == 0, kc == KC - 1, reads=[self.slotbuf[sw], hb])
                    return bk

                def chain(hh, which, bk):
                    h = 4 * g + hh
                    if which == 3:
                        self.act(zsT[hh], self.banks[bk][:], AF.Silu, reads=[self.bankbuf[bk]], writes=[zsb[hh]])
                        self.held.discard(bk)
                        return
                    ci = which * 4 + hh
                    chunk = which * 8 + h
                    ui = uic[0]
                    uic[0] += 1
                    Ut, Ub = U[ui % 2], ub[ui % 2]
                    At, Ab = ACC[ui % 2], ab_[ui % 2]
                    if Q == 0:
                        self.memset("pool", Ut[:, 0:3], 0.0, [Ub])
                    else:
                        self.cp("pool", Ut[:, 0:3], haloS[:, ci, 0:3], [hsb[ci]], [Ub])
                    self.act(Ut[:, 3:515], self.banks[bk][:], AF.Copy, reads=[self.bankbuf[bk]], writes=[Ub])
                    self.held.discard(bk)
                    self.cp("pool", haloS[:, ci, 0:3], Ut[:, 512:515], [Ub], [hsb[ci]])
                    wc = lambda k, chunk=chunk: self.cc("gconv", k * 24 + chunk)
                    self.act(At, Ut[:, 0:512], AF.Copy, reads=[Ub, self.cbuf], writes=[Ab], scale=wc(0))
                    for k in range(1, 4):
                        self.stt(At, Ut[:, k:k + 512], wc(k), At, ALU.mult, ALU.add, reads=[Ub, self.cbuf, Ab], writes=[Ab])
                    self.act(At, At, AF.Silu, reads=[Ab], writes=[Ab])
                    if which < 2:
                        pend_norm.append((hh, which, At, Ab))
                        return
                    if which < 2:
                        self.act(sqt, At, AF.Square, reads=[Ab], writes=[sqb])
                        b2 = self.take_bank()
                        self.mm(b2, self.banks[b2][:], self.ones_bf[:], sqt, True, True, reads=[sqb, self.kb])
                        self.act(lnv, self.banks[b2][:], AF.Ln, reads=[self.bankbuf[b2], self.kb], writes=[lnb], bias=self.eps_col)
                        self.act(rstd, lnv, AF.Exp, reads=[lnb], writes=[rsb], scale=-0.5)
                        if which == 0:
                            self.stt(qnT[hh], At, qscale, rstd, ALU.mult, ALU.mult, reads=[Ab, rsb], writes=[qnb[hh]])
                        else:
                            self.tt("dve", knT[hh], At, rstd, ALU.mult, reads=[Ab, rsb], writes=[knb[hh]])
                    else:
                        self.cp("dve", vbf, At, [Ab], [vbb])
                        bt = self.take_bank()
                        for blk in range(4):
                            self.tr(bt, bfbank(bt)[:, blk, :], vbf[:, blk * 128:(blk + 1) * 128], self.ident_bf[:], reads=[vbb, self.kb])
                        self.cp("act", vtok[hh], bfbank(bt), [self.bankbuf[bt]], [vtb[hh]])

                pend_norm = []

                def norms():
                    while pend_norm:
                        hh_, which_, At, Ab = pend_norm.pop(0)
                        self.act(sqt, At, AF.Square, reads=[Ab], writes=[sqb])
                        b2 = self.take_bank()
                        self.mm(b2, self.banks[b2][:], self.ones_bf[:], sqt, True, True, reads=[sqb, self.kb])
                        self.act(lnv, self.banks[b2][:], AF.Ln, reads=[self.bankbuf[b2], self.kb], writes=[lnb], bias=self.eps_col)
                        self.act(rstd, lnv, AF.Exp, reads=[lnb], writes=[rsb], scale=-0.5)
                        if which_ == 0:
                            self.stt(qnT[hh_], At, qscale, rstd, ALU.mult, ALU.mult, reads=[Ab, rsb], writes=[qnb[hh_]])
                        else:
                            self.tt("dve", knT[hh_], At, rstd, ALU.mult, reads=[Ab, rsb], writes=[knb[hh_]])

                units = [(hh, which) for hh in range(4) for which in (2, 0, 1, 3)]
                nextbank = proj(*units[0])
                for ui_, (hh, which) in enumerate(units):
                    h = 4 * g + hh
                    curbank = nextbank
                    if ui_ + 1 < len(units):
                        nextbank = proj(*units[ui_ + 1])
                    chain(hh, which, curbank)
                    if which != 3:
                        continue
                    norms()
                    bt = self.take_bank()
                    for blk in range(4):
                        self.tr(bt, bfbank(bt)[:, blk, :], knT[hh][:, blk * 128:(blk + 1) * 128], self.ident_bf[:], reads=[knb[hh], self.kb])
                    for blk in range(4):
                        self.act(kdtok[hh][:, blk, :], bfbank(bt)[:, blk, :], AF.Copy, reads=[self.bankbuf[bt], gb_], writes=[kdb[hh]],
                                 scale=egl[:, blk * 8 + h:blk * 8 + h + 1])
                    bd = self.take_bank()
                    for blk in range(4):
                        lg, lgbuf = lhsg[blk % 2], lgb[blk % 2]
                        self.act(lg, self.lstrict_f[:], AF.Copy, reads=[self.kb, gb_], writes=[lgbuf],
                                 scale=gtok[:, blk * 8 + h:blk * 8 + h + 1])
                        self.mm(bd, self.banks[bd][:, blk * 128:(blk + 1) * 128], lg, self.uincl_f[:], True, True, reads=[lgbuf, self.kb])
                    self.act(decT, self.banks[bd][:], AF.Exp, reads=[self.bankbuf[bd]], writes=[decb])
                    d3 = decT.rearrange("p (b c) -> p b c", b=4)
                    t3 = tmp.rearrange("p (b c) -> p b c", b=4)
                    bkk = self.take_bank()
                    for blk in range(4):
                        ks = knT[hh][:, blk * 128:(blk + 1) * 128]
                        self.mm(bkk, self.banks[bkk][:, blk * 128:(blk + 1) * 128], ks, ks, True, True, reads=[knb[hh]])
                    self.tt("dve", tmp, self.banks[bkk][:], decT, ALU.mult, reads=[self.bankbuf[bkk], decb], writes=[tmpb])
                    for blk in range(4):
                        self.stt(Pm[hh][:, blk, :], t3[:, blk, :], negbeta[:, blk * 8 + h:blk * 8 + h + 1], self.msu_f[:], ALU.mult, ALU.mult,
                                 reads=[tmpb, gb_, self.kb], writes=[pmb[hh]])
                    bt = self.take_bank()
                    for blk in range(4):
                        self.tr(bt, bfbank(bt)[:, blk, :], Pm[hh][:, blk, :], self.ident_bf[:], reads=[pmb[hh], self.kb])
                    self.cp("act", Qm[hh], bfbank(bt), [self.bankbuf[bt]], [qmb[hh]])
                    self.tt("pool", TTm[hh], Pm[hh], self.ident_bf[:].unsqueeze(1).broadcast_to([128, 4, 128]), ALU.add,
                            reads=[pmb[hh], self.kb], writes=[ttb[hh]])
                    bq = self.take_bank()
                    for blk in range(4):
                        self.mm(bq, self.banks[bq][:, blk * 128:(blk + 1) * 128], knT[hh][:, blk * 128:(blk + 1) * 128],
                                qnT[hh][:, blk * 128:(blk + 1) * 128], True, True, reads=[knb[hh], qnb[hh]])
                    self.tt("dve", tmp, self.banks[bq][:], decT, ALU.mult, reads=[self.bankbuf[bq], decb], writes=[tmpb])
                    self.tt("dve", QKD[hh], t3, self.uincl_f[:].unsqueeze(1).broadcast_to([128, 4, 128]), ALU.mult,
                            reads=[tmpb, self.kb], writes=[qkb[hh]])
                for lev in range(1, 7):
                    for hh in range(4):
                        bq = self.take_bank()
                        for blk in range(4):
                            self.mm(bq, self.banks[bq][:, blk * 128:(blk + 1) * 128], Pm[hh][:, blk, :], Qm[hh][:, blk, :], True, True,
                                    reads=[pmb[hh], qmb[hh]])
                        if lev < 6:
                            bp = self.take_bank()
                            for blk in range(4):
                                self.mm(bp, self.banks[bp][:, blk * 128:(blk + 1) * 128], Qm[hh][:, blk, :], Pm[hh][:, blk, :], True, True,
                                        reads=[pmb[hh], qmb[hh]])
                            self.cp("act", Pm[hh], fbank(bp), [self.bankbuf[bp]], [pmb[hh]])
                        self.cp("dve", Qm[hh], fbank(bq), [self.bankbuf[bq]], [qmb[hh]])
                        br = self.take_bank()
                        for blk in range(4):
                            self.mm(br, self.banks[br][:, blk * 128:(blk + 1) * 128], Qm[hh][:, blk, :], TTm[hh][:, blk, :], True, True,
                                    reads=[qmb[hh], ttb[hh]])
                        self.tt("dve", TTm[hh], fbank(br), TTm[hh], ALU.add, reads=[self.bankbuf[br], ttb[hh]], writes=[ttb[hh]])
                rbuf = vbf.rearrange("p (b c) -> p b c", b=4)
                otok = decT.rearrange("p (b c) -> p b c", b=4)
                o2s = tmp.rearrange("p (b c) -> p b c", b=4)
                for blk in range(4):
                    bs = slice(blk * 128, (blk + 1) * 128)
                    col = lambda a, hh: a[:, blk * 8 + 4 * g + hh:blk * 8 + 4 * g + hh + 1]
                    bks = self.take_bank()
                    for hh in range(4):
                        self.mm(bks, self.banks[bks][:, hh * 128:(hh + 1) * 128], knT[hh][:, bs], Sb[:, hh, :], True, True,
                                reads=[knb[hh], sbb])
                    for hh in range(4):
                        self.stt(rbuf[:, hh, :], self.banks[bks][:, hh * 128:(hh + 1) * 128], col(negeg, hh), vtok[hh][:, blk, :],
                                 ALU.mult, ALU.add, reads=[self.bankbuf[bks], gb_, vtb[hh]], writes=[vbb])
                    bvn = self.take_bank()
                    for hh in range(4):
                        self.mm(bvn, self.banks[bvn][:, hh * 128:(hh + 1) * 128], TTm[hh][:, blk, :], rbuf[:, hh, :], True, True,
                                reads=[ttb[hh], vbb])
                    for hh in range(4):
                        self.act(vnew[:, hh, :], self.banks[bvn][:, hh * 128:(hh + 1) * 128], AF.Copy, reads=[self.bankbuf[bvn], gb_],
                                 writes=[vnb], scale=col(beta, hh))
                    bo1 = self.take_bank()
                    for hh in range(4):
                        self.mm(bo1, self.banks[bo1][:, hh * 128:(hh + 1) * 128], qnT[hh][:, bs], Sb[:, hh, :], True, True,
                                reads=[qnb[hh], sbb])
                    bo2 = self.take_bank()
                    for hh in range(4):
                        self.mm(bo2, self.banks[bo2][:, hh * 128:(hh + 1) * 128], QKD[hh][:, blk, :], vnew[:, hh, :], True, True,
                                reads=[qkb[hh], vnb])
                    self.cp("act", tmp, self.banks[bo2][:], [self.bankbuf[bo2]], [tmpb])
                    for hh in range(4):
                        self.stt(otok[:, hh, :], self.banks[bo1][:, hh * 128:(hh + 1) * 128], col(eg, hh), o2s[:, hh, :],
                                 ALU.mult, ALU.add, reads=[self.bankbuf[bo1], gb_, tmpb], writes=[decb])
                    bsu = self.take_bank()
                    for hh in range(4):
                        self.mm(bsu, self.banks[bsu][:, hh * 128:(hh + 1) * 128], kdtok[hh][:, blk, :], vnew[:, hh, :], True, True,
                                reads=[kdb[hh], vnb])
                    for hh in range(4):
                        self.stt(Sf[:, hh, :], Sf[:, hh, :], col(gl, hh), self.banks[bsu][:, hh * 128:(hh + 1) * 128],
                                 ALU.mult, ALU.add, reads=[self.bankbuf[bsu], gb_, sfb], writes=[sfb])
                    self.cp("act", Sb, Sf, [sfb], [sbb])
                    self.tt("pool", tmp, decT, decT, ALU.mult, reads=[decb], writes=[tmpb])
                    P.op("dve", lambda e: e.tensor_reduce(out=ssq[:, 0:4], in_=o2s, axis=AX.X, op=ALU.add), reads=[tmpb], writes=[ssb])
                    self.act(ssq[:, 0:4], ssq[:, 0:4], AF.Ln, reads=[ssb, self.kb], writes=[ssb], bias=self.eps_col, scale=1.0 / 128)
                    self.act(ssq[:, 0:4], ssq[:, 0:4], AF.Exp, reads=[ssb], writes=[ssb], scale=-0.5)
                    for hh in range(4):
                        self.act(onb_[:, hh, :], otok[:, hh, :], AF.Copy, reads=[decb, ssb], writes=[onbuf], scale=ssq[:, hh:hh + 1])
                    bt = self.take_bank()
                    for hh in range(4):
                        self.tr(bt, bfbank(bt)[:, hh, :], onb_[:, hh, :], self.ident_bf[:], reads=[onbuf, self.kb])
                    for hh in range(4):
                        self.stt(zsT[hh][:, bs], bfbank(bt)[:, hh, :], self.cc("gog"), zsT[hh][:, bs], ALU.mult, ALU.mult,
                                 reads=[self.bankbuf[bt], self.cbuf, zsb[hh]], writes=[zsb[hh]])
                wos = [self.wload(self.d_gwout[j], 2048) for j in range(4)]
                for dc in range(KC):
                    wo = self.ring[wos[dc // 2]][:].rearrange("p (k c) -> p k c", k=KC)
                    bk = self.take_bank()
                    for hh in range(4):
                        self.mm(bk, self.banks[bk][:], wo[:, 4 * g + hh, (dc % 2) * 128:(dc % 2 + 1) * 128], zsT[hh],
                                hh == 0, hh == 3, reads=[self.slotbuf[wos[dc // 2]], zsb[hh]])
                    self.tt("dve", self.xT[:, dc, sl], self.banks[bk][:], self.xT[:, dc, sl], ALU.add,
                            reads=[self.bankbuf[bk], self.xbuf[dc][Q]], writes=[self.xbuf[dc][Q]])

    def fox(self, layer):
        self.common_init()
        P = self.P
        HD = 128
        hT = self.carve(0, 8192).bitcast(BF16).rearrange("p (c t) -> p c t", c=KC)
        for t in range(NT):
            self.rmsnorm_tile(t, "mixg", layer, hT[:, :, t * TT:(t + 1) * TT], self.hbuf[t], 8192)
        o = 9728
        knT = self.carve(o, 4096).bitcast(BF16).rearrange("p (h t) -> p h t", h=4); o += 4096
        knb = [[Buf("kn") for _ in range(NT)] for _ in range(4)]
        vtok = self.carve(o, 4096).bitcast(BF16).rearrange("p (b c) -> p b c", b=16); o += 4096
        vb = [Buf("v%d" % b) for b in range(16)]
        qT = self.carve(o, 1024).bitcast(BF16).rearrange("p (h t) -> p h t", h=4); o += 1024
        qb = [Buf("q%d" % h) for h in range(4)]
        oT = self.carve(o, 1024).bitcast(BF16).rearrange("p (h t) -> p h t", h=4); o += 1024
        ob = [Buf("o%d" % h) for h in range(4)]
        NP = 3
        pT = [self.carve(o + i * 256, 256).bitcast(BF16) for i in range(NP)]; o += NP * 256
        pb = [Buf("p%d" % i) for i in range(NP)]
        raw = self.carve(o, 512); o += 512
        sqt = self.carve(o, 256).bitcast(BF16); o += 256
        lnv = self.carve(o, 512); o += 512
        rstd = self.carve(o, 512); o += 512
        rawb, sqb, lnb, rsb = Buf("raw"), Buf("sq"), Buf("ln"), Buf("rs")
        rden = self.carve(o, 512); o += 512
        rdb = Buf("rden")
        erow = self.carve(o, 512); o += 512
        cT = self.carve(o, 128).rearrange("p (b h) -> p b h", b=16); o += 128
        cmidb = self.carve(o, 32).rearrange("p (q h) -> p q h", q=4); o += 32
        biasA = self.carve(o, 128).rearrange("p (b h) -> p b h", b=16); o += 128
        small = self.carve(o, 32); o += 32
        wf = self.carve(o, 32).bitcast(BF16).rearrange("p (k c) -> p k c", k=KC); o += 32
        assert o <= self.ARENA, o
        carry = small[:, 0:8]
        xs = erow[:, 0:32]
        tots = erow[:, 32:64]
        cb = Buf("cstuff")
        wfb = Buf("wf")
        P.op("pool", lambda e: e.dma_start(out=wf.rearrange("p k c -> p (k c)"), in_=self.d_fwf), writes=[wfb], dma_out=wfb)
        self.memset("pool", carry, 0.0, [cb])
        scale = float(HD) ** -0.5
        for g in range(2):
            for Q in range(NT):
                sl = slice(Q * TT, (Q + 1) * TT)
                hTt = hT[:, :, sl]
                hb = self.hbuf[Q]
                if g == 0:
                    bk = self.take_bank()
                    for blk in range(4):
                        for kc in range(KC):
                            self.mm(bk, self.banks[bk][:, blk * 8:(blk + 1) * 8], hTt[:, kc, blk * 128:(blk + 1) * 128], wf[:, kc, :],
                                    kc == 0, kc == KC - 1, reads=[wfb, hb])
                    x3 = xs.rearrange("p (b h) -> p b h", b=4)
                    self.tt("dve", x3, self.banks[bk][:, 0:32].rearrange("p (b h) -> p b h", b=4),
                            self.cst[:, CL["fbfb"]:CL["fbfb"] + 8].unsqueeze(1).broadcast_to([128, 4, 8]), ALU.add,
                            reads=[self.bankbuf[bk], self.cbuf], writes=[cb])
                    self.act(xs, xs, AF.Exp, reads=[cb], writes=[cb], scale=-1.0)
                    self.act(xs, xs, AF.Ln, reads=[cb, self.kb], writes=[cb], bias=self.one_col)
                    b1 = self.take_bank()
                    self.mm(b1, self.banks[b1][:, 0:32], self.uincl_f[:], xs, True, True, reads=[cb, self.kb])
                    b2 = self.take_bank()
                    self.mm(b2, self.banks[b2][:, 0:32], self.ones_f[:], xs, True, True, reads=[cb, self.kb])
                    self.cp("dve", tots, self.banks[b2][:, 0:32], [self.bankbuf[b2]], [cb])
                    for blk in range(4):
                        self.tt("dve", cT[:, 4 * Q + blk, :], self.banks[b1][:, blk * 8:(blk + 1) * 8], carry, ALU.add,
                                reads=[self.bankbuf[b1], cb], writes=[cb])
                        self.tt("dve", carry, carry, tots[:, blk * 8:(blk + 1) * 8], ALU.add, reads=[cb], writes=[cb])
                        if blk == 1:
                            self.cp("dve", cmidb[:, Q, :], carry, [cb], [cb])
                nj = 4 * Q + 4
                DBG = int(os.environ.get("FOXDBG", "9"))
                if DBG <= 1:
                    continue
                self.tt("dve", biasA[:, 0:nj, :], cT[:, 0:nj, :], cmidb[:, Q:Q + 1, :].broadcast_to([128, nj, 8]), ALU.subtract,
                        reads=[cb], writes=[cb])
                for which in range(2):
                    for hp in range(2):
                        sw = self.wload(self.d_fwin[which * 4 + 2 * g + hp], 2048)
                        w_ = self.ring[sw][:].rearrange("p (k c) -> p k c", k=KC)
                        for h2 in range(2):
                            hh = 2 * hp + h2
                            bk = self.take_bank()
                            for kc in range(KC):
                                self.mm(bk, self.banks[bk][:], w_[:, kc, h2 * 128:(h2 + 1) * 128], hTt[:, kc, :],
                                        kc == 0, kc == KC - 1, reads=[self.slotbuf[sw], hb])
                            self.cp("dve", raw, self.banks[bk][:], [self.bankbuf[bk]], [rawb])
                            self.act(sqt, raw, AF.Square, reads=[rawb], writes=[sqb])
                            b2 = self.take_bank()
                            self.mm(b2, self.banks[b2][:], self.ones_bf[:], sqt, True, True, reads=[sqb, self.kb])
                            self.act(lnv, self.banks[b2][:], AF.Ln, reads=[self.bankbuf[b2], self.kb], writes=[lnb],
                                     bias=self.eps_col, scale=1.0 / HD)
                            self.act(rstd, lnv, AF.Exp, reads=[lnb], writes=[rsb], scale=-0.5)
                            if which == 0:
                                self.stt(qT[:, hh, :], raw, self.cc("fqg"), rstd, ALU.mult, ALU.mult,
                                         reads=[rawb, rsb, self.cbuf], writes=[qb[hh]])
                            else:
                                self.stt(knT[:, hh, sl], raw, self.cc("fkg"), rstd, ALU.mult, ALU.mult,
                                         reads=[rawb, rsb, self.cbuf], writes=[knb[hh][Q]])
                for hp in range(2):
                    sw = self.wload(self.d_fwin[8 + 2 * g + hp], 2048)
                    w_ = self.ring[sw][:].rearrange("p (k c) -> p k c", k=KC)
                    for blk in range(4):
                        bk = self.take_bank()
                        for kc in range(KC):
                            self.mm(bk, self.banks[bk][:, 0:256], hTt[:, kc, blk * 128:(blk + 1) * 128], w_[:, kc, :],
                                    kc == 0, kc == KC - 1, reads=[self.slotbuf[sw], hb])
                        self.act(vtok[:, 4 * Q + blk, hp * 256:(hp + 1) * 256], self.banks[bk][:, 0:256], AF.Copy,
                                 reads=[self.bankbuf[bk]], writes=[vb[4 * Q + blk]])
                pi = 0
                if DBG <= 2:
                    continue
                for hh in range(4):
                    h = 4 * g + hh
                    bo = self.take_bank()
                    self.held.add(bo)
                    bd = self.take_bank()
                    self.held.add(bd)
                    def s_stage(j):
                        off = max(0, (j - 4 * Q) * 128)
                        bs = self.take_bank()
                        self.mm(bs, self.banks[bs][:, off:512], knT[:, hh, j * 128:(j + 1) * 128], qT[:, hh, off:512],
                                True, True, reads=[knb[hh][j // 4], qb[hh]])
                        return bs

                    def pv_stage(j, bs, pi_):
                        off = max(0, (j - 4 * Q) * 128)
                        p_, pbuf = pT[pi_ % NP], pb[pi_ % NP]
                        self.act(p_[:, off:512], self.banks[bs][:, off:512], AF.Exp, reads=[self.bankbuf[bs], cb], writes=[pbuf],
                                 bias=biasA[:, j, h:h + 1], scale=scale)
                        if j >= 4 * Q:
                            self.tt("pool", p_[:, off:off + 128], p_[:, off:off + 128], self.uincl_bf[:], ALU.mult,
                                    reads=[pbuf, self.kb], writes=[pbuf])
                        return p_, pbuf, off

                    def acc_stage(j, p_, pbuf, off):
                        self.mm(bo, self.banks[bo][:, off:512], vtok[:, j, hh * 128:(hh + 1) * 128], p_[:, off:512],
                                j == 0, j == nj - 1, reads=[vb[j], pbuf])
                        self.mm(bd, self.banks[bd][:, off:512], self.ones_bf[:], p_[:, off:512],
                                j == 0, j == nj - 1, reads=[pbuf, self.kb])
                    bs_next = s_stage(0)
                    for j in range(nj):
                        bs_cur = bs_next
                        pp = pv_stage(j, bs_cur, pi)
                        pi += 1
                        if j + 1 < nj:
                            bs_next = s_stage(j + 1)
                        acc_stage(j, *pp)
                    P.op("dve", lambda e, bd=bd: e.reciprocal(out=rden, in_=self.banks[bd][:]), reads=[self.bankbuf[bd]], writes=[rdb])
                    self.tt("dve", oT[:, hh, :], self.banks[bo][:], rden, ALU.mult, reads=[self.bankbuf[bo], rdb], writes=[ob[hh]])
                    self.held.discard(bo)
                    self.held.discard(bd)
                wos = [self.wload(self.d_fwout[j], 2048) for j in range(4)]
                for dc in range(KC):
                    wo = self.ring[wos[dc // 2]][:].rearrange("p (k c) -> p k c", k=KC)
                    bk = self.take_bank()
                    for hh in range(4):
                        self.mm(bk, self.banks[bk][:], wo[:, 4 * g + hh, (dc % 2) * 128:(dc % 2 + 1) * 128], oT[:, hh, :],
                                hh == 0, hh == 3, reads=[self.slotbuf[wos[dc // 2]], ob[hh]])
                    self.tt("dve", self.xT[:, dc, sl], self.banks[bk][:], self.xT[:, dc, sl], ALU.add,
                            reads=[self.bankbuf[bk], self.xbuf[dc][Q]], writes=[self.xbuf[dc][Q]])


ALL_STAGES = [("conf", 0), ("ffn", 0), ("gdn", 1), ("ffn", 1), ("fox", 2), ("ffn", 2), ("conf", 3), ("ffn", 3)]


def build_nc(stages):
    nc = bass.Bass("TRN2", target_bir_lowering=False)
    k = K(nc, stages)
    k.build()
    return nc, k.used_inputs


def prep_shared(inp):
    sh = {}
    sh["cst"] = pack_consts(inp)
    sh["cwin"] = np.stack([tile_w(inp["conv_w_in"][i]).reshape(8, 128, 2048) for i in range(2)])
    sh["cwout"] = np.stack([tile_w(inp["conv_w_out"][i]).reshape(4, 128, 2048) for i in range(2)])
    gw = np.asarray(inp["gdn_w_in"][0], np.float32)
    sh["gwin"] = tile_w(gw[:, :4096]).reshape(16, 128, 2048)
    sh["gwab"] = np.ascontiguousarray(gw[:, 4096:4112].reshape(8, 128, 16).transpose(1, 0, 2)).reshape(128, 128)
    sh["gwout"] = tile_w(inp["gdn_w_out"][0]).reshape(4, 128, 2048)
    fw = np.asarray(inp["fox_w_in"][0], np.float32)
    sh["fwin"] = tile_w(fw[:, :3072]).reshape(12, 128, 2048)
    sh["fwf"] = np.ascontiguousarray(fw[:, 3072:3080].reshape(8, 128, 8).transpose(1, 0, 2)).reshape(128, 64)
    sh["fwout"] = tile_w(inp["fox_w_out"][0]).reshape(4, 128, 2048)
    sh["wup"] = np.stack([tile_w(inp["ffn_w_up"][l]).reshape(22, 128, 2048) for l in range(4)])
    sh["wdn"] = np.stack([tile_wk(inp["ffn_w_down"][l]).reshape(11, 128, 2048) for l in range(4)])
    return sh


def run(inp, stages, ncores=8, trace=False):
    x = np.asarray(inp["x"], np.float32)
    sh = prep_shared(inp)
    nc, used = build_nc(stages)
    sh = {k_: v for k_, v in sh.items() if k_ in used}
    in_maps = []
    for b in range(ncores):
        m = dict(sh)
        m["xT"] = np.ascontiguousarray(x[b].T).reshape(KC, 128, S)
        in_maps.append(m)
    res = run_bass_kernel_spmd(nc, in_maps, core_ids=list(range(ncores)), trace=trace)
    out = np.stack([np.asarray(r["yT"], np.float32).reshape(D, S).T for r in res.results])
    return out, res


def kernel(**inputs):
    out, _ = run(inputs, ALL_STAGES, ncores=8)
    return out.astype(np.float32)
```

```python
import contextlib
import os
import numpy as np
import concourse.bass as bass
import concourse.mybir as mybir
from concourse.bass_utils import run_bass_kernel_spmd

F32 = mybir.dt.float32
BF16 = mybir.dt.bfloat16
AF = mybir.ActivationFunctionType
ALU = mybir.AluOpType
AX = mybir.AxisListType

S = 2048
D = 1024
TT = 512
NT = 4
KC = 8
FF = 2816
EPS = 1e-6
ENGS = ["pe", "act", "dve", "pool", "sp"]
BLOCK_ATTR = {"pe": "tensor", "act": "scalar", "dve": "vector", "pool": "gpsimd", "sp": "sync"}


class Buf:
    __slots__ = ("name", "last_w", "readers", "sem", "dma_cnt", "excl")

    def __init__(self, name, excl=False):
        self.name = name
        self.excl = excl
        self.last_w = None
        self.readers = []
        self.sem = None
        self.dma_cnt = 0


class Op:
    __slots__ = ("eng", "fn", "waits", "signal", "idx", "dma_buf", "clock", "sigcnt")

    def __init__(self, eng, fn, idx, dma_buf=None):
        self.eng = eng
        self.fn = fn
        self.idx = idx
        self.waits = []
        self.signal = False
        self.dma_buf = dma_buf
        self.clock = None
        self.sigcnt = None


class Prog:
    def __init__(self, nc):
        self.nc = nc
        self.ops = {e: [] for e in ENGS}
        self.obs = {e: {} for e in ENGS}
        self.dma_bufs = []
        self.pending = {e: [] for e in ENGS}

    def barrier(self):
        toks = [("eng", e, len(self.ops[e]) - 1) for e in ENGS if self.ops[e] and self.ops[e][-1].dma_buf is None]
        for e in ENGS:
            if self.ops[e] and self.ops[e][-1].dma_buf is not None:
                for o in reversed(self.ops[e]):
                    if o.dma_buf is None:
                        toks.append(("eng", e, o.idx))
                        break
        for e in ENGS:
            self.pending[e] = list(toks)

    def _need(self, op, tok):
        e = op.eng
        if tok[0] == "eng":
            _, se, si = tok
            if self.obs[e].get(se, -1) >= si:
                return
            src = self.ops[se][si]
            src.signal = True
            op.waits.append(tok)
            self.obs[e][se] = si
            if src.clock:
                for k, v in src.clock.items():
                    if self.obs[e].get(k, -1) < v:
                        self.obs[e][k] = v
        else:
            _, b, cnt = tok
            key = ("dma", id(b))
            if self.obs[e].get(key, -1) >= cnt:
                return
            op.waits.append(tok)
            self.obs[e][key] = cnt

    def op(self, eng, fn, reads=(), writes=(), dma_out=None, nobarrier=False):
        lst = self.ops[eng]
        o = Op(eng, fn, len(lst), dma_buf=dma_out)
        if any(r.excl for r in reads):
            writes = list(writes) + [r for r in reads if r.excl and r not in writes]
            reads = [r for r in reads if not r.excl]
        best = {}
        for r in reads:
            t = r.last_w
            if t is not None:
                k = t[1] if t[0] == "eng" else ("dma", id(t[1]))
                if k not in best or best[k][2] < t[2]:
                    best[k] = t
        for w in writes:
            for t in [w.last_w] + w.readers:
                if t is not None:
                    k = t[1] if t[0] == "eng" else ("dma", id(t[1]))
                    if k not in best or best[k][2] < t[2]:
                        best[k] = t
        if self.pending[eng] and not nobarrier:
            for t in self.pending[eng]:
                if t[1] == eng:
                    continue
                k = t[1]
                if k not in best or best[k][2] < t[2]:
                    best[k] = t
            self.pending[eng] = []
        for t in best.values():
            if eng == "pe" and t[0] == "eng" and t[1] == "pe":
                continue
            self._need(o, t)
        if dma_out is not None:
            if dma_out.sem is None:
                self.dma_bufs.append(dma_out)
                dma_out.sem = True
            dma_out.dma_cnt += 1
            tok = ("dma", dma_out, dma_out.dma_cnt)
        else:
            tok = ("eng", eng, o.idx)
        o.clock = dict(self.obs[eng])
        for r in reads:
            r.readers.append(tok)
        for w in writes:
            w.last_w = tok
            w.readers = []
        lst.append(o)
        return tok

    def emit(self, final_waits=()):
        nc = self.nc
        CH = 2000
        with contextlib.ExitStack() as st:
            for e in ENGS:
                c = 0
                for o in self.ops[e]:
                    if o.signal and o.dma_buf is None:
                        o.sigcnt = c
                        c += 1
                    else:
                        o.sigcnt = c - 1
            nsig = {e: sum(1 for o in self.ops[e] if o.signal and o.dma_buf is None) for e in ENGS}
            esem = {e: [st.enter_context(nc.semaphore("s_%s%d" % (e, i))) for i in range(max(1, (nsig[e] + CH - 1) // CH))]
                    for e in ENGS}
            for i, b in enumerate(self.dma_bufs):
                b.sem = st.enter_context(nc.semaphore("d%d" % i))
            block = st.enter_context(nc.Block())
            for e in ENGS:
                ops = self.ops[e]
                fw = [b for (fe, b) in final_waits if fe == e]
                if not ops and not fw:
                    continue

                def body(eng, ops=ops, e=e, fw=fw):
                    for o in ops:
                        for t in o.waits:
                            if t[0] == "eng":
                                k = self.ops[t[1]][t[2]].sigcnt
                                eng.wait_ge(esem[t[1]][k // CH], k % CH + 1)
                            else:
                                eng.wait_ge(t[1].sem, 16 * t[2])
                        ins = o.fn(eng)
                        if o.dma_buf is not None:
                            ins.then_inc(o.dma_buf.sem, 16)
                        elif o.signal:
                            ins.then_inc(esem[e][o.sigcnt // CH], 1)
                    for b in fw:
                        eng.wait_ge(b.sem, 16 * b.dma_cnt)

                getattr(block, BLOCK_ATTR[e])(body)


def _cst_layout():
    lay = {}
    off = 0
    for name, n in [("mixg", 32), ("ffng", 32), ("cbin", 32), ("cwdw", 2 * 31 * 8), ("cbdw", 16),
                    ("clng", 16), ("clnb", 16), ("gconv", 96), ("gog", 1), ("fqg", 1), ("fkg", 1),
                    ("fwdw", 4 * 3 * 44), ("fbf", 1), ("galog", 8), ("gdtb", 8), ("fbfb", 8)]:
        lay[name] = off
        off += n
    return lay, off


CL, NCST = _cst_layout()


def _cols(v):
    v = np.asarray(v, dtype=np.float32).reshape(-1, 128)
    return v.T


def pack_consts(inp):
    c = np.zeros((128, NCST), np.float32)

    def put(name, arr):
        arr = np.asarray(arr, np.float32)
        c[:, CL[name]:CL[name] + arr.shape[1]] = arr

    put("mixg", _cols(inp["mix_norm_g"].reshape(-1)))
    put("ffng", _cols(inp["ffn_norm_g"].reshape(-1)))
    put("cbin", _cols(inp["conv_b_in"].reshape(-1)))
    put("cwdw", _cols(inp["conv_w_dw"].reshape(-1)))
    put("cbdw", _cols(inp["conv_b_dw"].reshape(-1)))
    put("clng", _cols(inp["conv_ln_g"].reshape(-1)))
    put("clnb", _cols(inp["conv_ln_b"].reshape(-1)))
    put("gconv", _cols(inp["gdn_conv_w"].reshape(-1)))
    put("gog", _cols(inp["gdn_o_norm_g"].reshape(-1)))
    put("fqg", _cols(inp["fox_q_norm_g"].reshape(-1)))
    put("fkg", _cols(inp["fox_k_norm_g"].reshape(-1)))
    put("fwdw", _cols(inp["ffn_w_dw"].reshape(-1)))
    bf = np.zeros((128, 1), np.float32)
    bf[:8, 0] = np.asarray(inp["fox_b_f"], np.float32).reshape(-1)
    put("fbf", bf)
    put("galog", np.broadcast_to(np.asarray(inp["gdn_a_log"], np.float32).reshape(1, 8), (128, 8)))
    put("gdtb", np.broadcast_to(np.asarray(inp["gdn_dt_bias"], np.float32).reshape(1, 8), (128, 8)))
    put("fbfb", np.broadcast_to(np.asarray(inp["fox_b_f"], np.float32).reshape(1, 8), (128, 8)))
    return c


def tile_w(w, width=256):
    w = np.asarray(w, np.float32)
    K, N = w.shape
    return np.ascontiguousarray(w.reshape(K // 128, 128, N // width, width).transpose(2, 1, 0, 3))


def tile_wk(w, nk=2):
    w = np.asarray(w, np.float32)
    K, N = w.shape
    return np.ascontiguousarray(w.reshape(K // (128 * nk), nk, 128, N).transpose(0, 2, 1, 3))


class K:
    def __init__(self, nc, stages):
        self.nc = nc
        self.P = Prog(nc)
        self.stages = stages
        self.st = contextlib.ExitStack()
        self.bank_rr = 0
        self.slot_rr = 0
        self.uid = 0

    def sb(self, name, shape, dt):
        return self.st.enter_context(self.nc.sbuf_tensor(name, shape, dt))

    def B(self, name=""):
        self.uid += 1
        return Buf("%s%d" % (name, self.uid))

    def take_bank(self):
        while True:
            i = self.bank_rr % 8
            self.bank_rr += 1
            if i not in self.held:
                return i

    def take_slot(self):
        i = self.slot_rr % self.NSLOT
        self.slot_rr += 1
        return i

    def mm(self, bank, out, lhsT, rhs, start, stop, reads, extra_w=()):
        self.P.op("pe", lambda e: e.matmul(out, lhsT, rhs, start=start, stop=stop),
                  reads=reads, writes=[self.bankbuf[bank]] + list(extra_w))

    def tr(self, bank, out, in_, ident, reads):
        self.P.op("pe", lambda e: e.transpose(out, in_, ident), reads=reads, writes=[self.bankbuf[bank]])

    def act(self, out, in_, func, reads, writes, bias=None, scale=None, eng="act"):
        kw = {}
        if bias is not None:
            kw["bias"] = bias
        if scale is not None:
            kw["scale"] = scale
        self.P.op("act", lambda e: e.activation(out=out, in_=in_, func=func, **kw), reads=reads, writes=writes)

    def tt(self, eng, out, in0, in1, op, reads, writes):
        self.P.op(eng, lambda e: e.tensor_tensor(out=out, in0=in0, in1=in1, op=op), reads=reads, writes=writes)

    def stt(self, out, in0, scalar, in1, op0, op1, reads, writes):
        self.P.op("dve", lambda e: e.scalar_tensor_tensor(out=out, in0=in0, scalar=scalar, in1=in1, op0=op0, op1=op1),
                  reads=reads, writes=writes)

    def ts(self, eng, out, in0, s1, s2, op0, op1, reads, writes):
        if op1 is None:
            self.P.op(eng, lambda e: e.tensor_scalar(out=out, in0=in0, scalar1=s1, scalar2=None, op0=op0),
                      reads=reads, writes=writes)
        else:
            self.P.op(eng, lambda e: e.tensor_scalar(out=out, in0=in0, scalar1=s1, scalar2=s2, op0=op0, op1=op1),
                      reads=reads, writes=writes)

    def cp(self, eng, out, in_, reads, writes):
        if eng == "act":
            self.P.op("act", lambda e: e.copy(out=out, in_=in_), reads=reads, writes=writes)
        else:
            self.P.op(eng, lambda e: e.tensor_copy(out=out, in_=in_), reads=reads, writes=writes)

    def memset(self, eng, ap, val, writes):
        self.P.op(eng, lambda e: e.memset(ap, val), writes=writes)

    def wload(self, src_ap, ncols_total, view=None):
        s = self.take_slot()
        dst = self.ring[s][:, 0:ncols_total]
        self.P.op("pool", lambda e: e.dma_start(out=dst, in_=src_ap), writes=[self.slotbuf[s]], dma_out=self.slotbuf[s], nobarrier=True)
        return s

    def build(self):
        nc = self.nc
        P = self.P
        dt = nc.dram_tensor
        self.xin = dt("xT", [KC, 128, S], F32, kind="ExternalInput").ap()
        self.cst_d = dt("cst", [128, NCST], F32, kind="ExternalInput").ap()
        kinds = set(k_ for k_, _ in self.stages)
        self.used_inputs = ["xT", "cst"]

        def din(name, shape, kind_):
            if kind_ not in kinds:
                return None
            self.used_inputs.append(name)
            return dt(name, shape, F32, kind="ExternalInput").ap()
        self.d_cwin = din("cwin", [2, 8, 128, 2048], "conf")
        self.d_cwout = din("cwout", [2, 4, 128, 2048], "conf")
        self.d_gwin = din("gwin", [16, 128, 2048], "gdn")
        self.d_gwab = din("gwab", [128, 128], "gdn")
        self.d_gwout = din("gwout", [4, 128, 2048], "gdn")
        self.d_fwin = din("fwin", [12, 128, 2048], "fox")
        self.d_fwf = din("fwf", [128, 64], "fox")
        self.d_fwout = din("fwout", [4, 128, 2048], "fox")
        self.d_wup = din("wup", [4, 22, 128, 2048], "ffn")
        self.d_wdn = din("wdn", [4, 11, 128, 2048], "ffn")
        self.yout = dt("yT", [KC, 128, S], F32, kind="ExternalOutput").ap()

        self.xT = self.sb("xT_sb", [128, KC, S], F32)
        self.cst = self.sb("cst_sb", [128, NCST], F32)
        self.NSLOT = 8
        self.ring = [self.sb("ring%d" % i, [128, 2048], BF16) for i in range(self.NSLOT)]
        self.slotbuf = [Buf("slot%d" % i) for i in range(self.NSLOT)]
        self.banks = [self.st.enter_context(nc.psum_tensor("bank%d" % i, [128, 512], F32)) for i in range(8)]
        self.bankbuf = [Buf("bank%d" % i, excl=True) for i in range(8)]
        self.held = set()
        self.xbuf = [[Buf("x%d_%d" % (c, t)) for t in range(NT)] for c in range(KC)]
        self.hbuf = [Buf("h%d" % t) for t in range(NT)]
        self.hb1 = Buf("htile")
        self.cbuf = Buf("cst")
        self.ones_bf = self.sb("ones_bf", [128, 128], BF16)
        self.ones_f = self.sb("ones_f", [128, 128], F32)
        self.ident_f = self.sb("ident_f", [128, 128], F32)
        self.ident_bf = self.sb("ident_bf", [128, 128], BF16)
        self.uincl_f = self.sb("uincl_f", [128, 128], F32)
        self.uincl_bf = self.sb("uincl_bf", [128, 128], BF16)
        self.lstrict_f = self.sb("lstrict_f", [128, 128], F32)
        self.msu_f = self.sb("msu_f", [128, 128], F32)
        self.kb = Buf("consts")
        self.ARENA = 25600
        self.arena = self.sb("arena", [128, self.ARENA], F32)

        P.op("sp", lambda e: e.dma_start(out=self.cst[:], in_=self.cst_d), writes=[self.cbuf], dma_out=self.cbuf)
        kb = self.kb
        self.memset("pool", self.ones_f[:], 1.0, [kb])
        self.memset("pool", self.ones_bf[:], 1.0, [kb])

        def asel(out, base, cm, step, op):
            P.op("pool", lambda e: e.affine_select(out=out, in_=self.ones_f[:], pattern=[[step, 128]], compare_op=op,
                                                   fill=0.0, base=base, channel_multiplier=cm), reads=[kb], writes=[kb])
        asel(self.ident_f[:], 0, -1, 1, ALU.is_equal)
        asel(self.uincl_f[:], 0, -1, 1, ALU.is_ge)
        asel(self.lstrict_f[:], -1, 1, -1, ALU.is_ge)
        asel(self.msu_f[:], -1, -1, 1, ALU.is_ge)
        self.cp("pool", self.ident_bf[:], self.ident_f[:], [kb], [kb])
        self.cp("pool", self.uincl_bf[:], self.uincl_f[:], [kb], [kb])

        for t in range(NT):
            for c in range(KC):
                P.op("sp", lambda e, c=c, t=t: e.dma_start(out=self.xT[:, c, t * TT:(t + 1) * TT],
                                                           in_=self.xin[c, :, t * TT:(t + 1) * TT]),
                     writes=[self.xbuf[c][t]], dma_out=self.xbuf[c][t])

        ia = 0
        for stg in self.stages:
            kind, layer = stg
            P.barrier()
            if kind == "conf":
                self.conformer(layer, ia)
                ia += 1
            elif kind == "gdn":
                self.gdn(layer)
            elif kind == "fox":
                self.fox(layer)
            elif kind == "ffn":
                self.ffn(layer)

        ob = Buf("out")
        for c in range(KC):
            P.op("sp", lambda e, c=c: e.dma_start(out=self.yout[c], in_=self.xT[:, c, :]),
                 reads=self.xbuf[c], writes=[ob], dma_out=ob)
        P.emit(final_waits=[("sp", ob)])
        self.st.close()

    def cc(self, name, idx=0):
        o = CL[name] + idx
        return self.cst[:, o:o + 1]

    def carve(self, off, nwords):
        assert off + nwords <= self.ARENA, (off, nwords)
        return self.arena[:, off:off + nwords]

    def rmsnorm_tile(self, t, gname, layer, hview, hb, ar_off):
        sl = slice(t * TT, (t + 1) * TT)
        sqs = [self.carve(ar_off + i * 256, 256).bitcast(BF16) for i in range(2)]
        lnv = self.carve(ar_off + 512, 512)
        rstd = self.carve(ar_off + 1024, 512)
        bl, br = self.nb_ln, self.nb_rstd
        bk = self.take_bank()
        for c in range(KC):
            sq, bsq = sqs[c % 2], self.nb_sq[c % 2]
            self.act(sq, self.xT[:, c, sl], AF.Square, reads=[self.xbuf[c][t]], writes=[bsq])
            self.mm(bk, self.banks[bk][:], self.ones_bf[:], sq, c == 0, c == KC - 1, reads=[bsq, self.kb])
        self.act(lnv, self.banks[bk][:], AF.Ln, reads=[self.bankbuf[bk], self.kb], writes=[bl], bias=self.eps_col, scale=1.0 / D)
        self.act(rstd, lnv, AF.Exp, reads=[bl], writes=[br], scale=-0.5)
        for c in range(KC):
            self.stt(hview[:, c, :], self.xT[:, c, sl], self.cc(gname, layer * 8 + c), rstd, ALU.mult, ALU.mult,
                     reads=[self.xbuf[c][t], br, self.cbuf], writes=[hb])

    def common_init(self):
        if getattr(self, "_ci", False):
            return
        self._ci = True
        self.nb_sq, self.nb_ln, self.nb_rstd = [Buf("sq0"), Buf("sq1")], Buf("ln"), Buf("rstd")
        self.eps_t = self.sb("eps_t", [128, 4], F32)
        self.memset("pool", self.eps_t[:, 0:1], EPS, [self.kb])
        self.memset("pool", self.eps_t[:, 1:2], 1.0, [self.kb])
        self.memset("pool", self.eps_t[:, 2:3], 0.0, [self.kb])
        self.eps_col = self.eps_t[:, 0:1]
        self.one_col = self.eps_t[:, 1:2]
        self.zero_col = self.eps_t[:, 2:3]

    def ffn(self, layer):
        self.common_init()
        P = self.P
        hT = self.carve(0, 8192).bitcast(BF16).rearrange("p (c t) -> p c t", c=KC)
        self.hT = hT
        for t in range(NT):
            self.rmsnorm_tile(t, "ffng", layer, hT[:, :, t * TT:(t + 1) * TT], self.hbuf[t], 8192)
        GOFF = 8192 + 1536
        gT = [self.carve(GOFF + i * 4096, 4096).bitcast(BF16).rearrange("p (c t) -> p c t", c=4) for i in range(2)]
        gbuf = [[[Buf("g") for _ in range(NT)] for _ in range(4)] for _ in range(2)]
        UOFF = GOFF + 2 * 4096
        NU = 6
        U = [self.carve(UOFF + i * 516, 516) for i in range(NU)]
        ubuf = [Buf("U") for _ in range(NU)]
        AOFF = UOFF + NU * 516
        NA = 8
        ACC = [self.carve(AOFF + i * 512, 512) for i in range(NA)]
        abuf = [Buf("acc") for _ in range(NA)]
        urr = [0]
        arr = [0]
        hreads = self.hbuf

        parts = [(0, 2), (2, 4), (4, 6), (6, 8), (8, 10), (10, 11)]
        order = []
        for (u0, u1) in parts:
            for u in range(u0, u1):
                order.append(("g", u, self.d_wup[layer, u]))
                order.append(("u", u, self.d_wup[layer, 11 + u]))
            for u in range(u0, u1):
                order.append(("d", u, self.d_wdn[layer, u]))
        issued = {}
        nxt = [0]
        deferred = []
        pend_down = []

        def want(key, ahead):
            idx = [i for i, o_ in enumerate(order) if (o_[0], o_[1]) == key][0]
            while nxt[0] <= min(idx + ahead, len(order) - 1):
                o_ = order[nxt[0]]
                issued[(o_[0], o_[1])] = self.wload(o_[2], 2048)
                nxt[0] += 1
            return issued[key]

        for pi, (u0, u1) in enumerate(parts):
            g = gT[pi % 2]
            gb = gbuf[pi % 2]
            for u in range(u0, u1):
                if u == u0 + 1 or (u == u0 and u1 - u0 == 1 and False):
                    while pend_down:
                        pend_down.pop(0)()
                sg = want(("g", u), 5)
                su = want(("u", u), 4)
                wg = self.ring[sg][:].rearrange("p (k c) -> p k c", k=KC)
                wu = self.ring[su][:].rearrange("p (k c) -> p k c", k=KC)
                for c2 in range(2):
                    ci = 2 * u + c2
                    lc = ci - 2 * u0
                    prev = {"g": None, "u": None}
                    for t in range(NT):
                        sl = slice(t * TT, (t + 1) * TT)
                        accs = {}
                        for which, w_, slot_, col0 in (("g", wg, sg, ci), ("u", wu, su, 22 + ci)):
                            bk = self.take_bank()
                            for kc in range(KC):
                                self.mm(bk, self.banks[bk][:], w_[:, kc, c2 * 128:(c2 + 1) * 128], self.hT[:, kc, sl],
                                        kc == 0, kc == KC - 1, reads=[self.slotbuf[slot_], self.hbuf[t]])
                            ui = urr[0] % NU
                            urr[0] += 1
                            Ut, Ub = U[ui], ubuf[ui]
                            if t == 0:
                                self.memset("pool", Ut[:, 0:2], 0.0, [Ub])
                            else:
                                pU, pB = prev[which]
                                self.cp("pool", Ut[:, 0:2], pU[:, 512:514], [pB], [Ub])
                            self.act(Ut[:, 2:514], self.banks[bk][:], AF.Copy, reads=[self.bankbuf[bk]], writes=[Ub])
                            prev[which] = (Ut, Ub)
                            ai = arr[0] % NA
                            arr[0] += 1
                            At, Ab = ACC[ai], abuf[ai]
                            wcol = lambda k, col0=col0: self.cc("fwdw", (layer * 3 + k) * 44 + col0)
                            self.act(At, Ut[:, 0:512], AF.Copy, reads=[Ub, self.cbuf], writes=[Ab], scale=wcol(0))
                            self.stt(At, Ut[:, 1:513], wcol(1), At, ALU.mult, ALU.add, reads=[Ub, self.cbuf, Ab], writes=[Ab])
                            self.stt(At, Ut[:, 2:514], wcol(2), At, ALU.mult, ALU.add, reads=[Ub, self.cbuf, Ab], writes=[Ab])
                            accs[which] = (At, Ab)
                        Ag, Agb = accs["g"]
                        Au, Aub = accs["u"]

                        def tail(Ag=Ag, Agb=Agb, Au=Au, Aub=Aub, dst=g[:, lc, sl], db=gb[lc][t]):
                            self.act(Ag, Ag, AF.Silu, reads=[Agb], writes=[Agb])
                            self.tt("pool", dst, Ag, Au, ALU.mult, reads=[Agb, Aub], writes=[db])
                        if deferred:
                            deferred.pop(0)()
                        deferred.append(tail)
            dslots = [want(("d", u), 3) for u in range(u0, u1)]

            def down(g=g, gb=gb, nch=2 * (u1 - u0), dslots=dslots):
                while deferred:
                    deferred.pop(0)()
                for dc in range(KC):
                    for t in range(NT):
                        sl = slice(t * TT, (t + 1) * TT)
                        bk = self.take_bank()
                        for lc in range(nch):
                            s_ = dslots[lc // 2]
                            wd = self.ring[s_][:].rearrange("p (k c) -> p k c", k=2)
                            self.mm(bk, self.banks[bk][:], wd[:, lc % 2, dc * 128:(dc + 1) * 128], g[:, lc, sl],
                                    lc == 0, lc == nch - 1, reads=[self.slotbuf[s_], gb[lc][t]])
                        self.tt("dve", self.xT[:, dc, sl], self.banks[bk][:], self.xT[:, dc, sl], ALU.add,
                                reads=[self.bankbuf[bk], self.xbuf[dc][t]], writes=[self.xbuf[dc][t]])
            pend_down.append(down)
        while pend_down:
            pend_down.pop(0)()

    def conformer(self, layer, ia):
        self.common_init()
        P = self.P
        hTt = self.carve(0, 2048).bitcast(BF16).rearrange("p (c t) -> p c t", c=KC)
        G = self.carve(3584, 2176).bitcast(BF16).rearrange("p (c t) -> p c t", c=KC)
        gb = [Buf("G%d" % c) for c in range(KC)]
        DG = [self.carve(5760 + i * 1984, 1984).bitcast(BF16).rearrange("p (k m) -> p k m", k=31) for i in range(2)]
        dgb = [Buf("dg0"), Buf("dg1")]
        CO = self.carve(9728, 4096).rearrange("p (c t) -> p c t", c=KC)
        cob = [Buf("co%d" % c) for c in range(KC)]
        UB = [self.carve(13824 + i * 256, 256).bitcast(BF16) for i in range(2)]
        SQ = [self.carve(14336 + i * 256, 256).bitcast(BF16) for i in range(2)]
        ubb = [Buf("ub0"), Buf("ub1")]
        sqb = [Buf("sqb0"), Buf("sqb1")]
        mean = self.carve(14848, 512)
        msq = self.carve(15360, 512)
        lnv = self.carve(15872, 512)
        rstd = self.carve(16384, 512)
        stb = Buf("stats")
        TMP = [self.carve(16896 + i * 512, 512) for i in range(2)]
        tmb = [Buf("tmp0"), Buf("tmp1")]
        sT = self.carve(17920, 2048).bitcast(BF16).rearrange("p (c t) -> p c t", c=KC)
        stbuf = [Buf("sT%d" % c) for c in range(KC)]
        SG = [self.carve(19968 + i * 512, 512) for i in range(2)]
        sgb = [Buf("sg0"), Buf("sg1")]
        for c in range(KC):
            self.memset("pool", G[:, c, 0:32], 0.0, [gb[c]])
        for t in range(NT):
            sl = slice(t * TT, (t + 1) * TT)
            self.rmsnorm_tile(t, "mixg", layer, hTt, self.hb1, 2048)
            s1 = self.take_bank()
            self.held.add(s1)
            s2 = self.take_bank()
            self.held.add(s2)
            wst = {}

            def head(c):
                if c % 2 == 0:
                    wst["sv"] = self.wload(self.d_cwin[ia, c // 2], 2048)
                    wst["sg"] = self.wload(self.d_cwin[ia, 4 + c // 2], 2048)
                sv, sg_ = wst["sv"], wst["sg"]
                wv = self.ring[sv][:].rearrange("p (k c) -> p k c", k=KC)
                wg = self.ring[sg_][:].rearrange("p (k c) -> p k c", k=KC)
                bv = self.take_bank()
                for kc in range(KC):
                    self.mm(bv, self.banks[bv][:], wv[:, kc, (c % 2) * 128:(c % 2 + 1) * 128], hTt[:, kc, :],
                            kc == 0, kc == KC - 1, reads=[self.slotbuf[sv], self.hb1])
                bg = self.take_bank()
                for kc in range(KC):
                    self.mm(bg, self.banks[bg][:], wg[:, kc, (c % 2) * 128:(c % 2 + 1) * 128], hTt[:, kc, :],
                            kc == 0, kc == KC - 1, reads=[self.slotbuf[sg_], self.hb1])
                return bv, bg

            def tail(c, bv, bg):
                dg, dgbuf = DG[c % 2], dgb[c % 2]
                wb = CL["cwdw"] + ia * 31 * 8 + c
                wtaps = self.cst[:, wb:wb + 30 * 8 + 1:8]
                self.tt("dve", dg, self.ident_bf[:].unsqueeze(1).broadcast_to([128, 31, 128]),
                        wtaps.unsqueeze(2).broadcast_to([128, 31, 128]), ALU.mult, reads=[self.kb, self.cbuf], writes=[dgbuf])
                sg, sgbuf = SG[c % 2], sgb[c % 2]
                self.act(sg, self.banks[bg][:], AF.Sigmoid, reads=[self.bankbuf[bg], self.cbuf], writes=[sgbuf],
                         bias=self.cc("cbin", ia * 16 + 8 + c))
                self.stt(G[:, c, 30:542], self.banks[bv][:], self.cc("cbin", ia * 16 + c), sg, ALU.add, ALU.mult,
                         reads=[self.bankbuf[bv], sgbuf, self.cbuf], writes=[gb[c]])
                bc = self.take_bank()
                for k in range(31):
                    self.mm(bc, self.banks[bc][:], dg[:, k, :], G[:, c, k:k + 512], k == 0, k == 30, reads=[dgbuf, gb[c]])
                self.act(CO[:, c, :], self.banks[bc][:], AF.Identity, reads=[self.bankbuf[bc], self.cbuf], writes=[cob[c]],
                         bias=self.cc("cbdw", ia * 8 + c))
                self.cp("pool", G[:, c, 0:30], G[:, c, 512:542], [gb[c]], [gb[c]])
                ub, ubbuf = UB[c % 2], ubb[c % 2]
                sq, sqbuf = SQ[c % 2], sqb[c % 2]
                self.cp("dve", ub, CO[:, c, :], [cob[c]], [ubbuf])
                self.act(sq, CO[:, c, :], AF.Square, reads=[cob[c]], writes=[sqbuf])
                self.mm(s1, self.banks[s1][:], self.ones_bf[:], ub, c == 0, c == KC - 1, reads=[ubbuf, self.kb])
                self.mm(s2, self.banks[s2][:], self.ones_bf[:], sq, c == 0, c == KC - 1, reads=[sqbuf, self.kb])
            hb_ = head(0)
            for c in range(KC):
                cur = hb_
                if c + 1 < KC:
                    hb_ = head(c + 1)
                tail(c, *cur)
            self.ts("dve", mean, self.banks[s1][:], 1.0 / D, None, ALU.mult, None, reads=[self.bankbuf[s1]], writes=[stb])
            self.tt("dve", msq, mean, mean, ALU.mult, reads=[stb], writes=[stb])
            self.stt(msq, self.banks[s2][:], 1.0 / D, msq, ALU.mult, ALU.subtract, reads=[self.bankbuf[s2], stb], writes=[stb])
            self.act(lnv, msq, AF.Ln, reads=[stb, self.kb], writes=[stb], bias=self.eps_col)
            self.act(rstd, lnv, AF.Exp, reads=[stb], writes=[stb], scale=-0.5)
            self.held.discard(s1)
            self.held.discard(s2)
            for c in range(KC):
                tm, tmbuf = TMP[c % 2], tmb[c % 2]
                self.tt("dve", tm, CO[:, c, :], mean, ALU.subtract, reads=[cob[c], stb], writes=[tmbuf])
                self.tt("dve", tm, tm, rstd, ALU.mult, reads=[tmbuf, stb], writes=[tmbuf])
                self.act(sT[:, c, :], tm, AF.Silu, reads=[tmbuf, self.cbuf], writes=[stbuf[c]],
                         bias=self.cc("clnb", ia * 8 + c), scale=self.cc("clng", ia * 8 + c))
            wos = [self.wload(self.d_cwout[ia, j], 2048) for j in range(4)]
            for dc in range(KC):
                wo = self.ring[wos[dc // 2]][:].rearrange("p (k c) -> p k c", k=KC)
                bk = self.take_bank()
                for kc in range(KC):
                    self.mm(bk, self.banks[bk][:], wo[:, kc, (dc % 2) * 128:(dc % 2 + 1) * 128], sT[:, kc, :],
                            kc == 0, kc == KC - 1, reads=[self.slotbuf[wos[dc // 2]], stbuf[kc]])
                self.tt("dve", self.xT[:, dc, sl], self.banks[bk][:], self.xT[:, dc, sl], ALU.add,
                        reads=[self.bankbuf[bk], self.xbuf[dc][t]], writes=[self.xbuf[dc][t]])

    def gdn(self, layer):
        self.common_init()
        P = self.P
        hT = self.carve(0, 8192).bitcast(BF16).rearrange("p (c t) -> p c t", c=KC)
        for t in range(NT):
            self.rmsnorm_tile(t, "mixg", layer, hT[:, :, t * TT:(t + 1) * TT], self.hbuf[t], 8192)
        o = [9728]

        def al(n):
            a = self.carve(o[0], n)
            o[0] += n
            return a

        def bf4(n=256):
            return al(n).bitcast(BF16).rearrange("p (b c) -> p b c", b=4)

        gtok, gcs, eg, negeg, egl, gl, beta, negbeta = [al(32) for _ in range(8)]
        abt = al(64)
        gb_ = Buf("gates")
        qnT = [al(256).bitcast(BF16) for _ in range(4)]
        knT = [al(256).bitcast(BF16) for _ in range(4)]
        vtok = [bf4() for _ in range(4)]
        kdtok = [bf4() for _ in range(4)]
        zsT = [al(256).bitcast(BF16) for _ in range(4)]
        TTm = [bf4() for _ in range(4)]
        QKD = [bf4() for _ in range(4)]
        Pm = [bf4() for _ in range(4)]
        Qm = [bf4() for _ in range(4)]
        nm = lambda n: [Buf(n + str(i)) for i in range(4)]
        qnb, knb, vtb, kdb, zsb, ttb, qkb, pmb, qmb = [nm(n) for n in ("qn", "kn", "vt", "kd", "zs", "tt", "qk", "pm", "qm")]
        U = [al(516) for _ in range(2)]
        ub = [Buf("U0"), Buf("U1")]
        ACC = [al(512) for _ in range(2)]
        ab_ = [Buf("A0"), Buf("A1")]
        sqt = al(256).bitcast(BF16)
        lnv = al(512)
        rstd = al(512)
        sqb, lnb, rsb = Buf("sq"), Buf("ln"), Buf("rs")
        decT = al(512)
        decb = Buf("dec")
        tmp = al(512)
        tmpb = Buf("tmp")
        lhsg = [al(128) for _ in range(2)]
        lgb = [Buf("lg0"), Buf("lg1")]
        vbf = al(256).bitcast(BF16)
        vbb = Buf("vbf")
        vnew = bf4()
        vnb = Buf("vnew")
        onb_ = bf4()
        onbuf = Buf("on")
        Sf = al(512).rearrange("p (h c) -> p h c", h=4)
        Sb = bf4()
        sfb, sbb = Buf("Sf"), Buf("Sb")
        haloS = al(48).rearrange("p (c k) -> p c k", c=12)
        hsb = [Buf("hs%d" % i) for i in range(12)]
        ssq = al(8)
        ssb = Buf("ssq")
        wab = al(64).bitcast(BF16).rearrange("p (k c) -> p k c", k=KC)
        negA = al(8)
        assert o[0] <= self.ARENA, o[0]
        wabb = Buf("wab")
        P.op("pool", lambda e: e.dma_start(out=wab.rearrange("p k c -> p (k c)"), in_=self.d_gwab), writes=[wabb], dma_out=wabb)
        nab = Buf("negA")
        self.act(negA, self.cst[:, CL["galog"]:CL["galog"] + 8], AF.Exp, reads=[self.cbuf], writes=[nab])
        self.ts("dve", negA, negA, -1.0, None, ALU.mult, None, reads=[nab], writes=[nab])
        dtb = self.cst[:, CL["gdtb"]:CL["gdtb"] + 8]
        v4 = lambda a: a.rearrange("p (b h) -> p b h", b=4)
        bfbank = lambda bk: self.banks[bk][:].bitcast(BF16)[:, 0:512].rearrange("p (b c) -> p b c", b=4)
        fbank = lambda bk: self.banks[bk][:].rearrange("p (b c) -> p b c", b=4)
        qscale = 128.0 ** -0.5
        for g in range(2):
            self.memset("pool", Sf[:], 0.0, [sfb])
            self.memset("pool", Sb[:], 0.0, [sbb])
            for Q in range(NT):
                sl = slice(Q * TT, (Q + 1) * TT)
                hTt = hT[:, :, sl]
                hb = self.hbuf[Q]
                bk = self.take_bank()
                for blk in range(4):
                    for kc in range(KC):
                        self.mm(bk, self.banks[bk][:, blk * 16:(blk + 1) * 16], hTt[:, kc, blk * 128:(blk + 1) * 128], wab[:, kc, :],
                                kc == 0, kc == KC - 1, reads=[wabb, hb])
                self.cp("dve", abt, self.banks[bk][:, 0:64], [self.bankbuf[bk]], [gb_])
                ab3 = abt.rearrange("p (b c) -> p b c", b=4)
                self.act(v4(beta), ab3[:, :, 8:16], AF.Exp, reads=[gb_], writes=[gb_], scale=-1.0)
                self.act(beta, beta, AF.Ln, reads=[gb_, self.kb], writes=[gb_], bias=self.one_col)
                self.act(beta, beta, AF.Exp, reads=[gb_], writes=[gb_], scale=-1.0)
                self.ts("dve", negbeta, beta, -1.0, None, ALU.mult, None, reads=[gb_], writes=[gb_])
                self.tt("dve", v4(gtok), ab3[:, :, 0:8], dtb.unsqueeze(1).broadcast_to([128, 4, 8]), ALU.add, reads=[gb_, self.cbuf], writes=[gb_])
                self.act(gtok, gtok, AF.Exp, reads=[gb_], writes=[gb_])
                self.act(gtok, gtok, AF.Ln, reads=[gb_, self.kb], writes=[gb_], bias=self.one_col)
                self.tt("dve", v4(gtok), v4(gtok), negA.unsqueeze(1).broadcast_to([128, 4, 8]), ALU.mult, reads=[gb_, nab], writes=[gb_])
                bk = self.take_bank()
                self.mm(bk, self.banks[bk][:, 0:32], self.uincl_f[:], gtok, True, True, reads=[gb_, self.kb])
                b2 = self.take_bank()
                self.mm(b2, self.banks[b2][:, 0:32], self.ones_f[:], gtok, True, True, reads=[gb_, self.kb])
                self.cp("dve", gcs, self.banks[bk][:, 0:32], [self.bankbuf[bk]], [gb_])
                self.act(eg, gcs, AF.Exp, reads=[gb_], writes=[gb_])
                self.ts("dve", negeg, eg, -1.0, None, ALU.mult, None, reads=[gb_], writes=[gb_])
                self.tt("dve", egl, self.banks[b2][:, 0:32], gcs, ALU.subtract, reads=[self.bankbuf[b2], gb_], writes=[gb_])
                self.act(egl, egl, AF.Exp, reads=[gb_], writes=[gb_])
                self.act(gl, self.banks[b2][:, 0:32], AF.Exp, reads=[self.bankbuf[b2]], writes=[gb_])
                uic = [0]

                def proj(hh, which):
                    h = 4 * g + hh
                    sw = self.wload(self.d_gwin[which * 4 + h // 2], 2048)
                    w_ = self.ring[sw][:].rearrange("p (k c) -> p k c", k=KC)
                    bk = self.take_bank()
                    self.held.add(bk)
                    for kc in range(KC):
                        self.mm(bk, self.banks[bk][:], w_[:, kc, (h % 2) * 128:(h % 2 + 1) * 128], hTt[:, kc, :],
                                kc == 0, kc == KC - 1, reads=[self.slotbuf[sw], hb])
                    return bk

                def chain(hh, which, bk):
                    h = 4 * g + hh
                    if which == 3:
                        self.act(zsT[hh], self.banks[bk][:], AF.Silu, reads=[self.bankbuf[bk]], writes=[zsb[hh]])
                        self.held.discard(bk)
                        return
                    ci = which * 4 + hh
                    chunk = which * 8 + h
                    ui = uic[0]
                    uic[0] += 1
                    Ut, Ub = U[ui % 2], ub[ui % 2]
                    At, Ab = ACC[ui % 2], ab_[ui % 2]
                    if Q == 0:
                        self.memset("pool", Ut[:, 0:3], 0.0, [Ub])
                    else:
                        self.cp("pool", Ut[:, 0:3], haloS[:, ci, 0:3], [hsb[ci]], [Ub])
                    self.act(Ut[:, 3:515], self.banks[bk][:], AF.Copy, reads=[self.bankbuf[bk]], writes=[Ub])
                    self.held.discard(bk)
                    self.cp("pool", haloS[:, ci, 0:3], Ut[:, 512:515], [Ub], [hsb[ci]])
                    wc = lambda k, chunk=chunk: self.cc("gconv", k * 24 + chunk)
                    self.act(At, Ut[:, 0:512], AF.Copy, reads=[Ub, self.cbuf], writes=[Ab], scale=wc(0))
                    for k in range(1, 4):
                        self.stt(At, Ut[:, k:k + 512], wc(k), At, ALU.mult, ALU.add, reads=[Ub, self.cbuf, Ab], writes=[Ab])
                    self.act(At, At, AF.Silu, reads=[Ab], writes=[Ab])
                    if which < 2:
                        pend_norm.append((hh, which, At, Ab))
                        return
                    if which < 2:
                        self.act(sqt, At, AF.Square, reads=[Ab], writes=[sqb])
                        b2 = self.take_bank()
                        self.mm(b2, self.banks[b2][:], self.ones_bf[:], sqt, True, True, reads=[sqb, self.kb])
                        self.act(lnv, self.banks[b2][:], AF.Ln, reads=[self.bankbuf[b2], self.kb], writes=[lnb], bias=self.eps_col)
                        self.act(rstd, lnv, AF.Exp, reads=[lnb], writes=[rsb], scale=-0.5)
                        if which == 0:
                            self.stt(qnT[hh], At, qscale, rstd, ALU.mult, ALU.mult, reads=[Ab, rsb], writes=[qnb[hh]])
                        else:
                            self.tt("dve", knT[hh], At, rstd, ALU.mult, reads=[Ab, rsb], writes=[knb[hh]])
                    else:
                        self.cp("dve", vbf, At, [Ab], [vbb])
                        bt = self.take_bank()
                        for blk in range(4):
                            self.tr(bt, bfbank(bt)[:, blk, :], vbf[:, blk * 128:(blk + 1) * 128], self.ident_bf[:], reads=[vbb, self.kb])
                        self.cp("act", vtok[hh], bfbank(bt), [self.bankbuf[bt]], [vtb[hh]])

                pend_norm = []

                def norms():
                    while pend_norm:
                        hh_, which_, At, Ab = pend_norm.pop(0)
                        self.act(sqt, At, AF.Square, reads=[Ab], writes=[sqb])
                        b2 = self.take_bank()
                        self.mm(b2, self.banks[b2][:], self.ones_bf[:], sqt, True, True, reads=[sqb, self.kb])
                        self.act(lnv, self.banks[b2][:], AF.Ln, reads=[self.bankbuf[b2], self.kb], writes=[lnb], bias=self.eps_col)
                        self.act(rstd, lnv, AF.Exp, reads=[lnb], writes=[rsb], scale=-0.5)
                        if which_ == 0:
                            self.stt(qnT[hh_], At, qscale, rstd, ALU.mult, ALU.mult, reads=[Ab, rsb], writes=[qnb[hh_]])
                        else:
                            self.tt("dve", knT[hh_], At, rstd, ALU.mult, reads=[Ab, rsb], writes=[knb[hh_]])

                units = [(hh, which) for hh in range(4) for which in (2, 0, 1, 3)]
                nextbank = proj(*units[0])
                for ui_, (hh, which) in enumerate(units):
                    h = 4 * g + hh
                    curbank = nextbank
                    if ui_ + 1 < len(units):
                        nextbank = proj(*units[ui_ + 1])
                    chain(hh, which, curbank)
                    if which != 3:
                        continue
                    norms()
                    bt = self.take_bank()
                    for blk in range(4):
                        self.tr(bt, bfbank(bt)[:, blk, :], knT[hh][:, blk * 128:(blk + 1) * 128], self.ident_bf[:], reads=[knb[hh], self.kb])
                    for blk in range(4):
                        self.act(kdtok[hh][:, blk, :], bfbank(bt)[:, blk, :], AF.Copy, reads=[self.bankbuf[bt], gb_], writes=[kdb[hh]],
                                 scale=egl[:, blk * 8 + h:blk * 8 + h + 1])
                    bd = self.take_bank()
                    for blk in range(4):
                        lg, lgbuf = lhsg[blk % 2], lgb[blk % 2]
                        self.act(lg, self.lstrict_f[:], AF.Copy, reads=[self.kb, gb_], writes=[lgbuf],
                                 scale=gtok[:, blk * 8 + h:blk * 8 + h + 1])
                        self.mm(bd, self.banks[bd][:, blk * 128:(blk + 1) * 128], lg, self.uincl_f[:], True, True, reads=[lgbuf, self.kb])
                    self.act(decT, self.banks[bd][:], AF.Exp, reads=[self.bankbuf[bd]], writes=[decb])
                    d3 = decT.rearrange("p (b c) -> p b c", b=4)
                    t3 = tmp.rearrange("p (b c) -> p b c", b=4)
                    bkk = self.take_bank()
                    for blk in range(4):
                        ks = knT[hh][:, blk * 128:(blk + 1) * 128]
                        self.mm(bkk, self.banks[bkk][:, blk * 128:(blk + 1) * 128], ks, ks, True, True, reads=[knb[hh]])
                    self.tt("dve", tmp, self.banks[bkk][:], decT, ALU.mult, reads=[self.bankbuf[bkk], decb], writes=[tmpb])
                    for blk in range(4):
                        self.stt(Pm[hh][:, blk, :], t3[:, blk, :], negbeta[:, blk * 8 + h:blk * 8 + h + 1], self.msu_f[:], ALU.mult, ALU.mult,
                                 reads=[tmpb, gb_, self.kb], writes=[pmb[hh]])
                    bt = self.take_bank()
                    for blk in range(4):
                        self.tr(bt, bfbank(bt)[:, blk, :], Pm[hh][:, blk, :], self.ident_bf[:], reads=[pmb[hh], self.kb])
                    self.cp("act", Qm[hh], bfbank(bt), [self.bankbuf[bt]], [qmb[hh]])
                    self.tt("pool", TTm[hh], Pm[hh], self.ident_bf[:].unsqueeze(1).broadcast_to([128, 4, 128]), ALU.add,
                            reads=[pmb[hh], self.kb], writes=[ttb[hh]])
                    bq = self.take_bank()
                    for blk in range(4):
                        self.mm(bq, self.banks[bq][:, blk * 128:(blk + 1) * 128], knT[hh][:, blk * 128:(blk + 1) * 128],
                                qnT[hh][:, blk * 128:(blk + 1) * 128], True, True, reads=[knb[hh], qnb[hh]])
                    self.tt("dve", tmp, self.banks[bq][:], decT, ALU.mult, reads=[self.bankbuf[bq], decb], writes=[tmpb])
                    self.tt("dve", QKD[hh], t3, self.uincl_f[:].unsqueeze(1).broadcast_to([128, 4, 128]), ALU.mult,
                            reads=[tmpb, self.kb], writes=[qkb[hh]])
                for lev in range(1, 7):
                    for hh in range(4):
                        bq = self.take_bank()
                        for blk in range(4):
                            self.mm(bq, self.banks[bq][:, blk * 128:(blk + 1) * 128], Pm[hh][:, blk, :], Qm[hh][:, blk, :], True, True,
                                    reads=[pmb[hh], qmb[hh]])
                        if lev < 6:
                            bp = self.take_bank()
                            for blk in range(4):
                                self.mm(bp, self.banks[bp][:, blk * 128:(blk + 1) * 128], Qm[hh][:, blk, :], Pm[hh][:, blk, :], True, True,
                                        reads=[pmb[hh], qmb[hh]])
                            self.cp("act", Pm[hh], fbank(bp), [self.bankbuf[bp]], [pmb[hh]])
                        self.cp("dve", Qm[hh], fbank(bq), [self.bankbuf[bq]], [qmb[hh]])
                        br = self.take_bank()
                        for blk in range(4):
                            self.mm(br, self.banks[br][:, blk * 128:(blk + 1) * 128], Qm[hh][:, blk, :], TTm[hh][:, blk, :], True, True,
                                    reads=[qmb[hh], ttb[hh]])
                        self.tt("dve", TTm[hh], fbank(br), TTm[hh], ALU.add, reads=[self.bankbuf[br], ttb[hh]], writes=[ttb[hh]])
                rbuf = vbf.rearrange("p (b c) -> p b c", b=4)
                otok = decT.rearrange("p (b c) -> p b c", b=4)
                o2s = tmp.rearrange("p (b c) -> p b c", b=4)
                for blk in range(4):
                    bs = slice(blk * 128, (blk + 1) * 128)
                    col = lambda a, hh: a[:, blk * 8 + 4 * g + hh:blk * 8 + 4 * g + hh + 1]
                    bks = self.take_bank()
                    for hh in range(4):
                        self.mm(bks, self.banks[bks][:, hh * 128:(hh + 1) * 128], knT[hh][:, bs], Sb[:, hh, :], True, True,
                                reads=[knb[hh], sbb])
                    for hh in range(4):
                        self.stt(rbuf[:, hh, :], self.banks[bks][:, hh * 128:(hh + 1) * 128], col(negeg, hh), vtok[hh][:, blk, :],
                                 ALU.mult, ALU.add, reads=[self.bankbuf[bks], gb_, vtb[hh]], writes=[vbb])
                    bvn = self.take_bank()
                    for hh in range(4):
                        self.mm(bvn, self.banks[bvn][:, hh * 128:(hh + 1) * 128], TTm[hh][:, blk, :], rbuf[:, hh, :], True, True,
                                reads=[ttb[hh], vbb])
                    for hh in range(4):
                        self.act(vnew[:, hh, :], self.banks[bvn][:, hh * 128:(hh + 1) * 128], AF.Copy, reads=[self.bankbuf[bvn], gb_],
                                 writes=[vnb], scale=col(beta, hh))
                    bo1 = self.take_bank()
                    for hh in range(4):
                        self.mm(bo1, self.banks[bo1][:, hh * 128:(hh + 1) * 128], qnT[hh][:, bs], Sb[:, hh, :], True, True,
                                reads=[qnb[hh], sbb])
                    bo2 = self.take_bank()
                    for hh in range(4):
                        self.mm(bo2, self.banks[bo2][:, hh * 128:(hh + 1) * 128], QKD[hh][:, blk, :], vnew[:, hh, :], True, True,
                                reads=[qkb[hh], vnb])
                    self.cp("act", tmp, self.banks[bo2][:], [self.bankbuf[bo2]], [tmpb])
                    for hh in range(4):
                        self.stt(otok[:, hh, :], self.banks[bo1][:, hh * 128:(hh + 1) * 128], col(eg, hh), o2s[:, hh, :],
                                 ALU.mult, ALU.add, reads=[self.bankbuf[bo1], gb_, tmpb], writes=[decb])
                    bsu = self.take_bank()
                    for hh in range(4):
                        self.mm(bsu, self.banks[bsu][:, hh * 128:(hh + 1) * 128], kdtok[hh][:, blk, :], vnew[:, hh, :], True, True,
                                reads=[kdb[hh], vnb])
                    for hh in range(4):
                        self.stt(Sf[:, hh, :], Sf[:, hh, :], col(gl, hh), self.banks[bsu][:, hh * 128:(hh + 1) * 128],
                                 ALU.mult, ALU.add, reads=[self.bankbuf[bsu], gb_, sfb], writes=[sfb])
                    self.cp("act", Sb, Sf, [sfb], [sbb])
                    self.tt("pool", tmp, decT, decT, ALU.mult, reads=[decb], writes=[tmpb])
                    P.op("dve", lambda e: e.tensor_reduce(out=ssq[:, 0:4], in_=o2s, axis=AX.X, op=ALU.add), reads=[tmpb], writes=[ssb])
                    self.act(ssq[:, 0:4], ssq[:, 0:4], AF.Ln, reads=[ssb, self.kb], writes=[ssb], bias=self.eps_col, scale=1.0 / 128)
                    self.act(ssq[:, 0:4], ssq[:, 0:4], AF.Exp, reads=[ssb], writes=[ssb], scale=-0.5)
                    for hh in range(4):
                        self.act(onb_[:, hh, :], otok[:, hh, :], AF.Copy, reads=[decb, ssb], writes=[onbuf], scale=ssq[:, hh:hh + 1])
                    bt = self.take_bank()
                    for hh in range(4):
                        self.tr(bt, bfbank(bt)[:, hh, :], onb_[:, hh, :], self.ident_bf[:], reads=[onbuf, self.kb])
                    for hh in range(4):
                        self.stt(zsT[hh][:, bs], bfbank(bt)[:, hh, :], self.cc("gog"), zsT[hh][:, bs], ALU.mult, ALU.mult,
                                 reads=[self.bankbuf[bt], self.cbuf, zsb[hh]], writes=[zsb[hh]])
                wos = [self.wload(self.d_gwout[j], 2048) for j in range(4)]
                for dc in range(KC):
                    wo = self.ring[wos[dc // 2]][:].rearrange("p (k c) -> p k c", k=KC)
                    bk = self.take_bank()
                    for hh in range(4):
                        self.mm(bk, self.banks[bk][:], wo[:, 4 * g + hh, (dc % 2) * 128:(dc % 2 + 1) * 128], zsT[hh],
                                hh == 0, hh == 3, reads=[self.slotbuf[wos[dc // 2]], zsb[hh]])
                    self.tt("dve", self.xT[:, dc, sl], self.banks[bk][:], self.xT[:, dc, sl], ALU.add,
                            reads=[self.bankbuf[bk], self.xbuf[dc][Q]], writes=[self.xbuf[dc][Q]])

    def fox(self, layer):
        self.common_init()
        P = self.P
        HD = 128
        hT = self.carve(0, 8192).bitcast(BF16).rearrange("p (c t) -> p c t", c=KC)
        for t in range(NT):
            self.rmsnorm_tile(t, "mixg", layer, hT[:, :, t * TT:(t + 1) * TT], self.hbuf[t], 8192)
        o = 9728
        knT = self.carve(o, 4096).bitcast(BF16).rearrange("p (h t) -> p h t", h=4); o += 4096
        knb = [[Buf("kn") for _ in range(NT)] for _ in range(4)]
        vtok = self.carve(o, 4096).bitcast(BF16).rearrange("p (b c) -> p b c", b=16); o += 4096
        vb = [Buf("v%d" % b) for b in range(16)]
        qT = self.carve(o, 1024).bitcast(BF16).rearrange("p (h t) -> p h t", h=4); o += 1024
        qb = [Buf("q%d" % h) for h in range(4)]
        oT = self.carve(o, 1024).bitcast(BF16).rearrange("p (h t) -> p h t", h=4); o += 1024
        ob = [Buf("o%d" % h) for h in range(4)]
        NP = 3
        pT = [self.carve(o + i * 256, 256).bitcast(BF16) for i in range(NP)]; o += NP * 256
        pb = [Buf("p%d" % i) for i in range(NP)]
        raw = self.carve(o, 512); o += 512
        sqt = self.carve(o, 256).bitcast(BF16); o += 256
        lnv = self.carve(o, 512); o += 512
        rstd = self.carve(o, 512); o += 512
        rawb, sqb, lnb, rsb = Buf("raw"), Buf("sq"), Buf("ln"), Buf("rs")
        rden = self.carve(o, 512); o += 512
        rdb = Buf("rden")
        erow = self.carve(o, 512); o += 512
        cT = self.carve(o, 128).rearrange("p (b h) -> p b h", b=16); o += 128
        cmidb = self.carve(o, 32).rearrange("p (q h) -> p q h", q=4); o += 32
        biasA = self.carve(o, 128).rearrange("p (b h) -> p b h", b=16); o += 128
        small = self.carve(o, 32); o += 32
        wf = self.carve(o, 32).bitcast(BF16).rearrange("p (k c) -> p k c", k=KC); o += 32
        assert o <= self.ARENA, o
        carry = small[:, 0:8]
        xs = erow[:, 0:32]
        tots = erow[:, 32:64]
        cb = Buf("cstuff")
        wfb = Buf("wf")
        P.op("pool", lambda e: e.dma_start(out=wf.rearrange("p k c -> p (k c)"), in_=self.d_fwf), writes=[wfb], dma_out=wfb)
        self.memset("pool", carry, 0.0, [cb])
        scale = float(HD) ** -0.5
        for g in range(2):
            for Q in range(NT):
                sl = slice(Q * TT, (Q + 1) * TT)
                hTt = hT[:, :, sl]
                hb = self.hbuf[Q]
                if g == 0:
                    bk = self.take_bank()
                    for blk in range(4):
                        for kc in range(KC):
                            self.mm(bk, self.banks[bk][:, blk * 8:(blk + 1) * 8], hTt[:, kc, blk * 128:(blk + 1) * 128], wf[:, kc, :],
                                    kc == 0, kc == KC - 1, reads=[wfb, hb])
                    x3 = xs.rearrange("p (b h) -> p b h", b=4)
                    self.tt("dve", x3, self.banks[bk][:, 0:32].rearrange("p (b h) -> p b h", b=4),
                            self.cst[:, CL["fbfb"]:CL["fbfb"] + 8].unsqueeze(1).broadcast_to([128, 4, 8]), ALU.add,
                            reads=[self.bankbuf[bk], self.cbuf], writes=[cb])
                    self.act(xs, xs, AF.Exp, reads=[cb], writes=[cb], scale=-1.0)
                    self.act(xs, xs, AF.Ln, reads=[cb, self.kb], writes=[cb], bias=self.one_col)
                    b1 = self.take_bank()
                    self.mm(b1, self.banks[b1][:, 0:32], self.uincl_f[:], xs, True, True, reads=[cb, self.kb])
                    b2 = self.take_bank()
                    self.mm(b2, self.banks[b2][:, 0:32], self.ones_f[:], xs, True, True, reads=[cb, self.kb])
                    self.cp("dve", tots, self.banks[b2][:, 0:32], [self.bankbuf[b2]], [cb])
                    for blk in range(4):
                        self.tt("dve", cT[:, 4 * Q + blk, :], self.banks[b1][:, blk * 8:(blk + 1) * 8], carry, ALU.add,
                                reads=[self.bankbuf[b1], cb], writes=[cb])
                        self.tt("dve", carry, carry, tots[:, blk * 8:(blk + 1) * 8], ALU.add, reads=[cb], writes=[cb])
                        if blk == 1:
                            self.cp("dve", cmidb[:, Q, :], carry, [cb], [cb])
                nj = 4 * Q + 4
                DBG = int(os.environ.get("FOXDBG", "9"))
                if DBG <= 1:
                    continue
                self.tt("dve", biasA[:, 0:nj, :], cT[:, 0:nj, :], cmidb[:, Q:Q + 1, :].broadcast_to([128, nj, 8]), ALU.subtract,
                        reads=[cb], writes=[cb])
                wsl = {}

                def fproj(which, hh):
                    hp, h2 = hh // 2, hh % 2
                    if h2 == 0:
                        wsl[(which, hp)] = self.wload(self.d_fwin[which * 4 + 2 * g + hp], 2048)
                    sw = wsl[(which, hp)]
                    w_ = self.ring[sw][:].rearrange("p (k c) -> p k c", k=KC)
                    bk = self.take_bank()
                    self.held.add(bk)
                    for kc in range(KC):
                        self.mm(bk, self.banks[bk][:], w_[:, kc, h2 * 128:(h2 + 1) * 128], hTt[:, kc, :],
                                kc == 0, kc == KC - 1, reads=[self.slotbuf[sw], hb])
                    return bk

                def fnorm(which, hh, bk):
                    self.cp("dve", raw, self.banks[bk][:], [self.bankbuf[bk]], [rawb])
                    self.held.discard(bk)
                    self.act(sqt, raw, AF.Square, reads=[rawb], writes=[sqb])
                    b2 = self.take_bank()
                    self.mm(b2, self.banks[b2][:], self.ones_bf[:], sqt, True, True, reads=[sqb, self.kb])
                    self.act(lnv, self.banks[b2][:], AF.Ln, reads=[self.bankbuf[b2], self.kb], writes=[lnb],
                             bias=self.eps_col, scale=1.0 / HD)
                    self.act(rstd, lnv, AF.Exp, reads=[lnb], writes=[rsb], scale=-0.5)
                    if which == 0:
                        self.stt(qT[:, hh, :], raw, self.cc("fqg"), rstd, ALU.mult, ALU.mult,
                                 reads=[rawb, rsb, self.cbuf], writes=[qb[hh]])
                    else:
                        self.stt(knT[:, hh, sl], raw, self.cc("fkg"), rstd, ALU.mult, ALU.mult,
                                 reads=[rawb, rsb, self.cbuf], writes=[knb[hh][Q]])
                funits = [(which, hh) for which in range(2) for hh in range(4)]
                nb_ = fproj(*funits[0])
                for fi, (which, hh) in enumerate(funits):
                    cb_ = nb_
                    if fi + 1 < len(funits):
                        nb_ = fproj(*funits[fi + 1])
                    fnorm(which, hh, cb_)
                for hp in range(2):
                    sw = self.wload(self.d_fwin[8 + 2 * g + hp], 2048)
                    w_ = self.ring[sw][:].rearrange("p (k c) -> p k c", k=KC)
                    for blk in range(4):
                        bk = self.take_bank()
                        for kc in range(KC):
                            self.mm(bk, self.banks[bk][:, 0:256], hTt[:, kc, blk * 128:(blk + 1) * 128], w_[:, kc, :],
                                    kc == 0, kc == KC - 1, reads=[self.slotbuf[sw], hb])
                        self.act(vtok[:, 4 * Q + blk, hp * 256:(hp + 1) * 256], self.banks[bk][:, 0:256], AF.Copy,
                                 reads=[self.bankbuf[bk]], writes=[vb[4 * Q + blk]])
                pi = 0
                if DBG <= 2:
                    continue
                for hh in range(4):
                    h = 4 * g + hh
                    bo = self.take_bank()
                    self.held.add(bo)
                    bd = self.take_bank()
                    self.held.add(bd)
                    def s_stage(j):
                        off = max(0, (j - 4 * Q) * 128)
                        bs = self.take_bank()
                        self.mm(bs, self.banks[bs][:, off:512], knT[:, hh, j * 128:(j + 1) * 128], qT[:, hh, off:512],
                                True, True, reads=[knb[hh][j // 4], qb[hh]])
                        return bs

                    def pv_stage(j, bs, pi_):
                        off = max(0, (j - 4 * Q) * 128)
                        p_, pbuf = pT[pi_ % NP], pb[pi_ % NP]
                        self.act(p_[:, off:512], self.banks[bs][:, off:512], AF.Exp, reads=[self.bankbuf[bs], cb], writes=[pbuf],
                                 bias=biasA[:, j, h:h + 1], scale=scale)
                        if j >= 4 * Q:
                            self.tt("pool", p_[:, off:off + 128], p_[:, off:off + 128], self.uincl_bf[:], ALU.mult,
                                    reads=[pbuf, self.kb], writes=[pbuf])
                        return p_, pbuf, off

                    def acc_stage(j, p_, pbuf, off):
                        self.mm(bo, self.banks[bo][:, off:512], vtok[:, j, hh * 128:(hh + 1) * 128], p_[:, off:512],
                                j == 0, j == nj - 1, reads=[vb[j], pbuf])
                        self.mm(bd, self.banks[bd][:, off:512], self.ones_bf[:], p_[:, off:512],
                                j == 0, j == nj - 1, reads=[pbuf, self.kb])
                    bs_next = s_stage(0)
                    for j in range(nj):
                        bs_cur = bs_next
                        pp = pv_stage(j, bs_cur, pi)
                        pi += 1
                        if j + 1 < nj:
                            bs_next = s_stage(j + 1)
                        acc_stage(j, *pp)
                    P.op("dve", lambda e, bd=bd: e.reciprocal(out=rden, in_=self.banks[bd][:]), reads=[self.bankbuf[bd]], writes=[rdb])
                    self.tt("dve", oT[:, hh, :], self.banks[bo][:], rden, ALU.mult, reads=[self.bankbuf[bo], rdb], writes=[ob[hh]])
                    self.held.discard(bo)
                    self.held.discard(bd)
                wos = [self.wload(self.d_fwout[j], 2048) for j in range(4)]
                for dc in range(KC):
                    wo = self.ring[wos[dc // 2]][:].rearrange("p (k c) -> p k c", k=KC)
                    bk = self.take_bank()
                    for hh in range(4):
                        self.mm(bk, self.banks[bk][:], wo[:, 4 * g + hh, (dc % 2) * 128:(dc % 2 + 1) * 128], oT[:, hh, :],
                                hh == 0, hh == 3, reads=[self.slotbuf[wos[dc // 2]], ob[hh]])
                    self.tt("dve", self.xT[:, dc, sl], self.banks[bk][:], self.xT[:, dc, sl], ALU.add,
                            reads=[self.bankbuf[bk], self.xbuf[dc][Q]], writes=[self.xbuf[dc][Q]])


ALL_STAGES = [("conf", 0), ("ffn", 0), ("gdn", 1), ("ffn", 1), ("fox", 2), ("ffn", 2), ("conf", 3), ("ffn", 3)]


def build_nc(stages):
    nc = bass.Bass("TRN2", target_bir_lowering=False)
    k = K(nc, stages)
    k.build()
    return nc, k.used_inputs


def prep_shared(inp):
    sh = {}
    sh["cst"] = pack_consts(inp)
    sh["cwin"] = np.stack([tile_w(inp["conv_w_in"][i]).reshape(8, 128, 2048) for i in range(2)])
    sh["cwout"] = np.stack([tile_w(inp["conv_w_out"][i]).reshape(4, 128, 2048) for i in range(2)])
    gw = np.asarray(inp["gdn_w_in"][0], np.float32)
    sh["gwin"] = tile_w(gw[:, :4096]).reshape(16, 128, 2048)
    sh["gwab"] = np.ascontiguousarray(gw[:, 4096:4112].reshape(8, 128, 16).transpose(1, 0, 2)).reshape(128, 128)
    sh["gwout"] = tile_w(inp["gdn_w_out"][0]).reshape(4, 128, 2048)
    fw = np.asarray(inp["fox_w_in"][0], np.float32)
    sh["fwin"] = tile_w(fw[:, :3072]).reshape(12, 128, 2048)
    sh["fwf"] = np.ascontiguousarray(fw[:, 3072:3080].reshape(8, 128, 8).transpose(1, 0, 2)).reshape(128, 64)
    sh["fwout"] = tile_w(inp["fox_w_out"][0]).reshape(4, 128, 2048)
    sh["wup"] = np.stack([tile_w(inp["ffn_w_up"][l]).reshape(22, 128, 2048) for l in range(4)])
    sh["wdn"] = np.stack([tile_wk(inp["ffn_w_down"][l]).reshape(11, 128, 2048) for l in range(4)])
    return sh


def run(inp, stages, ncores=8, trace=False):
    x = np.asarray(inp["x"], np.float32)
    sh = prep_shared(inp)
    nc, used = build_nc(stages)
    sh = {k_: v for k_, v in sh.items() if k_ in used}
    in_maps = []
    for b in range(ncores):
        m = dict(sh)
        m["xT"] = np.ascontiguousarray(x[b].T).reshape(KC, 128, S)
        in_maps.append(m)
    res = run_bass_kernel_spmd(nc, in_maps, core_ids=list(range(ncores)), trace=trace)
    out = np.stack([np.asarray(r["yT"], np.float32).reshape(D, S).T for r in res.results])
    return out, res


def kernel(**inputs):
    out, _ = run(inputs, ALL_STAGES, ncores=8)
    return out.astype(np.float32)
```

```python
import contextlib
import os
import numpy as np
import concourse.bass as bass
import concourse.mybir as mybir
from concourse.bass_utils import run_bass_kernel_spmd

F32 = mybir.dt.float32
BF16 = mybir.dt.bfloat16
AF = mybir.ActivationFunctionType
ALU = mybir.AluOpType
AX = mybir.AxisListType

S = 2048
D = 1024
TT = 512
NT = 4
KC = 8
FF = 2816
EPS = 1e-6
ENGS = ["pe", "act", "dve", "pool", "sp"]
BLOCK_ATTR = {"pe": "tensor", "act": "scalar", "dve": "vector", "pool": "gpsimd", "sp": "sync"}


class Buf:
    __slots__ = ("name", "last_w", "readers", "sem", "dma_cnt", "excl")

    def __init__(self, name, excl=False):
        self.name = name
        self.excl = excl
        self.last_w = None
        self.readers = []
        self.sem = None
        self.dma_cnt = 0


class Op:
    __slots__ = ("eng", "fn", "waits", "signal", "idx", "dma_buf", "clock", "sigcnt")

    def __init__(self, eng, fn, idx, dma_buf=None):
        self.eng = eng
        self.fn = fn
        self.idx = idx
        self.waits = []
        self.signal = False
        self.dma_buf = dma_buf
        self.clock = None
        self.sigcnt = None


class Prog:
    def __init__(self, nc):
        self.nc = nc
        self.ops = {e: [] for e in ENGS}
        self.obs = {e: {} for e in ENGS}
        self.dma_bufs = []
        self.pending = {e: [] for e in ENGS}

    def barrier(self):
        toks = [("eng", e, len(self.ops[e]) - 1) for e in ENGS if self.ops[e] and self.ops[e][-1].dma_buf is None]
        for e in ENGS:
            if self.ops[e] and self.ops[e][-1].dma_buf is not None:
                for o in reversed(self.ops[e]):
                    if o.dma_buf is None:
                        toks.append(("eng", e, o.idx))
                        break
        for e in ENGS:
            self.pending[e] = list(toks)

    def _need(self, op, tok):
        e = op.eng
        if tok[0] == "eng":
            _, se, si = tok
            if self.obs[e].get(se, -1) >= si:
                return
            src = self.ops[se][si]
            src.signal = True
            op.waits.append(tok)
            self.obs[e][se] = si
            if src.clock:
                for k, v in src.clock.items():
                    if self.obs[e].get(k, -1) < v:
                        self.obs[e][k] = v
        else:
            _, b, cnt = tok
            key = ("dma", id(b))
            if self.obs[e].get(key, -1) >= cnt:
                return
            op.waits.append(tok)
            self.obs[e][key] = cnt

    def op(self, eng, fn, reads=(), writes=(), dma_out=None, nobarrier=False):
        lst = self.ops[eng]
        o = Op(eng, fn, len(lst), dma_buf=dma_out)
        if any(r.excl for r in reads):
            writes = list(writes) + [r for r in reads if r.excl and r not in writes]
            reads = [r for r in reads if not r.excl]
        best = {}
        for r in reads:
            t = r.last_w
            if t is not None:
                k = t[1] if t[0] == "eng" else ("dma", id(t[1]))
                if k not in best or best[k][2] < t[2]:
                    best[k] = t
        for w in writes:
            for t in [w.last_w] + w.readers:
                if t is not None:
                    k = t[1] if t[0] == "eng" else ("dma", id(t[1]))
                    if k not in best or best[k][2] < t[2]:
                        best[k] = t
        if self.pending[eng] and not nobarrier:
            for t in self.pending[eng]:
                if t[1] == eng:
                    continue
                k = t[1]
                if k not in best or best[k][2] < t[2]:
                    best[k] = t
            self.pending[eng] = []
        for t in best.values():
            if eng == "pe" and t[0] == "eng" and t[1] == "pe":
                continue
            self._need(o, t)
        if dma_out is not None:
            if dma_out.sem is None:
                self.dma_bufs.append(dma_out)
                dma_out.sem = True
            dma_out.dma_cnt += 1
            tok = ("dma", dma_out, dma_out.dma_cnt)
        else:
            tok = ("eng", eng, o.idx)
        o.clock = dict(self.obs[eng])
        for r in reads:
            r.readers.append(tok)
        for w in writes:
            w.last_w = tok
            w.readers = []
        lst.append(o)
        return tok

    def emit(self, final_waits=()):
        nc = self.nc
        CH = 2000
        with contextlib.ExitStack() as st:
            for e in ENGS:
                c = 0
                for o in self.ops[e]:
                    if o.signal and o.dma_buf is None:
                        o.sigcnt = c
                        c += 1
                    else:
                        o.sigcnt = c - 1
            nsig = {e: sum(1 for o in self.ops[e] if o.signal and o.dma_buf is None) for e in ENGS}
            esem = {e: [st.enter_context(nc.semaphore("s_%s%d" % (e, i))) for i in range(max(1, (nsig[e] + CH - 1) // CH))]
                    for e in ENGS}
            for i, b in enumerate(self.dma_bufs):
                b.sem = st.enter_context(nc.semaphore("d%d" % i))
            block = st.enter_context(nc.Block())
            for e in ENGS:
                ops = self.ops[e]
                fw = [b for (fe, b) in final_waits if fe == e]
                if not ops and not fw:
                    continue

                def body(eng, ops=ops, e=e, fw=fw):
                    for o in ops:
                        for t in o.waits:
                            if t[0] == "eng":
                                k = self.ops[t[1]][t[2]].sigcnt
                                eng.wait_ge(esem[t[1]][k // CH], k % CH + 1)
                            else:
                                eng.wait_ge(t[1].sem, 16 * t[2])
                        ins = o.fn(eng)
                        if o.dma_buf is not None:
                            ins.then_inc(o.dma_buf.sem, 16)
                        elif o.signal:
                            ins.then_inc(esem[e][o.sigcnt // CH], 1)
                    for b in fw:
                        eng.wait_ge(b.sem, 16 * b.dma_cnt)

                getattr(block, BLOCK_ATTR[e])(body)


def _cst_layout():
    lay = {}
    off = 0
    for name, n in [("mixg", 32), ("ffng", 32), ("cbin", 32), ("cwdw", 2 * 31 * 8), ("cbdw", 16),
                    ("clng", 16), ("clnb", 16), ("gconv", 96), ("gog", 1), ("fqg", 1), ("fkg", 1),
                    ("fwdw", 4 * 3 * 44), ("fbf", 1), ("galog", 8), ("gdtb", 8), ("fbfb", 8)]:
        lay[name] = off
        off += n
    return lay, off


CL, NCST = _cst_layout()


def _cols(v):
    v = np.asarray(v, dtype=np.float32).reshape(-1, 128)
    return v.T


def pack_consts(inp):
    c = np.zeros((128, NCST), np.float32)

    def put(name, arr):
        arr = np.asarray(arr, np.float32)
        c[:, CL[name]:CL[name] + arr.shape[1]] = arr

    put("mixg", _cols(inp["mix_norm_g"].reshape(-1)))
    put("ffng", _cols(inp["ffn_norm_g"].reshape(-1)))
    put("cbin", _cols(inp["conv_b_in"].reshape(-1)))
    put("cwdw", _cols(inp["conv_w_dw"].reshape(-1)))
    put("cbdw", _cols(inp["conv_b_dw"].reshape(-1)))
    put("clng", _cols(inp["conv_ln_g"].reshape(-1)))
    put("clnb", _cols(inp["conv_ln_b"].reshape(-1)))
    put("gconv", _cols(inp["gdn_conv_w"].reshape(-1)))
    put("gog", _cols(inp["gdn_o_norm_g"].reshape(-1)))
    put("fqg", _cols(inp["fox_q_norm_g"].reshape(-1)))
    put("fkg", _cols(inp["fox_k_norm_g"].reshape(-1)))
    put("fwdw", _cols(inp["ffn_w_dw"].reshape(-1)))
    bf = np.zeros((128, 1), np.float32)
    bf[:8, 0] = np.asarray(inp["fox_b_f"], np.float32).reshape(-1)
    put("fbf", bf)
    put("galog", np.broadcast_to(np.asarray(inp["gdn_a_log"], np.float32).reshape(1, 8), (128, 8)))
    put("gdtb", np.broadcast_to(np.asarray(inp["gdn_dt_bias"], np.float32).reshape(1, 8), (128, 8)))
    put("fbfb", np.broadcast_to(np.asarray(inp["fox_b_f"], np.float32).reshape(1, 8), (128, 8)))
    return c


def tile_w(w, width=256):
    w = np.asarray(w, np.float32)
    K, N = w.shape
    return np.ascontiguousarray(w.reshape(K // 128, 128, N // width, width).transpose(2, 1, 0, 3))


def tile_wk(w, nk=2):
    w = np.asarray(w, np.float32)
    K, N = w.shape
    return np.ascontiguousarray(w.reshape(K // (128 * nk), nk, 128, N).transpose(0, 2, 1, 3))


class K:
    def __init__(self, nc, stages):
        self.nc = nc
        self.P = Prog(nc)
        self.stages = stages
        self.st = contextlib.ExitStack()
        self.bank_rr = 0
        self.slot_rr = 0
        self.uid = 0

    def sb(self, name, shape, dt):
        return self.st.enter_context(self.nc.sbuf_tensor(name, shape, dt))

    def B(self, name=""):
        self.uid += 1
        return Buf("%s%d" % (name, self.uid))

    def take_bank(self):
        while True:
            i = self.bank_rr % 8
            self.bank_rr += 1
            if i not in self.held:
                return i

    def take_slot(self):
        i = self.slot_rr % self.NSLOT
        self.slot_rr += 1
        return i

    def mm(self, bank, out, lhsT, rhs, start, stop, reads, extra_w=()):
        self.P.op("pe", lambda e: e.matmul(out, lhsT, rhs, start=start, stop=stop),
                  reads=reads, writes=[self.bankbuf[bank]] + list(extra_w))

    def tr(self, bank, out, in_, ident, reads):
        self.P.op("pe", lambda e: e.transpose(out, in_, ident), reads=reads, writes=[self.bankbuf[bank]])

    def act(self, out, in_, func, reads, writes, bias=None, scale=None, eng="act"):
        kw = {}
        if bias is not None:
            kw["bias"] = bias
        if scale is not None:
            kw["scale"] = scale
        self.P.op("act", lambda e: e.activation(out=out, in_=in_, func=func, **kw), reads=reads, writes=writes)

    def tt(self, eng, out, in0, in1, op, reads, writes):
        self.P.op(eng, lambda e: e.tensor_tensor(out=out, in0=in0, in1=in1, op=op), reads=reads, writes=writes)

    def stt(self, out, in0, scalar, in1, op0, op1, reads, writes):
        self.P.op("dve", lambda e: e.scalar_tensor_tensor(out=out, in0=in0, scalar=scalar, in1=in1, op0=op0, op1=op1),
                  reads=reads, writes=writes)

    def ts(self, eng, out, in0, s1, s2, op0, op1, reads, writes):
        if op1 is None:
            self.P.op(eng, lambda e: e.tensor_scalar(out=out, in0=in0, scalar1=s1, scalar2=None, op0=op0),
                      reads=reads, writes=writes)
        else:
            self.P.op(eng, lambda e: e.tensor_scalar(out=out, in0=in0, scalar1=s1, scalar2=s2, op0=op0, op1=op1),
                      reads=reads, writes=writes)

    def cp(self, eng, out, in_, reads, writes):
        if eng == "act":
            self.P.op("act", lambda e: e.copy(out=out, in_=in_), reads=reads, writes=writes)
        else:
            self.P.op(eng, lambda e: e.tensor_copy(out=out, in_=in_), reads=reads, writes=writes)

    def memset(self, eng, ap, val, writes):
        self.P.op(eng, lambda e: e.memset(ap, val), writes=writes)

    def wload(self, src_ap, ncols_total, view=None):
        s = self.take_slot()
        dst = self.ring[s][:, 0:ncols_total]
        self.P.op("pool", lambda e: e.dma_start(out=dst, in_=src_ap), writes=[self.slotbuf[s]], dma_out=self.slotbuf[s], nobarrier=True)
        return s

    def build(self):
        nc = self.nc
        P = self.P
        dt = nc.dram_tensor
        self.xin = dt("xT", [KC, 128, S], F32, kind="ExternalInput").ap()
        self.cst_d = dt("cst", [128, NCST], F32, kind="ExternalInput").ap()
        kinds = set(k_ for k_, _ in self.stages)
        self.used_inputs = ["xT", "cst"]

        def din(name, shape, kind_):
            if kind_ not in kinds:
                return None
            self.used_inputs.append(name)
            return dt(name, shape, F32, kind="ExternalInput").ap()
        self.d_cwin = din("cwin", [2, 8, 128, 2048], "conf")
        self.d_cwout = din("cwout", [2, 4, 128, 2048], "conf")
        self.d_gwin = din("gwin", [16, 128, 2048], "gdn")
        self.d_gwab = din("gwab", [128, 128], "gdn")
        self.d_gwout = din("gwout", [4, 128, 2048], "gdn")
        self.d_fwin = din("fwin", [12, 128, 2048], "fox")
        self.d_fwf = din("fwf", [128, 64], "fox")
        self.d_fwout = din("fwout", [4, 128, 2048], "fox")
        self.d_wup = din("wup", [4, 22, 128, 2048], "ffn")
        self.d_wdn = din("wdn", [4, 11, 128, 2048], "ffn")
        self.yout = dt("yT", [KC, 128, S], F32, kind="ExternalOutput").ap()

        self.xT = self.sb("xT_sb", [128, KC, S], F32)
        self.cst = self.sb("cst_sb", [128, NCST], F32)
        self.NSLOT = 8
        self.ring = [self.sb("ring%d" % i, [128, 2048], BF16) for i in range(self.NSLOT)]
        self.slotbuf = [Buf("slot%d" % i) for i in range(self.NSLOT)]
        self.banks = [self.st.enter_context(nc.psum_tensor("bank%d" % i, [128, 512], F32)) for i in range(8)]
        self.bankbuf = [Buf("bank%d" % i, excl=True) for i in range(8)]
        self.held = set()
        self.xbuf = [[Buf("x%d_%d" % (c, t)) for t in range(NT)] for c in range(KC)]
        self.hbuf = [Buf("h%d" % t) for t in range(NT)]
        self.hb1 = Buf("htile")
        self.cbuf = Buf("cst")
        self.ones_bf = self.sb("ones_bf", [128, 128], BF16)
        self.ones_f = self.sb("ones_f", [128, 128], F32)
        self.ident_f = self.sb("ident_f", [128, 128], F32)
        self.ident_bf = self.sb("ident_bf", [128, 128], BF16)
        self.uincl_f = self.sb("uincl_f", [128, 128], F32)
        self.uincl_bf = self.sb("uincl_bf", [128, 128], BF16)
        self.lstrict_f = self.sb("lstrict_f", [128, 128], F32)
        self.msu_f = self.sb("msu_f", [128, 128], F32)
        self.kb = Buf("consts")
        self.ARENA = 25600
        self.arena = self.sb("arena", [128, self.ARENA], F32)

        P.op("sp", lambda e: e.dma_start(out=self.cst[:], in_=self.cst_d), writes=[self.cbuf], dma_out=self.cbuf)
        kb = self.kb
        self.memset("pool", self.ones_f[:], 1.0, [kb])
        self.memset("pool", self.ones_bf[:], 1.0, [kb])

        def asel(out, base, cm, step, op):
            P.op("pool", lambda e: e.affine_select(out=out, in_=self.ones_f[:], pattern=[[step, 128]], compare_op=op,
                                                   fill=0.0, base=base, channel_multiplier=cm), reads=[kb], writes=[kb])
        asel(self.ident_f[:], 0, -1, 1, ALU.is_equal)
        asel(self.uincl_f[:], 0, -1, 1, ALU.is_ge)
        asel(self.lstrict_f[:], -1, 1, -1, ALU.is_ge)
        asel(self.msu_f[:], -1, -1, 1, ALU.is_ge)
        self.cp("pool", self.ident_bf[:], self.ident_f[:], [kb], [kb])
        self.cp("pool", self.uincl_bf[:], self.uincl_f[:], [kb], [kb])

        for t in range(NT):
            for c in range(KC):
                P.op("sp", lambda e, c=c, t=t: e.dma_start(out=self.xT[:, c, t * TT:(t + 1) * TT],
                                                           in_=self.xin[c, :, t * TT:(t + 1) * TT]),
                     writes=[self.xbuf[c][t]], dma_out=self.xbuf[c][t])

        ia = 0
        for stg in self.stages:
            kind, layer = stg
            P.barrier()
            if kind == "conf":
                self.conformer(layer, ia)
                ia += 1
            elif kind == "gdn":
                self.gdn(layer)
            elif kind == "fox":
                self.fox(layer)
            elif kind == "ffn":
                self.ffn(layer)

        ob = Buf("out")
        for c in range(KC):
            P.op("sp", lambda e, c=c: e.dma_start(out=self.yout[c], in_=self.xT[:, c, :]),
                 reads=self.xbuf[c], writes=[ob], dma_out=ob)
        P.emit(final_waits=[("sp", ob)])
        self.st.close()

    def cc(self, name, idx=0):
        o = CL[name] + idx
        return self.cst[:, o:o + 1]

    def carve(self, off, nwords):
        assert off + nwords <= self.ARENA, (off, nwords)
        return self.arena[:, off:off + nwords]

    def rmsnorm_tile(self, t, gname, layer, hview, hb, ar_off):
        sl = slice(t * TT, (t + 1) * TT)
        sqs = [self.carve(ar_off + i * 256, 256).bitcast(BF16) for i in range(2)]
        lnv = self.carve(ar_off + 512, 512)
        rstd = self.carve(ar_off + 1024, 512)
        bl, br = self.nb_ln, self.nb_rstd
        bk = self.take_bank()
        for c in range(KC):
            sq, bsq = sqs[c % 2], self.nb_sq[c % 2]
            self.act(sq, self.xT[:, c, sl], AF.Square, reads=[self.xbuf[c][t]], writes=[bsq])
            self.mm(bk, self.banks[bk][:], self.ones_bf[:], sq, c == 0, c == KC - 1, reads=[bsq, self.kb])
        self.act(lnv, self.banks[bk][:], AF.Ln, reads=[self.bankbuf[bk], self.kb], writes=[bl], bias=self.eps_col, scale=1.0 / D)
        self.act(rstd, lnv, AF.Exp, reads=[bl], writes=[br], scale=-0.5)
        for c in range(KC):
            self.stt(hview[:, c, :], self.xT[:, c, sl], self.cc(gname, layer * 8 + c), rstd, ALU.mult, ALU.mult,
                     reads=[self.xbuf[c][t], br, self.cbuf], writes=[hb])

    def common_init(self):
        if getattr(self, "_ci", False):
            return
        self._ci = True
        self.nb_sq, self.nb_ln, self.nb_rstd = [Buf("sq0"), Buf("sq1")], Buf("ln"), Buf("rstd")
        self.eps_t = self.sb("eps_t", [128, 4], F32)
        self.memset("pool", self.eps_t[:, 0:1], EPS, [self.kb])
        self.memset("pool", self.eps_t[:, 1:2], 1.0, [self.kb])
        self.memset("pool", self.eps_t[:, 2:3], 0.0, [self.kb])
        self.eps_col = self.eps_t[:, 0:1]
        self.one_col = self.eps_t[:, 1:2]
        self.zero_col = self.eps_t[:, 2:3]

    def ffn(self, layer):
        self.common_init()
        P = self.P
        hT = self.carve(0, 8192).bitcast(BF16).rearrange("p (c t) -> p c t", c=KC)
        self.hT = hT
        for t in range(NT):
            self.rmsnorm_tile(t, "ffng", layer, hT[:, :, t * TT:(t + 1) * TT], self.hbuf[t], 8192)
        GOFF = 8192 + 1536
        gT = [self.carve(GOFF + i * 4096, 4096).bitcast(BF16).rearrange("p (c t) -> p c t", c=4) for i in range(2)]
        gbuf = [[[Buf("g") for _ in range(NT)] for _ in range(4)] for _ in range(2)]
        UOFF = GOFF + 2 * 4096
        NU = 6
        U = [self.carve(UOFF + i * 516, 516) for i in range(NU)]
        ubuf = [Buf("U") for _ in range(NU)]
        AOFF = UOFF + NU * 516
        NA = 8
        ACC = [self.carve(AOFF + i * 512, 512) for i in range(NA)]
        abuf = [Buf("acc") for _ in range(NA)]
        urr = [0]
        arr = [0]
        hreads = self.hbuf

        parts = [(0, 2), (2, 4), (4, 6), (6, 8), (8, 10), (10, 11)]
        order = []
        for (u0, u1) in parts:
            for u in range(u0, u1):
                order.append(("g", u, self.d_wup[layer, u]))
                order.append(("u", u, self.d_wup[layer, 11 + u]))
            for u in range(u0, u1):
                order.append(("d", u, self.d_wdn[layer, u]))
        issued = {}
        nxt = [0]
        deferred = []
        pend_down = []

        def want(key, ahead):
            idx = [i for i, o_ in enumerate(order) if (o_[0], o_[1]) == key][0]
            while nxt[0] <= min(idx + ahead, len(order) - 1):
                o_ = order[nxt[0]]
                issued[(o_[0], o_[1])] = self.wload(o_[2], 2048)
                nxt[0] += 1
            return issued[key]

        for pi, (u0, u1) in enumerate(parts):
            g = gT[pi % 2]
            gb = gbuf[pi % 2]
            for u in range(u0, u1):
                if u == u0 + 1 or (u == u0 and u1 - u0 == 1 and False):
                    while pend_down:
                        pend_down.pop(0)()
                sg = want(("g", u), 5)
                su = want(("u", u), 4)
                wg = self.ring[sg][:].rearrange("p (k c) -> p k c", k=KC)
                wu = self.ring[su][:].rearrange("p (k c) -> p k c", k=KC)
                for c2 in range(2):
                    ci = 2 * u + c2
                    lc = ci - 2 * u0
                    prev = {"g": None, "u": None}
                    for t in range(NT):
                        sl = slice(t * TT, (t + 1) * TT)
                        accs = {}
                        for which, w_, slot_, col0 in (("g", wg, sg, ci), ("u", wu, su, 22 + ci)):
                            bk = self.take_bank()
                            for kc in range(KC):
                                self.mm(bk, self.banks[bk][:], w_[:, kc, c2 * 128:(c2 + 1) * 128], self.hT[:, kc, sl],
                                        kc == 0, kc == KC - 1, reads=[self.slotbuf[slot_], self.hbuf[t]])
                            ui = urr[0] % NU
                            urr[0] += 1
                            Ut, Ub = U[ui], ubuf[ui]
                            if t == 0:
                                self.memset("pool", Ut[:, 0:2], 0.0, [Ub])
                            else:
                                pU, pB = prev[which]
                                self.cp("pool", Ut[:, 0:2], pU[:, 512:514], [pB], [Ub])
                            self.act(Ut[:, 2:514], self.banks[bk][:], AF.Copy, reads=[self.bankbuf[bk]], writes=[Ub])
                            prev[which] = (Ut, Ub)
                            ai = arr[0] % NA
                            arr[0] += 1
                            At, Ab = ACC[ai], abuf[ai]
                            wcol = lambda k, col0=col0: self.cc("fwdw", (layer * 3 + k) * 44 + col0)
                            self.act(At, Ut[:, 0:512], AF.Copy, reads=[Ub, self.cbuf], writes=[Ab], scale=wcol(0))
                            self.stt(At, Ut[:, 1:513], wcol(1), At, ALU.mult, ALU.add, reads=[Ub, self.cbuf, Ab], writes=[Ab])
                            self.stt(At, Ut[:, 2:514], wcol(2), At, ALU.mult, ALU.add, reads=[Ub, self.cbuf, Ab], writes=[Ab])
                            accs[which] = (At, Ab)
                        Ag, Agb = accs["g"]
                        Au, Aub = accs["u"]

                        def tail(Ag=Ag, Agb=Agb, Au=Au, Aub=Aub, dst=g[:, lc, sl], db=gb[lc][t]):
                            self.act(Ag, Ag, AF.Silu, reads=[Agb], writes=[Agb])
                            self.tt("pool", dst, Ag, Au, ALU.mult, reads=[Agb, Aub], writes=[db])
                        if deferred:
                            deferred.pop(0)()
                        deferred.append(tail)
            dslots = [want(("d", u), 3) for u in range(u0, u1)]

            def down(g=g, gb=gb, nch=2 * (u1 - u0), dslots=dslots):
                while deferred:
                    deferred.pop(0)()
                for dc in range(KC):
                    for t in range(NT):
                        sl = slice(t * TT, (t + 1) * TT)
                        bk = self.take_bank()
                        for lc in range(nch):
                            s_ = dslots[lc // 2]
                            wd = self.ring[s_][:].rearrange("p (k c) -> p k c", k=2)
                            self.mm(bk, self.banks[bk][:], wd[:, lc % 2, dc * 128:(dc + 1) * 128], g[:, lc, sl],
                                    lc == 0, lc == nch - 1, reads=[self.slotbuf[s_], gb[lc][t]])
                        self.tt("dve", self.xT[:, dc, sl], self.banks[bk][:], self.xT[:, dc, sl], ALU.add,
                                reads=[self.bankbuf[bk], self.xbuf[dc][t]], writes=[self.xbuf[dc][t]])
            pend_down.append(down)
        while pend_down:
            pend_down.pop(0)()

    def conformer(self, layer, ia):
        self.common_init()
        P = self.P
        hTt = self.carve(0, 2048).bitcast(BF16).rearrange("p (c t) -> p c t", c=KC)
        G = self.carve(3584, 2176).bitcast(BF16).rearrange("p (c t) -> p c t", c=KC)
        gb = [Buf("G%d" % c) for c in range(KC)]
        DG = [self.carve(5760 + i * 1984, 1984).bitcast(BF16).rearrange("p (k m) -> p k m", k=31) for i in range(2)]
        dgb = [Buf("dg0"), Buf("dg1")]
        CO = self.carve(9728, 4096).rearrange("p (c t) -> p c t", c=KC)
        cob = [Buf("co%d" % c) for c in range(KC)]
        UB = [self.carve(13824 + i * 256, 256).bitcast(BF16) for i in range(2)]
        SQ = [self.carve(14336 + i * 256, 256).bitcast(BF16) for i in range(2)]
        ubb = [Buf("ub0"), Buf("ub1")]
        sqb = [Buf("sqb0"), Buf("sqb1")]
        mean = self.carve(14848, 512)
        msq = self.carve(15360, 512)
        lnv = self.carve(15872, 512)
        rstd = self.carve(16384, 512)
        stb = Buf("stats")
        TMP = [self.carve(16896 + i * 512, 512) for i in range(2)]
        tmb = [Buf("tmp0"), Buf("tmp1")]
        sT = self.carve(17920, 2048).bitcast(BF16).rearrange("p (c t) -> p c t", c=KC)
        stbuf = [Buf("sT%d" % c) for c in range(KC)]
        SG = [self.carve(19968 + i * 512, 512) for i in range(2)]
        sgb = [Buf("sg0"), Buf("sg1")]
        for c in range(KC):
            self.memset("pool", G[:, c, 0:32], 0.0, [gb[c]])
        for t in range(NT):
            sl = slice(t * TT, (t + 1) * TT)
            self.rmsnorm_tile(t, "mixg", layer, hTt, self.hb1, 2048)
            s1 = self.take_bank()
            self.held.add(s1)
            s2 = self.take_bank()
            self.held.add(s2)
            wst = {}

            def head(c):
                if c % 2 == 0:
                    wst["sv"] = self.wload(self.d_cwin[ia, c // 2], 2048)
                    wst["sg"] = self.wload(self.d_cwin[ia, 4 + c // 2], 2048)
                sv, sg_ = wst["sv"], wst["sg"]
                wv = self.ring[sv][:].rearrange("p (k c) -> p k c", k=KC)
                wg = self.ring[sg_][:].rearrange("p (k c) -> p k c", k=KC)
                bv = self.take_bank()
                for kc in range(KC):
                    self.mm(bv, self.banks[bv][:], wv[:, kc, (c % 2) * 128:(c % 2 + 1) * 128], hTt[:, kc, :],
                            kc == 0, kc == KC - 1, reads=[self.slotbuf[sv], self.hb1])
                bg = self.take_bank()
                for kc in range(KC):
                    self.mm(bg, self.banks[bg][:], wg[:, kc, (c % 2) * 128:(c % 2 + 1) * 128], hTt[:, kc, :],
                            kc == 0, kc == KC - 1, reads=[self.slotbuf[sg_], self.hb1])
                return bv, bg

            def tail(c, bv, bg):
                dg, dgbuf = DG[c % 2], dgb[c % 2]
                wb = CL["cwdw"] + ia * 31 * 8 + c
                wtaps = self.cst[:, wb:wb + 30 * 8 + 1:8]
                self.tt("dve", dg, self.ident_bf[:].unsqueeze(1).broadcast_to([128, 31, 128]),
                        wtaps.unsqueeze(2).broadcast_to([128, 31, 128]), ALU.mult, reads=[self.kb, self.cbuf], writes=[dgbuf])
                sg, sgbuf = SG[c % 2], sgb[c % 2]
                self.act(sg, self.banks[bg][:], AF.Sigmoid, reads=[self.bankbuf[bg], self.cbuf], writes=[sgbuf],
                         bias=self.cc("cbin", ia * 16 + 8 + c))
                self.stt(G[:, c, 30:542], self.banks[bv][:], self.cc("cbin", ia * 16 + c), sg, ALU.add, ALU.mult,
                         reads=[self.bankbuf[bv], sgbuf, self.cbuf], writes=[gb[c]])
                bc = self.take_bank()
                for k in range(31):
                    self.mm(bc, self.banks[bc][:], dg[:, k, :], G[:, c, k:k + 512], k == 0, k == 30, reads=[dgbuf, gb[c]])
                self.act(CO[:, c, :], self.banks[bc][:], AF.Identity, reads=[self.bankbuf[bc], self.cbuf], writes=[cob[c]],
                         bias=self.cc("cbdw", ia * 8 + c))
                self.cp("pool", G[:, c, 0:30], G[:, c, 512:542], [gb[c]], [gb[c]])
                ub, ubbuf = UB[c % 2], ubb[c % 2]
                sq, sqbuf = SQ[c % 2], sqb[c % 2]
                self.cp("dve", ub, CO[:, c, :], [cob[c]], [ubbuf])
                self.act(sq, CO[:, c, :], AF.Square, reads=[cob[c]], writes=[sqbuf])
                self.mm(s1, self.banks[s1][:], self.ones_bf[:], ub, c == 0, c == KC - 1, reads=[ubbuf, self.kb])
                self.mm(s2, self.banks[s2][:], self.ones_bf[:], sq, c == 0, c == KC - 1, reads=[sqbuf, self.kb])
            hb_ = head(0)
            for c in range(KC):
                cur = hb_
                if c + 1 < KC:
                    hb_ = head(c + 1)
                tail(c, *cur)
            self.ts("dve", mean, self.banks[s1][:], 1.0 / D, None, ALU.mult, None, reads=[self.bankbuf[s1]], writes=[stb])
            self.tt("dve", msq, mean, mean, ALU.mult, reads=[stb], writes=[stb])
            self.stt(msq, self.banks[s2][:], 1.0 / D, msq, ALU.mult, ALU.subtract, reads=[self.bankbuf[s2], stb], writes=[stb])
            self.act(lnv, msq, AF.Ln, reads=[stb, self.kb], writes=[stb], bias=self.eps_col)
            self.act(rstd, lnv, AF.Exp, reads=[stb], writes=[stb], scale=-0.5)
            self.held.discard(s1)
            self.held.discard(s2)
            for c in range(KC):
                tm, tmbuf = TMP[c % 2], tmb[c % 2]
                self.tt("dve", tm, CO[:, c, :], mean, ALU.subtract, reads=[cob[c], stb], writes=[tmbuf])
                self.tt("dve", tm, tm, rstd, ALU.mult, reads=[tmbuf, stb], writes=[tmbuf])
                self.act(sT[:, c, :], tm, AF.Silu, reads=[tmbuf, self.cbuf], writes=[stbuf[c]],
                         bias=self.cc("clnb", ia * 8 + c), scale=self.cc("clng", ia * 8 + c))
            wos = [self.wload(self.d_cwout[ia, j], 2048) for j in range(4)]
            for dc in range(KC):
                wo = self.ring[wos[dc // 2]][:].rearrange("p (k c) -> p k c", k=KC)
                bk = self.take_bank()
                for kc in range(KC):
                    self.mm(bk, self.banks[bk][:], wo[:, kc, (dc % 2) * 128:(dc % 2 + 1) * 128], sT[:, kc, :],
                            kc == 0, kc == KC - 1, reads=[self.slotbuf[wos[dc // 2]], stbuf[kc]])
                self.tt("dve", self.xT[:, dc, sl], self.banks[bk][:], self.xT[:, dc, sl], ALU.add,
                        reads=[self.bankbuf[bk], self.xbuf[dc][t]], writes=[self.xbuf[dc][t]])

    def gdn(self, layer):
        self.common_init()
        P = self.P
        hT = self.carve(0, 8192).bitcast(BF16).rearrange("p (c t) -> p c t", c=KC)
        for t in range(NT):
            self.rmsnorm_tile(t, "mixg", layer, hT[:, :, t * TT:(t + 1) * TT], self.hbuf[t], 8192)
        o = [9728]

        def al(n):
            a = self.carve(o[0], n)
            o[0] += n
            return a

        def bf4(n=256):
            return al(n).bitcast(BF16).rearrange("p (b c) -> p b c", b=4)

        gtok, gcs, eg, negeg, egl, gl, beta, negbeta = [al(32) for _ in range(8)]
        abt = al(64)
        gb_ = Buf("gates")
        qnT = [al(256).bitcast(BF16) for _ in range(4)]
        knT = [al(256).bitcast(BF16) for _ in range(4)]
        vtok = [bf4() for _ in range(4)]
        kdtok = [bf4() for _ in range(4)]
        zsT = [al(256).bitcast(BF16) for _ in range(4)]
        TTm = [bf4() for _ in range(4)]
        QKD = [bf4() for _ in range(4)]
        Pm = [bf4() for _ in range(4)]
        Qm = [bf4() for _ in range(4)]
        nm = lambda n: [Buf(n + str(i)) for i in range(4)]
        qnb, knb, vtb, kdb, zsb, ttb, qkb, pmb, qmb = [nm(n) for n in ("qn", "kn", "vt", "kd", "zs", "tt", "qk", "pm", "qm")]
        U = [al(516) for _ in range(2)]
        ub = [Buf("U0"), Buf("U1")]
        ACC = [al(512) for _ in range(2)]
        ab_ = [Buf("A0"), Buf("A1")]
        sqt = al(256).bitcast(BF16)
        lnv = al(512)
        rstd = al(512)
        sqb, lnb, rsb = Buf("sq"), Buf("ln"), Buf("rs")
        decT = al(512)
        decb = Buf("dec")
        tmp = al(512)
        tmpb = Buf("tmp")
        lhsg = [al(128) for _ in range(2)]
        lgb = [Buf("lg0"), Buf("lg1")]
        vbf = al(256).bitcast(BF16)
        vbb = Buf("vbf")
        vnew = bf4()
        vnb = Buf("vnew")
        onb_ = bf4()
        onbuf = Buf("on")
        Sf = al(512).rearrange("p (h c) -> p h c", h=4)
        Sb = bf4()
        sfb, sbb = Buf("Sf"), Buf("Sb")
        haloS = al(48).rearrange("p (c k) -> p c k", c=12)
        hsb = [Buf("hs%d" % i) for i in range(12)]
        ssq = al(8)
        ssb = Buf("ssq")
        wab = al(64).bitcast(BF16).rearrange("p (k c) -> p k c", k=KC)
        negA = al(8)
        assert o[0] <= self.ARENA, o[0]
        wabb = Buf("wab")
        P.op("pool", lambda e: e.dma_start(out=wab.rearrange("p k c -> p (k c)"), in_=self.d_gwab), writes=[wabb], dma_out=wabb)
        nab = Buf("negA")
        self.act(negA, self.cst[:, CL["galog"]:CL["galog"] + 8], AF.Exp, reads=[self.cbuf], writes=[nab])
        self.ts("dve", negA, negA, -1.0, None, ALU.mult, None, reads=[nab], writes=[nab])
        dtb = self.cst[:, CL["gdtb"]:CL["gdtb"] + 8]
        v4 = lambda a: a.rearrange("p (b h) -> p b h", b=4)
        bfbank = lambda bk: self.banks[bk][:].bitcast(BF16)[:, 0:512].rearrange("p (b c) -> p b c", b=4)
        fbank = lambda bk: self.banks[bk][:].rearrange("p (b c) -> p b c", b=4)
        qscale = 128.0 ** -0.5
        for g in range(2):
            self.memset("pool", Sf[:], 0.0, [sfb])
            self.memset("pool", Sb[:], 0.0, [sbb])
            for Q in range(NT):
                sl = slice(Q * TT, (Q + 1) * TT)
                hTt = hT[:, :, sl]
                hb = self.hbuf[Q]
                bk = self.take_bank()
                for blk in range(4):
                    for kc in range(KC):
                        self.mm(bk, self.banks[bk][:, blk * 16:(blk + 1) * 16], hTt[:, kc, blk * 128:(blk + 1) * 128], wab[:, kc, :],
                                kc == 0, kc == KC - 1, reads=[wabb, hb])
                self.cp("dve", abt, self.banks[bk][:, 0:64], [self.bankbuf[bk]], [gb_])
                ab3 = abt.rearrange("p (b c) -> p b c", b=4)
                self.act(v4(beta), ab3[:, :, 8:16], AF.Exp, reads=[gb_], writes=[gb_], scale=-1.0)
                self.act(beta, beta, AF.Ln, reads=[gb_, self.kb], writes=[gb_], bias=self.one_col)
                self.act(beta, beta, AF.Exp, reads=[gb_], writes=[gb_], scale=-1.0)
                self.ts("dve", negbeta, beta, -1.0, None, ALU.mult, None, reads=[gb_], writes=[gb_])
                self.tt("dve", v4(gtok), ab3[:, :, 0:8], dtb.unsqueeze(1).broadcast_to([128, 4, 8]), ALU.add, reads=[gb_, self.cbuf], writes=[gb_])
                self.act(gtok, gtok, AF.Exp, reads=[gb_], writes=[gb_])
                self.act(gtok, gtok, AF.Ln, reads=[gb_, self.kb], writes=[gb_], bias=self.one_col)
                self.tt("dve", v4(gtok), v4(gtok), negA.unsqueeze(1).broadcast_to([128, 4, 8]), ALU.mult, reads=[gb_, nab], writes=[gb_])
                bk = self.take_bank()
                self.mm(bk, self.banks[bk][:, 0:32], self.uincl_f[:], gtok, True, True, reads=[gb_, self.kb])
                b2 = self.take_bank()
                self.mm(b2, self.banks[b2][:, 0:32], self.ones_f[:], gtok, True, True, reads=[gb_, self.kb])
                self.cp("dve", gcs, self.banks[bk][:, 0:32], [self.bankbuf[bk]], [gb_])
                self.act(eg, gcs, AF.Exp, reads=[gb_], writes=[gb_])
                self.ts("dve", negeg, eg, -1.0, None, ALU.mult, None, reads=[gb_], writes=[gb_])
                self.tt("dve", egl, self.banks[b2][:, 0:32], gcs, ALU.subtract, reads=[self.bankbuf[b2], gb_], writes=[gb_])
                self.act(egl, egl, AF.Exp, reads=[gb_], writes=[gb_])
                self.act(gl, self.banks[b2][:, 0:32], AF.Exp, reads=[self.bankbuf[b2]], writes=[gb_])
                uic = [0]

                pair_slots = {}

                def proj(hh, which):
                    h = 4 * g + hh
                    if hh % 2 == 0 and which == 2:
                        for w2 in (2, 0, 1, 3):
                            pair_slots[w2] = self.wload(self.d_gwin[w2 * 4 + h // 2], 2048)
                    sw = pair_slots[which]
                    w_ = self.ring[sw][:].rearrange("p (k c) -> p k c", k=KC)
                    bk = self.take_bank()
                    self.held.add(bk)
                    for kc in range(KC):
                        self.mm(bk, self.banks[bk][:], w_[:, kc, (h % 2) * 128:(h % 2 + 1) * 128], hTt[:, kc, :],
                                kc == 0, kc == KC - 1, reads=[self.slotbuf[sw], hb])
                    return bk

                def chain(hh, which, bk):
                    h = 4 * g + hh
                    if which == 3:
                        self.act(zsT[hh], self.banks[bk][:], AF.Silu, reads=[self.bankbuf[bk]], writes=[zsb[hh]])
                        self.held.discard(bk)
                        return
                    ci = which * 4 + hh
                    chunk = which * 8 + h
                    ui = uic[0]
                    uic[0] += 1
                    Ut, Ub = U[ui % 2], ub[ui % 2]
                    At, Ab = ACC[ui % 2], ab_[ui % 2]
                    if Q == 0:
                        self.memset("pool", Ut[:, 0:3], 0.0, [Ub])
                    else:
                        self.cp("pool", Ut[:, 0:3], haloS[:, ci, 0:3], [hsb[ci]], [Ub])
                    self.act(Ut[:, 3:515], self.banks[bk][:], AF.Copy, reads=[self.bankbuf[bk]], writes=[Ub])
                    self.held.discard(bk)
                    self.cp("pool", haloS[:, ci, 0:3], Ut[:, 512:515], [Ub], [hsb[ci]])
                    wc = lambda k, chunk=chunk: self.cc("gconv", k * 24 + chunk)
                    self.act(At, Ut[:, 0:512], AF.Copy, reads=[Ub, self.cbuf], writes=[Ab], scale=wc(0))
                    for k in range(1, 4):
                        self.stt(At, Ut[:, k:k + 512], wc(k), At, ALU.mult, ALU.add, reads=[Ub, self.cbuf, Ab], writes=[Ab])
                    self.act(At, At, AF.Silu, reads=[Ab], writes=[Ab])
                    if which < 2:
                        pend_norm.append((hh, which, At, Ab))
                        return
                    if which < 2:
                        self.act(sqt, At, AF.Square, reads=[Ab], writes=[sqb])
                        b2 = self.take_bank()
                        self.mm(b2, self.banks[b2][:], self.ones_bf[:], sqt, True, True, reads=[sqb, self.kb])
                        self.act(lnv, self.banks[b2][:], AF.Ln, reads=[self.bankbuf[b2], self.kb], writes=[lnb], bias=self.eps_col)
                        self.act(rstd, lnv, AF.Exp, reads=[lnb], writes=[rsb], scale=-0.5)
                        if which == 0:
                            self.stt(qnT[hh], At, qscale, rstd, ALU.mult, ALU.mult, reads=[Ab, rsb], writes=[qnb[hh]])
                        else:
                            self.tt("dve", knT[hh], At, rstd, ALU.mult, reads=[Ab, rsb], writes=[knb[hh]])
                    else:
                        self.cp("dve", vbf, At, [Ab], [vbb])
                        bt = self.take_bank()
                        for blk in range(4):
                            self.tr(bt, bfbank(bt)[:, blk, :], vbf[:, blk * 128:(blk + 1) * 128], self.ident_bf[:], reads=[vbb, self.kb])
                        self.cp("act", vtok[hh], bfbank(bt), [self.bankbuf[bt]], [vtb[hh]])

                pend_norm = []

                def norms():
                    while pend_norm:
                        hh_, which_, At, Ab = pend_norm.pop(0)
                        self.act(sqt, At, AF.Square, reads=[Ab], writes=[sqb])
                        b2 = self.take_bank()
                        self.mm(b2, self.banks[b2][:], self.ones_bf[:], sqt, True, True, reads=[sqb, self.kb])
                        self.act(lnv, self.banks[b2][:], AF.Ln, reads=[self.bankbuf[b2], self.kb], writes=[lnb], bias=self.eps_col)
                        self.act(rstd, lnv, AF.Exp, reads=[lnb], writes=[rsb], scale=-0.5)
                        if which_ == 0:
                            self.stt(qnT[hh_], At, qscale, rstd, ALU.mult, ALU.mult, reads=[Ab, rsb], writes=[qnb[hh_]])
                        else:
                            self.tt("dve", knT[hh_], At, rstd, ALU.mult, reads=[Ab, rsb], writes=[knb[hh_]])

                units = [(hh, which) for hh in range(4) for which in (2, 0, 1, 3)]
                nextbank = proj(*units[0])
                for ui_, (hh, which) in enumerate(units):
                    h = 4 * g + hh
                    curbank = nextbank
                    if ui_ + 1 < len(units):
                        nextbank = proj(*units[ui_ + 1])
                    chain(hh, which, curbank)
                    if which != 3:
                        continue
                    norms()
                    bt = self.take_bank()
                    for blk in range(4):
                        self.tr(bt, bfbank(bt)[:, blk, :], knT[hh][:, blk * 128:(blk + 1) * 128], self.ident_bf[:], reads=[knb[hh], self.kb])
                    for blk in range(4):
                        self.act(kdtok[hh][:, blk, :], bfbank(bt)[:, blk, :], AF.Copy, reads=[self.bankbuf[bt], gb_], writes=[kdb[hh]],
                                 scale=egl[:, blk * 8 + h:blk * 8 + h + 1])
                    bd = self.take_bank()
                    for blk in range(4):
                        lg, lgbuf = lhsg[blk % 2], lgb[blk % 2]
                        self.act(lg, self.lstrict_f[:], AF.Copy, reads=[self.kb, gb_], writes=[lgbuf],
                                 scale=gtok[:, blk * 8 + h:blk * 8 + h + 1])
                        self.mm(bd, self.banks[bd][:, blk * 128:(blk + 1) * 128], lg, self.uincl_f[:], True, True, reads=[lgbuf, self.kb])
                    self.act(decT, self.banks[bd][:], AF.Exp, reads=[self.bankbuf[bd]], writes=[decb])
                    d3 = decT.rearrange("p (b c) -> p b c", b=4)
                    t3 = tmp.rearrange("p (b c) -> p b c", b=4)
                    bkk = self.take_bank()
                    for blk in range(4):
                        ks = knT[hh][:, blk * 128:(blk + 1) * 128]
                        self.mm(bkk, self.banks[bkk][:, blk * 128:(blk + 1) * 128], ks, ks, True, True, reads=[knb[hh]])
                    self.tt("dve", tmp, self.banks[bkk][:], decT, ALU.mult, reads=[self.bankbuf[bkk], decb], writes=[tmpb])
                    for blk in range(4):
                        self.stt(Pm[hh][:, blk, :], t3[:, blk, :], negbeta[:, blk * 8 + h:blk * 8 + h + 1], self.msu_f[:], ALU.mult, ALU.mult,
                                 reads=[tmpb, gb_, self.kb], writes=[pmb[hh]])
                    bt = self.take_bank()
                    for blk in range(4):
                        self.tr(bt, bfbank(bt)[:, blk, :], Pm[hh][:, blk, :], self.ident_bf[:], reads=[pmb[hh], self.kb])
                    self.cp("act", Qm[hh], bfbank(bt), [self.bankbuf[bt]], [qmb[hh]])
                    self.tt("pool", TTm[hh], Pm[hh], self.ident_bf[:].unsqueeze(1).broadcast_to([128, 4, 128]), ALU.add,
                            reads=[pmb[hh], self.kb], writes=[ttb[hh]])
                    bq = self.take_bank()
                    for blk in range(4):
                        self.mm(bq, self.banks[bq][:, blk * 128:(blk + 1) * 128], knT[hh][:, blk * 128:(blk + 1) * 128],
                                qnT[hh][:, blk * 128:(blk + 1) * 128], True, True, reads=[knb[hh], qnb[hh]])
                    self.tt("dve", tmp, self.banks[bq][:], decT, ALU.mult, reads=[self.bankbuf[bq], decb], writes=[tmpb])
                    self.tt("dve", QKD[hh], t3, self.uincl_f[:].unsqueeze(1).broadcast_to([128, 4, 128]), ALU.mult,
                            reads=[tmpb, self.kb], writes=[qkb[hh]])
                for lev in range(1, 7):
                    for hh in range(4):
                        bq = self.take_bank()
                        for blk in range(4):
                            self.mm(bq, self.banks[bq][:, blk * 128:(blk + 1) * 128], Pm[hh][:, blk, :], Qm[hh][:, blk, :], True, True,
                                    reads=[pmb[hh], qmb[hh]])
                        if lev < 6:
                            bp = self.take_bank()
                            for blk in range(4):
                                self.mm(bp, self.banks[bp][:, blk * 128:(blk + 1) * 128], Qm[hh][:, blk, :], Pm[hh][:, blk, :], True, True,
                                        reads=[pmb[hh], qmb[hh]])
                            self.cp("act", Pm[hh], fbank(bp), [self.bankbuf[bp]], [pmb[hh]])
                        self.cp("dve", Qm[hh], fbank(bq), [self.bankbuf[bq]], [qmb[hh]])
                        br = self.take_bank()
                        for blk in range(4):
                            self.mm(br, self.banks[br][:, blk * 128:(blk + 1) * 128], Qm[hh][:, blk, :], TTm[hh][:, blk, :], True, True,
                                    reads=[qmb[hh], ttb[hh]])
                        self.tt("dve", TTm[hh], fbank(br), TTm[hh], ALU.add, reads=[self.bankbuf[br], ttb[hh]], writes=[ttb[hh]])
                rbuf = vbf.rearrange("p (b c) -> p b c", b=4)
                otok = decT.rearrange("p (b c) -> p b c", b=4)
                o2s = tmp.rearrange("p (b c) -> p b c", b=4)
                for blk in range(4):
                    bs = slice(blk * 128, (blk + 1) * 128)
                    col = lambda a, hh: a[:, blk * 8 + 4 * g + hh:blk * 8 + 4 * g + hh + 1]
                    bks = self.take_bank()
                    for hh in range(4):
                        self.mm(bks, self.banks[bks][:, hh * 128:(hh + 1) * 128], knT[hh][:, bs], Sb[:, hh, :], True, True,
                                reads=[knb[hh], sbb])
                    for hh in range(4):
                        self.stt(rbuf[:, hh, :], self.banks[bks][:, hh * 128:(hh + 1) * 128], col(negeg, hh), vtok[hh][:, blk, :],
                                 ALU.mult, ALU.add, reads=[self.bankbuf[bks], gb_, vtb[hh]], writes=[vbb])
                    bvn = self.take_bank()
                    for hh in range(4):
                        self.mm(bvn, self.banks[bvn][:, hh * 128:(hh + 1) * 128], TTm[hh][:, blk, :], rbuf[:, hh, :], True, True,
                                reads=[ttb[hh], vbb])
                    for hh in range(4):
                        self.act(vnew[:, hh, :], self.banks[bvn][:, hh * 128:(hh + 1) * 128], AF.Copy, reads=[self.bankbuf[bvn], gb_],
                                 writes=[vnb], scale=col(beta, hh))
                    bo1 = self.take_bank()
                    for hh in range(4):
                        self.mm(bo1, self.banks[bo1][:, hh * 128:(hh + 1) * 128], qnT[hh][:, bs], Sb[:, hh, :], True, True,
                                reads=[qnb[hh], sbb])
                    bo2 = self.take_bank()
                    for hh in range(4):
                        self.mm(bo2, self.banks[bo2][:, hh * 128:(hh + 1) * 128], QKD[hh][:, blk, :], vnew[:, hh, :], True, True,
                                reads=[qkb[hh], vnb])
                    self.cp("act", tmp, self.banks[bo2][:], [self.bankbuf[bo2]], [tmpb])
                    for hh in range(4):
                        self.stt(otok[:, hh, :], self.banks[bo1][:, hh * 128:(hh + 1) * 128], col(eg, hh), o2s[:, hh, :],
                                 ALU.mult, ALU.add, reads=[self.bankbuf[bo1], gb_, tmpb], writes=[decb])
                    bsu = self.take_bank()
                    for hh in range(4):
                        self.mm(bsu, self.banks[bsu][:, hh * 128:(hh + 1) * 128], kdtok[hh][:, blk, :], vnew[:, hh, :], True, True,
                                reads=[kdb[hh], vnb])
                    for hh in range(4):
                        self.stt(Sf[:, hh, :], Sf[:, hh, :], col(gl, hh), self.banks[bsu][:, hh * 128:(hh + 1) * 128],
                                 ALU.mult, ALU.add, reads=[self.bankbuf[bsu], gb_, sfb], writes=[sfb])
                    self.cp("act", Sb, Sf, [sfb], [sbb])
                    self.tt("pool", tmp, decT, decT, ALU.mult, reads=[decb], writes=[tmpb])
                    P.op("dve", lambda e: e.tensor_reduce(out=ssq[:, 0:4], in_=o2s, axis=AX.X, op=ALU.add), reads=[tmpb], writes=[ssb])
                    self.act(ssq[:, 0:4], ssq[:, 0:4], AF.Ln, reads=[ssb, self.kb], writes=[ssb], bias=self.eps_col, scale=1.0 / 128)
                    self.act(ssq[:, 0:4], ssq[:, 0:4], AF.Exp, reads=[ssb], writes=[ssb], scale=-0.5)
                    for hh in range(4):
                        self.act(onb_[:, hh, :], otok[:, hh, :], AF.Copy, reads=[decb, ssb], writes=[onbuf], scale=ssq[:, hh:hh + 1])
                    bt = self.take_bank()
                    for hh in range(4):
                        self.tr(bt, bfbank(bt)[:, hh, :], onb_[:, hh, :], self.ident_bf[:], reads=[onbuf, self.kb])
                    for hh in range(4):
                        self.stt(zsT[hh][:, bs], bfbank(bt)[:, hh, :], self.cc("gog"), zsT[hh][:, bs], ALU.mult, ALU.mult,
                                 reads=[self.bankbuf[bt], self.cbuf, zsb[hh]], writes=[zsb[hh]])
                wos = [self.wload(self.d_gwout[j], 2048) for j in range(4)]
                for dc in range(KC):
                    wo = self.ring[wos[dc // 2]][:].rearrange("p (k c) -> p k c", k=KC)
                    bk = self.take_bank()
                    for hh in range(4):
                        self.mm(bk, self.banks[bk][:], wo[:, 4 * g + hh, (dc % 2) * 128:(dc % 2 + 1) * 128], zsT[hh],
                                hh == 0, hh == 3, reads=[self.slotbuf[wos[dc // 2]], zsb[hh]])
                    self.tt("dve", self.xT[:, dc, sl], self.banks[bk][:], self.xT[:, dc, sl], ALU.add,
                            reads=[self.bankbuf[bk], self.xbuf[dc][Q]], writes=[self.xbuf[dc][Q]])

    def fox(self, layer):
        self.common_init()
        P = self.P
        HD = 128
        hT = self.carve(0, 8192).bitcast(BF16).rearrange("p (c t) -> p c t", c=KC)
        for t in range(NT):
            self.rmsnorm_tile(t, "mixg", layer, hT[:, :, t * TT:(t + 1) * TT], self.hbuf[t], 8192)
        o = 9728
        knT = self.carve(o, 4096).bitcast(BF16).rearrange("p (h t) -> p h t", h=4); o += 4096
        knb = [[Buf("kn") for _ in range(NT)] for _ in range(4)]
        vtok = self.carve(o, 4096).bitcast(BF16).rearrange("p (b c) -> p b c", b=16); o += 4096
        vb = [Buf("v%d" % b) for b in range(16)]
        qT = self.carve(o, 1024).bitcast(BF16).rearrange("p (h t) -> p h t", h=4); o += 1024
        qb = [Buf("q%d" % h) for h in range(4)]
        oT = self.carve(o, 1024).bitcast(BF16).rearrange("p (h t) -> p h t", h=4); o += 1024
        ob = [Buf("o%d" % h) for h in range(4)]
        NP = 3
        pT = [self.carve(o + i * 256, 256).bitcast(BF16) for i in range(NP)]; o += NP * 256
        pb = [Buf("p%d" % i) for i in range(NP)]
        raw = self.carve(o, 512); o += 512
        sqt = self.carve(o, 256).bitcast(BF16); o += 256
        lnv = self.carve(o, 512); o += 512
        rstd = self.carve(o, 512); o += 512
        rawb, sqb, lnb, rsb = Buf("raw"), Buf("sq"), Buf("ln"), Buf("rs")
        rden = self.carve(o, 512); o += 512
        rdb = Buf("rden")
        erow = self.carve(o, 512); o += 512
        cT = self.carve(o, 128).rearrange("p (b h) -> p b h", b=16); o += 128
        cmidb = self.carve(o, 32).rearrange("p (q h) -> p q h", q=4); o += 32
        biasA = self.carve(o, 128).rearrange("p (b h) -> p b h", b=16); o += 128
        small = self.carve(o, 32); o += 32
        wf = self.carve(o, 32).bitcast(BF16).rearrange("p (k c) -> p k c", k=KC); o += 32
        assert o <= self.ARENA, o
        carry = small[:, 0:8]
        xs = erow[:, 0:32]
        tots = erow[:, 32:64]
        cb = Buf("cstuff")
        wfb = Buf("wf")
        P.op("pool", lambda e: e.dma_start(out=wf.rearrange("p k c -> p (k c)"), in_=self.d_fwf), writes=[wfb], dma_out=wfb)
        self.memset("pool", carry, 0.0, [cb])
        scale = float(HD) ** -0.5
        for g in range(2):
            for Q in range(NT):
                sl = slice(Q * TT, (Q + 1) * TT)
                hTt = hT[:, :, sl]
                hb = self.hbuf[Q]
                if g == 0:
                    bk = self.take_bank()
                    for blk in range(4):
                        for kc in range(KC):
                            self.mm(bk, self.banks[bk][:, blk * 8:(blk + 1) * 8], hTt[:, kc, blk * 128:(blk + 1) * 128], wf[:, kc, :],
                                    kc == 0, kc == KC - 1, reads=[wfb, hb])
                    x3 = xs.rearrange("p (b h) -> p b h", b=4)
                    self.tt("dve", x3, self.banks[bk][:, 0:32].rearrange("p (b h) -> p b h", b=4),
                            self.cst[:, CL["fbfb"]:CL["fbfb"] + 8].unsqueeze(1).broadcast_to([128, 4, 8]), ALU.add,
                            reads=[self.bankbuf[bk], self.cbuf], writes=[cb])
                    self.act(xs, xs, AF.Exp, reads=[cb], writes=[cb], scale=-1.0)
                    self.act(xs, xs, AF.Ln, reads=[cb, self.kb], writes=[cb], bias=self.one_col)
                    b1 = self.take_bank()
                    self.mm(b1, self.banks[b1][:, 0:32], self.uincl_f[:], xs, True, True, reads=[cb, self.kb])
                    b2 = self.take_bank()
                    self.mm(b2, self.banks[b2][:, 0:32], self.ones_f[:], xs, True, True, reads=[cb, self.kb])
                    self.cp("dve", tots, self.banks[b2][:, 0:32], [self.bankbuf[b2]], [cb])
                    for blk in range(4):
                        self.tt("dve", cT[:, 4 * Q + blk, :], self.banks[b1][:, blk * 8:(blk + 1) * 8], carry, ALU.add,
                                reads=[self.bankbuf[b1], cb], writes=[cb])
                        self.tt("dve", carry, carry, tots[:, blk * 8:(blk + 1) * 8], ALU.add, reads=[cb], writes=[cb])
                        if blk == 1:
                            self.cp("dve", cmidb[:, Q, :], carry, [cb], [cb])
                nj = 4 * Q + 4
                DBG = int(os.environ.get("FOXDBG", "9"))
                if DBG <= 1:
                    continue
                self.tt("dve", biasA[:, 0:nj, :], cT[:, 0:nj, :], cmidb[:, Q:Q + 1, :].broadcast_to([128, nj, 8]), ALU.subtract,
                        reads=[cb], writes=[cb])
                wsl = {}

                def fproj(which, hh):
                    hp, h2 = hh // 2, hh % 2
                    if h2 == 0:
                        wsl[(which, hp)] = self.wload(self.d_fwin[which * 4 + 2 * g + hp], 2048)
                    sw = wsl[(which, hp)]
                    w_ = self.ring[sw][:].rearrange("p (k c) -> p k c", k=KC)
                    bk = self.take_bank()
                    self.held.add(bk)
                    for kc in range(KC):
                        self.mm(bk, self.banks[bk][:], w_[:, kc, h2 * 128:(h2 + 1) * 128], hTt[:, kc, :],
                                kc == 0, kc == KC - 1, reads=[self.slotbuf[sw], hb])
                    return bk

                def fnorm(which, hh, bk):
                    self.cp("dve", raw, self.banks[bk][:], [self.bankbuf[bk]], [rawb])
                    self.held.discard(bk)
                    self.act(sqt, raw, AF.Square, reads=[rawb], writes=[sqb])
                    b2 = self.take_bank()
                    self.mm(b2, self.banks[b2][:], self.ones_bf[:], sqt, True, True, reads=[sqb, self.kb])
                    self.act(lnv, self.banks[b2][:], AF.Ln, reads=[self.bankbuf[b2], self.kb], writes=[lnb],
                             bias=self.eps_col, scale=1.0 / HD)
                    self.act(rstd, lnv, AF.Exp, reads=[lnb], writes=[rsb], scale=-0.5)
                    if which == 0:
                        self.stt(qT[:, hh, :], raw, self.cc("fqg"), rstd, ALU.mult, ALU.mult,
                                 reads=[rawb, rsb, self.cbuf], writes=[qb[hh]])
                    else:
                        self.stt(knT[:, hh, sl], raw, self.cc("fkg"), rstd, ALU.mult, ALU.mult,
                                 reads=[rawb, rsb, self.cbuf], writes=[knb[hh][Q]])
                funits = [(which, hh) for which in range(2) for hh in range(4)]
                nb_ = fproj(*funits[0])
                for fi, (which, hh) in enumerate(funits):
                    cb_ = nb_
                    if fi + 1 < len(funits):
                        nb_ = fproj(*funits[fi + 1])
                    fnorm(which, hh, cb_)
                for hp in range(2):
                    sw = self.wload(self.d_fwin[8 + 2 * g + hp], 2048)
                    w_ = self.ring[sw][:].rearrange("p (k c) -> p k c", k=KC)
                    for blk in range(4):
                        bk = self.take_bank()
                        for kc in range(KC):
                            self.mm(bk, self.banks[bk][:, 0:256], hTt[:, kc, blk * 128:(blk + 1) * 128], w_[:, kc, :],
                                    kc == 0, kc == KC - 1, reads=[self.slotbuf[sw], hb])
                        self.act(vtok[:, 4 * Q + blk, hp * 256:(hp + 1) * 256], self.banks[bk][:, 0:256], AF.Copy,
                                 reads=[self.bankbuf[bk]], writes=[vb[4 * Q + blk]])
                pi = 0
                if DBG <= 2:
                    continue
                for hh in range(4):
                    h = 4 * g + hh
                    bo = self.take_bank()
                    self.held.add(bo)
                    bd = self.take_bank()
                    self.held.add(bd)
                    def s_stage(j):
                        off = max(0, (j - 4 * Q) * 128)
                        bs = self.take_bank()
                        self.mm(bs, self.banks[bs][:, off:512], knT[:, hh, j * 128:(j + 1) * 128], qT[:, hh, off:512],
                                True, True, reads=[knb[hh][j // 4], qb[hh]])
                        return bs

                    def pv_stage(j, bs, pi_):
                        off = max(0, (j - 4 * Q) * 128)
                        p_, pbuf = pT[pi_ % NP], pb[pi_ % NP]
                        self.act(p_[:, off:512], self.banks[bs][:, off:512], AF.Exp, reads=[self.bankbuf[bs], cb], writes=[pbuf],
                                 bias=biasA[:, j, h:h + 1], scale=scale)
                        if j >= 4 * Q:
                            self.tt("pool", p_[:, off:off + 128], p_[:, off:off + 128], self.uincl_bf[:], ALU.mult,
                                    reads=[pbuf, self.kb], writes=[pbuf])
                        return p_, pbuf, off

                    def acc_stage(j, p_, pbuf, off):
                        self.mm(bo, self.banks[bo][:, off:512], vtok[:, j, hh * 128:(hh + 1) * 128], p_[:, off:512],
                                j == 0, j == nj - 1, reads=[vb[j], pbuf])
                        self.mm(bd, self.banks[bd][:, off:512], self.ones_bf[:], p_[:, off:512],
                                j == 0, j == nj - 1, reads=[pbuf, self.kb])
                    bs_next = s_stage(0)
                    for j in range(nj):
                        bs_cur = bs_next
                        pp = pv_stage(j, bs_cur, pi)
                        pi += 1
                        if j + 1 < nj:
                            bs_next = s_stage(j + 1)
                        acc_stage(j, *pp)
                    P.op("dve", lambda e, bd=bd: e.reciprocal(out=rden, in_=self.banks[bd][:]), reads=[self.bankbuf[bd]], writes=[rdb])
                    self.tt("dve", oT[:, hh, :], self.banks[bo][:], rden, ALU.mult, reads=[self.bankbuf[bo], rdb], writes=[ob[hh]])
                    self.held.discard(bo)
                    self.held.discard(bd)
                wos = [self.wload(self.d_fwout[j], 2048) for j in range(4)]
                for dc in range(KC):
                    wo = self.ring[wos[dc // 2]][:].rearrange("p (k c) -> p k c", k=KC)
                    bk = self.take_bank()
                    for hh in range(4):
                        self.mm(bk, self.banks[bk][:], wo[:, 4 * g + hh, (dc % 2) * 128:(dc % 2 + 1) * 128], oT[:, hh, :],
                                hh == 0, hh == 3, reads=[self.slotbuf[wos[dc // 2]], ob[hh]])
                    self.tt("dve", self.xT[:, dc, sl], self.banks[bk][:], self.xT[:, dc, sl], ALU.add,
                            reads=[self.bankbuf[bk], self.xbuf[dc][Q]], writes=[self.xbuf[dc][Q]])


ALL_STAGES = [("conf", 0), ("ffn", 0), ("gdn", 1), ("ffn", 1), ("fox", 2), ("ffn", 2), ("conf", 3), ("ffn", 3)]


def build_nc(stages):
    nc = bass.Bass("TRN2", target_bir_lowering=False)
    k = K(nc, stages)
    k.build()
    return nc, k.used_inputs


def prep_shared(inp):
    sh = {}
    sh["cst"] = pack_consts(inp)
    sh["cwin"] = np.stack([tile_w(inp["conv_w_in"][i]).reshape(8, 128, 2048) for i in range(2)])
    sh["cwout"] = np.stack([tile_w(inp["conv_w_out"][i]).reshape(4, 128, 2048) for i in range(2)])
    gw = np.asarray(inp["gdn_w_in"][0], np.float32)
    sh["gwin"] = tile_w(gw[:, :4096]).reshape(16, 128, 2048)
    sh["gwab"] = np.ascontiguousarray(gw[:, 4096:4112].reshape(8, 128, 16).transpose(1, 0, 2)).reshape(128, 128)
    sh["gwout"] = tile_w(inp["gdn_w_out"][0]).reshape(4, 128, 2048)
    fw = np.asarray(inp["fox_w_in"][0], np.float32)
    sh["fwin"] = tile_w(fw[:, :3072]).reshape(12, 128, 2048)
    sh["fwf"] = np.ascontiguousarray(fw[:, 3072:3080].reshape(8, 128, 8).transpose(1, 0, 2)).reshape(128, 64)
    sh["fwout"] = tile_w(inp["fox_w_out"][0]).reshape(4, 128, 2048)
    sh["wup"] = np.stack([tile_w(inp["ffn_w_up"][l]).reshape(22, 128, 2048) for l in range(4)])
    sh["wdn"] = np.stack([tile_wk(inp["ffn_w_down"][l]).reshape(11, 128, 2048) for l in range(4)])
    return sh


def run(inp, stages, ncores=8, trace=False):
    x = np.asarray(inp["x"], np.float32)
    sh = prep_shared(inp)
    nc, used = build_nc(stages)
    sh = {k_: v for k_, v in sh.items() if k_ in used}
    in_maps = []
    for b in range(ncores):
        m = dict(sh)
        m["xT"] = np.ascontiguousarray(x[b].T).reshape(KC, 128, S)
        in_maps.append(m)
    res = run_bass_kernel_spmd(nc, in_maps, core_ids=list(range(ncores)), trace=trace)
    out = np.stack([np.asarray(r["yT"], np.float32).reshape(D, S).T for r in res.results])
    return out, res


def kernel(**inputs):
    out, _ = run(inputs, ALL_STAGES, ncores=8)
    return out.astype(np.float32)
```

```python
import contextlib
import os
import numpy as np
import concourse.bass as bass
import concourse.mybir as mybir
from concourse.bass_utils import run_bass_kernel_spmd

F32 = mybir.dt.float32
BF16 = mybir.dt.bfloat16
AF = mybir.ActivationFunctionType
ALU = mybir.AluOpType
AX = mybir.AxisListType

S = 2048
D = 1024
TT = 512
NT = 4
KC = 8
FF = 2816
EPS = 1e-6
ENGS = ["pe", "act", "dve", "pool", "sp"]
BLOCK_ATTR = {"pe": "tensor", "act": "scalar", "dve": "vector", "pool": "gpsimd", "sp": "sync"}


class Buf:
    __slots__ = ("name", "last_w", "readers", "sem", "dma_cnt", "excl")

    def __init__(self, name, excl=False):
        self.name = name
        self.excl = excl
        self.last_w = None
        self.readers = []
        self.sem = None
        self.dma_cnt = 0


class Op:
    __slots__ = ("eng", "fn", "waits", "signal", "idx", "dma_buf", "clock", "sigcnt")

    def __init__(self, eng, fn, idx, dma_buf=None):
        self.eng = eng
        self.fn = fn
        self.idx = idx
        self.waits = []
        self.signal = False
        self.dma_buf = dma_buf
        self.clock = None
        self.sigcnt = None


class Prog:
    def __init__(self, nc):
        self.nc = nc
        self.ops = {e: [] for e in ENGS}
        self.obs = {e: {} for e in ENGS}
        self.dma_bufs = []
        self.pending = {e: [] for e in ENGS}

    def barrier(self):
        toks = [("eng", e, len(self.ops[e]) - 1) for e in ENGS if self.ops[e] and self.ops[e][-1].dma_buf is None]
        for e in ENGS:
            if self.ops[e] and self.ops[e][-1].dma_buf is not None:
                for o in reversed(self.ops[e]):
                    if o.dma_buf is None:
                        toks.append(("eng", e, o.idx))
                        break
        for e in ENGS:
            self.pending[e] = list(toks)

    def _need(self, op, tok):
        e = op.eng
        if tok[0] == "eng":
            _, se, si = tok
            if self.obs[e].get(se, -1) >= si:
                return
            src = self.ops[se][si]
            src.signal = True
            op.waits.append(tok)
            self.obs[e][se] = si
            if src.clock:
                for k, v in src.clock.items():
                    if self.obs[e].get(k, -1) < v:
                        self.obs[e][k] = v
        else:
            _, b, cnt = tok
            key = ("dma", id(b))
            if self.obs[e].get(key, -1) >= cnt:
                return
            op.waits.append(tok)
            self.obs[e][key] = cnt

    def op(self, eng, fn, reads=(), writes=(), dma_out=None, nobarrier=False):
        lst = self.ops[eng]
        o = Op(eng, fn, len(lst), dma_buf=dma_out)
        if any(r.excl for r in reads):
            writes = list(writes) + [r for r in reads if r.excl and r not in writes]
            reads = [r for r in reads if not r.excl]
        best = {}
        for r in reads:
            t = r.last_w
            if t is not None:
                k = t[1] if t[0] == "eng" else ("dma", id(t[1]))
                if k not in best or best[k][2] < t[2]:
                    best[k] = t
        for w in writes:
            for t in [w.last_w] + w.readers:
                if t is not None:
                    k = t[1] if t[0] == "eng" else ("dma", id(t[1]))
                    if k not in best or best[k][2] < t[2]:
                        best[k] = t
        if self.pending[eng] and not nobarrier:
            for t in self.pending[eng]:
                if t[1] == eng:
                    continue
                k = t[1]
                if k not in best or best[k][2] < t[2]:
                    best[k] = t
            self.pending[eng] = []
        for t in best.values():
            if eng == "pe" and t[0] == "eng" and t[1] == "pe":
                continue
            self._need(o, t)
        if dma_out is not None:
            if dma_out.sem is None:
                self.dma_bufs.append(dma_out)
                dma_out.sem = True
            dma_out.dma_cnt += 1
            tok = ("dma", dma_out, dma_out.dma_cnt)
        else:
            tok = ("eng", eng, o.idx)
        o.clock = dict(self.obs[eng])
        for r in reads:
            r.readers.append(tok)
        for w in writes:
            w.last_w = tok
            w.readers = []
        lst.append(o)
        return tok

    def emit(self, final_waits=()):
        nc = self.nc
        CH = 2000
        with contextlib.ExitStack() as st:
            for e in ENGS:
                c = 0
                for o in self.ops[e]:
                    if o.signal and o.dma_buf is None:
                        o.sigcnt = c
                        c += 1
                    else:
                        o.sigcnt = c - 1
            nsig = {e: sum(1 for o in self.ops[e] if o.signal and o.dma_buf is None) for e in ENGS}
            esem = {e: [st.enter_context(nc.semaphore("s_%s%d" % (e, i))) for i in range(max(1, (nsig[e] + CH - 1) // CH))]
                    for e in ENGS}
            for i, b in enumerate(self.dma_bufs):
                b.sem = st.enter_context(nc.semaphore("d%d" % i))
            block = st.enter_context(nc.Block())
            for e in ENGS:
                ops = self.ops[e]
                fw = [b for (fe, b) in final_waits if fe == e]
                if not ops and not fw:
                    continue

                def body(eng, ops=ops, e=e, fw=fw):
                    for o in ops:
                        for t in o.waits:
                            if t[0] == "eng":
                                k = self.ops[t[1]][t[2]].sigcnt
                                eng.wait_ge(esem[t[1]][k // CH], k % CH + 1)
                            else:
                                eng.wait_ge(t[1].sem, 16 * t[2])
                        ins = o.fn(eng)
                        if o.dma_buf is not None:
                            ins.then_inc(o.dma_buf.sem, 16)
                        elif o.signal:
                            ins.then_inc(esem[e][o.sigcnt // CH], 1)
                    for b in fw:
                        eng.wait_ge(b.sem, 16 * b.dma_cnt)

                getattr(block, BLOCK_ATTR[e])(body)


def _cst_layout():
    lay = {}
    off = 0
    for name, n in [("mixg", 32), ("ffng", 32), ("cbin", 32), ("cwdw", 2 * 31 * 8), ("cbdw", 16),
                    ("clng", 16), ("clnb", 16), ("gconv", 96), ("gog", 1), ("fqg", 1), ("fkg", 1),
                    ("fwdw", 4 * 3 * 44), ("fbf", 1), ("galog", 8), ("gdtb", 8), ("fbfb", 8)]:
        lay[name] = off
        off += n
    return lay, off


CL, NCST = _cst_layout()


def _cols(v):
    v = np.asarray(v, dtype=np.float32).reshape(-1, 128)
    return v.T


def pack_consts(inp):
    c = np.zeros((128, NCST), np.float32)

    def put(name, arr):
        arr = np.asarray(arr, np.float32)
        c[:, CL[name]:CL[name] + arr.shape[1]] = arr

    put("mixg", _cols(inp["mix_norm_g"].reshape(-1)))
    put("ffng", _cols(inp["ffn_norm_g"].reshape(-1)))
    put("cbin", _cols(inp["conv_b_in"].reshape(-1)))
    put("cwdw", _cols(inp["conv_w_dw"].reshape(-1)))
    put("cbdw", _cols(inp["conv_b_dw"].reshape(-1)))
    put("clng", _cols(inp["conv_ln_g"].reshape(-1)))
    put("clnb", _cols(inp["conv_ln_b"].reshape(-1)))
    put("gconv", _cols(inp["gdn_conv_w"].reshape(-1)))
    put("gog", _cols(inp["gdn_o_norm_g"].reshape(-1)))
    put("fqg", _cols(inp["fox_q_norm_g"].reshape(-1)))
    put("fkg", _cols(inp["fox_k_norm_g"].reshape(-1)))
    put("fwdw", _cols(inp["ffn_w_dw"].reshape(-1)))
    bf = np.zeros((128, 1), np.float32)
    bf[:8, 0] = np.asarray(inp["fox_b_f"], np.float32).reshape(-1)
    put("fbf", bf)
    put("galog", np.broadcast_to(np.asarray(inp["gdn_a_log"], np.float32).reshape(1, 8), (128, 8)))
    put("gdtb", np.broadcast_to(np.asarray(inp["gdn_dt_bias"], np.float32).reshape(1, 8), (128, 8)))
    put("fbfb", np.broadcast_to(np.asarray(inp["fox_b_f"], np.float32).reshape(1, 8), (128, 8)))
    return c


def tile_w(w, width=256):
    w = np.asarray(w, np.float32)
    K, N = w.shape
    return np.ascontiguousarray(w.reshape(K // 128, 128, N // width, width).transpose(2, 1, 0, 3))


def tile_wk(w, nk=2):
    w = np.asarray(w, np.float32)
    K, N = w.shape
    return np.ascontiguousarray(w.reshape(K // (128 * nk), nk, 128, N).transpose(0, 2, 1, 3))


class K:
    def __init__(self, nc, stages):
        self.nc = nc
        self.P = Prog(nc)
        self.stages = stages
        self.st = contextlib.ExitStack()
        self.bank_rr = 0
        self.slot_rr = 0
        self.uid = 0

    def sb(self, name, shape, dt):
        return self.st.enter_context(self.nc.sbuf_tensor(name, shape, dt))

    def B(self, name=""):
        self.uid += 1
        return Buf("%s%d" % (name, self.uid))

    def take_bank(self):
        while True:
            i = self.bank_rr % 8
            self.bank_rr += 1
            if i not in self.held:
                return i

    def take_slot(self):
        i = self.slot_rr % self.NSLOT
        self.slot_rr += 1
        return i

    def mm(self, bank, out, lhsT, rhs, start, stop, reads, extra_w=()):
        self.P.op("pe", lambda e: e.matmul(out, lhsT, rhs, start=start, stop=stop),
                  reads=reads, writes=[self.bankbuf[bank]] + list(extra_w))

    def tr(self, bank, out, in_, ident, reads):
        self.P.op("pe", lambda e: e.transpose(out, in_, ident), reads=reads, writes=[self.bankbuf[bank]])

    def act(self, out, in_, func, reads, writes, bias=None, scale=None, eng="act"):
        kw = {}
        if bias is not None:
            kw["bias"] = bias
        if scale is not None:
            kw["scale"] = scale
        self.P.op("act", lambda e: e.activation(out=out, in_=in_, func=func, **kw), reads=reads, writes=writes)

    def tt(self, eng, out, in0, in1, op, reads, writes):
        self.P.op(eng, lambda e: e.tensor_tensor(out=out, in0=in0, in1=in1, op=op), reads=reads, writes=writes)

    def stt(self, out, in0, scalar, in1, op0, op1, reads, writes):
        self.P.op("dve", lambda e: e.scalar_tensor_tensor(out=out, in0=in0, scalar=scalar, in1=in1, op0=op0, op1=op1),
                  reads=reads, writes=writes)

    def ts(self, eng, out, in0, s1, s2, op0, op1, reads, writes):
        if op1 is None:
            self.P.op(eng, lambda e: e.tensor_scalar(out=out, in0=in0, scalar1=s1, scalar2=None, op0=op0),
                      reads=reads, writes=writes)
        else:
            self.P.op(eng, lambda e: e.tensor_scalar(out=out, in0=in0, scalar1=s1, scalar2=s2, op0=op0, op1=op1),
                      reads=reads, writes=writes)

    def cp(self, eng, out, in_, reads, writes):
        if eng == "act":
            self.P.op("act", lambda e: e.copy(out=out, in_=in_), reads=reads, writes=writes)
        else:
            self.P.op(eng, lambda e: e.tensor_copy(out=out, in_=in_), reads=reads, writes=writes)

    def memset(self, eng, ap, val, writes):
        self.P.op(eng, lambda e: e.memset(ap, val), writes=writes)

    def wload(self, src_ap, ncols_total, view=None):
        s = self.take_slot()
        dst = self.ring[s][:, 0:ncols_total]
        self.P.op("pool", lambda e: e.dma_start(out=dst, in_=src_ap), writes=[self.slotbuf[s]], dma_out=self.slotbuf[s], nobarrier=True)
        return s

    def build(self):
        nc = self.nc
        P = self.P
        dt = nc.dram_tensor
        self.xin = dt("xT", [KC, 128, S], F32, kind="ExternalInput").ap()
        self.cst_d = dt("cst", [128, NCST], F32, kind="ExternalInput").ap()
        kinds = set(k_ for k_, _ in self.stages)
        self.used_inputs = ["xT", "cst"]

        def din(name, shape, kind_):
            if kind_ not in kinds:
                return None
            self.used_inputs.append(name)
            return dt(name, shape, F32, kind="ExternalInput").ap()
        self.d_cwin = din("cwin", [2, 8, 128, 2048], "conf")
        self.d_cwout = din("cwout", [2, 4, 128, 2048], "conf")
        self.d_gwin = din("gwin", [16, 128, 2048], "gdn")
        self.d_gwab = din("gwab", [128, 128], "gdn")
        self.d_gwout = din("gwout", [4, 128, 2048], "gdn")
        self.d_fwin = din("fwin", [12, 128, 2048], "fox")
        self.d_fwf = din("fwf", [128, 64], "fox")
        self.d_fwout = din("fwout", [4, 128, 2048], "fox")
        self.d_wup = din("wup", [4, 22, 128, 2048], "ffn")
        self.d_wdn = din("wdn", [4, 11, 128, 2048], "ffn")
        self.yout = dt("yT", [KC, 128, S], F32, kind="ExternalOutput").ap()

        self.xT = self.sb("xT_sb", [128, KC, S], F32)
        self.cst = self.sb("cst_sb", [128, NCST], F32)
        self.NSLOT = 8
        self.ring = [self.sb("ring%d" % i, [128, 2048], BF16) for i in range(self.NSLOT)]
        self.slotbuf = [Buf("slot%d" % i) for i in range(self.NSLOT)]
        self.banks = [self.st.enter_context(nc.psum_tensor("bank%d" % i, [128, 512], F32)) for i in range(8)]
        self.bankbuf = [Buf("bank%d" % i, excl=True) for i in range(8)]
        self.held = set()
        self.xbuf = [[Buf("x%d_%d" % (c, t)) for t in range(NT)] for c in range(KC)]
        self.hbuf = [Buf("h%d" % t) for t in range(NT)]
        self.hb1 = Buf("htile")
        self.cbuf = Buf("cst")
        self.ones_bf = self.sb("ones_bf", [128, 128], BF16)
        self.ones_f = self.sb("ones_f", [128, 128], F32)
        self.ident_f = self.sb("ident_f", [128, 128], F32)
        self.ident_bf = self.sb("ident_bf", [128, 128], BF16)
        self.uincl_f = self.sb("uincl_f", [128, 128], F32)
        self.uincl_bf = self.sb("uincl_bf", [128, 128], BF16)
        self.lstrict_f = self.sb("lstrict_f", [128, 128], F32)
        self.msu_f = self.sb("msu_f", [128, 128], F32)
        self.kb = Buf("consts")
        self.ARENA = 25600
        self.arena = self.sb("arena", [128, self.ARENA], F32)

        P.op("sp", lambda e: e.dma_start(out=self.cst[:], in_=self.cst_d), writes=[self.cbuf], dma_out=self.cbuf)
        kb = self.kb
        self.memset("pool", self.ones_f[:], 1.0, [kb])
        self.memset("pool", self.ones_bf[:], 1.0, [kb])

        def asel(out, base, cm, step, op):
            P.op("pool", lambda e: e.affine_select(out=out, in_=self.ones_f[:], pattern=[[step, 128]], compare_op=op,
                                                   fill=0.0, base=base, channel_multiplier=cm), reads=[kb], writes=[kb])
        asel(self.ident_f[:], 0, -1, 1, ALU.is_equal)
        asel(self.uincl_f[:], 0, -1, 1, ALU.is_ge)
        asel(self.lstrict_f[:], -1, 1, -1, ALU.is_ge)
        asel(self.msu_f[:], -1, -1, 1, ALU.is_ge)
        self.cp("pool", self.ident_bf[:], self.ident_f[:], [kb], [kb])
        self.cp("pool", self.uincl_bf[:], self.uincl_f[:], [kb], [kb])

        for t in range(NT):
            for c in range(KC):
                P.op("sp", lambda e, c=c, t=t: e.dma_start(out=self.xT[:, c, t * TT:(t + 1) * TT],
                                                           in_=self.xin[c, :, t * TT:(t + 1) * TT]),
                     writes=[self.xbuf[c][t]], dma_out=self.xbuf[c][t])

        ia = 0
        for stg in self.stages:
            kind, layer = stg
            P.barrier()
            if kind == "conf":
                self.conformer(layer, ia)
                ia += 1
            elif kind == "gdn":
                self.gdn(layer)
            elif kind == "fox":
                self.fox(layer)
            elif kind == "ffn":
                self.ffn(layer)

        ob = Buf("out")
        for c in range(KC):
            P.op("sp", lambda e, c=c: e.dma_start(out=self.yout[c], in_=self.xT[:, c, :]),
                 reads=self.xbuf[c], writes=[ob], dma_out=ob)
        P.emit(final_waits=[("sp", ob)])
        self.st.close()

    def cc(self, name, idx=0):
        o = CL[name] + idx
        return self.cst[:, o:o + 1]

    def carve(self, off, nwords):
        assert off + nwords <= self.ARENA, (off, nwords)
        return self.arena[:, off:off + nwords]

    def rmsnorm_tile(self, t, gname, layer, hview, hb, ar_off):
        sl = slice(t * TT, (t + 1) * TT)
        sqs = [self.carve(ar_off + i * 256, 256).bitcast(BF16) for i in range(2)]
        lnv = self.carve(ar_off + 512, 512)
        rstd = self.carve(ar_off + 1024, 512)
        bl, br = self.nb_ln, self.nb_rstd
        bk = self.take_bank()
        for c in range(KC):
            sq, bsq = sqs[c % 2], self.nb_sq[c % 2]
            self.act(sq, self.xT[:, c, sl], AF.Square, reads=[self.xbuf[c][t]], writes=[bsq])
            self.mm(bk, self.banks[bk][:], self.ones_bf[:], sq, c == 0, c == KC - 1, reads=[bsq, self.kb])
        self.act(lnv, self.banks[bk][:], AF.Ln, reads=[self.bankbuf[bk], self.kb], writes=[bl], bias=self.eps_col, scale=1.0 / D)
        self.act(rstd, lnv, AF.Exp, reads=[bl], writes=[br], scale=-0.5)
        for c in range(KC):
            self.stt(hview[:, c, :], self.xT[:, c, sl], self.cc(gname, layer * 8 + c), rstd, ALU.mult, ALU.mult,
                     reads=[self.xbuf[c][t], br, self.cbuf], writes=[hb])

    def common_init(self):
        if getattr(self, "_ci", False):
            return
        self._ci = True
        self.nb_sq, self.nb_ln, self.nb_rstd = [Buf("sq0"), Buf("sq1")], Buf("ln"), Buf("rstd")
        self.eps_t = self.sb("eps_t", [128, 4], F32)
        self.memset("pool", self.eps_t[:, 0:1], EPS, [self.kb])
        self.memset("pool", self.eps_t[:, 1:2], 1.0, [self.kb])
        self.memset("pool", self.eps_t[:, 2:3], 0.0, [self.kb])
        self.eps_col = self.eps_t[:, 0:1]
        self.one_col = self.eps_t[:, 1:2]
        self.zero_col = self.eps_t[:, 2:3]

    def ffn(self, layer):
        self.common_init()
        P = self.P
        hT = self.carve(0, 8192).bitcast(BF16).rearrange("p (c t) -> p c t", c=KC)
        self.hT = hT
        for t in range(NT):
            self.rmsnorm_tile(t, "ffng", layer, hT[:, :, t * TT:(t + 1) * TT], self.hbuf[t], 8192)
        GOFF = 8192 + 1536
        gT = [self.carve(GOFF + i * 4096, 4096).bitcast(BF16).rearrange("p (c t) -> p c t", c=4) for i in range(2)]
        gbuf = [[[Buf("g") for _ in range(NT)] for _ in range(4)] for _ in range(2)]
        UOFF = GOFF + 2 * 4096
        NU = 6
        U = [self.carve(UOFF + i * 516, 516) for i in range(NU)]
        ubuf = [Buf("U") for _ in range(NU)]
        AOFF = UOFF + NU * 516
        NA = 8
        ACC = [self.carve(AOFF + i * 512, 512) for i in range(NA)]
        abuf = [Buf("acc") for _ in range(NA)]
        urr = [0]
        arr = [0]
        hreads = self.hbuf

        parts = [(0, 2), (2, 4), (4, 6), (6, 8), (8, 10), (10, 11)]
        order = []
        for (u0, u1) in parts:
            for u in range(u0, u1):
                order.append(("g", u, self.d_wup[layer, u]))
                order.append(("u", u, self.d_wup[layer, 11 + u]))
            for u in range(u0, u1):
                order.append(("d", u, self.d_wdn[layer, u]))
        issued = {}
        nxt = [0]
        deferred = []
        pend_down = []

        def want(key, ahead):
            idx = [i for i, o_ in enumerate(order) if (o_[0], o_[1]) == key][0]
            while nxt[0] <= min(idx + ahead, len(order) - 1):
                o_ = order[nxt[0]]
                issued[(o_[0], o_[1])] = self.wload(o_[2], 2048)
                nxt[0] += 1
            return issued[key]

        for pi, (u0, u1) in enumerate(parts):
            g = gT[pi % 2]
            gb = gbuf[pi % 2]
            for u in range(u0, u1):
                if u == u0 + 1 or (u == u0 and u1 - u0 == 1 and False):
                    while pend_down:
                        pend_down.pop(0)()
                sg = want(("g", u), 5)
                su = want(("u", u), 4)
                wg = self.ring[sg][:].rearrange("p (k c) -> p k c", k=KC)
                wu = self.ring[su][:].rearrange("p (k c) -> p k c", k=KC)
                for c2 in range(2):
                    ci = 2 * u + c2
                    lc = ci - 2 * u0
                    prev = {"g": None, "u": None}
                    for t in range(NT):
                        sl = slice(t * TT, (t + 1) * TT)
                        accs = {}
                        for which, w_, slot_, col0 in (("g", wg, sg, ci), ("u", wu, su, 22 + ci)):
                            bk = self.take_bank()
                            for kc in range(KC):
                                self.mm(bk, self.banks[bk][:], w_[:, kc, c2 * 128:(c2 + 1) * 128], self.hT[:, kc, sl],
                                        kc == 0, kc == KC - 1, reads=[self.slotbuf[slot_], self.hbuf[t]])
                            ui = urr[0] % NU
                            urr[0] += 1
                            Ut, Ub = U[ui], ubuf[ui]
                            if t == 0:
                                self.memset("pool", Ut[:, 0:2], 0.0, [Ub])
                            else:
                                pU, pB = prev[which]
                                self.cp("pool", Ut[:, 0:2], pU[:, 512:514], [pB], [Ub])
                            self.act(Ut[:, 2:514], self.banks[bk][:], AF.Copy, reads=[self.bankbuf[bk]], writes=[Ub])
                            prev[which] = (Ut, Ub)
                            ai = arr[0] % NA
                            arr[0] += 1
                            At, Ab = ACC[ai], abuf[ai]
                            wcol = lambda k, col0=col0: self.cc("fwdw", (layer * 3 + k) * 44 + col0)
                            self.act(At, Ut[:, 0:512], AF.Copy, reads=[Ub, self.cbuf], writes=[Ab], scale=wcol(0))
                            self.stt(At, Ut[:, 1:513], wcol(1), At, ALU.mult, ALU.add, reads=[Ub, self.cbuf, Ab], writes=[Ab])
                            self.stt(At, Ut[:, 2:514], wcol(2), At, ALU.mult, ALU.add, reads=[Ub, self.cbuf, Ab], writes=[Ab])
                            accs[which] = (At, Ab)
                        Ag, Agb = accs["g"]
                        Au, Aub = accs["u"]

                        def tail(Ag=Ag, Agb=Agb, Au=Au, Aub=Aub, dst=g[:, lc, sl], db=gb[lc][t]):
                            self.act(Ag, Ag, AF.Silu, reads=[Agb], writes=[Agb])
                            self.tt("pool", dst, Ag, Au, ALU.mult, reads=[Agb, Aub], writes=[db])
                        if deferred:
                            deferred.pop(0)()
                        deferred.append(tail)
            dslots = [want(("d", u), 3) for u in range(u0, u1)]

            def down(g=g, gb=gb, nch=2 * (u1 - u0), dslots=dslots):
                while deferred:
                    deferred.pop(0)()
                for dc in range(KC):
                    for t in range(NT):
                        sl = slice(t * TT, (t + 1) * TT)
                        bk = self.take_bank()
                        for lc in range(nch):
                            s_ = dslots[lc // 2]
                            wd = self.ring[s_][:].rearrange("p (k c) -> p k c", k=2)
                            self.mm(bk, self.banks[bk][:], wd[:, lc % 2, dc * 128:(dc + 1) * 128], g[:, lc, sl],
                                    lc == 0, lc == nch - 1, reads=[self.slotbuf[s_], gb[lc][t]])
                        self.tt("dve", self.xT[:, dc, sl], self.banks[bk][:], self.xT[:, dc, sl], ALU.add,
                                reads=[self.bankbuf[bk], self.xbuf[dc][t]], writes=[self.xbuf[dc][t]])
            pend_down.append(down)
        while pend_down:
            pend_down.pop(0)()

    def conformer(self, layer, ia):
        self.common_init()
        P = self.P
        hTt = self.carve(0, 2048).bitcast(BF16).rearrange("p (c t) -> p c t", c=KC)
        G = self.carve(3584, 2176).bitcast(BF16).rearrange("p (c t) -> p c t", c=KC)
        gb = [Buf("G%d" % c) for c in range(KC)]
        DG = [self.carve(5760 + i * 1984, 1984).bitcast(BF16).rearrange("p (k m) -> p k m", k=31) for i in range(2)]
        dgb = [Buf("dg0"), Buf("dg1")]
        CO = self.carve(9728, 4096).rearrange("p (c t) -> p c t", c=KC)
        cob = [Buf("co%d" % c) for c in range(KC)]
        UB = [self.carve(13824 + i * 256, 256).bitcast(BF16) for i in range(2)]
        SQ = [self.carve(14336 + i * 256, 256).bitcast(BF16) for i in range(2)]
        ubb = [Buf("ub0"), Buf("ub1")]
        sqb = [Buf("sqb0"), Buf("sqb1")]
        mean = self.carve(14848, 512)
        msq = self.carve(15360, 512)
        lnv = self.carve(15872, 512)
        rstd = self.carve(16384, 512)
        stb = Buf("stats")
        TMP = [self.carve(16896 + i * 512, 512) for i in range(2)]
        tmb = [Buf("tmp0"), Buf("tmp1")]
        sT = self.carve(17920, 2048).bitcast(BF16).rearrange("p (c t) -> p c t", c=KC)
        stbuf = [Buf("sT%d" % c) for c in range(KC)]
        SG = [self.carve(19968 + i * 512, 512) for i in range(2)]
        sgb = [Buf("sg0"), Buf("sg1")]
        for c in range(KC):
            self.memset("pool", G[:, c, 0:32], 0.0, [gb[c]])
        for t in range(NT):
            sl = slice(t * TT, (t + 1) * TT)
            self.rmsnorm_tile(t, "mixg", layer, hTt, self.hb1, 2048)
            s1 = self.take_bank()
            self.held.add(s1)
            s2 = self.take_bank()
            self.held.add(s2)
            wst = {}

            def head(c):
                if c % 2 == 0:
                    wst["sv"] = self.wload(self.d_cwin[ia, c // 2], 2048)
                    wst["sg"] = self.wload(self.d_cwin[ia, 4 + c // 2], 2048)
                sv, sg_ = wst["sv"], wst["sg"]
                wv = self.ring[sv][:].rearrange("p (k c) -> p k c", k=KC)
                wg = self.ring[sg_][:].rearrange("p (k c) -> p k c", k=KC)
                bv = self.take_bank()
                for kc in range(KC):
                    self.mm(bv, self.banks[bv][:], wv[:, kc, (c % 2) * 128:(c % 2 + 1) * 128], hTt[:, kc, :],
                            kc == 0, kc == KC - 1, reads=[self.slotbuf[sv], self.hb1])
                bg = self.take_bank()
                for kc in range(KC):
                    self.mm(bg, self.banks[bg][:], wg[:, kc, (c % 2) * 128:(c % 2 + 1) * 128], hTt[:, kc, :],
                            kc == 0, kc == KC - 1, reads=[self.slotbuf[sg_], self.hb1])
                return bv, bg

            def tail(c, bv, bg):
                dg, dgbuf = DG[c % 2], dgb[c % 2]
                wb = CL["cwdw"] + ia * 31 * 8 + c
                wtaps = self.cst[:, wb:wb + 30 * 8 + 1:8]
                self.tt("dve", dg, self.ident_bf[:].unsqueeze(1).broadcast_to([128, 31, 128]),
                        wtaps.unsqueeze(2).broadcast_to([128, 31, 128]), ALU.mult, reads=[self.kb, self.cbuf], writes=[dgbuf])
                sg, sgbuf = SG[c % 2], sgb[c % 2]
                self.act(sg, self.banks[bg][:], AF.Sigmoid, reads=[self.bankbuf[bg], self.cbuf], writes=[sgbuf],
                         bias=self.cc("cbin", ia * 16 + 8 + c))
                self.stt(G[:, c, 30:542], self.banks[bv][:], self.cc("cbin", ia * 16 + c), sg, ALU.add, ALU.mult,
                         reads=[self.bankbuf[bv], sgbuf, self.cbuf], writes=[gb[c]])
                bc = self.take_bank()
                for k in range(31):
                    self.mm(bc, self.banks[bc][:], dg[:, k, :], G[:, c, k:k + 512], k == 0, k == 30, reads=[dgbuf, gb[c]])
                self.act(CO[:, c, :], self.banks[bc][:], AF.Identity, reads=[self.bankbuf[bc], self.cbuf], writes=[cob[c]],
                         bias=self.cc("cbdw", ia * 8 + c))
                self.cp("pool", G[:, c, 0:30], G[:, c, 512:542], [gb[c]], [gb[c]])
                ub, ubbuf = UB[c % 2], ubb[c % 2]
                sq, sqbuf = SQ[c % 2], sqb[c % 2]
                self.cp("dve", ub, CO[:, c, :], [cob[c]], [ubbuf])
                self.act(sq, CO[:, c, :], AF.Square, reads=[cob[c]], writes=[sqbuf])
                self.mm(s1, self.banks[s1][:], self.ones_bf[:], ub, c == 0, c == KC - 1, reads=[ubbuf, self.kb])
                self.mm(s2, self.banks[s2][:], self.ones_bf[:], sq, c == 0, c == KC - 1, reads=[sqbuf, self.kb])
            hb_ = head(0)
            for c in range(KC):
                cur = hb_
                if c + 1 < KC:
                    hb_ = head(c + 1)
                tail(c, *cur)
            wos = [self.wload(self.d_cwout[ia, j], 2048) for j in range(4)]
            self.ts("dve", mean, self.banks[s1][:], 1.0 / D, None, ALU.mult, None, reads=[self.bankbuf[s1]], writes=[stb])
            self.tt("dve", msq, mean, mean, ALU.mult, reads=[stb], writes=[stb])
            self.stt(msq, self.banks[s2][:], 1.0 / D, msq, ALU.mult, ALU.subtract, reads=[self.bankbuf[s2], stb], writes=[stb])
            self.act(lnv, msq, AF.Ln, reads=[stb, self.kb], writes=[stb], bias=self.eps_col)
            self.act(rstd, lnv, AF.Exp, reads=[stb], writes=[stb], scale=-0.5)
            self.held.discard(s1)
            self.held.discard(s2)
            for c in range(KC):
                tm, tmbuf = TMP[c % 2], tmb[c % 2]
                self.tt("dve", tm, CO[:, c, :], mean, ALU.subtract, reads=[cob[c], stb], writes=[tmbuf])
                self.tt("dve", tm, tm, rstd, ALU.mult, reads=[tmbuf, stb], writes=[tmbuf])
                self.act(sT[:, c, :], tm, AF.Silu, reads=[tmbuf, self.cbuf], writes=[stbuf[c]],
                         bias=self.cc("clnb", ia * 8 + c), scale=self.cc("clng", ia * 8 + c))
            for dc in range(KC):
                wo = self.ring[wos[dc // 2]][:].rearrange("p (k c) -> p k c", k=KC)
                bk = self.take_bank()
                for kc in range(KC):
                    self.mm(bk, self.banks[bk][:], wo[:, kc, (dc % 2) * 128:(dc % 2 + 1) * 128], sT[:, kc, :],
                            kc == 0, kc == KC - 1, reads=[self.slotbuf[wos[dc // 2]], stbuf[kc]])
                self.tt("dve", self.xT[:, dc, sl], self.banks[bk][:], self.xT[:, dc, sl], ALU.add,
                        reads=[self.bankbuf[bk], self.xbuf[dc][t]], writes=[self.xbuf[dc][t]])

    def gdn(self, layer):
        self.common_init()
        P = self.P
        hT = self.carve(0, 8192).bitcast(BF16).rearrange("p (c t) -> p c t", c=KC)
        for t in range(NT):
            self.rmsnorm_tile(t, "mixg", layer, hT[:, :, t * TT:(t + 1) * TT], self.hbuf[t], 8192)
        o = [9728]

        def al(n):
            a = self.carve(o[0], n)
            o[0] += n
            return a

        def bf4(n=256):
            return al(n).bitcast(BF16).rearrange("p (b c) -> p b c", b=4)

        gtok, gcs, eg, negeg, egl, gl, beta, negbeta = [al(32) for _ in range(8)]
        abt = al(64)
        gb_ = Buf("gates")
        qnT = [al(256).bitcast(BF16) for _ in range(4)]
        knT = [al(256).bitcast(BF16) for _ in range(4)]
        vtok = [bf4() for _ in range(4)]
        kdtok = [bf4() for _ in range(4)]
        zsT = [al(256).bitcast(BF16) for _ in range(4)]
        TTm = [bf4() for _ in range(4)]
        QKD = [bf4() for _ in range(4)]
        Pm = [bf4() for _ in range(4)]
        Qm = [bf4() for _ in range(4)]
        nm = lambda n: [Buf(n + str(i)) for i in range(4)]
        qnb, knb, vtb, kdb, zsb, ttb, qkb, pmb, qmb = [nm(n) for n in ("qn", "kn", "vt", "kd", "zs", "tt", "qk", "pm", "qm")]
        U = [al(516) for _ in range(2)]
        ub = [Buf("U0"), Buf("U1")]
        ACC = [al(512) for _ in range(2)]
        ab_ = [Buf("A0"), Buf("A1")]
        sqt = al(256).bitcast(BF16)
        lnv = al(512)
        rstd = al(512)
        sqb, lnb, rsb = Buf("sq"), Buf("ln"), Buf("rs")
        decT = al(512)
        decb = Buf("dec")
        tmp = al(512)
        tmpb = Buf("tmp")
        lhsg = [al(128) for _ in range(2)]
        lgb = [Buf("lg0"), Buf("lg1")]
        vbf = al(256).bitcast(BF16)
        vbb = Buf("vbf")
        vnew = bf4()
        vnb = Buf("vnew")
        onb_ = bf4()
        onbuf = Buf("on")
        Sf = al(512).rearrange("p (h c) -> p h c", h=4)
        Sb = bf4()
        sfb, sbb = Buf("Sf"), Buf("Sb")
        haloS = al(48).rearrange("p (c k) -> p c k", c=12)
        hsb = [Buf("hs%d" % i) for i in range(12)]
        ssq = al(8)
        ssb = Buf("ssq")
        wab = al(64).bitcast(BF16).rearrange("p (k c) -> p k c", k=KC)
        negA = al(8)
        assert o[0] <= self.ARENA, o[0]
        wabb = Buf("wab")
        P.op("pool", lambda e: e.dma_start(out=wab.rearrange("p k c -> p (k c)"), in_=self.d_gwab), writes=[wabb], dma_out=wabb)
        nab = Buf("negA")
        self.act(negA, self.cst[:, CL["galog"]:CL["galog"] + 8], AF.Exp, reads=[self.cbuf], writes=[nab])
        self.ts("dve", negA, negA, -1.0, None, ALU.mult, None, reads=[nab], writes=[nab])
        dtb = self.cst[:, CL["gdtb"]:CL["gdtb"] + 8]
        v4 = lambda a: a.rearrange("p (b h) -> p b h", b=4)
        bfbank = lambda bk: self.banks[bk][:].bitcast(BF16)[:, 0:512].rearrange("p (b c) -> p b c", b=4)
        fbank = lambda bk: self.banks[bk][:].rearrange("p (b c) -> p b c", b=4)
        qscale = 128.0 ** -0.5
        for g in range(2):
            self.memset("pool", Sf[:], 0.0, [sfb])
            self.memset("pool", Sb[:], 0.0, [sbb])
            for Q in range(NT):
                sl = slice(Q * TT, (Q + 1) * TT)
                hTt = hT[:, :, sl]
                hb = self.hbuf[Q]
                bk = self.take_bank()
                for blk in range(4):
                    for kc in range(KC):
                        self.mm(bk, self.banks[bk][:, blk * 16:(blk + 1) * 16], hTt[:, kc, blk * 128:(blk + 1) * 128], wab[:, kc, :],
                                kc == 0, kc == KC - 1, reads=[wabb, hb])
                self.cp("dve", abt, self.banks[bk][:, 0:64], [self.bankbuf[bk]], [gb_])
                ab3 = abt.rearrange("p (b c) -> p b c", b=4)
                self.act(v4(beta), ab3[:, :, 8:16], AF.Exp, reads=[gb_], writes=[gb_], scale=-1.0)
                self.act(beta, beta, AF.Ln, reads=[gb_, self.kb], writes=[gb_], bias=self.one_col)
                self.act(beta, beta, AF.Exp, reads=[gb_], writes=[gb_], scale=-1.0)
                self.ts("dve", negbeta, beta, -1.0, None, ALU.mult, None, reads=[gb_], writes=[gb_])
                self.tt("dve", v4(gtok), ab3[:, :, 0:8], dtb.unsqueeze(1).broadcast_to([128, 4, 8]), ALU.add, reads=[gb_, self.cbuf], writes=[gb_])
                self.act(gtok, gtok, AF.Exp, reads=[gb_], writes=[gb_])
                self.act(gtok, gtok, AF.Ln, reads=[gb_, self.kb], writes=[gb_], bias=self.one_col)
                self.tt("dve", v4(gtok), v4(gtok), negA.unsqueeze(1).broadcast_to([128, 4, 8]), ALU.mult, reads=[gb_, nab], writes=[gb_])
                bk = self.take_bank()
                self.mm(bk, self.banks[bk][:, 0:32], self.uincl_f[:], gtok, True, True, reads=[gb_, self.kb])
                b2 = self.take_bank()
                self.mm(b2, self.banks[b2][:, 0:32], self.ones_f[:], gtok, True, True, reads=[gb_, self.kb])
                self.cp("dve", gcs, self.banks[bk][:, 0:32], [self.bankbuf[bk]], [gb_])
                self.act(eg, gcs, AF.Exp, reads=[gb_], writes=[gb_])
                self.ts("dve", negeg, eg, -1.0, None, ALU.mult, None, reads=[gb_], writes=[gb_])
                self.tt("dve", egl, self.banks[b2][:, 0:32], gcs, ALU.subtract, reads=[self.bankbuf[b2], gb_], writes=[gb_])
                self.act(egl, egl, AF.Exp, reads=[gb_], writes=[gb_])
                self.act(gl, self.banks[b2][:, 0:32], AF.Exp, reads=[self.bankbuf[b2]], writes=[gb_])
                uic = [0]

                pair_slots = {}

                def proj(hh, which):
                    h = 4 * g + hh
                    if hh % 2 == 0 and which == 2:
                        for w2 in (2, 0, 1, 3):
                            pair_slots[w2] = self.wload(self.d_gwin[w2 * 4 + h // 2], 2048)
                    sw = pair_slots[which]
                    w_ = self.ring[sw][:].rearrange("p (k c) -> p k c", k=KC)
                    bk = self.take_bank()
                    self.held.add(bk)
                    for kc in range(KC):
                        self.mm(bk, self.banks[bk][:], w_[:, kc, (h % 2) * 128:(h % 2 + 1) * 128], hTt[:, kc, :],
                                kc == 0, kc == KC - 1, reads=[self.slotbuf[sw], hb])
                    return bk

                def chain(hh, which, bk):
                    h = 4 * g + hh
                    if which == 3:
                        self.act(zsT[hh], self.banks[bk][:], AF.Silu, reads=[self.bankbuf[bk]], writes=[zsb[hh]])
                        self.held.discard(bk)
                        return
                    ci = which * 4 + hh
                    chunk = which * 8 + h
                    ui = uic[0]
                    uic[0] += 1
                    Ut, Ub = U[ui % 2], ub[ui % 2]
                    At, Ab = ACC[ui % 2], ab_[ui % 2]
                    if Q == 0:
                        self.memset("pool", Ut[:, 0:3], 0.0, [Ub])
                    else:
                        self.cp("pool", Ut[:, 0:3], haloS[:, ci, 0:3], [hsb[ci]], [Ub])
                    self.act(Ut[:, 3:515], self.banks[bk][:], AF.Copy, reads=[self.bankbuf[bk]], writes=[Ub])
                    self.held.discard(bk)
                    self.cp("pool", haloS[:, ci, 0:3], Ut[:, 512:515], [Ub], [hsb[ci]])
                    wc = lambda k, chunk=chunk: self.cc("gconv", k * 24 + chunk)
                    self.act(At, Ut[:, 0:512], AF.Copy, reads=[Ub, self.cbuf], writes=[Ab], scale=wc(0))
                    for k in range(1, 4):
                        self.stt(At, Ut[:, k:k + 512], wc(k), At, ALU.mult, ALU.add, reads=[Ub, self.cbuf, Ab], writes=[Ab])
                    self.act(At, At, AF.Silu, reads=[Ab], writes=[Ab])
                    if which < 2:
                        pend_norm.append((hh, which, At, Ab))
                        return
                    if which < 2:
                        self.act(sqt, At, AF.Square, reads=[Ab], writes=[sqb])
                        b2 = self.take_bank()
                        self.mm(b2, self.banks[b2][:], self.ones_bf[:], sqt, True, True, reads=[sqb, self.kb])
                        self.act(lnv, self.banks[b2][:], AF.Ln, reads=[self.bankbuf[b2], self.kb], writes=[lnb], bias=self.eps_col)
                        self.act(rstd, lnv, AF.Exp, reads=[lnb], writes=[rsb], scale=-0.5)
                        if which == 0:
                            self.stt(qnT[hh], At, qscale, rstd, ALU.mult, ALU.mult, reads=[Ab, rsb], writes=[qnb[hh]])
                        else:
                            self.tt("dve", knT[hh], At, rstd, ALU.mult, reads=[Ab, rsb], writes=[knb[hh]])
                    else:
                        self.cp("dve", vbf, At, [Ab], [vbb])
                        bt = self.take_bank()
                        for blk in range(4):
                            self.tr(bt, bfbank(bt)[:, blk, :], vbf[:, blk * 128:(blk + 1) * 128], self.ident_bf[:], reads=[vbb, self.kb])
                        self.cp("act", vtok[hh], bfbank(bt), [self.bankbuf[bt]], [vtb[hh]])

                pend_norm = []

                def norms():
                    while pend_norm:
                        hh_, which_, At, Ab = pend_norm.pop(0)
                        self.act(sqt, At, AF.Square, reads=[Ab], writes=[sqb])
                        b2 = self.take_bank()
                        self.mm(b2, self.banks[b2][:], self.ones_bf[:], sqt, True, True, reads=[sqb, self.kb])
                        self.act(lnv, self.banks[b2][:], AF.Ln, reads=[self.bankbuf[b2], self.kb], writes=[lnb], bias=self.eps_col)
                        self.act(rstd, lnv, AF.Exp, reads=[lnb], writes=[rsb], scale=-0.5)
                        if which_ == 0:
                            self.stt(qnT[hh_], At, qscale, rstd, ALU.mult, ALU.mult, reads=[Ab, rsb], writes=[qnb[hh_]])
                        else:
                            self.tt("dve", knT[hh_], At, rstd, ALU.mult, reads=[Ab, rsb], writes=[knb[hh_]])

                units = [(hh, which) for hh in range(4) for which in (2, 0, 1, 3)]
                nextbank = proj(*units[0])
                for ui_, (hh, which) in enumerate(units):
                    h = 4 * g + hh
                    curbank = nextbank
                    if ui_ + 1 < len(units):
                        nextbank = proj(*units[ui_ + 1])
                    chain(hh, which, curbank)
                    if which != 3:
                        continue
                    norms()
                    bt = self.take_bank()
                    for blk in range(4):
                        self.tr(bt, bfbank(bt)[:, blk, :], knT[hh][:, blk * 128:(blk + 1) * 128], self.ident_bf[:], reads=[knb[hh], self.kb])
                    for blk in range(4):
                        self.act(kdtok[hh][:, blk, :], bfbank(bt)[:, blk, :], AF.Copy, reads=[self.bankbuf[bt], gb_], writes=[kdb[hh]],
                                 scale=egl[:, blk * 8 + h:blk * 8 + h + 1])
                    bd = self.take_bank()
                    for blk in range(4):
                        lg, lgbuf = lhsg[blk % 2], lgb[blk % 2]
                        self.act(lg, self.lstrict_f[:], AF.Copy, reads=[self.kb, gb_], writes=[lgbuf],
                                 scale=gtok[:, blk * 8 + h:blk * 8 + h + 1])
                        self.mm(bd, self.banks[bd][:, blk * 128:(blk + 1) * 128], lg, self.uincl_f[:], True, True, reads=[lgbuf, self.kb])
                    self.act(decT, self.banks[bd][:], AF.Exp, reads=[self.bankbuf[bd]], writes=[decb])
                    d3 = decT.rearrange("p (b c) -> p b c", b=4)
                    t3 = tmp.rearrange("p (b c) -> p b c", b=4)
                    bkk = self.take_bank()
                    for blk in range(4):
                        ks = knT[hh][:, blk * 128:(blk + 1) * 128]
                        self.mm(bkk, self.banks[bkk][:, blk * 128:(blk + 1) * 128], ks, ks, True, True, reads=[knb[hh]])
                    self.tt("dve", tmp, self.banks[bkk][:], decT, ALU.mult, reads=[self.bankbuf[bkk], decb], writes=[tmpb])
                    for blk in range(4):
                        self.stt(Pm[hh][:, blk, :], t3[:, blk, :], negbeta[:, blk * 8 + h:blk * 8 + h + 1], self.msu_f[:], ALU.mult, ALU.mult,
                                 reads=[tmpb, gb_, self.kb], writes=[pmb[hh]])
                    bt = self.take_bank()
                    for blk in range(4):
                        self.tr(bt, bfbank(bt)[:, blk, :], Pm[hh][:, blk, :], self.ident_bf[:], reads=[pmb[hh], self.kb])
                    self.cp("act", Qm[hh], bfbank(bt), [self.bankbuf[bt]], [qmb[hh]])
                    self.tt("pool", TTm[hh], Pm[hh], self.ident_bf[:].unsqueeze(1).broadcast_to([128, 4, 128]), ALU.add,
                            reads=[pmb[hh], self.kb], writes=[ttb[hh]])
                    bq = self.take_bank()
                    for blk in range(4):
                        self.mm(bq, self.banks[bq][:, blk * 128:(blk + 1) * 128], knT[hh][:, blk * 128:(blk + 1) * 128],
                                qnT[hh][:, blk * 128:(blk + 1) * 128], True, True, reads=[knb[hh], qnb[hh]])
                    self.tt("dve", tmp, self.banks[bq][:], decT, ALU.mult, reads=[self.bankbuf[bq], decb], writes=[tmpb])
                    self.tt("dve", QKD[hh], t3, self.uincl_f[:].unsqueeze(1).broadcast_to([128, 4, 128]), ALU.mult,
                            reads=[tmpb, self.kb], writes=[qkb[hh]])
                wos = [self.wload(self.d_gwout[j], 2048) for j in range(4)]
                for lev in range(1, 7):
                    for hh in range(4):
                        bq = self.take_bank()
                        for blk in range(4):
                            self.mm(bq, self.banks[bq][:, blk * 128:(blk + 1) * 128], Pm[hh][:, blk, :], Qm[hh][:, blk, :], True, True,
                                    reads=[pmb[hh], qmb[hh]])
                        if lev < 6:
                            bp = self.take_bank()
                            for blk in range(4):
                                self.mm(bp, self.banks[bp][:, blk * 128:(blk + 1) * 128], Qm[hh][:, blk, :], Pm[hh][:, blk, :], True, True,
                                        reads=[pmb[hh], qmb[hh]])
                            self.cp("act", Pm[hh], fbank(bp), [self.bankbuf[bp]], [pmb[hh]])
                        self.cp("dve", Qm[hh], fbank(bq), [self.bankbuf[bq]], [qmb[hh]])
                        br = self.take_bank()
                        for blk in range(4):
                            self.mm(br, self.banks[br][:, blk * 128:(blk + 1) * 128], Qm[hh][:, blk, :], TTm[hh][:, blk, :], True, True,
                                    reads=[qmb[hh], ttb[hh]])
                        self.tt("dve", TTm[hh], fbank(br), TTm[hh], ALU.add, reads=[self.bankbuf[br], ttb[hh]], writes=[ttb[hh]])
                rbuf = vbf.rearrange("p (b c) -> p b c", b=4)
                otok = decT.rearrange("p (b c) -> p b c", b=4)
                o2s = tmp.rearrange("p (b c) -> p b c", b=4)
                for blk in range(4):
                    bs = slice(blk * 128, (blk + 1) * 128)
                    col = lambda a, hh: a[:, blk * 8 + 4 * g + hh:blk * 8 + 4 * g + hh + 1]
                    bks = self.take_bank()
                    for hh in range(4):
                        self.mm(bks, self.banks[bks][:, hh * 128:(hh + 1) * 128], knT[hh][:, bs], Sb[:, hh, :], True, True,
                                reads=[knb[hh], sbb])
                    for hh in range(4):
                        self.stt(rbuf[:, hh, :], self.banks[bks][:, hh * 128:(hh + 1) * 128], col(negeg, hh), vtok[hh][:, blk, :],
                                 ALU.mult, ALU.add, reads=[self.bankbuf[bks], gb_, vtb[hh]], writes=[vbb])
                    bvn = self.take_bank()
                    for hh in range(4):
                        self.mm(bvn, self.banks[bvn][:, hh * 128:(hh + 1) * 128], TTm[hh][:, blk, :], rbuf[:, hh, :], True, True,
                                reads=[ttb[hh], vbb])
                    for hh in range(4):
                        self.act(vnew[:, hh, :], self.banks[bvn][:, hh * 128:(hh + 1) * 128], AF.Copy, reads=[self.bankbuf[bvn], gb_],
                                 writes=[vnb], scale=col(beta, hh))
                    bo1 = self.take_bank()
                    for hh in range(4):
                        self.mm(bo1, self.banks[bo1][:, hh * 128:(hh + 1) * 128], qnT[hh][:, bs], Sb[:, hh, :], True, True,
                                reads=[qnb[hh], sbb])
                    bo2 = self.take_bank()
                    for hh in range(4):
                        self.mm(bo2, self.banks[bo2][:, hh * 128:(hh + 1) * 128], QKD[hh][:, blk, :], vnew[:, hh, :], True, True,
                                reads=[qkb[hh], vnb])
                    self.cp("act", tmp, self.banks[bo2][:], [self.bankbuf[bo2]], [tmpb])
                    for hh in range(4):
                        self.stt(otok[:, hh, :], self.banks[bo1][:, hh * 128:(hh + 1) * 128], col(eg, hh), o2s[:, hh, :],
                                 ALU.mult, ALU.add, reads=[self.bankbuf[bo1], gb_, tmpb], writes=[decb])
                    bsu = self.take_bank()
                    for hh in range(4):
                        self.mm(bsu, self.banks[bsu][:, hh * 128:(hh + 1) * 128], kdtok[hh][:, blk, :], vnew[:, hh, :], True, True,
                                reads=[kdb[hh], vnb])
                    for hh in range(4):
                        self.stt(Sf[:, hh, :], Sf[:, hh, :], col(gl, hh), self.banks[bsu][:, hh * 128:(hh + 1) * 128],
                                 ALU.mult, ALU.add, reads=[self.bankbuf[bsu], gb_, sfb], writes=[sfb])
                    self.cp("act", Sb, Sf, [sfb], [sbb])
                    self.tt("pool", tmp, decT, decT, ALU.mult, reads=[decb], writes=[tmpb])
                    P.op("dve", lambda e: e.tensor_reduce(out=ssq[:, 0:4], in_=o2s, axis=AX.X, op=ALU.add), reads=[tmpb], writes=[ssb])
                    self.act(ssq[:, 0:4], ssq[:, 0:4], AF.Ln, reads=[ssb, self.kb], writes=[ssb], bias=self.eps_col, scale=1.0 / 128)
                    self.act(ssq[:, 0:4], ssq[:, 0:4], AF.Exp, reads=[ssb], writes=[ssb], scale=-0.5)
                    for hh in range(4):
                        self.act(onb_[:, hh, :], otok[:, hh, :], AF.Copy, reads=[decb, ssb], writes=[onbuf], scale=ssq[:, hh:hh + 1])
                    bt = self.take_bank()
                    for hh in range(4):
                        self.tr(bt, bfbank(bt)[:, hh, :], onb_[:, hh, :], self.ident_bf[:], reads=[onbuf, self.kb])
                    for hh in range(4):
                        self.stt(zsT[hh][:, bs], bfbank(bt)[:, hh, :], self.cc("gog"), zsT[hh][:, bs], ALU.mult, ALU.mult,
                                 reads=[self.bankbuf[bt], self.cbuf, zsb[hh]], writes=[zsb[hh]])
                for dc in range(KC):
                    wo = self.ring[wos[dc // 2]][:].rearrange("p (k c) -> p k c", k=KC)
                    bk = self.take_bank()
                    for hh in range(4):
                        self.mm(bk, self.banks[bk][:], wo[:, 4 * g + hh, (dc % 2) * 128:(dc % 2 + 1) * 128], zsT[hh],
                                hh == 0, hh == 3, reads=[self.slotbuf[wos[dc // 2]], zsb[hh]])
                    self.tt("dve", self.xT[:, dc, sl], self.banks[bk][:], self.xT[:, dc, sl], ALU.add,
                            reads=[self.bankbuf[bk], self.xbuf[dc][Q]], writes=[self.xbuf[dc][Q]])

    def fox(self, layer):
        self.common_init()
        P = self.P
        HD = 128
        hT = self.carve(0, 8192).bitcast(BF16).rearrange("p (c t) -> p c t", c=KC)
        for t in range(NT):
            self.rmsnorm_tile(t, "mixg", layer, hT[:, :, t * TT:(t + 1) * TT], self.hbuf[t], 8192)
        o = 9728
        knT = self.carve(o, 4096).bitcast(BF16).rearrange("p (h t) -> p h t", h=4); o += 4096
        knb = [[Buf("kn") for _ in range(NT)] for _ in range(4)]
        vtok = self.carve(o, 4096).bitcast(BF16).rearrange("p (b c) -> p b c", b=16); o += 4096
        vb = [Buf("v%d" % b) for b in range(16)]
        qT = self.carve(o, 1024).bitcast(BF16).rearrange("p (h t) -> p h t", h=4); o += 1024
        qb = [Buf("q%d" % h) for h in range(4)]
        oT = self.carve(o, 1024).bitcast(BF16).rearrange("p (h t) -> p h t", h=4); o += 1024
        ob = [Buf("o%d" % h) for h in range(4)]
        NP = 3
        pT = [self.carve(o + i * 256, 256).bitcast(BF16) for i in range(NP)]; o += NP * 256
        pb = [Buf("p%d" % i) for i in range(NP)]
        raw = self.carve(o, 512); o += 512
        sqt = self.carve(o, 256).bitcast(BF16); o += 256
        lnv = self.carve(o, 512); o += 512
        rstd = self.carve(o, 512); o += 512
        rawb, sqb, lnb, rsb = Buf("raw"), Buf("sq"), Buf("ln"), Buf("rs")
        rden = self.carve(o, 512); o += 512
        rdb = Buf("rden")
        erow = self.carve(o, 512); o += 512
        cT = self.carve(o, 128).rearrange("p (b h) -> p b h", b=16); o += 128
        cmidb = self.carve(o, 32).rearrange("p (q h) -> p q h", q=4); o += 32
        biasA = self.carve(o, 128).rearrange("p (b h) -> p b h", b=16); o += 128
        small = self.carve(o, 32); o += 32
        wf = self.carve(o, 32).bitcast(BF16).rearrange("p (k c) -> p k c", k=KC); o += 32
        assert o <= self.ARENA, o
        carry = small[:, 0:8]
        xs = erow[:, 0:32]
        tots = erow[:, 32:64]
        cb = Buf("cstuff")
        wfb = Buf("wf")
        P.op("pool", lambda e: e.dma_start(out=wf.rearrange("p k c -> p (k c)"), in_=self.d_fwf), writes=[wfb], dma_out=wfb)
        self.memset("pool", carry, 0.0, [cb])
        scale = float(HD) ** -0.5
        for g in range(2):
            for Q in range(NT):
                sl = slice(Q * TT, (Q + 1) * TT)
                hTt = hT[:, :, sl]
                hb = self.hbuf[Q]
                if g == 0:
                    bk = self.take_bank()
                    for blk in range(4):
                        for kc in range(KC):
                            self.mm(bk, self.banks[bk][:, blk * 8:(blk + 1) * 8], hTt[:, kc, blk * 128:(blk + 1) * 128], wf[:, kc, :],
                                    kc == 0, kc == KC - 1, reads=[wfb, hb])
                    x3 = xs.rearrange("p (b h) -> p b h", b=4)
                    self.tt("dve", x3, self.banks[bk][:, 0:32].rearrange("p (b h) -> p b h", b=4),
                            self.cst[:, CL["fbfb"]:CL["fbfb"] + 8].unsqueeze(1).broadcast_to([128, 4, 8]), ALU.add,
                            reads=[self.bankbuf[bk], self.cbuf], writes=[cb])
                    self.act(xs, xs, AF.Exp, reads=[cb], writes=[cb], scale=-1.0)
                    self.act(xs, xs, AF.Ln, reads=[cb, self.kb], writes=[cb], bias=self.one_col)
                    b1 = self.take_bank()
                    self.mm(b1, self.banks[b1][:, 0:32], self.uincl_f[:], xs, True, True, reads=[cb, self.kb])
                    b2 = self.take_bank()
                    self.mm(b2, self.banks[b2][:, 0:32], self.ones_f[:], xs, True, True, reads=[cb, self.kb])
                    self.cp("dve", tots, self.banks[b2][:, 0:32], [self.bankbuf[b2]], [cb])
                    for blk in range(4):
                        self.tt("dve", cT[:, 4 * Q + blk, :], self.banks[b1][:, blk * 8:(blk + 1) * 8], carry, ALU.add,
                                reads=[self.bankbuf[b1], cb], writes=[cb])
                        self.tt("dve", carry, carry, tots[:, blk * 8:(blk + 1) * 8], ALU.add, reads=[cb], writes=[cb])
                        if blk == 1:
                            self.cp("dve", cmidb[:, Q, :], carry, [cb], [cb])
                nj = 4 * Q + 4
                DBG = int(os.environ.get("FOXDBG", "9"))
                if DBG <= 1:
                    continue
                self.tt("dve", biasA[:, 0:nj, :], cT[:, 0:nj, :], cmidb[:, Q:Q + 1, :].broadcast_to([128, nj, 8]), ALU.subtract,
                        reads=[cb], writes=[cb])
                wsl = {}

                def fproj(which, hh):
                    hp, h2 = hh // 2, hh % 2
                    if h2 == 0:
                        wsl[(which, hp)] = self.wload(self.d_fwin[which * 4 + 2 * g + hp], 2048)
                    sw = wsl[(which, hp)]
                    w_ = self.ring[sw][:].rearrange("p (k c) -> p k c", k=KC)
                    bk = self.take_bank()
                    self.held.add(bk)
                    for kc in range(KC):
                        self.mm(bk, self.banks[bk][:], w_[:, kc, h2 * 128:(h2 + 1) * 128], hTt[:, kc, :],
                                kc == 0, kc == KC - 1, reads=[self.slotbuf[sw], hb])
                    return bk

                def fnorm(which, hh, bk):
                    self.cp("dve", raw, self.banks[bk][:], [self.bankbuf[bk]], [rawb])
                    self.held.discard(bk)
                    self.act(sqt, raw, AF.Square, reads=[rawb], writes=[sqb])
                    b2 = self.take_bank()
                    self.mm(b2, self.banks[b2][:], self.ones_bf[:], sqt, True, True, reads=[sqb, self.kb])
                    self.act(lnv, self.banks[b2][:], AF.Ln, reads=[self.bankbuf[b2], self.kb], writes=[lnb],
                             bias=self.eps_col, scale=1.0 / HD)
                    self.act(rstd, lnv, AF.Exp, reads=[lnb], writes=[rsb], scale=-0.5)
                    if which == 0:
                        self.stt(qT[:, hh, :], raw, self.cc("fqg"), rstd, ALU.mult, ALU.mult,
                                 reads=[rawb, rsb, self.cbuf], writes=[qb[hh]])
                    else:
                        self.stt(knT[:, hh, sl], raw, self.cc("fkg"), rstd, ALU.mult, ALU.mult,
                                 reads=[rawb, rsb, self.cbuf], writes=[knb[hh][Q]])
                funits = [(which, hh) for which in range(2) for hh in range(4)]
                nb_ = fproj(*funits[0])
                for fi, (which, hh) in enumerate(funits):
                    cb_ = nb_
                    if fi + 1 < len(funits):
                        nb_ = fproj(*funits[fi + 1])
                    fnorm(which, hh, cb_)
                for hp in range(2):
                    sw = self.wload(self.d_fwin[8 + 2 * g + hp], 2048)
                    w_ = self.ring[sw][:].rearrange("p (k c) -> p k c", k=KC)
                    for blk in range(4):
                        bk = self.take_bank()
                        for kc in range(KC):
                            self.mm(bk, self.banks[bk][:, 0:256], hTt[:, kc, blk * 128:(blk + 1) * 128], w_[:, kc, :],
                                    kc == 0, kc == KC - 1, reads=[self.slotbuf[sw], hb])
                        self.act(vtok[:, 4 * Q + blk, hp * 256:(hp + 1) * 256], self.banks[bk][:, 0:256], AF.Copy,
                                 reads=[self.bankbuf[bk]], writes=[vb[4 * Q + blk]])
                wos = [self.wload(self.d_fwout[j], 2048) for j in range(4)]
                pi = 0
                if DBG <= 2:
                    continue
                for hh in range(4):
                    h = 4 * g + hh
                    bo = self.take_bank()
                    self.held.add(bo)
                    bd = self.take_bank()
                    self.held.add(bd)
                    def s_stage(j):
                        off = max(0, (j - 4 * Q) * 128)
                        bs = self.take_bank()
                        self.mm(bs, self.banks[bs][:, off:512], knT[:, hh, j * 128:(j + 1) * 128], qT[:, hh, off:512],
                                True, True, reads=[knb[hh][j // 4], qb[hh]])
                        return bs

                    def pv_stage(j, bs, pi_):
                        off = max(0, (j - 4 * Q) * 128)
                        p_, pbuf = pT[pi_ % NP], pb[pi_ % NP]
                        self.act(p_[:, off:512], self.banks[bs][:, off:512], AF.Exp, reads=[self.bankbuf[bs], cb], writes=[pbuf],
                                 bias=biasA[:, j, h:h + 1], scale=scale)
                        if j >= 4 * Q:
                            self.tt("pool", p_[:, off:off + 128], p_[:, off:off + 128], self.uincl_bf[:], ALU.mult,
                                    reads=[pbuf, self.kb], writes=[pbuf])
                        return p_, pbuf, off

                    def acc_stage(j, p_, pbuf, off):
                        self.mm(bo, self.banks[bo][:, off:512], vtok[:, j, hh * 128:(hh + 1) * 128], p_[:, off:512],
                                j == 0, j == nj - 1, reads=[vb[j], pbuf])
                        self.mm(bd, self.banks[bd][:, off:512], self.ones_bf[:], p_[:, off:512],
                                j == 0, j == nj - 1, reads=[pbuf, self.kb])
                    bs_next = s_stage(0)
                    for j in range(nj):
                        bs_cur = bs_next
                        pp = pv_stage(j, bs_cur, pi)
                        pi += 1
                        if j + 1 < nj:
                            bs_next = s_stage(j + 1)
                        acc_stage(j, *pp)
                    P.op("dve", lambda e, bd=bd: e.reciprocal(out=rden, in_=self.banks[bd][:]), reads=[self.bankbuf[bd]], writes=[rdb])
                    self.tt("dve", oT[:, hh, :], self.banks[bo][:], rden, ALU.mult, reads=[self.bankbuf[bo], rdb], writes=[ob[hh]])
                    self.held.discard(bo)
                    self.held.discard(bd)
                for dc in range(KC):
                    wo = self.ring[wos[dc // 2]][:].rearrange("p (k c) -> p k c", k=KC)
                    bk = self.take_bank()
                    for hh in range(4):
                        self.mm(bk, self.banks[bk][:], wo[:, 4 * g + hh, (dc % 2) * 128:(dc % 2 + 1) * 128], oT[:, hh, :],
                                hh == 0, hh == 3, reads=[self.slotbuf[wos[dc // 2]], ob[hh]])
                    self.tt("dve", self.xT[:, dc, sl], self.banks[bk][:], self.xT[:, dc, sl], ALU.add,
                            reads=[self.bankbuf[bk], self.xbuf[dc][Q]], writes=[self.xbuf[dc][Q]])


ALL_STAGES = [("conf", 0), ("ffn", 0), ("gdn", 1), ("ffn", 1), ("fox", 2), ("ffn", 2), ("conf", 3), ("ffn", 3)]


def build_nc(stages):
    nc = bass.Bass("TRN2", target_bir_lowering=False)
    k = K(nc, stages)
    k.build()
    return nc, k.used_inputs


def prep_shared(inp):
    sh = {}
    sh["cst"] = pack_consts(inp)
    sh["cwin"] = np.stack([tile_w(inp["conv_w_in"][i]).reshape(8, 128, 2048) for i in range(2)])
    sh["cwout"] = np.stack([tile_w(inp["conv_w_out"][i]).reshape(4, 128, 2048) for i in range(2)])
    gw = np.asarray(inp["gdn_w_in"][0], np.float32)
    sh["gwin"] = tile_w(gw[:, :4096]).reshape(16, 128, 2048)
    sh["gwab"] = np.ascontiguousarray(gw[:, 4096:4112].reshape(8, 128, 16).transpose(1, 0, 2)).reshape(128, 128)
    sh["gwout"] = tile_w(inp["gdn_w_out"][0]).reshape(4, 128, 2048)
    fw = np.asarray(inp["fox_w_in"][0], np.float32)
    sh["fwin"] = tile_w(fw[:, :3072]).reshape(12, 128, 2048)
    sh["fwf"] = np.ascontiguousarray(fw[:, 3072:3080].reshape(8, 128, 8).transpose(1, 0, 2)).reshape(128, 64)
    sh["fwout"] = tile_w(inp["fox_w_out"][0]).reshape(4, 128, 2048)
    sh["wup"] = np.stack([tile_w(inp["ffn_w_up"][l]).reshape(22, 128, 2048) for l in range(4)])
    sh["wdn"] = np.stack([tile_wk(inp["ffn_w_down"][l]).reshape(11, 128, 2048) for l in range(4)])
    return sh


def run(inp, stages, ncores=8, trace=False):
    x = np.asarray(inp["x"], np.float32)
    sh = prep_shared(inp)
    nc, used = build_nc(stages)
    sh = {k_: v for k_, v in sh.items() if k_ in used}
    in_maps = []
    for b in range(ncores):
        m = dict(sh)
        m["xT"] = np.ascontiguousarray(x[b].T).reshape(KC, 128, S)
        in_maps.append(m)
    res = run_bass_kernel_spmd(nc, in_maps, core_ids=list(range(ncores)), trace=trace)
    out = np.stack([np.asarray(r["yT"], np.float32).reshape(D, S).T for r in res.results])
    return out, res


def kernel(**inputs):
    out, _ = run(inputs, ALL_STAGES, ncores=8)
    return out.astype(np.float32)
```
